# Optimizing a Trainium2 kernel written in Bass

```python
import math
import jax, jax.numpy as jnp
from jax import lax
import numpy as np

D_MODEL = 1024
BATCH = 8
SEQ = 2048
DEPTH = 1
DEC_BATCH = 16
DEC_SEQ = 64
PAST_LEN = 1024

CHUNK = 64
Q_BLOCK = 128
SB_HEADS = 8
SB_HEAD_DIM = 64
SB_WIDTH = SB_HEADS * SB_HEAD_DIM
ML_HEADS = 4
ML_HEAD_DIM = 128
ML_WIDTH = ML_HEADS * ML_HEAD_DIM
CONV_W = 4
D_FF = -(-8 * D_MODEL // (3 * 256)) * 256
EPS = 1e-6
IN_SPLITS = (SB_WIDTH, SB_WIDTH, SB_WIDTH,
             2 * ML_WIDTH,
             ML_WIDTH, ML_WIDTH,
             ML_HEADS, ML_HEADS,
             D_MODEL, D_MODEL)
IN_WIDTH = 3 * SB_WIDTH + 4 * ML_WIDTH + 2 * ML_HEADS + 2 * D_MODEL

kernel_name = 'stickbreak_mlstm_parallel_adaln_stream_step'


def _rmsnorm(x, g):
    xf = x.astype(jnp.float32)
    y = xf * lax.rsqrt(jnp.mean(xf * xf, axis=-1, keepdims=True) + EPS) * g.astype(jnp.float32)
    return y.astype(x.dtype)


def _stick_breaking(q, k, v, q_start):
    B, L, H, dh = q.shape
    T = k.shape[1]
    blk = min(Q_BLOCK, L)
    nb = L // blk
    qb = jnp.moveaxis(q.reshape(B, nb, blk, H, dh), 1, 0)
    pos = (q_start + jnp.arange(L)).reshape(nb, blk)
    k_pos = jnp.arange(T)
    scale = 1.0 / math.sqrt(dh)

    def one_block(args):
        qi, pi = args
        z = jnp.einsum('bqhd,bkhd->bhqk', qi, k).astype(jnp.float32) * scale
        mask = k_pos[None, :] < pi[:, None]
        log_1mb = jnp.where(mask, jax.nn.log_sigmoid(-z), 0.0)
        tail = lax.cumsum(log_1mb, axis=3, reverse=True) - log_1mb
        a = jnp.where(mask, jnp.exp(jax.nn.log_sigmoid(z) + tail), 0.0)
        return jnp.einsum('bhqk,bkhd->bqhd', a.astype(v.dtype), v)

    out = lax.map(one_block, (qb, pos))
    return jnp.moveaxis(out, 0, 1).reshape(B, L, H * dh)


def _mlstm_chunk(carry, inp):
    C, n, m = carry
    q, k, v, log_i, log_f = inp
    L = q.shape[2]
    b = jnp.cumsum(log_f, axis=-1)
    tril = jnp.tril(jnp.ones((L, L), bool))
    d_log = b[..., :, None] - b[..., None, :] + log_i[..., None, :]
    d_log = jnp.where(tril, d_log, -jnp.inf)
    inter = b + m[..., None]
    m_t = jnp.maximum(inter, jnp.max(d_log, axis=-1))
    w_intra = jnp.exp(d_log - m_t[..., None])
    w_inter = jnp.exp(inter - m_t)
    s = jnp.einsum('bhtd,bhsd->bhts', q, k) * w_intra
    num = jnp.einsum('bhts,bhsd->bhtd', s, v) + w_inter[..., None] * jnp.einsum('bhvk,bhtk->bhtv', C, q)
    den = jnp.sum(s, axis=-1) + w_inter * jnp.einsum('bhk,bhtk->bht', n, q)
    h = num / jnp.maximum(jnp.abs(den), jnp.exp(-m_t))[..., None]
    b_last = b[..., -1]
    g = b_last[..., None] - b + log_i
    m_new = jnp.maximum(b_last + m, jnp.max(g, axis=-1))
    wg = jnp.exp(g - m_new[..., None])
    decay = jnp.exp(b_last + m - m_new)
    C_new = decay[..., None, None] * C + jnp.einsum('bhs,bhsv,bhsk->bhvk', wg, v, k)
    n_new = decay[..., None] * n + jnp.einsum('bhs,bhsk->bhk', wg, k)
    return (C_new, n_new, m_new), h


def _to_chunks(a, n_chunks, csz):
    B, H = a.shape[:2]
    a = a.reshape((B, H, n_chunks, csz) + a.shape[3:])
    return jnp.moveaxis(a, 2, 0)


def _mlstm(q, k, v, log_i, log_f, C0, n0, m0):
    B, L, H, d = q.shape
    f32 = jnp.float32
    qh, kh, vh = (jnp.swapaxes(a.astype(f32), 1, 2) for a in (q, k, v))
    li, lf = (jnp.swapaxes(a.astype(f32), 1, 2) for a in (log_i, log_f))
    csz = min(CHUNK, L)
    nc = L // csz
    xs = tuple(_to_chunks(a, nc, csz) for a in (qh, kh, vh, li, lf))
    carry0 = (C0.astype(f32), n0.astype(f32), m0.astype(f32))
    (C1, n1, m1), hs = lax.scan(_mlstm_chunk, carry0, xs)
    h = jnp.moveaxis(hs, 0, 2).reshape(B, H, L, d)
    return jnp.swapaxes(h, 1, 2), C1, n1, m1


def _layer(x, c, sb_k_past, sb_v_past, C0, n0, m0, conv0,
           norm1_g, norm2_g, w_ada, b_ada, w_in, b_if, w_conv, b_conv, ml_norm_g,
           w_a, w_b, w_out, w_ff_gate, w_ff_up, w_ff_down):
    B, L, _ = x.shape
    q_start = sb_k_past.shape[1]
    mod = jax.nn.silu(c) @ w_ada + b_ada
    sh1, sc1, g1, sh2, sc2, g2 = jnp.split(mod[:, None, :], 6, axis=-1)

    u = _rmsnorm(x, norm1_g) * (1 + sc1) + sh1
    proj = u @ w_in
    idx = np.cumsum(IN_SPLITS)[:-1]
    sq, sk, sv, mqk_pre, mv, mo, mi, mf, ga, gb = jnp.split(proj, idx, axis=-1)

    k_new = sk.reshape(B, L, SB_HEADS, SB_HEAD_DIM)
    v_new = sv.reshape(B, L, SB_HEADS, SB_HEAD_DIM)
    k_all = jnp.concatenate([sb_k_past.astype(k_new.dtype), k_new], axis=1)
    v_all = jnp.concatenate([sb_v_past.astype(v_new.dtype), v_new], axis=1)
    y_a = _stick_breaking(sq.reshape(B, L, SB_HEADS, SB_HEAD_DIM), k_all, v_all, q_start)

    xpad = jnp.concatenate([conv0.astype(mqk_pre.dtype), mqk_pre], axis=1)
    conv = b_conv + sum(xpad[:, j:j + L, :] * w_conv[j] for j in range(CONV_W))
    conv = jax.nn.silu(conv)
    conv_new = xpad[:, L:, :]
    mq, mk = jnp.split(conv, 2, axis=-1)
    mq = mq.reshape(B, L, ML_HEADS, ML_HEAD_DIM)
    mk = mk.reshape(B, L, ML_HEADS, ML_HEAD_DIM) * (1.0 / math.sqrt(ML_HEAD_DIM))
    mvh = mv.reshape(B, L, ML_HEADS, ML_HEAD_DIM)
    log_i = (mi + b_if[:ML_HEADS]).astype(jnp.float32)
    log_f = jax.nn.log_sigmoid((mf + b_if[ML_HEADS:]).astype(jnp.float32))
    h, C1, n1, m1 = _mlstm(mq, mk, mvh, log_i, log_f, C0, n0, m0)
    h = h * lax.rsqrt(jnp.mean(h * h, axis=-1, keepdims=True) + EPS)
    h = (h.reshape(B, L, ML_WIDTH) * ml_norm_g.astype(jnp.float32)).astype(x.dtype)
    y_b = jax.nn.sigmoid(mo) * h

    merged = jax.nn.sigmoid(ga) * (y_a @ w_a) + jax.nn.sigmoid(gb) * (y_b @ w_b)
    x = x + g1 * (merged @ w_out)

    u2 = _rmsnorm(x, norm2_g) * (1 + sc2) + sh2
    x = x + g2 * ((jax.nn.silu(u2 @ w_ff_gate) * (u2 @ w_ff_up)) @ w_ff_down)
    return x, (k_new, v_new, C1, n1, m1, conv_new)


def setup_inputs(seed: int = 0) -> dict:
    key = jax.random.key(seed)
    ks = iter(jax.random.split(key, 40))
    f32 = jnp.float32

    def nrm(shape, scale):
        return jax.random.normal(next(ks), shape, f32) * scale

    def gain(shape):
        return 1.0 + nrm(shape, 0.02)

    b_if = jnp.concatenate([nrm((DEPTH, ML_HEADS), 0.1),
                            jnp.linspace(3.0, 6.0, ML_HEADS, dtype=f32)[None, :] + nrm((DEPTH, ML_HEADS), 0.1)], axis=-1)
    return {
        'x_prompt': nrm((BATCH, SEQ, D_MODEL), 1.0),
        'x_sample': nrm((DEC_BATCH, DEC_SEQ, D_MODEL), 1.0),
        'c_prompt': nrm((BATCH, D_MODEL), 1.0),
        'c_sample': nrm((DEC_BATCH, D_MODEL), 1.0),
        'cache_sb_k': nrm((DEPTH, DEC_BATCH, PAST_LEN, SB_HEADS, SB_HEAD_DIM), 1.0),
        'cache_sb_v': nrm((DEPTH, DEC_BATCH, PAST_LEN, SB_HEADS, SB_HEAD_DIM), 1.0),
        'state_mlstm_C': nrm((DEPTH, DEC_BATCH, ML_HEADS, ML_HEAD_DIM, ML_HEAD_DIM), 0.3),
        'state_mlstm_n': nrm((DEPTH, DEC_BATCH, ML_HEADS, ML_HEAD_DIM), 0.3),
        'state_mlstm_m': nrm((DEPTH, DEC_BATCH, ML_HEADS), 1.0),
        'state_conv': nrm((DEPTH, DEC_BATCH, CONV_W - 1, 2 * ML_WIDTH), 1.0),
        'norm1_g': gain((DEPTH, D_MODEL)),
        'norm2_g': gain((DEPTH, D_MODEL)),
        'w_ada': nrm((DEPTH, D_MODEL, 6 * D_MODEL), 0.5 * D_MODEL ** -0.5),
        'b_ada': nrm((DEPTH, 6 * D_MODEL), 0.02),
        'w_in': nrm((DEPTH, D_MODEL, IN_WIDTH), D_MODEL ** -0.5),
        'b_if': b_if,
        'w_conv': nrm((DEPTH, CONV_W, 2 * ML_WIDTH), CONV_W ** -0.5),
        'b_conv': nrm((DEPTH, 2 * ML_WIDTH), 0.02),
        'ml_norm_g': gain((DEPTH, ML_WIDTH)),
        'w_a': nrm((DEPTH, SB_WIDTH, D_MODEL), SB_WIDTH ** -0.5),
        'w_b': nrm((DEPTH, ML_WIDTH, D_MODEL), ML_WIDTH ** -0.5),
        'w_out': nrm((DEPTH, D_MODEL, D_MODEL), D_MODEL ** -0.5),
        'w_ff_gate': nrm((DEPTH, D_MODEL, D_FF), D_MODEL ** -0.5),
        'w_ff_up': nrm((DEPTH, D_MODEL, D_FF), D_MODEL ** -0.5),
        'w_ff_down': nrm((DEPTH, D_FF, D_MODEL), D_FF ** -0.5),
        'final_g': gain((D_MODEL,)),
    }


def reference(x_prompt, x_sample, c_prompt, c_sample, cache_sb_k, cache_sb_v,
              state_mlstm_C, state_mlstm_n, state_mlstm_m, state_conv,
              norm1_g, norm2_g, w_ada, b_ada, w_in, b_if, w_conv, b_conv, ml_norm_g,
              w_a, w_b, w_out, w_ff_gate, w_ff_up, w_ff_down, final_g):
    f32 = jnp.float32
    bp = x_prompt.shape[0]
    hp, hs = x_prompt, x_sample
    p_states, s_states = [], []
    for l in range(DEPTH):
        params = (norm1_g[l], norm2_g[l], w_ada[l], b_ada[l], w_in[l], b_if[l], w_conv[l], b_conv[l],
                  ml_norm_g[l], w_a[l], w_b[l], w_out[l], w_ff_gate[l], w_ff_up[l], w_ff_down[l])
        empty_kv = jnp.zeros((bp, 0, SB_HEADS, SB_HEAD_DIM), x_prompt.dtype)
        hp, sp = _layer(hp, c_prompt, empty_kv, empty_kv,
                        jnp.zeros((bp, ML_HEADS, ML_HEAD_DIM, ML_HEAD_DIM), f32),
                        jnp.zeros((bp, ML_HEADS, ML_HEAD_DIM), f32),
                        jnp.zeros((bp, ML_HEADS), f32),
                        jnp.zeros((bp, CONV_W - 1, 2 * ML_WIDTH), x_prompt.dtype),
                        *params)
        hs, ss = _layer(hs, c_sample, cache_sb_k[l], cache_sb_v[l], state_mlstm_C[l],
                        state_mlstm_n[l], state_mlstm_m[l], state_conv[l], *params)
        p_states.append(sp)
        s_states.append(ss)
    y_prompt = _rmsnorm(hp, final_g)
    y_sample = _rmsnorm(hs, final_g)
    pk, pv, pC, pn, pm, pconv = (jnp.stack([st[i] for st in p_states]) for i in range(6))
    sk, sv, sC, sn, sm, sconv = (jnp.stack([st[i] for st in s_states]) for i in range(6))
    return (y_prompt, y_sample, pk, pv, pC, pn, pm, pconv, sk, sv, sC, sn, sm, sconv)
```

```python
import numpy as np
from contextlib import ExitStack
import concourse.bass as bass
import concourse.mybir as mybir
from concourse.bass_utils import run_bass_kernel_spmd

F32 = mybir.dt.float32
BF16 = mybir.dt.bfloat16
AF = mybir.ActivationFunctionType
ALU = mybir.AluOpType
AX = mybir.AxisListType

D = 1024
SEQ = 2048
NS = 2
LS = 64
PAST = 1024
DFF = 2816
INW = 5640
EPS = 1e-6
NROWS = SEQ + NS * LS


STRICT_SAME_ENGINE = True


class Buf:
    __slots__ = ("name", "last_w", "readers", "excl")

    def __init__(self, name, excl=False):
        self.name = name
        self.last_w = None
        self.readers = []
        self.excl = excl


class Op:
    __slots__ = ("eng", "fn", "reads", "writes", "dma", "seq", "signal", "waits",
                 "clock", "count", "idx")


class Prog:
    ENG = ("pe", "act", "dve", "pool", "sp")

    def __init__(self, nc):
        self.nc = nc
        self.ops = []
        self.e = {"pe": nc.tensor, "act": nc.scalar, "dve": nc.vector,
                  "pool": nc.gpsimd, "sp": nc.sync}

    def op(self, eng, fn, reads=(), writes=(), dma=None):
        o = Op()
        o.eng = eng
        o.fn = fn
        o.reads = [b for b in reads if b is not None and not b.excl]
        o.writes = [b for b in writes if b is not None] + [b for b in reads if b is not None and b.excl]
        o.dma = dma
        o.signal = False
        o.waits = []
        o.idx = len(self.ops)
        self.ops.append(o)
        return o

    def fence(self):
        last = {}
        for o in self.ops:
            last[o.eng if o.dma is None else "d:" + o.dma] = o
        return list(last.values())

    def lower(self, sem_ctx):
        ops = self.ops
        seqc = {k: 0 for k in self.ENG}
        dmac = {}
        eclock = {k: {} for k in self.ENG}
        for o in ops:
            if o.dma is None:
                seqc[o.eng] += 1
                o.seq = seqc[o.eng]
            else:
                dmac[o.dma] = dmac.get(o.dma, 0) + 1
                o.seq = dmac[o.dma]
            deps = {}
            for b in o.reads:
                if b.last_w is not None:
                    deps[b.last_w.idx] = b.last_w
            for b in o.writes:
                if b.last_w is not None:
                    deps[b.last_w.idx] = b.last_w
                for r in b.readers:
                    deps[r.idx] = r
            clk = eclock[o.eng]
            for p in deps.values():
                if p is o:
                    continue
                if p.dma is None:
                    key = p.eng
                    need = p.seq
                    if p.eng == o.eng and o.dma is None:
                        if o.eng == "pe":
                            continue
                        if (not STRICT_SAME_ENGINE) and o.eng != "pool" and not any(b.last_w is p for b in o.reads):
                            continue
                else:
                    key = "d:" + p.dma
                    need = dmac[p.dma] if not (o.dma == p.dma) else dmac[p.dma] - 1
                if clk.get(key, 0) >= need:
                    continue
                if p.dma is None:
                    p.signal = True
                o.waits.append((p, need))
                for k2, v2 in p.clock.items():
                    if clk.get(k2, 0) < v2:
                        clk[k2] = v2
                if clk.get(key, 0) < need:
                    clk[key] = need
            myclk = dict(clk)
            mykey = o.eng if o.dma is None else "d:" + o.dma
            myclk[mykey] = max(myclk.get(mykey, 0), o.seq)
            o.clock = myclk
            for b in o.reads:
                b.readers.append(o)
            for b in o.writes:
                b.last_w = o
                b.readers = []
        cnt = {k: 0 for k in self.ENG}
        for o in ops:
            if o.dma is None and o.signal:
                cnt[o.eng] += 1
                o.count = cnt[o.eng]
        esem = {}
        dsem = {}
        dtot = {}
        n_waits = 0

        def get_e(k):
            if k not in esem:
                esem[k] = sem_ctx("e_" + k)
            return esem[k]

        def get_d(k):
            if k not in dsem:
                dsem[k] = sem_ctx("d_" + k)
            return dsem[k]

        for o in ops:
            eng = self.e[o.eng]
            need = {}
            for p, nd in o.waits:
                if p.dma is None:
                    s = get_e(p.eng)
                    v = p.count
                else:
                    s = get_d(p.dma)
                    v = 16 * nd
                k = id(s)
                if k not in need or need[k][1] < v:
                    need[k] = (s, v)
            for s, v in need.values():
                eng.wait_ge(s, v)
                n_waits += 1
            ins = o.fn()
            if o.dma is not None:
                ins.then_inc(get_d(o.dma), 16)
                dtot[o.dma] = dtot.get(o.dma, 0) + 16
            elif o.signal:
                ins.then_inc(get_e(o.eng), 1)
        for k, s in dsem.items():
            self.e["sp"].wait_ge(s, dtot[k])
        self.stats = dict(n_ops=len(ops), n_waits=n_waits, n_dsem=len(dsem),
                          sig={k: cnt[k] for k in cnt})


CFG = {"phases": ("0", "1a", "1b", "2a", "2b"), "tiles": None}


def build_program():
    nc = bass.Bass("TRN2", target_bir_lowering=False)
    PH = CFG["phases"]

    def din(name, shape):
        return nc.dram_tensor(name, list(shape), F32, kind="ExternalInput").ap()

    def dout(name, shape):
        return nc.dram_tensor(name, list(shape), F32, kind="ExternalOutput").ap()

    xall = din("xall", [NROWS, D])
    c3 = din("c3", [3, D])
    cache_k = din("cache_k", [NS, PAST, 512])
    cache_v = din("cache_v", [NS, PAST, 512])
    C0 = din("C0", [NS, 4, 128, 128])
    n0 = din("n0", [NS, 4, 128])
    m0 = din("m0", [NS, 4])
    conv0 = din("conv0", [NS, 3, D])
    norm1_g = din("norm1_g", [1, D])
    norm2_g = din("norm2_g", [1, D])
    w_ada = din("w_ada", [D, 6 * D])
    b_ada = din("b_ada", [1, 6 * D])
    w_in = din("w_in", [D, INW])
    b_if = din("b_if", [8, 1])
    w_conv = din("w_conv", [4, D])
    b_conv = din("b_conv", [1, D])
    ml_norm_g = din("ml_norm_g", [1, 512])
    w_a = din("w_a", [512, D])
    w_b = din("w_b", [512, D])
    w_out = din("w_out", [D, D])
    w_ff_gate = din("w_ff_gate", [D, DFF])
    w_ff_up = din("w_ff_up", [D, DFF])
    w_ff_down = din("w_ff_down", [DFF, D])
    final_g = din("final_g", [1, D])

    y_all = dout("y_all", [NROWS, D])
    k_all = dout("k_all", [NROWS, 512])
    v_all = dout("v_all", [NROWS, 512])
    C_out = dout("C_out", [3, 4, 128, 128])
    n_out = dout("n_out", [3, 4, 128])
    m_out = dout("m_out", [3, 4])
    conv_out = dout("conv_out", [3, 3, D])

    mod_sc = nc.dram_tensor("mod_sc", [3, 6 * D], F32).ap()
    A_sc = nc.dram_tensor("A_sc", [NROWS, D], F32).ap()
    x1_sc = nc.dram_tensor("x1_sc", [NROWS, D], F32).ap()
    h_sc = nc.dram_tensor("h_sc", [NROWS, DFF], BF16).ap()

    tiles = [(0, t * 128, 128, t) for t in range(16)] + [(1, SEQ, 64, 0), (2, SEQ + 64, 64, 0)]
    if CFG["tiles"] is not None:
        tiles = [tiles[i] for i in CFG["tiles"]]

    with ExitStack() as st:
        P = Prog(nc)

        def sb(name, shape, dt=F32):
            return st.enter_context(nc.sbuf_tensor(name, list(shape), dt)), Buf(name)

        def ps(name, shape, dt=F32):
            return st.enter_context(nc.psum_tensor(name, list(shape), dt)), Buf(name, excl=True)

        SCRN = 20736
        SCR = st.enter_context(nc.sbuf_tensor("SCR", [128, SCRN], F32))
        scr = {"off": 0, "fence": []}

        def phase_begin():
            scr["off"] = 0
            scr["fence"] = P.fence()

        def fb(name):
            b = Buf(name)
            b.readers = list(scr["fence"])
            return b

        def cv(name, shape, dt=F32):
            n = 1
            for d_ in shape[1:]:
                n *= d_
            nf = (n + 1) // 2 if dt == BF16 else n
            nf = (nf + 7) // 8 * 8
            off = scr["off"]
            scr["off"] += nf
            assert scr["off"] <= SCRN, (name, scr["off"])
            v = SCR[0:shape[0], off:off + nf]
            if dt == BF16:
                v = v.bitcast(BF16)
            v = v[:, 0:n]
            if len(shape) == 3:
                v = v.rearrange("p (a b) -> p a b", a=shape[1])
            return v, fb(name)

        def E(eng, name, reads, writes, *a, **kw):
            m = getattr(P.e[eng], name)
            return P.op(eng, lambda: m(*a, **kw), reads, writes)

        def DMA(eng, key, out, in_, reads, writes, **kw):
            m = P.e[eng].dma_start
            return P.op(eng, lambda: m(out=out, in_=in_, **kw), reads, writes, dma=key)

        def MM(out, lhsT, rhs, start, stop, reads, writes):
            m = nc.tensor.matmul
            return P.op("pe", lambda: m(out, lhsT=lhsT, rhs=rhs, start=start, stop=stop), reads, writes)

        def TR(out, in_, ident, reads, writes):
            m = nc.tensor.transpose
            return P.op("pe", lambda: m(out=out, in_=in_, identity=ident), reads, writes)

        def ACT(out, in_, func, reads, writes, **kw):
            m = nc.scalar.activation
            return P.op("act", lambda: m(out=out, in_=in_, func=func, **kw), reads, writes)

        WB, bWB = sb("WB", [128, 46080], BF16)
        identb, bidb = sb("identb", [128, 128], BF16)
        identf, bidf = sb("identf", [128, 128], F32)
        onesb, bones = sb("onesb", [128, 512], BF16)
        zer, bzer = sb("zer", [128, 128], F32)
        sel, bsel = sb("sel", [4, 4, 128], F32)
        rstd2, brstd2 = sb("rstd2", [128, 18], F32)
        csT, bcsT = sb("csT", [128, 8, 3], BF16)
        mhalf, bmhalf = sb("mhalf", [128, 4], F32)
        cmask, bcmask = sb("cmask", [128, 128], BF16)
        gmb, bgmb = sb("gmb", [128, D], F32)
        shb, bshb = sb("shb", [128, D], F32)
        ggb, bggb = sb("ggb", [128, D], F32)
        xts = [sb("xt%d" % i, [128, D], F32) for i in range(2)]
        T1, bT1 = sb("T1", [128, D], F32)
        T2, bT2 = sb("T2", [128, D], F32)
        ub, bub = sb("ub", [128, D], BF16)
        uT, buT = sb("uT", [128, 8, 128], BF16)
        ssq, bssq = sb("ssq", [128, 4], F32)
        stg = []

        ptr, bptr = ps("ptr", [128, 1024], BF16)
        pat, bpat = ps("pat", [128, 1024], BF16)
        pa, bpa = ps("pa", [128, 512], F32)
        pb, bpb = ps("pb", [128, 512], F32)
        py, bpy = ps("py", [128, 512], F32)
        pm, bpm = ps("pm", [128, 512], F32)
        pz2, bpz2 = ps("pz2", [128, 1024], F32)
        bpz = [Buf("pz0", excl=True), Buf("pz1", excl=True)]
        pab = [(pa, bpa), (pb, bpb)]
        rot = {"ab": 0, "stg": 0, "x": 0}

        def next_ab():
            rot["ab"] = (rot["ab"] + 1) % len(pab)
            return pab[rot["ab"]]

        def next_stg():
            rot["stg"] = (rot["stg"] + 1) % 2
            return stg[rot["stg"]]

        E("pool", "memset", [], [bidf], identf[:], 1.0)
        E("pool", "affine_select", [bidf], [bidf], out=identf[:], in_=identf[:], pattern=[[-1, 128]],
          compare_op=ALU.is_equal, fill=0.0, base=0, channel_multiplier=1)
        E("pool", "tensor_copy", [bidf], [bidb], out=identb[:], in_=identf[:])
        E("pool", "memset", [], [bones], onesb[:], 1.0)
        E("pool", "memset", [], [bzer], zer[:], 0.0)
        E("pool", "memset", [], [bsel], sel[:], 1.0)
        E("pool", "affine_select", [bsel], [bsel], out=sel[:], in_=sel[:], pattern=[[-1, 4], [0, 128]],
          compare_op=ALU.is_equal, fill=0.0, base=0, channel_multiplier=1)
        E("pool", "memset", [], [brstd2], rstd2[:], 1.0)
        E("pool", "memset", [], [bmhalf], mhalf[:], -0.5)
        E("pool", "memset", [], [bcmask], cmask[:], -30000.0)
        E("pool", "affine_select", [bcmask], [bcmask], out=cmask[:], in_=cmask[:], pattern=[[1, 128]],
          compare_op=ALU.is_ge, fill=0.0, base=0, channel_multiplier=-1)

        def rstd_pow(out_ap, tmp_ap, ss_ap, npart, ncol, scale, rds, wrs):
            E("pool", "tensor_scalar", rds, wrs, out=tmp_ap, in0=ss_ap, scalar1=scale, scalar2=EPS, op0=ALU.mult, op1=ALU.add)
            E("pool", "tensor_tensor", wrs + [bmhalf], wrs, out=out_ap, in0=tmp_ap, in1=mhalf[0:npart, 0:ncol], op=ALU.pow)

        def load_w(dram, r0, nrows_chunks, c0, ncols, off, key):
            bufs = []
            for kc in range(nrows_chunks):
                for cc in range(0, ncols, 2048):
                    n = min(2048, ncols - cc)
                    bw = fb("W%s_%d_%d" % (key, kc, cc))
                    bufs.append(bw)
                    DMA("pool", "W" + key, WB[:, off + kc * ncols + cc: off + kc * ncols + cc + n],
                        dram[r0 + kc * 128: r0 + (kc + 1) * 128, c0 + cc: c0 + cc + n], [], [bw])
            return bufs

        def load_wg(dram, nrows_chunks, c0, groups, stride, off, kp, kbase=0):
            out = {}
            dv = dram.rearrange("(k p) n -> p k n", p=128)
            wv = WB[:, off: off + nrows_chunks * stride].rearrange("p (k c) -> p k c", k=nrows_chunks)
            for gi, (nm, l0, ncols) in enumerate(groups):
                bufs = []
                for cc in range(0, ncols, 2048):
                    n = min(2048, ncols - cc)
                    bw = fb("W%s_%s_%d" % (kp, nm, cc))
                    bufs.append(bw)
                    DMA("pool", "Wg%d" % (kbase + gi), wv[:, :, l0 + cc: l0 + cc + n],
                        dv[:, :, c0 + l0 + cc: c0 + l0 + cc + n], [], [bw])
                out[nm] = bufs
            return out

        def wview(off, nk, ncols):
            return WB[:, off: off + nk * ncols].rearrange("p (k c) -> p k c", k=nk)

        scr["fence"] = []
        WL_1a = load_wg(w_in, 8, 0, [("q", 0, 512), ("k", 512, 512), ("v", 1024, 512)], 1536, 0, "a")
        WL_1a["a"] = load_wg(w_a, 4, 0, [("a", 0, 1024)], 1024, 12288, "wa", kbase=3)["a"]

        phase_begin()
        stg[:] = [cv("stg%d" % i, [128, 512], F32) for i in range(2)]
        cT, bcT = cv("cT", [128, 8, 3], F32)
        for s_ in range(3):
            DMA("sp", "cst", cT[:, :, s_], c3[s_].rearrange("(k p) -> p k", p=128), [], [bcT],
                allow_slow_non_contiguous=True)
        ACT(csT[:], cT[:], AF.Silu, [bcT], [bcsT])
        WAs = [cv("WA%d" % i, [128, 8, 512], BF16) for i in range(2)]
        w_ada_v = w_ada.rearrange("(k p) n -> p k n", p=128)
        bmods = {"A": Buf("modA"), "B": Buf("modB"), "C": Buf("modC")}

        def mod_group(nch):
            return "A" if nch < 4 else ("B" if nch < 6 else "C")

        def mod_chunk_load(nch, WA, bWA, key):
            DMA("pool", key, WA[:], w_ada_v[:, :, nch * 512:(nch + 1) * 512], [], [bWA])

        def mod_chunk_compute(nch, WA, bWA, pp, bpp):
            sg, bsg = next_stg()
            DMA("sp", "bad", sg[0:3, :], b_ada[0:1, nch * 512:(nch + 1) * 512].broadcast_to([3, 512]), [], [bsg])
            for kc in range(8):
                MM(pp[0:3, :], csT[:, kc, :], WA[:, kc, :], kc == 0, kc == 7, [bcsT, bWA], [bpp])
            E("dve", "tensor_tensor", [bpp, bsg], [bsg], out=sg[0:3, :], in0=pp[0:3, :], in1=sg[0:3, :], op=ALU.add)
            DMA("sp", "modw", mod_sc[:, nch * 512:(nch + 1) * 512], sg[0:3, :], [bsg], [bmods[mod_group(nch)]])

        for nch in range(4 if "0" in PH else 0):
            WA, bWA = WAs[nch % 2]
            mod_chunk_load(nch, WA, bWA, "wa%d" % (nch % 2))
            mod_chunk_compute(nch, WA, bWA, pm, bpm)

        def load_mod(s, which, front=True, gate=True, gdst=None):
            base = 0 if which == 1 else 3 * D
            ng = norm1_g if which == 1 else norm2_g
            bf_ = bmods["A"] if which == 1 else bmods["C"]
            bg_ = bmods["B"] if which == 1 else bmods["C"]
            if front:
                DMA("sp", "modr", shb[:], mod_sc[s:s + 1, base:base + D].broadcast_to([128, D]), [bf_], [bshb])
                DMA("sp", "modr", gmb[:], mod_sc[s:s + 1, base + D:base + 2 * D].broadcast_to([128, D]), [bf_], [bgmb])
                DMA("sp", "modr", T2[:], ng[0:1, :].broadcast_to([128, D]), [], [bT2])
                E("dve", "scalar_tensor_tensor", [bgmb, bT2], [bgmb], out=gmb[:], in0=gmb[:], scalar=1.0, in1=T2[:],
                  op0=ALU.add, op1=ALU.mult)
            if gate:
                gd, bgd = gdst if gdst is not None else (ggb, bggb)
                DMA("sp", "modg", gd[:], mod_sc[s:s + 1, base + 2 * D:base + 3 * D].broadcast_to([128, D]), [bg_], [bgd])

        def rms_to_uT(xt, bxt, ntok, rstd_ap=None, dst=None):
            if rstd_ap is None:
                ACT(T2[0:ntok, :], xt[0:ntok, :], AF.Square, [bxt], [bT2, bssq], accum_out=ssq[0:ntok, 0:1])
                rstd_pow(ssq[0:ntok, 2:3], ssq[0:ntok, 1:2], ssq[0:ntok, 0:1], ntok, 1, 1.0 / D, [bssq], [bssq])
                rstd_ap = ssq[0:ntok, 2:3]
                rb = bssq
            else:
                rb = brstd2
            E("dve", "scalar_tensor_tensor", [bxt, rb, bgmb], [bT1], out=T1[0:ntok, :], in0=xt[0:ntok, :],
              scalar=rstd_ap, in1=gmb[0:ntok, :], op0=ALU.mult, op1=ALU.mult)
            E("pool", "tensor_tensor", [bT1, bshb], [bub], out=ub[0:ntok, :], in0=T1[0:ntok, :], in1=shb[0:ntok, :],
              op=ALU.add)
            for kc in range(8):
                TR(ptr[:, kc * 128: kc * 128 + ntok], ub[0:ntok, kc * 128:(kc + 1) * 128], identb[0:ntok, 0:ntok],
                   [bub, bidb], [bptr])
            uTd, buTd = dst if dst is not None else (uT, buT)
            ACT(uTd[:, :, 0:ntok], ptr[:].rearrange("p (k t) -> p k t", k=8)[:, :, 0:ntok], AF.Copy, [bptr], [buTd])

        epst, bepst = sb("epst", [128, 1], F32)
        E("pool", "memset", [], [bepst], epst[:], EPS)
        EPS_AP = epst

        def load_x(src, row0, ntok):
            rot["x"] ^= 1
            xt, bxt = xts[rot["x"]]
            DMA("sp", "xl%d" % rot["x"], xt[0:ntok, :], src[row0:row0 + ntok, :], [], [bxt])
            return xt, bxt

        def proj_tok(w3, c0, n, ntok, t0=0, wl=(), us=None):
            pp, bpp = next_ab()
            for kc in range(8):
                uTs_, buTs_ = us if us is not None else (uT, buT)
                MM(pp[0:ntok, 0:n], uTs_[:, kc, t0:t0 + ntok], w3[:, kc, c0:c0 + n], kc == 0, kc == 7, [buTs_] + list(wl), [bpp])
            return pp, bpp

        phase_begin()
        WL = WL_1a
        WLb_pre = load_wg(w_b, 4, 0, [("b", 0, 1024)], 1024, 32832, "wb", kbase=6)["b"]
        WLout_pre = load_wg(w_out, 8, 0, [("o", 0, 1024)], 1024, 36928, "wo", kbase=7)["o"]
        win_a = wview(0, 8, 1536)
        wa3 = wview(12288, 4, 1024)
        KTm = WB[:, 16384:24576].rearrange("p (c k) -> p c k", c=4)
        Vm = WB[:, 24576:32768].rearrange("p (t c) -> p t c", t=16)
        stor_main = dict(KT=KTm, Vst=Vm, bKT=[fb("KT%d" % i) for i in range(16)], bV=[fb("V%d" % i) for i in range(16)])
        KTa, _ = cv("KTalt", [128, 4, 1152], BF16)
        Va, _ = cv("Valt", [128, 9, 512], BF16)
        stor_alt = dict(KT=KTa, Vst=Va, bKT=[fb("KTa%d" % i) for i in range(9)], bV=[fb("Va%d" % i) for i in range(9)])
        stg[:] = [cv("stg%d" % i, [128, 512], F32) for i in range(2)]
        qTzs = [cv("qTz%d" % i, [128, 8, 128], BF16) for i in range(2)]
        for qz, bqz in qTzs:
            E("pool", "memset", [], [bqz], qz[:], 0.0)
        Ktok, bKtok = cv("Ktok", [128, 8, 512], BF16)
        NSL = 5
        att = [dict(g=cv("ag%d" % i, [128, 520], F32), Pb=cv("aP%d" % i, [128, 520], F32),
                    a=cv("aa%d" % i, [128, 512], BF16),
                    aT=cv("aaT%d" % i, [128, 4, 128], BF16)) for i in range(NSL)]
        ONE_REG = nc.gpsimd.to_reg(1.0)
        zerob, bzerob = cv("zerob", [128, 520], BF16)
        E("pool", "memset", [], [bzerob], zerob[:], 0.0)
        for A_ in att:
            E("pool", "memset", [], [A_["g"][1]], A_["g"][0][:], 1.0)
        ya, bya = cv("ya", [128, 512], BF16)
        yaT, byaT = cv("yaT", [128, 4, 128], BF16)
        Ast, bAst = cv("Ast", [128, D], F32)
        WAbg, bWAbg = cv("WAbg", [128, 8, 512], BF16)
        pzv = [pz2[:, 0:512], pz2[:, 512:1024]]
        patv2 = [(pat, bpat), (pm[:].bitcast(BF16), bpm)]
        job_ctr = [0]
        Abuf = {}
        x1buf = {}
        seq_loaded = [-1]

        def prologue_pieces(tile, tno, stor, first_of_seq):
            (s, row0, ntok, ti) = tile
            kpos0 = ti * 128 if s == 0 else PAST
            ktile = ti if s == 0 else 8
            qTz, bqTz = qTzs[tno % 2]
            KT, Vst, bKT, bV = stor["KT"], stor["Vst"], stor["bKT"], stor["bV"]
            cx = dict(s=s, row0=row0, ntok=ntok, kpos0=kpos0, ktile=ktile, qTz=qTz, bqTz=bqTz, stor=stor)
            hold = {}

            xslot = tno % 2
            xt, bxt = xts[xslot]
            hold = {}

            def PL():
                if first_of_seq:
                    load_mod(s, 1, gate=False)
                    if s > 0:
                        si = s - 1
                        DMA("pool", "kvc", Vst[:, 0:8, :], cache_v[si].rearrange("(k p) c -> p k c", p=128), [], bV[0:8])
                        DMA("pool", "kvc", Ktok[:], cache_k[si].rearrange("(k p) c -> p k c", p=128), [], [bKtok])
                DMA("sp", "xl%d" % xslot, xt[0:ntok, :], xall[row0:row0 + ntok, :], [], [bxt])

            def PK():
                for kt in range(8):
                    for c in range(4):
                        TR(ptr[:, c * 128:(c + 1) * 128], Ktok[:, kt, c * 128:(c + 1) * 128], identb[:],
                           [bKtok, bidb], [bptr])
                    ACT(KT[:, :, kt * 128:(kt + 1) * 128], ptr[:, 0:512].rearrange("p (c t) -> p c t", c=4), AF.Copy,
                        [bptr], [bKT[kt]])

            def PA():
                ACT(T2[0:ntok, :], xt[0:ntok, :], AF.Square, [bxt], [bT2, bssq], accum_out=ssq[0:ntok, 0:1])
                rstd_pow(ssq[0:ntok, 2:3], ssq[0:ntok, 1:2], ssq[0:ntok, 0:1], ntok, 1, 1.0 / D, [bssq], [bssq])

            def PB():
                E("dve", "scalar_tensor_tensor", [bxt, bssq, bgmb], [bT1], out=T1[0:ntok, :], in0=xt[0:ntok, :],
                  scalar=ssq[0:ntok, 2:3], in1=gmb[0:ntok, :], op0=ALU.mult, op1=ALU.mult)

            def PC():
                E("pool", "tensor_tensor", [bT1, bshb], [bub], out=ub[0:ntok, :], in0=T1[0:ntok, :], in1=shb[0:ntok, :],
                  op=ALU.add)

            def PD():
                for kc in range(8):
                    TR(ptr[:, kc * 128: kc * 128 + ntok], ub[0:ntok, kc * 128:(kc + 1) * 128], identb[0:ntok, 0:ntok],
                       [bub, bidb], [bptr])
                ACT(uT[:, :, 0:ntok], ptr[:].rearrange("p (k t) -> p k t", k=8)[:, :, 0:ntok], AF.Copy, [bptr], [buT])

            def Q1():
                pp, bpp = next_ab()
                hold["q"] = (pp, bpp)
                for c in range(4):
                    for kc in range(8):
                        MM(pp[:, c * 128: c * 128 + ntok], win_a[:, kc, c * 128:(c + 1) * 128], uT[:, kc, 0:ntok],
                           kc == 0, kc == 7, [buT] + WL["q"], [bpp])
                ppv = pp[:].rearrange("p (c t) -> p c t", c=4)
                qv = qTz[:].rearrange("p (c two) t -> p c two t", two=2)
                ACT(qv[0:64, :, 0, 0:ntok], ppv[0:64, :, 0:ntok], AF.Copy, [bpp], [bqTz])
                E("dve", "tensor_copy", [bpp], [bqTz], out=qv[64:128, :, 1, 0:ntok], in_=ppv[64:128, :, 0:ntok])

            def K1():
                pp, bpp = next_ab()
                for c in range(4):
                    for kc in range(8):
                        MM(pp[:, c * 128: c * 128 + ntok], win_a[:, kc, 512 + c * 128: 512 + (c + 1) * 128], uT[:, kc, 0:ntok],
                           kc == 0, kc == 7, [buT] + WL["k"], [bpp])
                ACT(KT[:, :, kpos0:kpos0 + ntok], pp[:].rearrange("p (c t) -> p c t", c=4)[:, :, 0:ntok], AF.Copy,
                    [bpp], [bKT[ktile]])

            def K2():
                pp, bpp = proj_tok(win_a, 512, 512, ntok, wl=WL["k"])
                sg, bsg = next_stg()
                E("dve", "tensor_copy", [bpp], [bsg], out=sg[0:ntok, :], in_=pp[0:ntok, :])
                DMA("sp", "ko", k_all[row0:row0 + ntok, :], sg[0:ntok, :], [bsg], [])

            def V1():
                pp, bpp = proj_tok(win_a, 1024, 512, ntok, wl=WL["v"])
                sg, bsg = next_stg()
                E("dve", "tensor_copy", [bpp], [bsg], out=sg[0:ntok, :], in_=pp[0:ntok, :])
                ACT(Vst[0:ntok, ktile, :], pp[0:ntok, :], AF.Copy, [bpp], [bV[ktile]])
                DMA("sp", "vo", v_all[row0:row0 + ntok, :], sg[0:ntok, :], [bsg], [])

            pk = PK if (first_of_seq and s > 0) else None
            return cx, [PL, None, pk, PA, PB, PC, None, PD, None, Q1, None, K1, None, K2, None, V1]

        def attention_jobs(cx):
            ntok, kpos0, qTz, bqTz = cx["ntok"], cx["kpos0"], cx["qTz"], cx["bqTz"]
            KT, Vst, bKT, bV = cx["stor"]["KT"], cx["stor"]["Vst"], cx["stor"]["bKT"], cx["stor"]["bV"]
            nk = kpos0 + ntok
            nblk = (nk + 511) // 512
            jobs = []
            for h in range(8):
                prev = None
                for b in range(nblk - 1, -1, -1):
                    job_ctr[0] += 1
                    J = dict(h=h, b=b, slot=job_ctr[0] % NSL, zi=job_ctr[0] % 2, prev=prev,
                             first=(b == nblk - 1), last=(b == 0))
                    jobs.append(J)
                    prev = J

            def geo(J):
                kb0 = J["b"] * 512
                nkb = min(512, nk - kb0)
                kts = list(range(kb0 // 128, (kb0 + nkb + 127) // 128))
                return kb0, nkb, kts

            def S0(J):
                kb0, nkb, kts = geo(J)
                MM(pzv[J["zi"]][0:ntok, 0:nkb], qTz[:, J["h"], 0:ntok], KT[:, J["h"] // 2, kb0:kb0 + nkb], True, not J["first"],
                   [bqTz] + [bKT[k] for k in kts], [bpz[J["zi"]]])
                if J["first"]:
                    MM(pzv[J["zi"]][0:ntok, nkb - ntok:nkb], identb[0:ntok, 0:ntok], cmask[0:ntok, 0:ntok], False, True,
                       [bidb, bcmask], [bpz[J["zi"]]])

            def S1(J):
                kb0, nkb, kts = geo(J)
                g_, bg = att[J["slot"]]["g"]
                pz = pzv[J["zi"]]
                ACT(g_[0:ntok, 512 - nkb:512], pz[0:ntok, 0:nkb], AF.Sigmoid, [bpz[J["zi"]]], [bg], scale=-0.125)

            def S2(J):
                kb0, nkb, kts = geo(J)
                A_ = att[J["slot"]]
                (g_, bg), (Pb, bP) = A_["g"], A_["Pb"]
                if J["prev"] is None:
                    init = 1.0
                    rd = [bg, bzerob]
                else:
                    pPb, bpP = att[J["prev"]["slot"]]["Pb"]
                    pk = geo(J["prev"])[1]
                    init = pPb[0:ntok, 512 - pk:512 - pk + 1]
                    rd = [bg, bzerob, bpP]
                E("dve", "tensor_tensor_scan", rd, [bP], out=Pb[0:ntok, 512 - nkb:513][:, ::-1],
                  data0=g_[0:ntok, 512 - nkb:513][:, ::-1], data1=zerob[0:ntok, 0:nkb + 1], initial=init,
                  op0=ALU.mult, op1=ALU.add)

            def S3(J):
                pass

            def S4(J):
                kb0, nkb, kts = geo(J)
                A_ = att[J["slot"]]
                (Pb, bP), (a_, ba) = A_["Pb"], A_["a"]
                E("pool", "tensor_tensor", [bP], [ba], out=a_[0:ntok, 0:nkb], in0=Pb[0:ntok, 512 - nkb + 1:513],
                  in1=Pb[0:ntok, 512 - nkb:512], op=ALU.subtract)

            def S5(J):
                kb0, nkb, kts = geo(J)
                a_, ba = att[J["slot"]]["a"]
                pT, bpT = patv2[J["zi"]]
                for j, kt in enumerate(kts):
                    ksz = min(128, nk - kt * 128)
                    TR(pT[0:ksz, j * 128: j * 128 + ntok], a_[0:ntok, j * 128: j * 128 + ksz],
                       identb[0:ntok, 0:ntok], [ba, bidb], [bpT])

            def S6(J):
                kb0, nkb, kts = geo(J)
                aT_, baT = att[J["slot"]]["aT"]
                pT, bpT = patv2[J["zi"]]
                nsub = len(kts)
                pv = pT[:, 0:512].rearrange("p (j t) -> p j t", j=4)
                lastk = min(128, nk - kts[-1] * 128)
                if lastk == 128:
                    ACT(aT_[:, 0:nsub, 0:ntok], pv[:, 0:nsub, 0:ntok], AF.Copy, [bpT], [baT])
                else:
                    if nsub > 1:
                        ACT(aT_[:, 0:nsub - 1, 0:ntok], pv[:, 0:nsub - 1, 0:ntok], AF.Copy, [bpT], [baT])
                    ACT(aT_[0:lastk, nsub - 1, 0:ntok], pv[0:lastk, nsub - 1, 0:ntok], AF.Copy, [bpT], [baT])

            def S7(J):
                kb0, nkb, kts = geo(J)
                aT_, baT = att[J["slot"]]["aT"]
                h = J["h"]
                nsub = len(kts)
                for j, kt in enumerate(kts):
                    ksz = min(128, nk - kt * 128)
                    MM(py[0:ntok, h * 64:(h + 1) * 64], aT_[0:ksz, j, 0:ntok], Vst[0:ksz, kt, h * 64:(h + 1) * 64],
                       J["first"] and j == 0, J["last"] and j == nsub - 1, [baT, bV[kt]], [bpy])

            return jobs, [S0, S1, S2, S3, S4, S5, S6, S7]

        def epilogue_pieces(cx):
            ntok, row0 = cx["ntok"], cx["row0"]

            def E0():
                ACT(ya[0:ntok, :], py[0:ntok, :], AF.Copy, [bpy], [bya])

            def E1():
                for c in range(4):
                    TR(ptr[:, c * 128: c * 128 + ntok], ya[0:ntok, c * 128:(c + 1) * 128], identb[0:ntok, 0:ntok],
                       [bya, bidb], [bptr])
                ACT(yaT[:, :, 0:ntok], ptr[:, 0:512].rearrange("p (c t) -> p c t", c=4)[:, :, 0:ntok], AF.Copy, [bptr], [byaT])

            def E2():
                for n in range(2):
                    pp, bpp = next_ab()
                    for c in range(4):
                        MM(pp[0:ntok, :], yaT[:, c, 0:ntok], wa3[:, c, n * 512:(n + 1) * 512], c == 0, c == 3,
                           [byaT] + WL["a"], [bpp])
                    E("dve", "tensor_copy", [bpp], [bAst], out=Ast[0:ntok, n * 512:(n + 1) * 512], in_=pp[0:ntok, :])
                bAsc = Buf("Asc%d" % row0)
                DMA("sp", "Aw", A_sc[row0:row0 + ntok, :], Ast[0:ntok, :], [bAst], [bAsc])
                Abuf[row0] = bAsc

            return [E0, E1, E2]

        if "1a" in PH:
            items = []
            jbase = 0
            starts = []
            njs = []
            seq_idx = -1
            prev_s = None
            for li_, tile in enumerate(tiles):
                first_of_seq = (tile[0] != prev_s)
                if first_of_seq:
                    seq_idx += 1
                    prev_s = tile[0]
                stor = stor_main if seq_idx % 2 == 0 else stor_alt
                cx, pieces = prologue_pieces(tile, li_, stor, first_of_seq)
                if li_ == 0:
                    for i_, pf in enumerate(pieces):
                        if pf is not None:
                            items.append((-100 + i_, 9.0, pf))
                else:
                    pst = starts[li_ - 1] + 1
                    if first_of_seq:
                        pst = max(pst, stor.get("last_step", -1) + 1)
                    pend = starts[li_ - 1] + njs[li_ - 1] - 1
                    avail = max(1, pend - pst)
                    L_ = len(pieces)
                    for i_, pf in enumerate(pieces):
                        if pf is not None:
                            items.append((pst + (i_ * avail) // L_, 9.0 + i_ * 0.01, pf))
                jobs, stages = attention_jobs(cx)
                starts.append(jbase)
                njs.append(len(jobs))
                NSTG = len(stages)
                for ji, J in enumerate(jobs):
                    for k, Sf in enumerate(stages):
                        items.append((jbase + ji + k, float(NSTG - 1 - k), (lambda Sf=Sf, J=J: Sf(J))))
                last_step = jbase + len(jobs) - 1 + (NSTG - 1)
                stor["last_step"] = last_step
                ep = epilogue_pieces(cx)
                items.append((last_step, 0.5, ep[0]))
                items.append((last_step + 2, 8.5, ep[1]))
                items.append((last_step + 4, 8.6, ep[2]))
                jbase += len(jobs)
            if "0" in PH:
                nsteps = jbase + 8
                gap = max(14, (nsteps - 30) // 8)
                for bi, nch in enumerate(range(4, 12)):
                    t_ = 10 + bi * gap
                    items.append((t_, 9.5, (lambda nch=nch: mod_chunk_load(nch, WAbg, bWAbg, "wabg"))))

                    def comp(nch=nch):
                        pp, bpp = next_ab()
                        mod_chunk_compute(nch, WAbg, bWAbg, pp, bpp)
                    items.append((t_ + 10, 9.6, comp))
            order = sorted(range(len(items)), key=lambda i: (items[i][0], items[i][1], i))
            for i in order:
                items[i][2]()

        phase_begin()
        NB = 4104
        WL = load_wg(w_in, 8, 1536, [("mqk", 0, 1024), ("gt", 2048, 8), ("mv", 1024, 512), ("mo", 1536, 512),
                                     ("gb", 3080, 1024), ("ga", 2056, 1024)], NB, 0, "b")
        WL["b"] = WLb_pre
        WL["out"] = WLout_pre
        winb = wview(0, 8, NB)
        wb3 = wview(32832, 4, 1024)
        wo3 = wview(36928, 8, 1024)
        wcv, bwcv = cv("wcv", [128, 8, 4], F32)
        bcv, bbcv = cv("bcv", [128, 8], F32)
        mlg, bmlg = cv("mlg", [64, 512], F32)
        bifi, bbifi = cv("bifi", [4, 1], F32)
        biff, bbiff = cv("biff", [4, 1], F32)
        for j_ in range(4):
            DMA("sp", "cst", wcv[:, :, j_], w_conv[j_].rearrange("(c p) -> p c", p=128), [], [bwcv], allow_slow_non_contiguous=True)
        DMA("sp", "cst", bcv[:], b_conv[0].rearrange("(c p) -> p c", p=128), [], [bbcv], allow_slow_non_contiguous=True)
        DMA("sp", "cst", mlg[:], ml_norm_g[0:1, :].broadcast_to([64, 512]), [], [bmlg])
        DMA("sp", "cst", bifi[:], b_if[0:4, :], [], [bbifi])
        DMA("sp", "cst", biff[:], b_if[4:8, :], [], [bbiff])
        SC_ = 5
        CS_ = 2
        xs3 = [xts[0], xts[1], cv("xt2", [128, D], F32)]
        uT3 = [(uT, buT), cv("uT1", [128, 8, 128], BF16), cv("uT2", [128, 8, 128], BF16)]
        mqk2 = [cv("mqkT%d" % i, [128, 8, 128], BF16) for i in range(2)]
        xp2 = [cv("xp%d" % i, [128, 8, 131], F32) for i in range(2)]
        gl2 = [dict(li=cv("gli%d" % i, [4, 128], F32), sf=cv("gsf%d" % i, [4, 128], F32),
                    lf=cv("glf%d" % i, [4, 128], F32)) for i in range(2)]
        ybT2 = [cv("ybT%d" % i, [128, 4, 128], BF16) for i in range(2)]
        At, bAt = cv("At", [128, D], F32)
        sgt, bsgt = cv("sgt", [128, 512], F32)
        mg, bmg = cv("mg", [128, D], BF16)
        mT, bmT = cv("mT", [128, 8, 128], BF16)
        CTs = [cv("CT%d" % i, [128, 4, 129], F32) for i in range(3)]
        ggbs = [(ggb, bggb), cv("ggb1", [128, D], F32)]
        C0t, bC0t = At[:, 0:512].rearrange("p (h k) -> p h k", h=4), bAt
        mseq, bmseq = cv("mseq", [4, 40], F32)
        CK = []
        for i in range(CS_):
            d_ = {}
            for nm in ("bb", "rr", "mmt", "wg", "wi", "emt"):
                d_[nm] = cv("g_%s%d" % (nm, i), [4, 64], F32)
            d_["dec"] = cv("g_dec%d" % i, [4, 128], F32)
            d_["gsm"] = cv("gsm%d" % i, [4, 8], F32)
            d_["gtok"] = cv("gtok%d" % i, [128, 16], F32)
            d_["Wt"] = cv("Wt%d" % i, [64, 4, 64], F32)
            d_["vaug"] = cv("vaug%d" % i, [64, 4, 129], BF16)
            d_["vw"] = cv("vw%d" % i, [64, 4, 129], BF16)
            d_["smo"] = cv("smo%d" % i, [64, 512], F32)
            d_["ktok"] = cv("ktok%d" % i, [64, 4, 128], BF16)
            d_["ST"] = cv("ST%d" % i, [64, 4, 64], BF16)
            d_["qsT"] = cv("qsT%d" % i, [128, 4, 64], BF16)
            d_["CTb"] = cv("CTb%d" % i, [128, 4, 129], BF16)
            d_["hn"] = cv("hn%d" % i, [64, 4, 129], F32)
            d_["hs"] = cv("hs%d" % i, [64, 16], F32)
            d_["T1h"] = cv("T1h%d" % i, [64, 4, 128], F32)
            d_["ybt"] = cv("ybt%d" % i, [64, 512], BF16)
            E("pool", "memset", [], [d_["vaug"][1]], d_["vaug"][0][:], 1.0)
            CK.append(d_)
        patf = pat[:, 512:1024].bitcast(F32)
        chunk_ctr = [0]
        gchunk = [0]

        def out_state(s, CT, bCT, mcol_final):
            def f():
                for h in range(4):
                    TR(pa[:, h * 128:(h + 1) * 128], CT[:, h, 0:128], identf[:], [bCT, bidf], [bpa])
                E("dve", "tensor_copy", [bpa], [bC0t], out=C0t, in_=pa[:].rearrange("p (h k) -> p h k", h=4))
                DMA("sp", "sto", C_out[s].rearrange("h v k -> v h k"), C0t, [bC0t], [])
                DMA("sp", "sto", n_out[s].rearrange("h k -> k h"), CT[:, :, 128], [bCT], [], allow_slow_non_contiguous=True)
                DMA("sp", "sto", m_out[s:s + 1, :].rearrange("o h -> h o"), mseq[:, mcol_final:mcol_final + 1],
                    [bmseq], [], allow_slow_non_contiguous=True)
            return f

        def out_conv(s, xp_last):
            def f():
                xpl, bxpl, nt_l = xp_last
                for j_ in range(3):
                    DMA("sp", "sto", conv_out[s, j_].rearrange("(c p) -> p c", p=128), xpl[:, :, nt_l + j_], [bxpl], [],
                        allow_slow_non_contiguous=True)
            return f

        def sched_tile(items, lt, tix, tile, prev_xp, sq):
            (s, row0, ntok, ti) = tile
            xt, bxt = xs3[lt % 3]
            uTs, buTs = uT3[lt % 3]
            mqkT, bmqk = mqk2[lt % 2]
            xp, bxp = xp2[lt % 2]
            G_ = gl2[lt % 2]
            (li, bli), (sf, bsf), (lf, blf) = G_["li"], G_["sf"], G_["lf"]
            ybT, bybT = ybT2[lt % 2]
            nch = ntok // 64
            base = 2 * lt * SC_
            CT, bCT = sq["CT"]
            gg_, bgg_ = sq["gg"]
            T1v = T1[:].rearrange("p (c t) -> p c t", c=8)
            T2v = T2[:].rearrange("p (c t) -> p c t", c=8)

            def F0():
                DMA("sp", "xl%d" % (lt % 3), xt[0:ntok, :], xall[row0:row0 + ntok, :], [], [bxt])
                ACT(T2[0:ntok, :], xt[0:ntok, :], AF.Square, [bxt], [bT2, bssq], accum_out=ssq[0:ntok, 0:1])
                rstd_pow(ssq[0:ntok, 2:3], ssq[0:ntok, 1:2], ssq[0:ntok, 0:1], ntok, 1, 1.0 / D, [bssq], [bssq])
                E("dve", "scalar_tensor_tensor", [bxt, bssq, bgmb], [bT1], out=T1[0:ntok, :], in0=xt[0:ntok, :],
                  scalar=ssq[0:ntok, 2:3], in1=gmb[0:ntok, :], op0=ALU.mult, op1=ALU.mult)
                E("pool", "tensor_tensor", [bT1, bshb], [bub], out=ub[0:ntok, :], in0=T1[0:ntok, :], in1=shb[0:ntok, :],
                  op=ALU.add)

            def F1():
                for kc in range(8):
                    TR(ptr[:, kc * 128: kc * 128 + ntok], ub[0:ntok, kc * 128:(kc + 1) * 128], identb[0:ntok, 0:ntok],
                       [bub, bidb], [bptr])
                ACT(uTs[:, :, 0:ntok], ptr[:].rearrange("p (k t) -> p k t", k=8)[:, :, 0:ntok], AF.Copy, [bptr], [buTs])

            def F2(half):
                def f():
                    if half == 0 and prev_xp is not None:
                        pxp, bpxp, pnt = prev_xp
                        E("pool", "tensor_copy", [bpxp], [bxp], out=xp[:, :, 0:3], in_=pxp[:, :, pnt:pnt + 3])
                    pp, bpp = next_ab()
                    for c4 in range(4):
                        ch = half * 4 + c4
                        for kc in range(8):
                            MM(pp[:, c4 * 128: c4 * 128 + ntok], winb[:, kc, ch * 128:(ch + 1) * 128], uTs[:, kc, 0:ntok],
                               kc == 0, kc == 7, [buTs] + WL["mqk"], [bpp])
                    ACT(xp[:, half * 4:(half + 1) * 4, 3:3 + ntok], pp[:].rearrange("p (c t) -> p c t", c=4)[:, :, 0:ntok],
                        AF.Copy, [bpp], [bxp])
                return f

            def F3(p):
                def f():
                    for ch in (2 * p, 2 * p + 1):
                        E("dve", "tensor_scalar", [bxp, bwcv, bbcv], [bT1], out=T1v[:, ch, 0:ntok], in0=xp[:, ch, 0:ntok],
                          scalar1=wcv[:, ch, 0:1], scalar2=bcv[:, ch:ch + 1], op0=ALU.mult, op1=ALU.add)
                        for j in range(1, 4):
                            E("dve", "scalar_tensor_tensor", [bxp, bwcv, bT1], [bT1], out=T1v[:, ch, 0:ntok],
                              in0=xp[:, ch, j:j + ntok], scalar=wcv[:, ch, j:j + 1], in1=T1v[:, ch, 0:ntok],
                              op0=ALU.mult, op1=ALU.add)

                def fpool():
                    tmpf = ub[:, 0:256].bitcast(F32)
                    for ch in (2 * p, 2 * p + 1):
                        E("pool", "tensor_scalar", [bxp, bwcv, bbcv], [bT1], out=T1v[:, ch, 0:ntok], in0=xp[:, ch, 0:ntok],
                          scalar1=wcv[:, ch, 0:1], scalar2=bcv[:, ch:ch + 1], op0=ALU.mult, op1=ALU.add)
                        for j in range(1, 4):
                            E("pool", "tensor_scalar", [bxp, bwcv], [bub], out=tmpf[:, 0:ntok], in0=xp[:, ch, j:j + ntok],
                              scalar1=wcv[:, ch, j:j + 1], scalar2=0.0, op0=ALU.mult, op1=ALU.add)
                            E("pool", "tensor_tensor", [bub, bT1], [bT1], out=T1v[:, ch, 0:ntok], in0=tmpf[:, 0:ntok],
                              in1=T1v[:, ch, 0:ntok], op=ALU.add)
                return fpool if p >= 2 else f

            def F4():
                ACT(T2v[:, :, 0:ntok], T1v[:, :, 0:ntok], AF.Sigmoid, [bT1], [bT2])
                E("dve", "tensor_tensor", [bT1, bT2], [bmqk], out=mqkT[:, 0:4, 0:ntok], in0=T1v[:, 0:4, 0:ntok],
                  in1=T2v[:, 0:4, 0:ntok], op=ALU.mult)
                E("dve", "scalar_tensor_tensor", [bT1, bT2], [bmqk], out=mqkT[:, 4:8, 0:ntok], in0=T1v[:, 4:8, 0:ntok],
                  scalar=float(1.0 / np.sqrt(128.0)), in1=T2v[:, 4:8, 0:ntok], op0=ALU.mult, op1=ALU.mult)

            def F5():
                for kc in range(8):
                    MM(pm[0:4, 0:ntok], winb[:, kc, 2048:2052], uTs[:, kc, 0:ntok], kc == 0, kc == 7, [buTs] + WL["gt"], [bpm])
                ACT(li[:, 0:ntok], pm[0:4, 0:ntok], AF.Identity, [bpm, bbifi], [bli], bias=bifi[:])
                for kc in range(8):
                    MM(pm[0:4, 128:128 + ntok], winb[:, kc, 2052:2056], uTs[:, kc, 0:ntok], kc == 0, kc == 7,
                       [buTs] + WL["gt"], [bpm])
                ACT(sf[:, 0:ntok], pm[0:4, 128:128 + ntok], AF.Sigmoid, [bpm, bbiff], [bsf], bias=biff[:])
                ACT(lf[:, 0:ntok], sf[:, 0:ntok], AF.Ln, [bsf], [blf])

            fl = [(0, F0), (1, F1), (2, F2(0)), (3, F2(1)), (3.5, F5), (4, F3(0)), (5, F3(1)), (6, F3(2)), (7, F3(3)),
                  (8, F4)]
            for j, f in fl:
                items.append((base + j, f))

            def chunk_stages(c):
                cs = slice(c * 64, (c + 1) * 64)
                g = gchunk[0]
                gchunk[0] += 1
                K = CK[g % CS_]
                Kn = CK[(g + 1) % CS_]
                Kp = CK[(g - 1) % CS_]
                mc = chunk_ctr[0]
                chunk_ctr[0] += 1
                mcur = mseq[:, mc:mc + 1]
                (bb_, bbb), (rr, brr), (mmt, bmmt), (wg, bwg), (wi, bwi), (emt, bemt), (dec, bdec) = (
                    K["bb"], K["rr"], K["mmt"], K["wg"], K["wi"], K["emt"], K["dec"])
                (gsm, bgsm), (gtok, bgtok), (Wt, bWt), (vaug, bvaug), (vw, bvw), (smo, bsmo) = (
                    K["gsm"], K["gtok"], K["Wt"], K["vaug"], K["vw"], K["smo"])
                (ktok, bktok), (STt, bST), (qsT, bqsT), (CTb, bCTb), (hn, bhn), (hs, bhs) = (
                    K["ktok"], K["ST"], K["qsT"], K["CTb"], K["hn"], K["hs"])
                hh, bhh = hn[:, :, 0:128], bhn
                (T1h, bT1h), (ybt, bybt) = K["T1h"], K["ybt"]
                mx2 = mmt[:, 63:64]

                def G0():
                    E("dve", "tensor_tensor_scan", [blf, bones], [bbb], out=bb_[:, :], data0=onesb[0:4, 0:64], data1=lf[:, cs],
                      initial=0.0, op0=ALU.mult, op1=ALU.add)
                    E("dve", "tensor_tensor", [bli, bbb], [brr], out=rr[:, :], in0=li[:, cs], in1=bb_[:, :], op=ALU.subtract)
                    E("dve", "tensor_tensor_scan", [brr, bones, bmseq], [bmmt], out=mmt[:, :], data0=onesb[0:4, 0:64],
                      data1=rr[:, :], initial=mcur, op0=ALU.mult, op1=ALU.max)
                    E("dve", "tensor_scalar", [bmmt], [bgsm], out=gsm[:, 0:1], in0=mx2, scalar1=-1.0, scalar2=None, op0=ALU.mult)
                    E("dve", "tensor_tensor", [bmseq, bmmt], [bgsm], out=gsm[:, 1:2], in0=mcur, in1=mx2, op=ALU.subtract)
                    E("dve", "tensor_tensor", [bbb, bmmt], [bmseq], out=mseq[:, mc + 1:mc + 2], in0=bb_[:, 63:64],
                      in1=mx2, op=ALU.add)
                    E("dve", "tensor_tensor", [bbb, bmmt], [bemt], out=emt[:, :], in0=bb_[:, :], in1=mmt[:, :], op=ALU.add)

                def G1():
                    ACT(wg[:, :], rr[:, :], AF.Exp, [brr, bgsm], [bwg], bias=gsm[:, 0:1])
                    ACT(dec[:, :], zer[0:4, :], AF.Exp, [bzer, bgsm], [bdec], bias=gsm[:, 1:2])
                    ACT(wi[:, :], mmt[:, :], AF.Exp, [bmmt, bmseq], [bwi], scale=-1.0, bias=mcur)
                    ACT(emt[:, :], emt[:, :], AF.Exp, [bemt], [bemt], scale=-1.0)

                def G2():
                    TR(pm[0:64, 256:260], rr[:, :], identf[0:4, 0:4], [brr, bidf], [bpm])
                    TR(pm[0:64, 260:264], emt[:, :], identf[0:4, 0:4], [bemt, bidf], [bpm])
                    TR(pm[0:64, 264:268], wg[:, :], identf[0:4, 0:4], [bwg, bidf], [bpm])
                    TR(pm[0:128, 268:272], dec[:, :], identf[0:4, 0:4], [bdec, bidf], [bpm])
                    E("dve", "tensor_copy", [bpm], [bgtok], out=gtok[0:64, 0:12], in_=pm[0:64, 256:268])
                    E("dve", "tensor_copy", [bpm], [bgtok], out=gtok[:, 12:16], in_=pm[:, 268:272])

                def G3():
                    for h in range(4):
                        MM(py[0:64, h * 64:(h + 1) * 64], sel[:, h, 0:64], mmt[:, :], True, True, [bsel, bmmt], [bpy])
                    for h in range(4):
                        MM(py[:, 256 + h * 64: 256 + (h + 1) * 64], sel[:, h, :], wi[:, :], True, True, [bsel, bwi], [bpy])
                    for h in range(4):
                        ACT(Wt[:, h, :], py[0:64, h * 64:(h + 1) * 64], AF.Exp, [bpy, bgtok], [bWt], scale=-1.0,
                            bias=gtok[0:64, h:h + 1])
                    E("dve", "tensor_tensor", [bpy, bmqk], [bqsT], out=qsT[:], in0=py[:, 256:512].rearrange("p (h t) -> p h t", h=4),
                      in1=mqkT[:, 0:4, cs], op=ALU.mult)
                    E("pool", "affine_select", [bWt], [bWt], out=Wt[:], in_=Wt[:], pattern=[[0, 4], [1, 64]],
                      compare_op=ALU.is_ge, fill=0.0, base=0, channel_multiplier=-1)

                def V0a():
                    if nch == 2 and c == 1:
                        return
                    pp, bpp = next_ab()
                    m_ = ntok if nch == 2 else 64
                    for kc in range(8):
                        MM(pp[0:m_, :], uTs[:, kc, 0:m_], winb[:, kc, 1024:1536], kc == 0, kc == 7, [buTs] + WL["mv"], [bpp])
                    E("dve", "tensor_copy", [bpp], [bvaug], out=vaug[:, :, 0:128], in_=pp[0:64, :].rearrange("p (h d) -> p h d", h=4))
                    if nch == 2:
                        vn, bvn = Kn["vaug"]
                        ACT(vn[:, :, 0:128], pp[64:128, :].rearrange("p (h d) -> p h d", h=4), AF.Copy, [bpp], [bvn])

                def V0b():
                    if nch == 2 and c == 0:
                        return
                    pp, bpp = next_ab()
                    m_ = ntok if nch == 2 else 64
                    for kc in range(8):
                        MM(pp[0:m_, :], uTs[:, kc, 0:m_], winb[:, kc, 1536:2048], kc == 0, kc == 7, [buTs] + WL["mo"], [bpp])
                    if nch == 2:
                        sp_, bsp_ = Kp["smo"]
                        ACT(sp_[:], pp[0:64, :], AF.Sigmoid, [bpp], [bsp_])
                        E("pool", "tensor_tensor", [bsp_, bmlg], [bsp_], out=sp_[:], in0=sp_[:], in1=mlg[:], op=ALU.mult)
                        ACT(smo[:], pp[64:128, :], AF.Sigmoid, [bpp], [bsmo])
                    else:
                        ACT(smo[:], pp[0:64, :], AF.Sigmoid, [bpp], [bsmo])
                    E("pool", "tensor_tensor", [bsmo, bmlg], [bsmo], out=smo[:], in0=smo[:], in1=mlg[:], op=ALU.mult)

                def V0c():
                    for h in range(4):
                        TR(pat[0:64, h * 128:(h + 1) * 128], mqkT[:, 4 + h, cs], identb[:], [bmqk, bidb], [bpat])
                    E("dve", "tensor_copy", [bpat], [bktok], out=ktok[:], in_=pat[0:64, 0:512].rearrange("p (h d) -> p h d", h=4))

                def U():
                    E("pool", "tensor_copy", [bCT], [bCTb], out=CTb[:], in_=CT[:])
                    E("dve", "tensor_tensor", [bvaug, bgtok], [bvw], out=vw[:], in0=vaug[:],
                      in1=gtok[0:64, 8:12].unsqueeze(2).broadcast_to([64, 4, 129]), op=ALU.mult)
                    for h in range(4):
                        o0 = 512 * (h // 2) + 129 * (h % 2)
                        MM(pz2[:, o0:o0 + 129], ktok[:, h, :], vw[:, h, :], True, True, [bktok, bvw], [bpz[0], bpz[1]])
                    for h in range(4):
                        o0 = 512 * (h // 2) + 129 * (h % 2)
                        E("dve", "scalar_tensor_tensor", [bCT, bgtok, bpz[0], bpz[1]], [bCT], out=CT[:, h, :], in0=CT[:, h, :],
                          scalar=gtok[:, 12 + h:13 + h], in1=pz2[:, o0:o0 + 129], op0=ALU.mult, op1=ALU.add)

                def V1():
                    for h in range(4):
                        MM(patf[0:64, h * 64:(h + 1) * 64], mqkT[:, 4 + h, cs], mqkT[:, h, cs], True, True, [bmqk], [bpat])
                    E("dve", "tensor_tensor", [bpat, bWt], [bST], out=STt[:], in0=patf[0:64, 0:256].rearrange("p (h t) -> p h t", h=4),
                      in1=Wt[:], op=ALU.mult)

                def N0():
                    for h in range(4):
                        o0 = 512 * (h // 2) + 129 * (h % 2)
                        MM(pz2[0:64, o0:o0 + 129], STt[:, h, :], vaug[:, h, :], True, False, [bST, bvaug], [bpz[0], bpz[1]])
                        MM(pz2[0:64, o0:o0 + 129], qsT[:, h, :], CTb[:, h, :], False, True, [bqsT, bCTb], [bpz[0], bpz[1]])
                    E("dve", "tensor_copy", [bpz[0], bpz[1]], [bhn], out=hn[:, 0:2, :], in_=pz2[0:64, 0:258].rearrange("p (h d) -> p h d", h=2))
                    E("dve", "tensor_copy", [bpz[0], bpz[1]], [bhn], out=hn[:, 2:4, :], in_=pz2[0:64, 512:770].rearrange("p (h d) -> p h d", h=2))

                def N1():
                    den = hn[:, :, 128]
                    E("dve", "scalar_tensor_tensor", [bhn], [bhs], out=hs[:, 0:4], in0=den, scalar=-1.0, in1=den,
                      op0=ALU.mult, op1=ALU.max)
                    E("dve", "tensor_tensor", [bhs, bgtok], [bhs], out=hs[:, 4:8], in0=hs[:, 0:4], in1=gtok[0:64, 4:8], op=ALU.max)
                    E("dve", "reciprocal", [bhs], [bhs], out=hs[:, 8:12], in_=hs[:, 4:8])
                    E("dve", "tensor_tensor", [bhn, bhs], [bhh], out=hh, in0=hn[:, :, 0:128],
                      in1=hs[:, 8:12].unsqueeze(2).broadcast_to([64, 4, 128]), op=ALU.mult)
                    E("dve", "tensor_tensor", [bhh], [bT1h], out=T1h[:], in0=hh, in1=hh, op=ALU.mult)
                    E("dve", "tensor_reduce", [bT1h], [bhs], out=hs[:, 12:16], in_=T1h[:], axis=AX.X, op=ALU.add)

                def N2():
                    rstd_pow(hs[:, 12:16], hs[:, 12:16], hs[:, 12:16], 64, 4, 1.0 / 128, [bhs], [bhs])

                def N3():
                    E("dve", "tensor_tensor", [bhh, bhs], [bT1h], out=T1h[:], in0=hh,
                      in1=hs[:, 12:16].unsqueeze(2).broadcast_to([64, 4, 128]), op=ALU.mult)
                    E("dve", "tensor_tensor", [bT1h, bsmo], [bybt], out=ybt[:], in0=T1h[:].rearrange("p h d -> p (h d)"),
                      in1=smo[:], op=ALU.mult)

                def N4():
                    for h in range(4):
                        TR(ptr[:, h * 64:(h + 1) * 64], ybt[:, h * 128:(h + 1) * 128], identb[0:64, 0:64],
                           [bybt, bidb], [bptr])
                    ACT(ybT[:, :, cs], ptr[:, 0:256].rearrange("p (h t) -> p h t", h=4), AF.Copy, [bptr], [bybT])

                return [G0, G1, G2, G3, V0a, V0b, V0c, U, V1, N0, N1, N2, N3, N4]

            for c in range(nch):
                st_ = chunk_stages(c)
                b0 = base + 7 + c * SC_
                for j, f in enumerate(st_):
                    items.append((b0 + j, f))
            dbase = base + 7 + (nch - 1) * SC_ + 14

            def D0():
                DMA("sp", "Ar", At[0:ntok, :], A_sc[row0:row0 + ntok, :], [Abuf.get(row0)], [bAt])
                for n in range(2):
                    pB, bpB = next_ab()
                    for c in range(4):
                        MM(pB[0:ntok, :], ybT[:, c, 0:ntok], wb3[:, c, n * 512:(n + 1) * 512], c == 0, c == 3, [bybT] + WL["b"], [bpB])
                    pG, bpG = next_ab()
                    for kc in range(8):
                        MM(pG[0:ntok, :], uTs[:, kc, 0:ntok], winb[:, kc, 3080 + n * 512: 3080 + (n + 1) * 512], kc == 0, kc == 7,
                           [buTs] + WL["gb"], [bpG])
                    ACT(sgt[0:ntok, :], pG[0:ntok, :], AF.Sigmoid, [bpG], [bsgt])
                    E("dve", "tensor_tensor", [bsgt, bpB], [bT2], out=T2[0:ntok, n * 512:(n + 1) * 512], in0=sgt[0:ntok, :],
                      in1=pB[0:ntok, :], op=ALU.mult)

            def D1():
                for n in range(2):
                    pG, bpG = next_ab()
                    for kc in range(8):
                        MM(pG[0:ntok, :], uTs[:, kc, 0:ntok], winb[:, kc, 2056 + n * 512: 2056 + (n + 1) * 512], kc == 0, kc == 7,
                           [buTs] + WL["ga"], [bpG])
                    ACT(sgt[0:ntok, :], pG[0:ntok, :], AF.Sigmoid, [bpG], [bsgt])
                    E("pool", "tensor_tensor", [bsgt, bAt], [bsgt], out=sgt[0:ntok, :], in0=sgt[0:ntok, :],
                      in1=At[0:ntok, n * 512:(n + 1) * 512], op=ALU.mult)
                    E("pool", "tensor_tensor", [bsgt, bT2], [bmg], out=mg[0:ntok, n * 512:(n + 1) * 512], in0=sgt[0:ntok, :],
                      in1=T2[0:ntok, n * 512:(n + 1) * 512], op=ALU.add)

            def D2():
                for kc in range(8):
                    TR(ptr[:, kc * 128: kc * 128 + ntok], mg[0:ntok, kc * 128:(kc + 1) * 128], identb[0:ntok, 0:ntok],
                       [bmg, bidb], [bptr])
                ACT(mT[:, :, 0:ntok], ptr[:].rearrange("p (k t) -> p k t", k=8)[:, :, 0:ntok], AF.Copy, [bptr], [bmT])

            def D3():
                for n in range(2):
                    pO, bpO = next_ab()
                    for kc in range(8):
                        MM(pO[0:ntok, :], mT[:, kc, 0:ntok], wo3[:, kc, n * 512:(n + 1) * 512], kc == 0, kc == 7, [bmT] + WL["out"], [bpO])
                    E("dve", "tensor_tensor", [bpO, bgg_], [bT2], out=T2[0:ntok, n * 512:(n + 1) * 512], in0=pO[0:ntok, :],
                      in1=gg_[0:ntok, n * 512:(n + 1) * 512], op=ALU.mult)
                E("pool", "tensor_tensor", [bT2, bxt], [bxt], out=xt[0:ntok, :], in0=T2[0:ntok, :], in1=xt[0:ntok, :], op=ALU.add)
                bx1 = Buf("x1sc%d" % row0)
                x1buf[row0] = bx1
                DMA("sp", "x1w%d" % (lt % 3), x1_sc[row0:row0 + ntok, :], xt[0:ntok, :], [bxt], [bx1])
                ACT(T2[0:ntok, :], xt[0:ntok, :], AF.Square, [bxt], [bT2, bssq], accum_out=ssq[0:ntok, 0:1])
                E("pool", "tensor_scalar", [bssq], [bssq], out=ssq[0:ntok, 1:2], in0=ssq[0:ntok, 0:1], scalar1=1.0 / D,
                  scalar2=EPS, op0=ALU.mult, op1=ALU.add)
                E("pool", "tensor_tensor", [bssq, bmhalf], [brstd2], out=rstd2[0:ntok, tix:tix + 1], in0=ssq[0:ntok, 1:2],
                  in1=mhalf[0:ntok, 0:1], op=ALU.pow)

            for j, f in enumerate([D0, D1, D2, D3]):
                items.append((dbase + j, f))
            sq["lastU"] = base + 7 + (nch - 1) * SC_ + 7
            sq["lastD3"] = dbase + 3
            sq["lastF4"] = base + 8
            return (xp, bxp, ntok)

        if "1b" in PH:
            seqs = []
            for tix, tile in enumerate(tiles):
                if not seqs or seqs[-1][0] != tile[0]:
                    seqs.append((tile[0], []))
                seqs[-1][1].append((tix, tile))
            items = []
            lt = 0
            d3_hist = []
            for k_, (s, tl) in enumerate(seqs):
                CTk, bCTk = CTs[k_ % 3]
                ggk = ggbs[k_ % 2]
                sq = dict(CT=(CTk, bCTk), gg=ggk)
                chunk_ctr[0] += 1
                mcol = chunk_ctr[0]
                base0 = 2 * lt * SC_
                xp0, bxp0 = xp2[lt % 2]

                def init_seq(s=s, CTk=CTk, bCTk=bCTk, mcol=mcol, xp0=xp0, bxp0=bxp0, first=(k_ == 0)):
                    load_mod(s, 1, gate=False)
                    if s == 0:
                        E("pool", "memset", [], [bCTk], CTk[:], 0.0)
                        if first:
                            E("pool", "memset", [], [bmseq], mseq[:], 0.0)
                        E("pool", "memset", [], [bxp0], xp0[:, :, 0:3], 0.0)
                    else:
                        si = s - 1
                        DMA("sp", "st", C0t, C0[si].rearrange("h v k -> v h k"), [], [bC0t])
                        for h in range(4):
                            TR(pa[:, h * 128:(h + 1) * 128], C0t[:, h, :], identf[:], [bC0t, bidf], [bpa])
                        E("dve", "tensor_copy", [bpa], [bCTk], out=CTk[:, :, 0:128], in_=pa[:].rearrange("p (h v) -> p h v", h=4))
                        DMA("sp", "st", CTk[:, :, 128], n0[si].rearrange("h k -> k h"), [], [bCTk], allow_slow_non_contiguous=True)
                        DMA("sp", "st", mseq[:, mcol:mcol + 1], m0[si:si + 1, :].rearrange("o h -> h o"), [], [bmseq],
                            allow_slow_non_contiguous=True)
                        for j_ in range(3):
                            DMA("sp", "st", xp0[:, :, j_], conv0[si, j_].rearrange("(c p) -> p c", p=128), [], [bxp0],
                                allow_slow_non_contiguous=True)

                items.append((base0 - 0.5, init_seq))
                gstep = base0 - 0.4
                if k_ >= 2:
                    gstep = max(gstep, d3_hist[k_ - 2] + 0.5)
                items.append((gstep, (lambda s=s, ggk=ggk: load_mod(s, 1, front=False, gdst=ggk))))
                prev_xp = None
                for (tix, tile) in tl:
                    prev_xp = sched_tile(items, lt, tix, tile, prev_xp, sq)
                    lt += 1
                items.append((sq["lastF4"] + 0.5, out_conv(s, prev_xp)))
                items.append((sq["lastU"] + 0.5, out_state(s, CTk, bCTk, chunk_ctr[0])))
                d3_hist.append(sq["lastD3"])
            order = sorted(range(len(items)), key=lambda i: (items[i][0], i))
            for i in order:
                items[i][1]()

        phase_begin()
        pab[:] = [(pa, bpa), (pb, bpb), (py, bpy), (pm, bpm), (pz2[:, 0:512], bpz[0]), (pz2[:, 512:1024], bpz[1])]
        WL = {}
        for ci_, c0_ in enumerate(range(0, DFF, 512)):
            n_ = min(512, DFF - c0_)
            WL["g%d" % ci_] = load_wg(w_ff_gate, 8, 0, [("g", c0_, n_)], DFF, 0, "fg%d" % ci_, kbase=2 * ci_)["g"]
            WL["u%d" % ci_] = load_wg(w_ff_up, 8, 0, [("u", c0_, n_)], DFF, 22528, "fu%d" % ci_, kbase=2 * ci_ + 1)["u"]
        wg3 = wview(0, 8, DFF)
        wu3 = wview(22528, 8, DFF)
        wdn, _ = cv("wdn", [128, 22 * D], BF16)
        xs3p = [xts[0], xts[1], cv("xt2p", [128, D], F32)]
        wd3 = wdn.rearrange("p (k c) -> p k c", k=22)
        WLd = {}
        dvd = w_ff_down.rearrange("(k p) n -> p k n", p=128)
        for n_ in range(2):
            bw = fb("Wd%d" % n_)
            DMA("pool", "Wg%d" % (12 + n_), wd3[:, :, n_ * 512:(n_ + 1) * 512], dvd[:, :, n_ * 512:(n_ + 1) * 512], [], [bw])
            WLd[n_] = [bw]
        sgts = [cv("sgt%d" % i, [128, 512], F32) for i in range(1)] * 2
        hb2 = [cv("hbuf%d" % i, [128, DFF], BF16) for i in range(2)]
        hT, bhT = cv("hT", [128, 22, 128], BF16)
        Tq, bTq = cv("O2_0", [128, D], F32)
        sq2 = [cv("sq2_%d" % i, [128, 4], F32) for i in range(2)]
        uT2a = [(uT, buT), cv("uT2a", [128, 8, 128], BF16)]
        ggbs2 = [(ggb, bggb), cv("ggb2", [128, D], F32)]
        fgb, bfgb = cv("fgb", [128, D], F32)
        DMA("sp", "cst", fgb[:], final_g[0:1, :].broadcast_to([128, D]), [], [bfgb])
        sgc = [0]
        seq2a = [-1]
        gsel = {}
        trb = [(ptr, bptr), (pat, bpat)]

        def ld2(tix):
            (s, row0, ntok, ti) = tiles[tix]
            xt, bxt = xs3p[tix % 3]
            DMA("sp", "xl%d" % (tix % 3), xt[0:ntok, :], x1_sc[row0:row0 + ntok, :], [x1buf.get(row0)], [bxt])

        def front2(tix):
            (s, row0, ntok, ti) = tiles[tix]
            if s != seq2a[0]:
                seq2a[0] = s
                load_mod(s, 2, gate=False)
            xt, bxt = xs3p[tix % 3]
            rms_to_uT(xt, bxt, ntok, rstd_ap=rstd2[0:ntok, tix:tix + 1], dst=uT2a[tix % 2])

        def up2(tix):
            (s, row0, ntok, ti) = tiles[tix]
            hbuf, bhbuf = hb2[tix % 2]
            for n0_ in range(0, DFF, 512):
                sgc[0] ^= 1
                sgt, bsgt = sgts[sgc[0]]
                n = min(512, DFF - n0_)
                pG, bpG = proj_tok(wg3, n0_, n, ntok, wl=WL["g%d" % (n0_ // 512)], us=uT2a[tix % 2])
                pU, bpU = proj_tok(wu3, n0_, n, ntok, wl=WL["u%d" % (n0_ // 512)], us=uT2a[tix % 2])
                ACT(sgt[0:ntok, 0:n], pG[0:ntok, 0:n], AF.Silu, [bpG], [bsgt])
                E("dve", "tensor_tensor", [bsgt, bpU], [bhbuf], out=hbuf[0:ntok, n0_:n0_ + n], in0=sgt[0:ntok, 0:n],
                  in1=pU[0:ntok, 0:n], op=ALU.mult)

        seqord = []
        for t_ in tiles:
            if t_[0] not in seqord:
                seqord.append(t_[0])

        def down2(tix):
            (s, row0, ntok, ti) = tiles[tix]
            k_ = seqord.index(s)
            gg2, bgg2 = ggbs2[k_ % 2]
            if s not in gsel:
                gsel[s] = True
                load_mod(s, 2, front=False, gdst=(gg2, bgg2))
            hbuf, bhbuf = hb2[tix % 2]
            xt, bxt = xs3p[tix % 3]
            sq_, bsq_ = sq2[tix % 2]
            for gi, g0 in enumerate(range(0, 22, 8)):
                ng = min(8, 22 - g0)
                pt_, bpt_ = trb[gi % 2]
                for j in range(ng):
                    k = g0 + j
                    TR(pt_[:, j * 128: j * 128 + ntok], hbuf[0:ntok, k * 128:(k + 1) * 128], identb[0:ntok, 0:ntok],
                       [bhbuf, bidb], [bpt_])
                ACT(hT[:, g0:g0 + ng, 0:ntok], pt_[:].rearrange("p (k t) -> p k t", k=8)[:, 0:ng, 0:ntok], AF.Copy,
                    [bpt_], [bhT])
            for n in range(2):
                pD, bpD = next_ab()
                for k in range(22):
                    MM(pD[0:ntok, :], hT[:, k, 0:ntok], wd3[:, k, n * 512:(n + 1) * 512], k == 0, k == 21, [bhT] + WLd[n], [bpD])
                E("dve", "tensor_tensor", [bpD, bgg2], [bTq], out=Tq[0:ntok, n * 512:(n + 1) * 512], in0=pD[0:ntok, :],
                  in1=gg2[0:ntok, n * 512:(n + 1) * 512], op=ALU.mult)
            E("pool", "tensor_tensor", [bTq, bxt], [bxt], out=xt[0:ntok, :], in0=Tq[0:ntok, :], in1=xt[0:ntok, :], op=ALU.add)
            ACT(Tq[0:ntok, :], xt[0:ntok, :], AF.Square, [bxt], [bTq, bsq_], accum_out=sq_[0:ntok, 0:1])
            rstd_pow(sq_[0:ntok, 2:3], sq_[0:ntok, 1:2], sq_[0:ntok, 0:1], ntok, 1, 1.0 / D, [bsq_], [bsq_])
            E("dve", "scalar_tensor_tensor", [bxt, bsq_, bfgb], [bTq], out=Tq[0:ntok, :], in0=xt[0:ntok, :],
              scalar=sq_[0:ntok, 2:3], in1=fgb[0:ntok, :], op0=ALU.mult, op1=ALU.mult)
            DMA("sp", "yo", y_all[row0:row0 + ntok, :], Tq[0:ntok, :], [bTq], [])

        P2ON = ("2a" in PH) or ("2b" in PH)
        if P2ON:
            NT_ = len(tiles)
            ld2(0)
            if NT_ > 1:
                ld2(1)
            front2(0)
            if NT_ > 1:
                front2(1)
            up2(0)
            for tix in range(NT_):
                if tix + 2 < NT_:
                    ld2(tix + 2)
                if tix + 1 < NT_:
                    up2(tix + 1)
                if tix + 2 < NT_:
                    front2(tix + 2)
                down2(tix)

        P.lower(lambda name: st.enter_context(nc.semaphore(name)))
        build_program.stats = P.stats
    return nc


_CACHE = {}


def kernel(**inp):
    f = lambda a: np.ascontiguousarray(np.asarray(a, dtype=np.float32))
    if "nc" not in _CACHE:
        _CACHE["nc"] = build_program()
    nc = _CACHE["nc"]
    x_prompt = f(inp["x_prompt"]); x_sample = f(inp["x_sample"])
    c_prompt = f(inp["c_prompt"]); c_sample = f(inp["c_sample"])
    ck = f(inp["cache_sb_k"])[0].reshape(16, PAST, 512)
    cv = f(inp["cache_sb_v"])[0].reshape(16, PAST, 512)
    sC = f(inp["state_mlstm_C"])[0]; sn = f(inp["state_mlstm_n"])[0]; sm = f(inp["state_mlstm_m"])[0]
    sconv = f(inp["state_conv"])[0]
    shared = {
        "norm1_g": f(inp["norm1_g"]).reshape(1, D), "norm2_g": f(inp["norm2_g"]).reshape(1, D),
        "w_ada": f(inp["w_ada"])[0], "b_ada": f(inp["b_ada"]).reshape(1, 6 * D),
        "w_in": f(inp["w_in"])[0], "b_if": f(inp["b_if"]).reshape(8, 1),
        "w_conv": f(inp["w_conv"])[0], "b_conv": f(inp["b_conv"]).reshape(1, D),
        "ml_norm_g": f(inp["ml_norm_g"]).reshape(1, 512),
        "w_a": f(inp["w_a"])[0], "w_b": f(inp["w_b"])[0], "w_out": f(inp["w_out"])[0],
        "w_ff_gate": f(inp["w_ff_gate"])[0], "w_ff_up": f(inp["w_ff_up"])[0], "w_ff_down": f(inp["w_ff_down"])[0],
        "final_g": f(inp["final_g"]).reshape(1, D),
    }
    in_maps = []
    for i in range(8):
        m = dict(shared)
        m["xall"] = np.concatenate([x_prompt[i], x_sample[2 * i], x_sample[2 * i + 1]], axis=0)
        m["c3"] = np.stack([c_prompt[i], c_sample[2 * i], c_sample[2 * i + 1]], axis=0)
        m["cache_k"] = ck[2 * i:2 * i + 2]
        m["cache_v"] = cv[2 * i:2 * i + 2]
        m["C0"] = sC[2 * i:2 * i + 2]
        m["n0"] = sn[2 * i:2 * i + 2]
        m["m0"] = sm[2 * i:2 * i + 2]
        m["conv0"] = sconv[2 * i:2 * i + 2]
        in_maps.append({k: np.ascontiguousarray(v) for k, v in m.items()})
    res = run_bass_kernel_spmd(nc, in_maps, core_ids=list(range(8)))
    R = res.results
    y_p = np.stack([R[i]["y_all"][:SEQ] for i in range(8)])
    y_s = np.stack([R[i]["y_all"][SEQ + j * LS: SEQ + (j + 1) * LS] for i in range(8) for j in range(2)])
    k_p = np.stack([R[i]["k_all"][:SEQ] for i in range(8)]).reshape(1, 8, SEQ, 8, 64)
    v_p = np.stack([R[i]["v_all"][:SEQ] for i in range(8)]).reshape(1, 8, SEQ, 8, 64)
    k_s = np.stack([R[i]["k_all"][SEQ + j * LS: SEQ + (j + 1) * LS] for i in range(8) for j in range(2)]).reshape(1, 16, LS, 8, 64)
    v_s = np.stack([R[i]["v_all"][SEQ + j * LS: SEQ + (j + 1) * LS] for i in range(8) for j in range(2)]).reshape(1, 16, LS, 8, 64)
    C_p = np.stack([R[i]["C_out"][0] for i in range(8)])[None]
    n_p = np.stack([R[i]["n_out"][0] for i in range(8)])[None]
    m_p = np.stack([R[i]["m_out"][0] for i in range(8)])[None]
    cv_p = np.stack([R[i]["conv_out"][0] for i in range(8)])[None]
    C_s = np.stack([R[i]["C_out"][1 + j] for i in range(8) for j in range(2)])[None]
    n_s = np.stack([R[i]["n_out"][1 + j] for i in range(8) for j in range(2)])[None]
    m_s = np.stack([R[i]["m_out"][1 + j] for i in range(8) for j in range(2)])[None]
    cv_s = np.stack([R[i]["conv_out"][1 + j] for i in range(8) for j in range(2)])[None]
    outs = (y_p, y_s, k_p, v_p, C_p, n_p, m_p, cv_p, k_s, v_s, C_s, n_s, m_s, cv_s)
    return tuple(np.ascontiguousarray(o, dtype=np.float32) for o in outs)
```

```python
import numpy as np
from contextlib import ExitStack
import concourse.bass as bass
import concourse.mybir as mybir
from concourse.bass_utils import run_bass_kernel_spmd

F32 = mybir.dt.float32
BF16 = mybir.dt.bfloat16
AF = mybir.ActivationFunctionType
ALU = mybir.AluOpType
AX = mybir.AxisListType

D = 1024
SEQ = 2048
NS = 2
LS = 64
PAST = 1024
DFF = 2816
INW = 5640
EPS = 1e-6
NROWS = SEQ + NS * LS


STRICT_SAME_ENGINE = True


class Buf:
    __slots__ = ("name", "last_w", "readers", "excl")

    def __init__(self, name, excl=False):
        self.name = name
        self.last_w = None
        self.readers = []
        self.excl = excl


class Op:
    __slots__ = ("eng", "fn", "reads", "writes", "dma", "seq", "signal", "waits",
                 "clock", "count", "idx", "attach")


class Prog:
    ENG = ("pe", "act", "dve", "pool", "sp")

    def __init__(self, nc):
        self.nc = nc
        self.ops = []
        self.e = {"pe": nc.tensor, "act": nc.scalar, "dve": nc.vector,
                  "pool": nc.gpsimd, "sp": nc.sync}

    def op(self, eng, fn, reads=(), writes=(), dma=None):
        o = Op()
        o.eng = eng
        o.fn = fn
        o.reads = [b for b in reads if b is not None and not b.excl]
        o.writes = [b for b in writes if b is not None] + [b for b in reads if b is not None and b.excl]
        o.dma = dma
        o.signal = False
        o.waits = []
        o.attach = (eng != "pe")
        o.idx = len(self.ops)
        self.ops.append(o)
        return o

    def fence(self):
        last = {}
        for o in self.ops:
            last[o.eng if o.dma is None else "d:" + o.dma] = o
        return list(last.values())

    def lower(self, sem_ctx):
        ops = self.ops
        seqc = {k: 0 for k in self.ENG}
        dmac = {}
        eclock = {k: {} for k in self.ENG}
        for o in ops:
            if o.dma is None:
                seqc[o.eng] += 1
                o.seq = seqc[o.eng]
            else:
                dmac[o.dma] = dmac.get(o.dma, 0) + 1
                o.seq = dmac[o.dma]
            deps = {}
            for b in o.reads:
                if b.last_w is not None:
                    deps[b.last_w.idx] = b.last_w
            for b in o.writes:
                if b.last_w is not None:
                    deps[b.last_w.idx] = b.last_w
                for r in b.readers:
                    deps[r.idx] = r
            clk = eclock[o.eng]
            for p in deps.values():
                if p is o:
                    continue
                if p.dma is None:
                    key = p.eng
                    need = p.seq
                    if p.eng == o.eng and o.dma is None:
                        if o.eng == "pe":
                            continue
                        if (not STRICT_SAME_ENGINE) and o.eng != "pool" and not any(b.last_w is p for b in o.reads):
                            continue
                else:
                    key = "d:" + p.dma
                    need = dmac[p.dma] if not (o.dma == p.dma) else dmac[p.dma] - 1
                if clk.get(key, 0) >= need:
                    continue
                if p.dma is None:
                    p.signal = True
                o.waits.append((p, need))
                for k2, v2 in p.clock.items():
                    if clk.get(k2, 0) < v2:
                        clk[k2] = v2
                if clk.get(key, 0) < need:
                    clk[key] = need
            myclk = dict(clk)
            mykey = o.eng if o.dma is None else "d:" + o.dma
            myclk[mykey] = max(myclk.get(mykey, 0), o.seq)
            o.clock = myclk
            for b in o.reads:
                b.readers.append(o)
            for b in o.writes:
                b.last_w = o
                b.readers = []
        cnt = {k: 0 for k in self.ENG}
        for o in ops:
            if o.dma is None and o.signal:
                cnt[o.eng] += 1
                o.count = cnt[o.eng]
        esem = {}
        dsem = {}
        dtot = {}
        n_waits = 0

        def get_e(k):
            if k not in esem:
                esem[k] = sem_ctx("e_" + k)
            return esem[k]

        def get_d(k):
            if k not in dsem:
                dsem[k] = sem_ctx("d_" + k)
            return dsem[k]

        for o in ops:
            eng = self.e[o.eng]
            need = {}
            for p, nd in o.waits:
                if p.dma is None:
                    s = get_e(p.eng)
                    v = p.count
                else:
                    s = get_d(p.dma)
                    v = 16 * nd
                k = id(s)
                if k not in need or need[k][1] < v:
                    need[k] = (s, v)
            nl = list(need.values())
            ride = None
            if o.attach and nl:
                ride = nl.pop()
            for s, v in nl:
                eng.wait_ge(s, v)
                n_waits += 1
            ins = o.fn()
            if ride is not None:
                ins._wait_ge(ride[0], eng.lower_val(ride[1]))
            if o.dma is not None:
                ins.then_inc(get_d(o.dma), 16)
                dtot[o.dma] = dtot.get(o.dma, 0) + 16
            elif o.signal:
                ins.then_inc(get_e(o.eng), 1)
        for k, s in dsem.items():
            self.e["sp"].wait_ge(s, dtot[k])
        self.stats = dict(n_ops=len(ops), n_waits=n_waits, n_dsem=len(dsem),
                          sig={k: cnt[k] for k in cnt})


CFG = {"phases": ("0", "1a", "1b", "2a", "2b"), "tiles": None}


def build_program():
    nc = bass.Bass("TRN2", target_bir_lowering=False)
    PH = CFG["phases"]

    def din(name, shape):
        return nc.dram_tensor(name, list(shape), F32, kind="ExternalInput").ap()

    def dout(name, shape):
        return nc.dram_tensor(name, list(shape), F32, kind="ExternalOutput").ap()

    xall = din("xall", [NROWS, D])
    c3 = din("c3", [3, D])
    cache_k = din("cache_k", [NS, PAST, 512])
    cache_v = din("cache_v", [NS, PAST, 512])
    C0 = din("C0", [NS, 4, 128, 128])
    n0 = din("n0", [NS, 4, 128])
    m0 = din("m0", [NS, 4])
    conv0 = din("conv0", [NS, 3, D])
    norm1_g = din("norm1_g", [1, D])
    norm2_g = din("norm2_g", [1, D])
    w_ada = din("w_ada", [D, 6 * D])
    b_ada = din("b_ada", [1, 6 * D])
    w_in = din("w_in", [D, INW])
    b_if = din("b_if", [8, 1])
    w_conv = din("w_conv", [4, D])
    b_conv = din("b_conv", [1, D])
    ml_norm_g = din("ml_norm_g", [1, 512])
    w_a = din("w_a", [512, D])
    w_b = din("w_b", [512, D])
    w_out = din("w_out", [D, D])
    w_ff_gate = din("w_ff_gate", [D, DFF])
    w_ff_up = din("w_ff_up", [D, DFF])
    w_ff_down = din("w_ff_down", [DFF, D])
    final_g = din("final_g", [1, D])

    y_all = dout("y_all", [NROWS, D])
    k_all = dout("k_all", [NROWS, 512])
    v_all = dout("v_all", [NROWS, 512])
    C_out = dout("C_out", [3, 4, 128, 128])
    n_out = dout("n_out", [3, 4, 128])
    m_out = dout("m_out", [3, 4])
    conv_out = dout("conv_out", [3, 3, D])

    mod_sc = nc.dram_tensor("mod_sc", [3, 6 * D], F32).ap()
    A_sc = nc.dram_tensor("A_sc", [NROWS, D], F32).ap()
    x1_sc = nc.dram_tensor("x1_sc", [NROWS, D], F32).ap()
    h_sc = nc.dram_tensor("h_sc", [NROWS, DFF], BF16).ap()

    tiles = [(0, t * 128, 128, t) for t in range(16)] + [(1, SEQ, 64, 0), (2, SEQ + 64, 64, 0)]
    if CFG["tiles"] is not None:
        tiles = [tiles[i] for i in CFG["tiles"]]

    with ExitStack() as st:
        P = Prog(nc)

        def sb(name, shape, dt=F32):
            return st.enter_context(nc.sbuf_tensor(name, list(shape), dt)), Buf(name)

        def ps(name, shape, dt=F32):
            return st.enter_context(nc.psum_tensor(name, list(shape), dt)), Buf(name, excl=True)

        SCRN = 20736
        SCR = st.enter_context(nc.sbuf_tensor("SCR", [128, SCRN], F32))
        scr = {"off": 0, "fence": []}

        def phase_begin():
            scr["off"] = 0
            scr["fence"] = P.fence()

        def fb(name):
            b = Buf(name)
            b.readers = list(scr["fence"])
            return b

        def cv(name, shape, dt=F32):
            n = 1
            for d_ in shape[1:]:
                n *= d_
            nf = (n + 1) // 2 if dt == BF16 else n
            nf = (nf + 7) // 8 * 8
            off = scr["off"]
            scr["off"] += nf
            assert scr["off"] <= SCRN, (name, scr["off"])
            v = SCR[0:shape[0], off:off + nf]
            if dt == BF16:
                v = v.bitcast(BF16)
            v = v[:, 0:n]
            if len(shape) == 3:
                v = v.rearrange("p (a b) -> p a b", a=shape[1])
            return v, fb(name)

        def E(eng, name, reads, writes, *a, **kw):
            m = getattr(P.e[eng], name)
            return P.op(eng, lambda: m(*a, **kw), reads, writes)

        def DMA(eng, key, out, in_, reads, writes, **kw):
            m = P.e[eng].dma_start
            return P.op(eng, lambda: m(out=out, in_=in_, **kw), reads, writes, dma=key)

        def MM(out, lhsT, rhs, start, stop, reads, writes):
            m = nc.tensor.matmul
            return P.op("pe", lambda: m(out, lhsT=lhsT, rhs=rhs, start=start, stop=stop), reads, writes)

        def TR(out, in_, ident, reads, writes):
            m = nc.tensor.transpose
            return P.op("pe", lambda: m(out=out, in_=in_, identity=ident), reads, writes)

        def ACT(out, in_, func, reads, writes, **kw):
            m = nc.scalar.activation
            o_ = P.op("act", lambda: m(out=out, in_=in_, func=func, **kw), reads, writes)
            if "accum_out" in kw:
                o_.attach = False
            return o_

        WB, bWB = sb("WB", [128, 46080], BF16)
        identb, bidb = sb("identb", [128, 128], BF16)
        identf, bidf = sb("identf", [128, 128], F32)
        onesb, bones = sb("onesb", [128, 512], BF16)
        zer, bzer = sb("zer", [128, 128], F32)
        sel, bsel = sb("sel", [4, 4, 128], F32)
        rstd2, brstd2 = sb("rstd2", [128, 18], F32)
        csT, bcsT = sb("csT", [128, 8, 3], BF16)
        mhalf, bmhalf = sb("mhalf", [128, 4], F32)
        cmask, bcmask = sb("cmask", [128, 128], BF16)
        gmb, bgmb = sb("gmb", [128, D], F32)
        shb, bshb = sb("shb", [128, D], F32)
        ggb, bggb = sb("ggb", [128, D], F32)
        xts = [sb("xt%d" % i, [128, D], F32) for i in range(2)]
        T1, bT1 = sb("T1", [128, D], F32)
        T2, bT2 = sb("T2", [128, D], F32)
        ub, bub = sb("ub", [128, D], BF16)
        uT, buT = sb("uT", [128, 8, 128], BF16)
        ssq, bssq = sb("ssq", [128, 4], F32)
        stg = []

        ptr, bptr = ps("ptr", [128, 1024], BF16)
        pat, bpat = ps("pat", [128, 1024], BF16)
        pa, bpa = ps("pa", [128, 512], F32)
        pb, bpb = ps("pb", [128, 512], F32)
        py, bpy = ps("py", [128, 512], F32)
        pm, bpm = ps("pm", [128, 512], F32)
        pz2, bpz2 = ps("pz2", [128, 1024], F32)
        bpz = [Buf("pz0", excl=True), Buf("pz1", excl=True)]
        pab = [(pa, bpa), (pb, bpb)]
        rot = {"ab": 0, "stg": 0, "x": 0}

        def next_ab():
            rot["ab"] = (rot["ab"] + 1) % len(pab)
            return pab[rot["ab"]]

        def next_stg():
            rot["stg"] = (rot["stg"] + 1) % 2
            return stg[rot["stg"]]

        E("pool", "memset", [], [bidf], identf[:], 1.0)
        E("pool", "affine_select", [bidf], [bidf], out=identf[:], in_=identf[:], pattern=[[-1, 128]],
          compare_op=ALU.is_equal, fill=0.0, base=0, channel_multiplier=1)
        E("pool", "tensor_copy", [bidf], [bidb], out=identb[:], in_=identf[:])
        E("pool", "memset", [], [bones], onesb[:], 1.0)
        E("pool", "memset", [], [bzer], zer[:], 0.0)
        E("pool", "memset", [], [bsel], sel[:], 1.0)
        E("pool", "affine_select", [bsel], [bsel], out=sel[:], in_=sel[:], pattern=[[-1, 4], [0, 128]],
          compare_op=ALU.is_equal, fill=0.0, base=0, channel_multiplier=1)
        E("pool", "memset", [], [brstd2], rstd2[:], 1.0)
        E("pool", "memset", [], [bmhalf], mhalf[:], -0.5)
        E("pool", "memset", [], [bcmask], cmask[:], -30000.0)
        E("pool", "affine_select", [bcmask], [bcmask], out=cmask[:], in_=cmask[:], pattern=[[1, 128]],
          compare_op=ALU.is_ge, fill=0.0, base=0, channel_multiplier=-1)

        def rstd_pow(out_ap, tmp_ap, ss_ap, npart, ncol, scale, rds, wrs):
            E("pool", "tensor_scalar", rds, wrs, out=tmp_ap, in0=ss_ap, scalar1=scale, scalar2=EPS, op0=ALU.mult, op1=ALU.add)
            E("pool", "tensor_tensor", wrs + [bmhalf], wrs, out=out_ap, in0=tmp_ap, in1=mhalf[0:npart, 0:ncol], op=ALU.pow)

        def load_w(dram, r0, nrows_chunks, c0, ncols, off, key):
            bufs = []
            for kc in range(nrows_chunks):
                for cc in range(0, ncols, 2048):
                    n = min(2048, ncols - cc)
                    bw = fb("W%s_%d_%d" % (key, kc, cc))
                    bufs.append(bw)
                    DMA("pool", "W" + key, WB[:, off + kc * ncols + cc: off + kc * ncols + cc + n],
                        dram[r0 + kc * 128: r0 + (kc + 1) * 128, c0 + cc: c0 + cc + n], [], [bw])
            return bufs

        def load_wg(dram, nrows_chunks, c0, groups, stride, off, kp, kbase=0):
            out = {}
            dv = dram.rearrange("(k p) n -> p k n", p=128)
            wv = WB[:, off: off + nrows_chunks * stride].rearrange("p (k c) -> p k c", k=nrows_chunks)
            for gi, (nm, l0, ncols) in enumerate(groups):
                bufs = []
                for cc in range(0, ncols, 2048):
                    n = min(2048, ncols - cc)
                    bw = fb("W%s_%s_%d" % (kp, nm, cc))
                    bufs.append(bw)
                    DMA("pool", "Wg%d" % (kbase + gi), wv[:, :, l0 + cc: l0 + cc + n],
                        dv[:, :, c0 + l0 + cc: c0 + l0 + cc + n], [], [bw])
                out[nm] = bufs
            return out

        def wview(off, nk, ncols):
            return WB[:, off: off + nk * ncols].rearrange("p (k c) -> p k c", k=nk)

        scr["fence"] = []
        WL_1a = load_wg(w_in, 8, 0, [("q", 0, 512), ("k", 512, 512), ("v", 1024, 512)], 1536, 0, "a")
        WL_1a["a"] = load_wg(w_a, 4, 0, [("a", 0, 1024)], 1024, 12288, "wa", kbase=3)["a"]

        phase_begin()
        stg[:] = [cv("stg%d" % i, [128, 512], F32) for i in range(2)]
        cT, bcT = cv("cT", [128, 8, 3], F32)
        for s_ in range(3):
            DMA("sp", "cst", cT[:, :, s_], c3[s_].rearrange("(k p) -> p k", p=128), [], [bcT],
                allow_slow_non_contiguous=True)
        ACT(csT[:], cT[:], AF.Silu, [bcT], [bcsT])
        WAs = [cv("WA%d" % i, [128, 8, 512], BF16) for i in range(2)]
        w_ada_v = w_ada.rearrange("(k p) n -> p k n", p=128)
        bmods = {"A": Buf("modA"), "B": Buf("modB"), "C": Buf("modC")}

        def mod_group(nch):
            return "A" if nch < 4 else ("B" if nch < 6 else "C")

        def mod_chunk_load(nch, WA, bWA, key):
            DMA("pool", key, WA[:], w_ada_v[:, :, nch * 512:(nch + 1) * 512], [], [bWA])

        def mod_chunk_compute(nch, WA, bWA, pp, bpp):
            sg, bsg = next_stg()
            DMA("sp", "bad", sg[0:3, :], b_ada[0:1, nch * 512:(nch + 1) * 512].broadcast_to([3, 512]), [], [bsg])
            for kc in range(8):
                MM(pp[0:3, :], csT[:, kc, :], WA[:, kc, :], kc == 0, kc == 7, [bcsT, bWA], [bpp])
            E("dve", "tensor_tensor", [bpp, bsg], [bsg], out=sg[0:3, :], in0=pp[0:3, :], in1=sg[0:3, :], op=ALU.add)
            DMA("sp", "modw", mod_sc[:, nch * 512:(nch + 1) * 512], sg[0:3, :], [bsg], [bmods[mod_group(nch)]])

        for nch in range(4 if "0" in PH else 0):
            WA, bWA = WAs[nch % 2]
            mod_chunk_load(nch, WA, bWA, "wa%d" % (nch % 2))
            mod_chunk_compute(nch, WA, bWA, pm, bpm)

        def load_mod(s, which, front=True, gate=True, gdst=None):
            base = 0 if which == 1 else 3 * D
            ng = norm1_g if which == 1 else norm2_g
            bf_ = bmods["A"] if which == 1 else bmods["C"]
            bg_ = bmods["B"] if which == 1 else bmods["C"]
            if front:
                DMA("sp", "modr", shb[:], mod_sc[s:s + 1, base:base + D].broadcast_to([128, D]), [bf_], [bshb])
                DMA("sp", "modr", gmb[:], mod_sc[s:s + 1, base + D:base + 2 * D].broadcast_to([128, D]), [bf_], [bgmb])
                DMA("sp", "modr", T2[:], ng[0:1, :].broadcast_to([128, D]), [], [bT2])
                E("dve", "scalar_tensor_tensor", [bgmb, bT2], [bgmb], out=gmb[:], in0=gmb[:], scalar=1.0, in1=T2[:],
                  op0=ALU.add, op1=ALU.mult)
            if gate:
                gd, bgd = gdst if gdst is not None else (ggb, bggb)
                DMA("sp", "modg", gd[:], mod_sc[s:s + 1, base + 2 * D:base + 3 * D].broadcast_to([128, D]), [bg_], [bgd])

        def rms_to_uT(xt, bxt, ntok, rstd_ap=None, dst=None):
            if rstd_ap is None:
                ACT(T2[0:ntok, :], xt[0:ntok, :], AF.Square, [bxt], [bT2, bssq], accum_out=ssq[0:ntok, 0:1])
                rstd_pow(ssq[0:ntok, 2:3], ssq[0:ntok, 1:2], ssq[0:ntok, 0:1], ntok, 1, 1.0 / D, [bssq], [bssq])
                rstd_ap = ssq[0:ntok, 2:3]
                rb = bssq
            else:
                rb = brstd2
            E("dve", "scalar_tensor_tensor", [bxt, rb, bgmb], [bT1], out=T1[0:ntok, :], in0=xt[0:ntok, :],
              scalar=rstd_ap, in1=gmb[0:ntok, :], op0=ALU.mult, op1=ALU.mult)
            E("pool", "tensor_tensor", [bT1, bshb], [bub], out=ub[0:ntok, :], in0=T1[0:ntok, :], in1=shb[0:ntok, :],
              op=ALU.add)
            for kc in range(8):
                TR(ptr[:, kc * 128: kc * 128 + ntok], ub[0:ntok, kc * 128:(kc + 1) * 128], identb[0:ntok, 0:ntok],
                   [bub, bidb], [bptr])
            uTd, buTd = dst if dst is not None else (uT, buT)
            ACT(uTd[:, :, 0:ntok], ptr[:].rearrange("p (k t) -> p k t", k=8)[:, :, 0:ntok], AF.Copy, [bptr], [buTd])

        epst, bepst = sb("epst", [128, 1], F32)
        E("pool", "memset", [], [bepst], epst[:], EPS)
        EPS_AP = epst

        def load_x(src, row0, ntok):
            rot["x"] ^= 1
            xt, bxt = xts[rot["x"]]
            DMA("sp", "xl%d" % rot["x"], xt[0:ntok, :], src[row0:row0 + ntok, :], [], [bxt])
            return xt, bxt

        def proj_tok(w3, c0, n, ntok, t0=0, wl=(), us=None):
            pp, bpp = next_ab()
            for kc in range(8):
                uTs_, buTs_ = us if us is not None else (uT, buT)
                MM(pp[0:ntok, 0:n], uTs_[:, kc, t0:t0 + ntok], w3[:, kc, c0:c0 + n], kc == 0, kc == 7, [buTs_] + list(wl), [bpp])
            return pp, bpp

        phase_begin()
        WL = WL_1a
        WLb_pre = load_wg(w_b, 4, 0, [("b", 0, 1024)], 1024, 32832, "wb", kbase=6)["b"]
        WLout_pre = load_wg(w_out, 8, 0, [("o", 0, 1024)], 1024, 36928, "wo", kbase=7)["o"]
        win_a = wview(0, 8, 1536)
        wa3 = wview(12288, 4, 1024)
        KTm = WB[:, 16384:24576].rearrange("p (c k) -> p c k", c=4)
        Vm = WB[:, 24576:32768].rearrange("p (t c) -> p t c", t=16)
        stor_main = dict(KT=KTm, Vst=Vm, bKT=[fb("KT%d" % i) for i in range(16)], bV=[fb("V%d" % i) for i in range(16)])
        KTa, _ = cv("KTalt", [128, 4, 1152], BF16)
        Va, _ = cv("Valt", [128, 9, 512], BF16)
        stor_alt = dict(KT=KTa, Vst=Va, bKT=[fb("KTa%d" % i) for i in range(9)], bV=[fb("Va%d" % i) for i in range(9)])
        stg[:] = [cv("stg%d" % i, [128, 512], F32) for i in range(2)]
        qTzs = [cv("qTz%d" % i, [128, 8, 128], BF16) for i in range(2)]
        for qz, bqz in qTzs:
            E("pool", "memset", [], [bqz], qz[:], 0.0)
        Ktok, bKtok = cv("Ktok", [128, 8, 512], BF16)
        NSL = 5
        att = [dict(g=cv("ag%d" % i, [128, 520], F32), Pb=cv("aP%d" % i, [128, 520], F32),
                    a=cv("aa%d" % i, [128, 512], BF16),
                    aT=cv("aaT%d" % i, [128, 4, 128], BF16)) for i in range(NSL)]
        ONE_REG = nc.gpsimd.to_reg(1.0)
        zerob, bzerob = cv("zerob", [128, 520], BF16)
        E("pool", "memset", [], [bzerob], zerob[:], 0.0)
        for A_ in att:
            E("pool", "memset", [], [A_["g"][1]], A_["g"][0][:], 1.0)
        ya, bya = cv("ya", [128, 512], BF16)
        yaT, byaT = cv("yaT", [128, 4, 128], BF16)
        Ast, bAst = cv("Ast", [128, D], F32)
        WAbg, bWAbg = cv("WAbg", [128, 8, 512], BF16)
        pzv = [pz2[:, 0:512], pz2[:, 512:1024]]
        patv2 = [(pat, bpat), (pm[:].bitcast(BF16), bpm)]
        job_ctr = [0]
        Abuf = {}
        x1buf = {}
        seq_loaded = [-1]

        def prologue_pieces(tile, tno, stor, first_of_seq):
            (s, row0, ntok, ti) = tile
            kpos0 = ti * 128 if s == 0 else PAST
            ktile = ti if s == 0 else 8
            qTz, bqTz = qTzs[tno % 2]
            KT, Vst, bKT, bV = stor["KT"], stor["Vst"], stor["bKT"], stor["bV"]
            cx = dict(s=s, row0=row0, ntok=ntok, kpos0=kpos0, ktile=ktile, qTz=qTz, bqTz=bqTz, stor=stor)
            hold = {}

            xslot = tno % 2
            xt, bxt = xts[xslot]
            hold = {}

            def PL():
                if first_of_seq:
                    load_mod(s, 1, gate=False)
                    if s > 0:
                        si = s - 1
                        DMA("pool", "kvc", Vst[:, 0:8, :], cache_v[si].rearrange("(k p) c -> p k c", p=128), [], bV[0:8])
                        DMA("pool", "kvc", Ktok[:], cache_k[si].rearrange("(k p) c -> p k c", p=128), [], [bKtok])
                DMA("sp", "xl%d" % xslot, xt[0:ntok, :], xall[row0:row0 + ntok, :], [], [bxt])

            def PK():
                for kt in range(8):
                    for c in range(4):
                        TR(ptr[:, c * 128:(c + 1) * 128], Ktok[:, kt, c * 128:(c + 1) * 128], identb[:],
                           [bKtok, bidb], [bptr])
                    ACT(KT[:, :, kt * 128:(kt + 1) * 128], ptr[:, 0:512].rearrange("p (c t) -> p c t", c=4), AF.Copy,
                        [bptr], [bKT[kt]])

            def PA():
                ACT(T2[0:ntok, :], xt[0:ntok, :], AF.Square, [bxt], [bT2, bssq], accum_out=ssq[0:ntok, 0:1])
                rstd_pow(ssq[0:ntok, 2:3], ssq[0:ntok, 1:2], ssq[0:ntok, 0:1], ntok, 1, 1.0 / D, [bssq], [bssq])

            def PB():
                E("dve", "scalar_tensor_tensor", [bxt, bssq, bgmb], [bT1], out=T1[0:ntok, :], in0=xt[0:ntok, :],
                  scalar=ssq[0:ntok, 2:3], in1=gmb[0:ntok, :], op0=ALU.mult, op1=ALU.mult)

            def PC():
                E("pool", "tensor_tensor", [bT1, bshb], [bub], out=ub[0:ntok, :], in0=T1[0:ntok, :], in1=shb[0:ntok, :],
                  op=ALU.add)

            def PD():
                for kc in range(8):
                    TR(ptr[:, kc * 128: kc * 128 + ntok], ub[0:ntok, kc * 128:(kc + 1) * 128], identb[0:ntok, 0:ntok],
                       [bub, bidb], [bptr])
                ACT(uT[:, :, 0:ntok], ptr[:].rearrange("p (k t) -> p k t", k=8)[:, :, 0:ntok], AF.Copy, [bptr], [buT])

            def Q1():
                pp, bpp = next_ab()
                hold["q"] = (pp, bpp)
                for c in range(4):
                    for kc in range(8):
                        MM(pp[:, c * 128: c * 128 + ntok], win_a[:, kc, c * 128:(c + 1) * 128], uT[:, kc, 0:ntok],
                           kc == 0, kc == 7, [buT] + WL["q"], [bpp])
                ppv = pp[:].rearrange("p (c t) -> p c t", c=4)
                qv = qTz[:].rearrange("p (c two) t -> p c two t", two=2)
                ACT(qv[0:64, :, 0, 0:ntok], ppv[0:64, :, 0:ntok], AF.Copy, [bpp], [bqTz])
                E("dve", "tensor_copy", [bpp], [bqTz], out=qv[64:128, :, 1, 0:ntok], in_=ppv[64:128, :, 0:ntok])

            def K1():
                pp, bpp = next_ab()
                for c in range(4):
                    for kc in range(8):
                        MM(pp[:, c * 128: c * 128 + ntok], win_a[:, kc, 512 + c * 128: 512 + (c + 1) * 128], uT[:, kc, 0:ntok],
                           kc == 0, kc == 7, [buT] + WL["k"], [bpp])
                ACT(KT[:, :, kpos0:kpos0 + ntok], pp[:].rearrange("p (c t) -> p c t", c=4)[:, :, 0:ntok], AF.Copy,
                    [bpp], [bKT[ktile]])

            def K2():
                pp, bpp = proj_tok(win_a, 512, 512, ntok, wl=WL["k"])
                sg, bsg = next_stg()
                E("dve", "tensor_copy", [bpp], [bsg], out=sg[0:ntok, :], in_=pp[0:ntok, :])
                DMA("sp", "ko", k_all[row0:row0 + ntok, :], sg[0:ntok, :], [bsg], [])

            def V1():
                pp, bpp = proj_tok(win_a, 1024, 512, ntok, wl=WL["v"])
                sg, bsg = next_stg()
                E("dve", "tensor_copy", [bpp], [bsg], out=sg[0:ntok, :], in_=pp[0:ntok, :])
                ACT(Vst[0:ntok, ktile, :], pp[0:ntok, :], AF.Copy, [bpp], [bV[ktile]])
                DMA("sp", "vo", v_all[row0:row0 + ntok, :], sg[0:ntok, :], [bsg], [])

            pk = PK if (first_of_seq and s > 0) else None
            return cx, [PL, None, pk, PA, PB, PC, None, PD, None, Q1, None, K1, None, K2, None, V1]

        def attention_jobs(cx):
            ntok, kpos0, qTz, bqTz = cx["ntok"], cx["kpos0"], cx["qTz"], cx["bqTz"]
            KT, Vst, bKT, bV = cx["stor"]["KT"], cx["stor"]["Vst"], cx["stor"]["bKT"], cx["stor"]["bV"]
            nk = kpos0 + ntok
            nblk = (nk + 511) // 512
            jobs = []
            for h in range(8):
                prev = None
                for b in range(nblk - 1, -1, -1):
                    job_ctr[0] += 1
                    J = dict(h=h, b=b, slot=job_ctr[0] % NSL, zi=job_ctr[0] % 2, prev=prev,
                             first=(b == nblk - 1), last=(b == 0))
                    jobs.append(J)
                    prev = J

            def geo(J):
                kb0 = J["b"] * 512
                nkb = min(512, nk - kb0)
                kts = list(range(kb0 // 128, (kb0 + nkb + 127) // 128))
                return kb0, nkb, kts

            def S0(J):
                kb0, nkb, kts = geo(J)
                MM(pzv[J["zi"]][0:ntok, 0:nkb], qTz[:, J["h"], 0:ntok], KT[:, J["h"] // 2, kb0:kb0 + nkb], True, not J["first"],
                   [bqTz] + [bKT[k] for k in kts], [bpz[J["zi"]]])
                if J["first"]:
                    MM(pzv[J["zi"]][0:ntok, nkb - ntok:nkb], identb[0:ntok, 0:ntok], cmask[0:ntok, 0:ntok], False, True,
                       [bidb, bcmask], [bpz[J["zi"]]])

            def S1(J):
                kb0, nkb, kts = geo(J)
                g_, bg = att[J["slot"]]["g"]
                pz = pzv[J["zi"]]
                ACT(g_[0:ntok, 512 - nkb:512], pz[0:ntok, 0:nkb], AF.Sigmoid, [bpz[J["zi"]]], [bg], scale=-0.125)

            def S2(J):
                kb0, nkb, kts = geo(J)
                A_ = att[J["slot"]]
                (g_, bg), (Pb, bP) = A_["g"], A_["Pb"]
                if J["prev"] is None:
                    init = 1.0
                    rd = [bg, bzerob]
                else:
                    pPb, bpP = att[J["prev"]["slot"]]["Pb"]
                    pk = geo(J["prev"])[1]
                    init = pPb[0:ntok, 512 - pk:512 - pk + 1]
                    rd = [bg, bzerob, bpP]
                E("dve", "tensor_tensor_scan", rd, [bP], out=Pb[0:ntok, 512 - nkb:513][:, ::-1],
                  data0=g_[0:ntok, 512 - nkb:513][:, ::-1], data1=zerob[0:ntok, 0:nkb + 1], initial=init,
                  op0=ALU.mult, op1=ALU.add)

            def S3(J):
                pass

            def S4(J):
                kb0, nkb, kts = geo(J)
                A_ = att[J["slot"]]
                (Pb, bP), (a_, ba) = A_["Pb"], A_["a"]
                E("pool", "tensor_tensor", [bP], [ba], out=a_[0:ntok, 0:nkb], in0=Pb[0:ntok, 512 - nkb + 1:513],
                  in1=Pb[0:ntok, 512 - nkb:512], op=ALU.subtract)

            def S5(J):
                kb0, nkb, kts = geo(J)
                a_, ba = att[J["slot"]]["a"]
                pT, bpT = patv2[J["zi"]]
                for j, kt in enumerate(kts):
                    ksz = min(128, nk - kt * 128)
                    TR(pT[0:ksz, j * 128: j * 128 + ntok], a_[0:ntok, j * 128: j * 128 + ksz],
                       identb[0:ntok, 0:ntok], [ba, bidb], [bpT])

            def S6(J):
                kb0, nkb, kts = geo(J)
                aT_, baT = att[J["slot"]]["aT"]
                pT, bpT = patv2[J["zi"]]
                nsub = len(kts)
                pv = pT[:, 0:512].rearrange("p (j t) -> p j t", j=4)
                lastk = min(128, nk - kts[-1] * 128)
                if lastk == 128:
                    ACT(aT_[:, 0:nsub, 0:ntok], pv[:, 0:nsub, 0:ntok], AF.Copy, [bpT], [baT])
                else:
                    if nsub > 1:
                        ACT(aT_[:, 0:nsub - 1, 0:ntok], pv[:, 0:nsub - 1, 0:ntok], AF.Copy, [bpT], [baT])
                    ACT(aT_[0:lastk, nsub - 1, 0:ntok], pv[0:lastk, nsub - 1, 0:ntok], AF.Copy, [bpT], [baT])

            def S7(J):
                kb0, nkb, kts = geo(J)
                aT_, baT = att[J["slot"]]["aT"]
                h = J["h"]
                nsub = len(kts)
                for j, kt in enumerate(kts):
                    ksz = min(128, nk - kt * 128)
                    MM(py[0:ntok, h * 64:(h + 1) * 64], aT_[0:ksz, j, 0:ntok], Vst[0:ksz, kt, h * 64:(h + 1) * 64],
                       J["first"] and j == 0, J["last"] and j == nsub - 1, [baT, bV[kt]], [bpy])

            return jobs, [S0, S1, S2, S3, S4, S5, S6, S7]

        def epilogue_pieces(cx):
            ntok, row0 = cx["ntok"], cx["row0"]

            def E0():
                ACT(ya[0:ntok, :], py[0:ntok, :], AF.Copy, [bpy], [bya])

            def E1():
                for c in range(4):
                    TR(ptr[:, c * 128: c * 128 + ntok], ya[0:ntok, c * 128:(c + 1) * 128], identb[0:ntok, 0:ntok],
                       [bya, bidb], [bptr])
                ACT(yaT[:, :, 0:ntok], ptr[:, 0:512].rearrange("p (c t) -> p c t", c=4)[:, :, 0:ntok], AF.Copy, [bptr], [byaT])

            def E2():
                for n in range(2):
                    pp, bpp = next_ab()
                    for c in range(4):
                        MM(pp[0:ntok, :], yaT[:, c, 0:ntok], wa3[:, c, n * 512:(n + 1) * 512], c == 0, c == 3,
                           [byaT] + WL["a"], [bpp])
                    E("dve", "tensor_copy", [bpp], [bAst], out=Ast[0:ntok, n * 512:(n + 1) * 512], in_=pp[0:ntok, :])
                bAsc = Buf("Asc%d" % row0)
                DMA("sp", "Aw", A_sc[row0:row0 + ntok, :], Ast[0:ntok, :], [bAst], [bAsc])
                Abuf[row0] = bAsc

            return [E0, E1, E2]

        if "1a" in PH:
            items = []
            jbase = 0
            starts = []
            njs = []
            seq_idx = -1
            prev_s = None
            for li_, tile in enumerate(tiles):
                first_of_seq = (tile[0] != prev_s)
                if first_of_seq:
                    seq_idx += 1
                    prev_s = tile[0]
                stor = stor_main if seq_idx % 2 == 0 else stor_alt
                cx, pieces = prologue_pieces(tile, li_, stor, first_of_seq)
                if li_ == 0:
                    for i_, pf in enumerate(pieces):
                        if pf is not None:
                            items.append((-100 + i_, 9.0, pf))
                else:
                    pst = starts[li_ - 1] + 1
                    if first_of_seq:
                        pst = max(pst, stor.get("last_step", -1) + 1)
                    pend = starts[li_ - 1] + njs[li_ - 1] - 1
                    avail = max(1, pend - pst)
                    L_ = len(pieces)
                    for i_, pf in enumerate(pieces):
                        if pf is not None:
                            items.append((pst + (i_ * avail) // L_, 9.0 + i_ * 0.01, pf))
                jobs, stages = attention_jobs(cx)
                starts.append(jbase)
                njs.append(len(jobs))
                NSTG = len(stages)
                for ji, J in enumerate(jobs):
                    for k, Sf in enumerate(stages):
                        items.append((jbase + ji + k, float(NSTG - 1 - k), (lambda Sf=Sf, J=J: Sf(J))))
                last_step = jbase + len(jobs) - 1 + (NSTG - 1)
                stor["last_step"] = last_step
                ep = epilogue_pieces(cx)
                items.append((last_step, 0.5, ep[0]))
                items.append((last_step + 2, 8.5, ep[1]))
                items.append((last_step + 4, 8.6, ep[2]))
                jbase += len(jobs)
            if "0" in PH:
                nsteps = jbase + 8
                gap = max(14, (nsteps - 30) // 8)
                for bi, nch in enumerate(range(4, 12)):
                    t_ = 10 + bi * gap
                    items.append((t_, 9.5, (lambda nch=nch: mod_chunk_load(nch, WAbg, bWAbg, "wabg"))))

                    def comp(nch=nch):
                        pp, bpp = next_ab()
                        mod_chunk_compute(nch, WAbg, bWAbg, pp, bpp)
                    items.append((t_ + 10, 9.6, comp))
            order = sorted(range(len(items)), key=lambda i: (items[i][0], items[i][1], i))
            for i in order:
                items[i][2]()

        phase_begin()
        NB = 4104
        WL = load_wg(w_in, 8, 1536, [("mqk", 0, 1024), ("gt", 2048, 8), ("mv", 1024, 512), ("mo", 1536, 512),
                                     ("gb", 3080, 1024), ("ga", 2056, 1024)], NB, 0, "b")
        WL["b"] = WLb_pre
        WL["out"] = WLout_pre
        winb = wview(0, 8, NB)
        wb3 = wview(32832, 4, 1024)
        wo3 = wview(36928, 8, 1024)
        wcv, bwcv = cv("wcv", [128, 8, 4], F32)
        bcv, bbcv = cv("bcv", [128, 8], F32)
        mlg, bmlg = cv("mlg", [64, 512], F32)
        bifi, bbifi = cv("bifi", [4, 1], F32)
        biff, bbiff = cv("biff", [4, 1], F32)
        for j_ in range(4):
            DMA("sp", "cst", wcv[:, :, j_], w_conv[j_].rearrange("(c p) -> p c", p=128), [], [bwcv], allow_slow_non_contiguous=True)
        DMA("sp", "cst", bcv[:], b_conv[0].rearrange("(c p) -> p c", p=128), [], [bbcv], allow_slow_non_contiguous=True)
        DMA("sp", "cst", mlg[:], ml_norm_g[0:1, :].broadcast_to([64, 512]), [], [bmlg])
        DMA("sp", "cst", bifi[:], b_if[0:4, :], [], [bbifi])
        DMA("sp", "cst", biff[:], b_if[4:8, :], [], [bbiff])
        SC_ = 5
        CS_ = 2
        xs3 = [xts[0], xts[1], cv("xt2", [128, D], F32)]
        uT3 = [(uT, buT), cv("uT1", [128, 8, 128], BF16), cv("uT2", [128, 8, 128], BF16)]
        mqk2 = [cv("mqkT%d" % i, [128, 8, 128], BF16) for i in range(2)]
        xp2 = [cv("xp%d" % i, [128, 8, 131], F32) for i in range(2)]
        gl2 = [dict(li=cv("gli%d" % i, [4, 128], F32), sf=cv("gsf%d" % i, [4, 128], F32),
                    lf=cv("glf%d" % i, [4, 128], F32)) for i in range(2)]
        ybT2 = [cv("ybT%d" % i, [128, 4, 128], BF16) for i in range(2)]
        At, bAt = cv("At", [128, D], F32)
        sgt, bsgt = cv("sgt", [128, 512], F32)
        mg, bmg = cv("mg", [128, D], BF16)
        mT, bmT = cv("mT", [128, 8, 128], BF16)
        CTs = [cv("CT%d" % i, [128, 4, 129], F32) for i in range(3)]
        ggbs = [(ggb, bggb), cv("ggb1", [128, D], F32)]
        C0t, bC0t = At[:, 0:512].rearrange("p (h k) -> p h k", h=4), bAt
        mseq, bmseq = cv("mseq", [4, 40], F32)
        CK = []
        for i in range(CS_):
            d_ = {}
            for nm in ("bb", "rr", "mmt", "wg", "wi", "emt"):
                d_[nm] = cv("g_%s%d" % (nm, i), [4, 64], F32)
            d_["dec"] = cv("g_dec%d" % i, [4, 128], F32)
            d_["gsm"] = cv("gsm%d" % i, [4, 8], F32)
            d_["gtok"] = cv("gtok%d" % i, [128, 16], F32)
            d_["Wt"] = cv("Wt%d" % i, [64, 4, 64], F32)
            d_["vaug"] = cv("vaug%d" % i, [64, 4, 129], BF16)
            d_["vw"] = cv("vw%d" % i, [64, 4, 129], BF16)
            d_["smo"] = cv("smo%d" % i, [64, 512], F32)
            d_["ktok"] = cv("ktok%d" % i, [64, 4, 128], BF16)
            d_["ST"] = cv("ST%d" % i, [64, 4, 64], BF16)
            d_["qsT"] = cv("qsT%d" % i, [128, 4, 64], BF16)
            d_["CTb"] = cv("CTb%d" % i, [128, 4, 129], BF16)
            d_["hn"] = cv("hn%d" % i, [64, 4, 129], F32)
            d_["hs"] = cv("hs%d" % i, [64, 16], F32)
            d_["T1h"] = cv("T1h%d" % i, [64, 4, 128], F32)
            d_["ybt"] = cv("ybt%d" % i, [64, 512], BF16)
            E("pool", "memset", [], [d_["vaug"][1]], d_["vaug"][0][:], 1.0)
            CK.append(d_)
        patf = pat[:, 512:1024].bitcast(F32)
        chunk_ctr = [0]
        gchunk = [0]

        def out_state(s, CT, bCT, mcol_final):
            def f():
                for h in range(4):
                    TR(pa[:, h * 128:(h + 1) * 128], CT[:, h, 0:128], identf[:], [bCT, bidf], [bpa])
                E("dve", "tensor_copy", [bpa], [bC0t], out=C0t, in_=pa[:].rearrange("p (h k) -> p h k", h=4))
                DMA("sp", "sto", C_out[s].rearrange("h v k -> v h k"), C0t, [bC0t], [])
                DMA("sp", "sto", n_out[s].rearrange("h k -> k h"), CT[:, :, 128], [bCT], [], allow_slow_non_contiguous=True)
                DMA("sp", "sto", m_out[s:s + 1, :].rearrange("o h -> h o"), mseq[:, mcol_final:mcol_final + 1],
                    [bmseq], [], allow_slow_non_contiguous=True)
            return f

        def out_conv(s, xp_last):
            def f():
                xpl, bxpl, nt_l = xp_last
                for j_ in range(3):
                    DMA("sp", "sto", conv_out[s, j_].rearrange("(c p) -> p c", p=128), xpl[:, :, nt_l + j_], [bxpl], [],
                        allow_slow_non_contiguous=True)
            return f

        def sched_tile(items, lt, tix, tile, prev_xp, sq):
            (s, row0, ntok, ti) = tile
            xt, bxt = xs3[lt % 3]
            uTs, buTs = uT3[lt % 3]
            mqkT, bmqk = mqk2[lt % 2]
            xp, bxp = xp2[lt % 2]
            G_ = gl2[lt % 2]
            (li, bli), (sf, bsf), (lf, blf) = G_["li"], G_["sf"], G_["lf"]
            ybT, bybT = ybT2[lt % 2]
            nch = ntok // 64
            base = 2 * lt * SC_
            CT, bCT = sq["CT"]
            gg_, bgg_ = sq["gg"]
            T1v = T1[:].rearrange("p (c t) -> p c t", c=8)
            T2v = T2[:].rearrange("p (c t) -> p c t", c=8)

            def F0():
                DMA("sp", "xl%d" % (lt % 3), xt[0:ntok, :], xall[row0:row0 + ntok, :], [], [bxt])
                ACT(T2[0:ntok, :], xt[0:ntok, :], AF.Square, [bxt], [bT2, bssq], accum_out=ssq[0:ntok, 0:1])
                rstd_pow(ssq[0:ntok, 2:3], ssq[0:ntok, 1:2], ssq[0:ntok, 0:1], ntok, 1, 1.0 / D, [bssq], [bssq])
                E("dve", "scalar_tensor_tensor", [bxt, bssq, bgmb], [bT1], out=T1[0:ntok, :], in0=xt[0:ntok, :],
                  scalar=ssq[0:ntok, 2:3], in1=gmb[0:ntok, :], op0=ALU.mult, op1=ALU.mult)
                E("pool", "tensor_tensor", [bT1, bshb], [bub], out=ub[0:ntok, :], in0=T1[0:ntok, :], in1=shb[0:ntok, :],
                  op=ALU.add)

            def F1():
                for kc in range(8):
                    TR(ptr[:, kc * 128: kc * 128 + ntok], ub[0:ntok, kc * 128:(kc + 1) * 128], identb[0:ntok, 0:ntok],
                       [bub, bidb], [bptr])
                ACT(uTs[:, :, 0:ntok], ptr[:].rearrange("p (k t) -> p k t", k=8)[:, :, 0:ntok], AF.Copy, [bptr], [buTs])

            def F2(half):
                def f():
                    if half == 0 and prev_xp is not None:
                        pxp, bpxp, pnt = prev_xp
                        E("pool", "tensor_copy", [bpxp], [bxp], out=xp[:, :, 0:3], in_=pxp[:, :, pnt:pnt + 3])
                    pp, bpp = next_ab()
                    for c4 in range(4):
                        ch = half * 4 + c4
                        for kc in range(8):
                            MM(pp[:, c4 * 128: c4 * 128 + ntok], winb[:, kc, ch * 128:(ch + 1) * 128], uTs[:, kc, 0:ntok],
                               kc == 0, kc == 7, [buTs] + WL["mqk"], [bpp])
                    ACT(xp[:, half * 4:(half + 1) * 4, 3:3 + ntok], pp[:].rearrange("p (c t) -> p c t", c=4)[:, :, 0:ntok],
                        AF.Copy, [bpp], [bxp])
                return f

            def F3(p):
                def f():
                    for ch in (2 * p, 2 * p + 1):
                        E("dve", "tensor_scalar", [bxp, bwcv, bbcv], [bT1], out=T1v[:, ch, 0:ntok], in0=xp[:, ch, 0:ntok],
                          scalar1=wcv[:, ch, 0:1], scalar2=bcv[:, ch:ch + 1], op0=ALU.mult, op1=ALU.add)
                        for j in range(1, 4):
                            E("dve", "scalar_tensor_tensor", [bxp, bwcv, bT1], [bT1], out=T1v[:, ch, 0:ntok],
                              in0=xp[:, ch, j:j + ntok], scalar=wcv[:, ch, j:j + 1], in1=T1v[:, ch, 0:ntok],
                              op0=ALU.mult, op1=ALU.add)

                def fpool():
                    tmpf = ub[:, 0:256].bitcast(F32)
                    for ch in (2 * p, 2 * p + 1):
                        E("pool", "tensor_scalar", [bxp, bwcv, bbcv], [bT1], out=T1v[:, ch, 0:ntok], in0=xp[:, ch, 0:ntok],
                          scalar1=wcv[:, ch, 0:1], scalar2=bcv[:, ch:ch + 1], op0=ALU.mult, op1=ALU.add)
                        for j in range(1, 4):
                            E("pool", "tensor_scalar", [bxp, bwcv], [bub], out=tmpf[:, 0:ntok], in0=xp[:, ch, j:j + ntok],
                              scalar1=wcv[:, ch, j:j + 1], scalar2=0.0, op0=ALU.mult, op1=ALU.add)
                            E("pool", "tensor_tensor", [bub, bT1], [bT1], out=T1v[:, ch, 0:ntok], in0=tmpf[:, 0:ntok],
                              in1=T1v[:, ch, 0:ntok], op=ALU.add)
                return fpool if p >= 2 else f

            def F4():
                ACT(T2v[:, :, 0:ntok], T1v[:, :, 0:ntok], AF.Sigmoid, [bT1], [bT2])
                E("dve", "tensor_tensor", [bT1, bT2], [bmqk], out=mqkT[:, 0:4, 0:ntok], in0=T1v[:, 0:4, 0:ntok],
                  in1=T2v[:, 0:4, 0:ntok], op=ALU.mult)
                E("dve", "scalar_tensor_tensor", [bT1, bT2], [bmqk], out=mqkT[:, 4:8, 0:ntok], in0=T1v[:, 4:8, 0:ntok],
                  scalar=float(1.0 / np.sqrt(128.0)), in1=T2v[:, 4:8, 0:ntok], op0=ALU.mult, op1=ALU.mult)

            def F5():
                for kc in range(8):
                    MM(pm[0:4, 0:ntok], winb[:, kc, 2048:2052], uTs[:, kc, 0:ntok], kc == 0, kc == 7, [buTs] + WL["gt"], [bpm])
                ACT(li[:, 0:ntok], pm[0:4, 0:ntok], AF.Identity, [bpm, bbifi], [bli], bias=bifi[:])
                for kc in range(8):
                    MM(pm[0:4, 128:128 + ntok], winb[:, kc, 2052:2056], uTs[:, kc, 0:ntok], kc == 0, kc == 7,
                       [buTs] + WL["gt"], [bpm])
                ACT(sf[:, 0:ntok], pm[0:4, 128:128 + ntok], AF.Sigmoid, [bpm, bbiff], [bsf], bias=biff[:])
                ACT(lf[:, 0:ntok], sf[:, 0:ntok], AF.Ln, [bsf], [blf])

            fl = [(0, F0), (1, F1), (2, F2(0)), (3, F2(1)), (3.5, F5), (4, F3(0)), (5, F3(1)), (6, F3(2)), (7, F3(3)),
                  (8, F4)]
            for j, f in fl:
                items.append((base + j, f))

            def chunk_stages(c):
                cs = slice(c * 64, (c + 1) * 64)
                g = gchunk[0]
                gchunk[0] += 1
                K = CK[g % CS_]
                Kn = CK[(g + 1) % CS_]
                Kp = CK[(g - 1) % CS_]
                mc = chunk_ctr[0]
                chunk_ctr[0] += 1
                mcur = mseq[:, mc:mc + 1]
                (bb_, bbb), (rr, brr), (mmt, bmmt), (wg, bwg), (wi, bwi), (emt, bemt), (dec, bdec) = (
                    K["bb"], K["rr"], K["mmt"], K["wg"], K["wi"], K["emt"], K["dec"])
                (gsm, bgsm), (gtok, bgtok), (Wt, bWt), (vaug, bvaug), (vw, bvw), (smo, bsmo) = (
                    K["gsm"], K["gtok"], K["Wt"], K["vaug"], K["vw"], K["smo"])
                (ktok, bktok), (STt, bST), (qsT, bqsT), (CTb, bCTb), (hn, bhn), (hs, bhs) = (
                    K["ktok"], K["ST"], K["qsT"], K["CTb"], K["hn"], K["hs"])
                hh, bhh = hn[:, :, 0:128], bhn
                (T1h, bT1h), (ybt, bybt) = K["T1h"], K["ybt"]
                mx2 = mmt[:, 63:64]

                def G0():
                    E("dve", "tensor_tensor_scan", [blf, bones], [bbb], out=bb_[:, :], data0=onesb[0:4, 0:64], data1=lf[:, cs],
                      initial=0.0, op0=ALU.mult, op1=ALU.add)
                    E("dve", "tensor_tensor", [bli, bbb], [brr], out=rr[:, :], in0=li[:, cs], in1=bb_[:, :], op=ALU.subtract)
                    E("dve", "tensor_tensor_scan", [brr, bones, bmseq], [bmmt], out=mmt[:, :], data0=onesb[0:4, 0:64],
                      data1=rr[:, :], initial=mcur, op0=ALU.mult, op1=ALU.max)
                    E("dve", "tensor_scalar", [bmmt], [bgsm], out=gsm[:, 0:1], in0=mx2, scalar1=-1.0, scalar2=None, op0=ALU.mult)
                    E("dve", "tensor_tensor", [bmseq, bmmt], [bgsm], out=gsm[:, 1:2], in0=mcur, in1=mx2, op=ALU.subtract)
                    E("dve", "tensor_tensor", [bbb, bmmt], [bmseq], out=mseq[:, mc + 1:mc + 2], in0=bb_[:, 63:64],
                      in1=mx2, op=ALU.add)
                    E("dve", "tensor_tensor", [bbb, bmmt], [bemt], out=emt[:, :], in0=bb_[:, :], in1=mmt[:, :], op=ALU.add)

                def G1():
                    ACT(wg[:, :], rr[:, :], AF.Exp, [brr, bgsm], [bwg], bias=gsm[:, 0:1])
                    ACT(dec[:, :], zer[0:4, :], AF.Exp, [bzer, bgsm], [bdec], bias=gsm[:, 1:2])
                    ACT(wi[:, :], mmt[:, :], AF.Exp, [bmmt, bmseq], [bwi], scale=-1.0, bias=mcur)
                    ACT(emt[:, :], emt[:, :], AF.Exp, [bemt], [bemt], scale=-1.0)

                def G2():
                    TR(pm[0:64, 256:260], rr[:, :], identf[0:4, 0:4], [brr, bidf], [bpm])
                    TR(pm[0:64, 260:264], emt[:, :], identf[0:4, 0:4], [bemt, bidf], [bpm])
                    TR(pm[0:64, 264:268], wg[:, :], identf[0:4, 0:4], [bwg, bidf], [bpm])
                    TR(pm[0:128, 268:272], dec[:, :], identf[0:4, 0:4], [bdec, bidf], [bpm])
                    E("dve", "tensor_copy", [bpm], [bgtok], out=gtok[0:64, 0:12], in_=pm[0:64, 256:268])
                    E("dve", "tensor_copy", [bpm], [bgtok], out=gtok[:, 12:16], in_=pm[:, 268:272])

                def G3():
                    for h in range(4):
                        MM(py[0:64, h * 64:(h + 1) * 64], sel[:, h, 0:64], mmt[:, :], True, True, [bsel, bmmt], [bpy])
                    for h in range(4):
                        MM(py[:, 256 + h * 64: 256 + (h + 1) * 64], sel[:, h, :], wi[:, :], True, True, [bsel, bwi], [bpy])
                    for h in range(4):
                        ACT(Wt[:, h, :], py[0:64, h * 64:(h + 1) * 64], AF.Exp, [bpy, bgtok], [bWt], scale=-1.0,
                            bias=gtok[0:64, h:h + 1])
                    E("dve", "tensor_tensor", [bpy, bmqk], [bqsT], out=qsT[:], in0=py[:, 256:512].rearrange("p (h t) -> p h t", h=4),
                      in1=mqkT[:, 0:4, cs], op=ALU.mult)
                    E("pool", "affine_select", [bWt], [bWt], out=Wt[:], in_=Wt[:], pattern=[[0, 4], [1, 64]],
                      compare_op=ALU.is_ge, fill=0.0, base=0, channel_multiplier=-1)

                def V0a():
                    if nch == 2 and c == 1:
                        return
                    pp, bpp = next_ab()
                    m_ = ntok if nch == 2 else 64
                    for kc in range(8):
                        MM(pp[0:m_, :], uTs[:, kc, 0:m_], winb[:, kc, 1024:1536], kc == 0, kc == 7, [buTs] + WL["mv"], [bpp])
                    E("dve", "tensor_copy", [bpp], [bvaug], out=vaug[:, :, 0:128], in_=pp[0:64, :].rearrange("p (h d) -> p h d", h=4))
                    if nch == 2:
                        vn, bvn = Kn["vaug"]
                        ACT(vn[:, :, 0:128], pp[64:128, :].rearrange("p (h d) -> p h d", h=4), AF.Copy, [bpp], [bvn])

                def V0b():
                    if nch == 2 and c == 0:
                        return
                    pp, bpp = next_ab()
                    m_ = ntok if nch == 2 else 64
                    for kc in range(8):
                        MM(pp[0:m_, :], uTs[:, kc, 0:m_], winb[:, kc, 1536:2048], kc == 0, kc == 7, [buTs] + WL["mo"], [bpp])
                    if nch == 2:
                        sp_, bsp_ = Kp["smo"]
                        ACT(sp_[:], pp[0:64, :], AF.Sigmoid, [bpp], [bsp_])
                        E("pool", "tensor_tensor", [bsp_, bmlg], [bsp_], out=sp_[:], in0=sp_[:], in1=mlg[:], op=ALU.mult)
                        ACT(smo[:], pp[64:128, :], AF.Sigmoid, [bpp], [bsmo])
                    else:
                        ACT(smo[:], pp[0:64, :], AF.Sigmoid, [bpp], [bsmo])
                    E("pool", "tensor_tensor", [bsmo, bmlg], [bsmo], out=smo[:], in0=smo[:], in1=mlg[:], op=ALU.mult)

                def V0c():
                    for h in range(4):
                        TR(pat[0:64, h * 128:(h + 1) * 128], mqkT[:, 4 + h, cs], identb[:], [bmqk, bidb], [bpat])
                    E("dve", "tensor_copy", [bpat], [bktok], out=ktok[:], in_=pat[0:64, 0:512].rearrange("p (h d) -> p h d", h=4))

                def U():
                    E("pool", "tensor_copy", [bCT], [bCTb], out=CTb[:], in_=CT[:])
                    E("dve", "tensor_tensor", [bvaug, bgtok], [bvw], out=vw[:], in0=vaug[:],
                      in1=gtok[0:64, 8:12].unsqueeze(2).broadcast_to([64, 4, 129]), op=ALU.mult)
                    for h in range(4):
                        o0 = 512 * (h // 2) + 129 * (h % 2)
                        MM(pz2[:, o0:o0 + 129], ktok[:, h, :], vw[:, h, :], True, True, [bktok, bvw], [bpz[0], bpz[1]])
                    for h in range(4):
                        o0 = 512 * (h // 2) + 129 * (h % 2)
                        E("dve", "scalar_tensor_tensor", [bCT, bgtok, bpz[0], bpz[1]], [bCT], out=CT[:, h, :], in0=CT[:, h, :],
                          scalar=gtok[:, 12 + h:13 + h], in1=pz2[:, o0:o0 + 129], op0=ALU.mult, op1=ALU.add)

                def V1():
                    for h in range(4):
                        MM(patf[0:64, h * 64:(h + 1) * 64], mqkT[:, 4 + h, cs], mqkT[:, h, cs], True, True, [bmqk], [bpat])
                    E("dve", "tensor_tensor", [bpat, bWt], [bST], out=STt[:], in0=patf[0:64, 0:256].rearrange("p (h t) -> p h t", h=4),
                      in1=Wt[:], op=ALU.mult)

                def N0():
                    for h in range(4):
                        o0 = 512 * (h // 2) + 129 * (h % 2)
                        MM(pz2[0:64, o0:o0 + 129], STt[:, h, :], vaug[:, h, :], True, False, [bST, bvaug], [bpz[0], bpz[1]])
                        MM(pz2[0:64, o0:o0 + 129], qsT[:, h, :], CTb[:, h, :], False, True, [bqsT, bCTb], [bpz[0], bpz[1]])
                    E("dve", "tensor_copy", [bpz[0], bpz[1]], [bhn], out=hn[:, 0:2, :], in_=pz2[0:64, 0:258].rearrange("p (h d) -> p h d", h=2))
                    E("dve", "tensor_copy", [bpz[0], bpz[1]], [bhn], out=hn[:, 2:4, :], in_=pz2[0:64, 512:770].rearrange("p (h d) -> p h d", h=2))

                def N1():
                    den = hn[:, :, 128]
                    E("dve", "scalar_tensor_tensor", [bhn], [bhs], out=hs[:, 0:4], in0=den, scalar=-1.0, in1=den,
                      op0=ALU.mult, op1=ALU.max)
                    E("dve", "tensor_tensor", [bhs, bgtok], [bhs], out=hs[:, 4:8], in0=hs[:, 0:4], in1=gtok[0:64, 4:8], op=ALU.max)
                    E("dve", "reciprocal", [bhs], [bhs], out=hs[:, 8:12], in_=hs[:, 4:8])
                    E("dve", "tensor_tensor", [bhn, bhs], [bhh], out=hh, in0=hn[:, :, 0:128],
                      in1=hs[:, 8:12].unsqueeze(2).broadcast_to([64, 4, 128]), op=ALU.mult)
                    E("dve", "tensor_tensor", [bhh], [bT1h], out=T1h[:], in0=hh, in1=hh, op=ALU.mult)
                    E("dve", "tensor_reduce", [bT1h], [bhs], out=hs[:, 12:16], in_=T1h[:], axis=AX.X, op=ALU.add)

                def N2():
                    rstd_pow(hs[:, 12:16], hs[:, 12:16], hs[:, 12:16], 64, 4, 1.0 / 128, [bhs], [bhs])

                def N3():
                    E("dve", "tensor_tensor", [bhh, bhs], [bT1h], out=T1h[:], in0=hh,
                      in1=hs[:, 12:16].unsqueeze(2).broadcast_to([64, 4, 128]), op=ALU.mult)
                    E("dve", "tensor_tensor", [bT1h, bsmo], [bybt], out=ybt[:], in0=T1h[:].rearrange("p h d -> p (h d)"),
                      in1=smo[:], op=ALU.mult)

                def N4():
                    for h in range(4):
                        TR(ptr[:, h * 64:(h + 1) * 64], ybt[:, h * 128:(h + 1) * 128], identb[0:64, 0:64],
                           [bybt, bidb], [bptr])
                    ACT(ybT[:, :, cs], ptr[:, 0:256].rearrange("p (h t) -> p h t", h=4), AF.Copy, [bptr], [bybT])

                return [G0, G1, G2, G3, V0a, V0b, V0c, U, V1, N0, N1, N2, N3, N4]

            for c in range(nch):
                st_ = chunk_stages(c)
                b0 = base + 7 + c * SC_
                for j, f in enumerate(st_):
                    items.append((b0 + j, f))
            dbase = base + 7 + (nch - 1) * SC_ + 14

            def D0():
                DMA("sp", "Ar", At[0:ntok, :], A_sc[row0:row0 + ntok, :], [Abuf.get(row0)], [bAt])
                for n in range(2):
                    pB, bpB = next_ab()
                    for c in range(4):
                        MM(pB[0:ntok, :], ybT[:, c, 0:ntok], wb3[:, c, n * 512:(n + 1) * 512], c == 0, c == 3, [bybT] + WL["b"], [bpB])
                    pG, bpG = next_ab()
                    for kc in range(8):
                        MM(pG[0:ntok, :], uTs[:, kc, 0:ntok], winb[:, kc, 3080 + n * 512: 3080 + (n + 1) * 512], kc == 0, kc == 7,
                           [buTs] + WL["gb"], [bpG])
                    ACT(sgt[0:ntok, :], pG[0:ntok, :], AF.Sigmoid, [bpG], [bsgt])
                    E("dve", "tensor_tensor", [bsgt, bpB], [bT2], out=T2[0:ntok, n * 512:(n + 1) * 512], in0=sgt[0:ntok, :],
                      in1=pB[0:ntok, :], op=ALU.mult)

            def D1():
                for n in range(2):
                    pG, bpG = next_ab()
                    for kc in range(8):
                        MM(pG[0:ntok, :], uTs[:, kc, 0:ntok], winb[:, kc, 2056 + n * 512: 2056 + (n + 1) * 512], kc == 0, kc == 7,
                           [buTs] + WL["ga"], [bpG])
                    ACT(sgt[0:ntok, :], pG[0:ntok, :], AF.Sigmoid, [bpG], [bsgt])
                    E("pool", "tensor_tensor", [bsgt, bAt], [bsgt], out=sgt[0:ntok, :], in0=sgt[0:ntok, :],
                      in1=At[0:ntok, n * 512:(n + 1) * 512], op=ALU.mult)
                    E("pool", "tensor_tensor", [bsgt, bT2], [bmg], out=mg[0:ntok, n * 512:(n + 1) * 512], in0=sgt[0:ntok, :],
                      in1=T2[0:ntok, n * 512:(n + 1) * 512], op=ALU.add)

            def D2():
                for kc in range(8):
                    TR(ptr[:, kc * 128: kc * 128 + ntok], mg[0:ntok, kc * 128:(kc + 1) * 128], identb[0:ntok, 0:ntok],
                       [bmg, bidb], [bptr])
                ACT(mT[:, :, 0:ntok], ptr[:].rearrange("p (k t) -> p k t", k=8)[:, :, 0:ntok], AF.Copy, [bptr], [bmT])

            def D3():
                for n in range(2):
                    pO, bpO = next_ab()
                    for kc in range(8):
                        MM(pO[0:ntok, :], mT[:, kc, 0:ntok], wo3[:, kc, n * 512:(n + 1) * 512], kc == 0, kc == 7, [bmT] + WL["out"], [bpO])
                    E("dve", "tensor_tensor", [bpO, bgg_], [bT2], out=T2[0:ntok, n * 512:(n + 1) * 512], in0=pO[0:ntok, :],
                      in1=gg_[0:ntok, n * 512:(n + 1) * 512], op=ALU.mult)
                E("pool", "tensor_tensor", [bT2, bxt], [bxt], out=xt[0:ntok, :], in0=T2[0:ntok, :], in1=xt[0:ntok, :], op=ALU.add)
                bx1 = Buf("x1sc%d" % row0)
                x1buf[row0] = bx1
                DMA("sp", "x1w%d" % (lt % 3), x1_sc[row0:row0 + ntok, :], xt[0:ntok, :], [bxt], [bx1])
                ACT(T2[0:ntok, :], xt[0:ntok, :], AF.Square, [bxt], [bT2, bssq], accum_out=ssq[0:ntok, 0:1])
                E("pool", "tensor_scalar", [bssq], [bssq], out=ssq[0:ntok, 1:2], in0=ssq[0:ntok, 0:1], scalar1=1.0 / D,
                  scalar2=EPS, op0=ALU.mult, op1=ALU.add)
                E("pool", "tensor_tensor", [bssq, bmhalf], [brstd2], out=rstd2[0:ntok, tix:tix + 1], in0=ssq[0:ntok, 1:2],
                  in1=mhalf[0:ntok, 0:1], op=ALU.pow)

            for j, f in enumerate([D0, D1, D2, D3]):
                items.append((dbase + j, f))
            sq["lastU"] = base + 7 + (nch - 1) * SC_ + 7
            sq["lastD3"] = dbase + 3
            sq["lastF4"] = base + 8
            return (xp, bxp, ntok)

        if "1b" in PH:
            seqs = []
            for tix, tile in enumerate(tiles):
                if not seqs or seqs[-1][0] != tile[0]:
                    seqs.append((tile[0], []))
                seqs[-1][1].append((tix, tile))
            items = []
            lt = 0
            d3_hist = []
            for k_, (s, tl) in enumerate(seqs):
                CTk, bCTk = CTs[k_ % 3]
                ggk = ggbs[k_ % 2]
                sq = dict(CT=(CTk, bCTk), gg=ggk)
                chunk_ctr[0] += 1
                mcol = chunk_ctr[0]
                base0 = 2 * lt * SC_
                xp0, bxp0 = xp2[lt % 2]

                def init_seq(s=s, CTk=CTk, bCTk=bCTk, mcol=mcol, xp0=xp0, bxp0=bxp0, first=(k_ == 0)):
                    load_mod(s, 1, gate=False)
                    if s == 0:
                        E("pool", "memset", [], [bCTk], CTk[:], 0.0)
                        if first:
                            E("pool", "memset", [], [bmseq], mseq[:], 0.0)
                        E("pool", "memset", [], [bxp0], xp0[:, :, 0:3], 0.0)
                    else:
                        si = s - 1
                        DMA("sp", "st", C0t, C0[si].rearrange("h v k -> v h k"), [], [bC0t])
                        for h in range(4):
                            TR(pa[:, h * 128:(h + 1) * 128], C0t[:, h, :], identf[:], [bC0t, bidf], [bpa])
                        E("dve", "tensor_copy", [bpa], [bCTk], out=CTk[:, :, 0:128], in_=pa[:].rearrange("p (h v) -> p h v", h=4))
                        DMA("sp", "st", CTk[:, :, 128], n0[si].rearrange("h k -> k h"), [], [bCTk], allow_slow_non_contiguous=True)
                        DMA("sp", "st", mseq[:, mcol:mcol + 1], m0[si:si + 1, :].rearrange("o h -> h o"), [], [bmseq],
                            allow_slow_non_contiguous=True)
                        for j_ in range(3):
                            DMA("sp", "st", xp0[:, :, j_], conv0[si, j_].rearrange("(c p) -> p c", p=128), [], [bxp0],
                                allow_slow_non_contiguous=True)

                items.append((base0 - 0.5, init_seq))
                gstep = base0 - 0.4
                if k_ >= 2:
                    gstep = max(gstep, d3_hist[k_ - 2] + 0.5)
                items.append((gstep, (lambda s=s, ggk=ggk: load_mod(s, 1, front=False, gdst=ggk))))
                prev_xp = None
                for (tix, tile) in tl:
                    prev_xp = sched_tile(items, lt, tix, tile, prev_xp, sq)
                    lt += 1
                items.append((sq["lastF4"] + 0.5, out_conv(s, prev_xp)))
                items.append((sq["lastU"] + 0.5, out_state(s, CTk, bCTk, chunk_ctr[0])))
                d3_hist.append(sq["lastD3"])
            order = sorted(range(len(items)), key=lambda i: (items[i][0], i))
            for i in order:
                items[i][1]()

        phase_begin()
        pab[:] = [(pa, bpa), (pb, bpb), (py, bpy), (pm, bpm), (pz2[:, 0:512], bpz[0]), (pz2[:, 512:1024], bpz[1])]
        WL = {}
        for ci_, c0_ in enumerate(range(0, DFF, 512)):
            n_ = min(512, DFF - c0_)
            WL["g%d" % ci_] = load_wg(w_ff_gate, 8, 0, [("g", c0_, n_)], DFF, 0, "fg%d" % ci_, kbase=2 * ci_)["g"]
            WL["u%d" % ci_] = load_wg(w_ff_up, 8, 0, [("u", c0_, n_)], DFF, 22528, "fu%d" % ci_, kbase=2 * ci_ + 1)["u"]
        wg3 = wview(0, 8, DFF)
        wu3 = wview(22528, 8, DFF)
        wdn, _ = cv("wdn", [128, 22 * D], BF16)
        xs3p = [xts[0], xts[1], cv("xt2p", [128, D], F32)]
        wd3 = wdn.rearrange("p (k c) -> p k c", k=22)
        WLd = {}
        dvd = w_ff_down.rearrange("(k p) n -> p k n", p=128)
        for n_ in range(2):
            bw = fb("Wd%d" % n_)
            DMA("pool", "Wg%d" % (12 + n_), wd3[:, :, n_ * 512:(n_ + 1) * 512], dvd[:, :, n_ * 512:(n_ + 1) * 512], [], [bw])
            WLd[n_] = [bw]
        sgts = [cv("sgt%d" % i, [128, 512], F32) for i in range(1)] * 2
        hb2 = [cv("hbuf%d" % i, [128, DFF], BF16) for i in range(2)]
        hT, bhT = cv("hT", [128, 22, 128], BF16)
        Tq, bTq = cv("O2_0", [128, D], F32)
        sq2 = [cv("sq2_%d" % i, [128, 4], F32) for i in range(2)]
        uT2a = [(uT, buT), cv("uT2a", [128, 8, 128], BF16)]
        ggbs2 = [(ggb, bggb), cv("ggb2", [128, D], F32)]
        fgb, bfgb = cv("fgb", [128, D], F32)
        DMA("sp", "cst", fgb[:], final_g[0:1, :].broadcast_to([128, D]), [], [bfgb])
        sgc = [0]
        seq2a = [-1]
        gsel = {}
        trb = [(ptr, bptr), (pat, bpat)]

        def ld2(tix):
            (s, row0, ntok, ti) = tiles[tix]
            xt, bxt = xs3p[tix % 3]
            DMA("sp", "xl%d" % (tix % 3), xt[0:ntok, :], x1_sc[row0:row0 + ntok, :], [x1buf.get(row0)], [bxt])

        def front2(tix):
            (s, row0, ntok, ti) = tiles[tix]
            if s != seq2a[0]:
                seq2a[0] = s
                load_mod(s, 2, gate=False)
            xt, bxt = xs3p[tix % 3]
            rms_to_uT(xt, bxt, ntok, rstd_ap=rstd2[0:ntok, tix:tix + 1], dst=uT2a[tix % 2])

        def up2(tix):
            (s, row0, ntok, ti) = tiles[tix]
            hbuf, bhbuf = hb2[tix % 2]
            for n0_ in range(0, DFF, 512):
                sgc[0] ^= 1
                sgt, bsgt = sgts[sgc[0]]
                n = min(512, DFF - n0_)
                pG, bpG = proj_tok(wg3, n0_, n, ntok, wl=WL["g%d" % (n0_ // 512)], us=uT2a[tix % 2])
                pU, bpU = proj_tok(wu3, n0_, n, ntok, wl=WL["u%d" % (n0_ // 512)], us=uT2a[tix % 2])
                ACT(sgt[0:ntok, 0:n], pG[0:ntok, 0:n], AF.Silu, [bpG], [bsgt])
                E("dve", "tensor_tensor", [bsgt, bpU], [bhbuf], out=hbuf[0:ntok, n0_:n0_ + n], in0=sgt[0:ntok, 0:n],
                  in1=pU[0:ntok, 0:n], op=ALU.mult)

        seqord = []
        for t_ in tiles:
            if t_[0] not in seqord:
                seqord.append(t_[0])

        def down2(tix):
            (s, row0, ntok, ti) = tiles[tix]
            k_ = seqord.index(s)
            gg2, bgg2 = ggbs2[k_ % 2]
            if s not in gsel:
                gsel[s] = True
                load_mod(s, 2, front=False, gdst=(gg2, bgg2))
            hbuf, bhbuf = hb2[tix % 2]
            xt, bxt = xs3p[tix % 3]
            sq_, bsq_ = sq2[tix % 2]
            for gi, g0 in enumerate(range(0, 22, 8)):
                ng = min(8, 22 - g0)
                pt_, bpt_ = trb[gi % 2]
                for j in range(ng):
                    k = g0 + j
                    TR(pt_[:, j * 128: j * 128 + ntok], hbuf[0:ntok, k * 128:(k + 1) * 128], identb[0:ntok, 0:ntok],
                       [bhbuf, bidb], [bpt_])
                ACT(hT[:, g0:g0 + ng, 0:ntok], pt_[:].rearrange("p (k t) -> p k t", k=8)[:, 0:ng, 0:ntok], AF.Copy,
                    [bpt_], [bhT])
            for n in range(2):
                pD, bpD = next_ab()
                for k in range(22):
                    MM(pD[0:ntok, :], hT[:, k, 0:ntok], wd3[:, k, n * 512:(n + 1) * 512], k == 0, k == 21, [bhT] + WLd[n], [bpD])
                E("dve", "tensor_tensor", [bpD, bgg2], [bTq], out=Tq[0:ntok, n * 512:(n + 1) * 512], in0=pD[0:ntok, :],
                  in1=gg2[0:ntok, n * 512:(n + 1) * 512], op=ALU.mult)
            E("pool", "tensor_tensor", [bTq, bxt], [bxt], out=xt[0:ntok, :], in0=Tq[0:ntok, :], in1=xt[0:ntok, :], op=ALU.add)
            ACT(Tq[0:ntok, :], xt[0:ntok, :], AF.Square, [bxt], [bTq, bsq_], accum_out=sq_[0:ntok, 0:1])
            rstd_pow(sq_[0:ntok, 2:3], sq_[0:ntok, 1:2], sq_[0:ntok, 0:1], ntok, 1, 1.0 / D, [bsq_], [bsq_])
            E("dve", "scalar_tensor_tensor", [bxt, bsq_, bfgb], [bTq], out=Tq[0:ntok, :], in0=xt[0:ntok, :],
              scalar=sq_[0:ntok, 2:3], in1=fgb[0:ntok, :], op0=ALU.mult, op1=ALU.mult)
            DMA("sp", "yo", y_all[row0:row0 + ntok, :], Tq[0:ntok, :], [bTq], [])

        P2ON = ("2a" in PH) or ("2b" in PH)
        if P2ON:
            NT_ = len(tiles)
            ld2(0)
            if NT_ > 1:
                ld2(1)
            front2(0)
            if NT_ > 1:
                front2(1)
            up2(0)
            for tix in range(NT_):
                if tix + 2 < NT_:
                    ld2(tix + 2)
                if tix + 1 < NT_:
                    up2(tix + 1)
                if tix + 2 < NT_:
                    front2(tix + 2)
                down2(tix)

        P.lower(lambda name: st.enter_context(nc.semaphore(name)))
        build_program.stats = P.stats
    return nc


_CACHE = {}


def kernel(**inp):
    f = lambda a: np.ascontiguousarray(np.asarray(a, dtype=np.float32))
    if "nc" not in _CACHE:
        _CACHE["nc"] = build_program()
    nc = _CACHE["nc"]
    x_prompt = f(inp["x_prompt"]); x_sample = f(inp["x_sample"])
    c_prompt = f(inp["c_prompt"]); c_sample = f(inp["c_sample"])
    ck = f(inp["cache_sb_k"])[0].reshape(16, PAST, 512)
    cv = f(inp["cache_sb_v"])[0].reshape(16, PAST, 512)
    sC = f(inp["state_mlstm_C"])[0]; sn = f(inp["state_mlstm_n"])[0]; sm = f(inp["state_mlstm_m"])[0]
    sconv = f(inp["state_conv"])[0]
    shared = {
        "norm1_g": f(inp["norm1_g"]).reshape(1, D), "norm2_g": f(inp["norm2_g"]).reshape(1, D),
        "w_ada": f(inp["w_ada"])[0], "b_ada": f(inp["b_ada"]).reshape(1, 6 * D),
        "w_in": f(inp["w_in"])[0], "b_if": f(inp["b_if"]).reshape(8, 1),
        "w_conv": f(inp["w_conv"])[0], "b_conv": f(inp["b_conv"]).reshape(1, D),
        "ml_norm_g": f(inp["ml_norm_g"]).reshape(1, 512),
        "w_a": f(inp["w_a"])[0], "w_b": f(inp["w_b"])[0], "w_out": f(inp["w_out"])[0],
        "w_ff_gate": f(inp["w_ff_gate"])[0], "w_ff_up": f(inp["w_ff_up"])[0], "w_ff_down": f(inp["w_ff_down"])[0],
        "final_g": f(inp["final_g"]).reshape(1, D),
    }
    in_maps = []
    for i in range(8):
        m = dict(shared)
        m["xall"] = np.concatenate([x_prompt[i], x_sample[2 * i], x_sample[2 * i + 1]], axis=0)
        m["c3"] = np.stack([c_prompt[i], c_sample[2 * i], c_sample[2 * i + 1]], axis=0)
        m["cache_k"] = ck[2 * i:2 * i + 2]
        m["cache_v"] = cv[2 * i:2 * i + 2]
        m["C0"] = sC[2 * i:2 * i + 2]
        m["n0"] = sn[2 * i:2 * i + 2]
        m["m0"] = sm[2 * i:2 * i + 2]
        m["conv0"] = sconv[2 * i:2 * i + 2]
        in_maps.append({k: np.ascontiguousarray(v) for k, v in m.items()})
    res = run_bass_kernel_spmd(nc, in_maps, core_ids=list(range(8)))
    R = res.results
    y_p = np.stack([R[i]["y_all"][:SEQ] for i in range(8)])
    y_s = np.stack([R[i]["y_all"][SEQ + j * LS: SEQ + (j + 1) * LS] for i in range(8) for j in range(2)])
    k_p = np.stack([R[i]["k_all"][:SEQ] for i in range(8)]).reshape(1, 8, SEQ, 8, 64)
    v_p = np.stack([R[i]["v_all"][:SEQ] for i in range(8)]).reshape(1, 8, SEQ, 8, 64)
    k_s = np.stack([R[i]["k_all"][SEQ + j * LS: SEQ + (j + 1) * LS] for i in range(8) for j in range(2)]).reshape(1, 16, LS, 8, 64)
    v_s = np.stack([R[i]["v_all"][SEQ + j * LS: SEQ + (j + 1) * LS] for i in range(8) for j in range(2)]).reshape(1, 16, LS, 8, 64)
    C_p = np.stack([R[i]["C_out"][0] for i in range(8)])[None]
    n_p = np.stack([R[i]["n_out"][0] for i in range(8)])[None]
    m_p = np.stack([R[i]["m_out"][0] for i in range(8)])[None]
    cv_p = np.stack([R[i]["conv_out"][0] for i in range(8)])[None]
    C_s = np.stack([R[i]["C_out"][1 + j] for i in range(8) for j in range(2)])[None]
    n_s = np.stack([R[i]["n_out"][1 + j] for i in range(8) for j in range(2)])[None]
    m_s = np.stack([R[i]["m_out"][1 + j] for i in range(8) for j in range(2)])[None]
    cv_s = np.stack([R[i]["conv_out"][1 + j] for i in range(8) for j in range(2)])[None]
    outs = (y_p, y_s, k_p, v_p, C_p, n_p, m_p, cv_p, k_s, v_s, C_s, n_s, m_s, cv_s)
    return tuple(np.ascontiguousarray(o, dtype=np.float32) for o in outs)
```

```python
import numpy as np
from contextlib import ExitStack
import concourse.bass as bass
import concourse.mybir as mybir
from concourse.bass_utils import run_bass_kernel_spmd

F32 = mybir.dt.float32
BF16 = mybir.dt.bfloat16
AF = mybir.ActivationFunctionType
ALU = mybir.AluOpType
AX = mybir.AxisListType

D = 1024
SEQ = 2048
NS = 2
LS = 64
PAST = 1024
DFF = 2816
INW = 5640
EPS = 1e-6
NROWS = SEQ + NS * LS


STRICT_SAME_ENGINE = False


class Buf:
    __slots__ = ("name", "last_w", "readers", "excl")

    def __init__(self, name, excl=False):
        self.name = name
        self.last_w = None
        self.readers = []
        self.excl = excl


class Op:
    __slots__ = ("eng", "fn", "reads", "writes", "dma", "seq", "signal", "waits",
                 "clock", "count", "idx", "attach")


class Prog:
    ENG = ("pe", "act", "dve", "pool", "sp")

    def __init__(self, nc):
        self.nc = nc
        self.ops = []
        self.e = {"pe": nc.tensor, "act": nc.scalar, "dve": nc.vector,
                  "pool": nc.gpsimd, "sp": nc.sync}

    def op(self, eng, fn, reads=(), writes=(), dma=None):
        o = Op()
        o.eng = eng
        o.fn = fn
        o.reads = [b for b in reads if b is not None and not b.excl]
        o.writes = [b for b in writes if b is not None] + [b for b in reads if b is not None and b.excl]
        o.dma = dma
        o.signal = False
        o.waits = []
        o.attach = (eng != "pe")
        o.idx = len(self.ops)
        self.ops.append(o)
        return o

    def fence(self):
        last = {}
        for o in self.ops:
            last[o.eng if o.dma is None else "d:" + o.dma] = o
        return list(last.values())

    def lower(self, sem_ctx):
        ops = self.ops
        seqc = {k: 0 for k in self.ENG}
        dmac = {}
        eclock = {k: {} for k in self.ENG}
        for o in ops:
            if o.dma is None:
                seqc[o.eng] += 1
                o.seq = seqc[o.eng]
            else:
                dmac[o.dma] = dmac.get(o.dma, 0) + 1
                o.seq = dmac[o.dma]
            deps = {}
            for b in o.reads:
                if b.last_w is not None:
                    deps[b.last_w.idx] = b.last_w
            for b in o.writes:
                if b.last_w is not None:
                    deps[b.last_w.idx] = b.last_w
                for r in b.readers:
                    deps[r.idx] = r
            clk = eclock[o.eng]
            for p in deps.values():
                if p is o:
                    continue
                if p.dma is None:
                    key = p.eng
                    need = p.seq
                    if p.eng == o.eng and o.dma is None:
                        if o.eng == "pe":
                            continue
                        if (not STRICT_SAME_ENGINE) and o.eng != "pool" and not any(b.last_w is p for b in o.reads):
                            continue
                else:
                    key = "d:" + p.dma
                    need = dmac[p.dma] if not (o.dma == p.dma) else dmac[p.dma] - 1
                if clk.get(key, 0) >= need:
                    continue
                if p.dma is None:
                    p.signal = True
                o.waits.append((p, need))
                for k2, v2 in p.clock.items():
                    if clk.get(k2, 0) < v2:
                        clk[k2] = v2
                if clk.get(key, 0) < need:
                    clk[key] = need
            myclk = dict(clk)
            mykey = o.eng if o.dma is None else "d:" + o.dma
            myclk[mykey] = max(myclk.get(mykey, 0), o.seq)
            o.clock = myclk
            for b in o.reads:
                b.readers.append(o)
            for b in o.writes:
                b.last_w = o
                b.readers = []
        cnt = {k: 0 for k in self.ENG}
        for o in ops:
            if o.dma is None and o.signal:
                cnt[o.eng] += 1
                o.count = cnt[o.eng]
        esem = {}
        dsem = {}
        dtot = {}
        n_waits = 0

        def get_e(k):
            if k not in esem:
                esem[k] = sem_ctx("e_" + k)
            return esem[k]

        def get_d(k):
            if k not in dsem:
                dsem[k] = sem_ctx("d_" + k)
            return dsem[k]

        for o in ops:
            eng = self.e[o.eng]
            need = {}
            for p, nd in o.waits:
                if p.dma is None:
                    s = get_e(p.eng)
                    v = p.count
                else:
                    s = get_d(p.dma)
                    v = 16 * nd
                k = id(s)
                if k not in need or need[k][1] < v:
                    need[k] = (s, v)
            nl = list(need.values())
            ride = None
            if o.attach and nl:
                ride = nl.pop()
            for s, v in nl:
                eng.wait_ge(s, v)
                n_waits += 1
            ins = o.fn()
            if ride is not None:
                ins._wait_ge(ride[0], eng.lower_val(ride[1]))
            if o.dma is not None:
                ins.then_inc(get_d(o.dma), 16)
                dtot[o.dma] = dtot.get(o.dma, 0) + 16
            elif o.signal:
                ins.then_inc(get_e(o.eng), 1)
        for k, s in dsem.items():
            self.e["sp"].wait_ge(s, dtot[k])
        self.stats = dict(n_ops=len(ops), n_waits=n_waits, n_dsem=len(dsem),
                          sig={k: cnt[k] for k in cnt})


CFG = {"phases": ("0", "1a", "1b", "2a", "2b"), "tiles": None}


def build_program():
    nc = bass.Bass("TRN2", target_bir_lowering=False)
    PH = CFG["phases"]

    def din(name, shape):
        return nc.dram_tensor(name, list(shape), F32, kind="ExternalInput").ap()

    def dout(name, shape):
        return nc.dram_tensor(name, list(shape), F32, kind="ExternalOutput").ap()

    xall = din("xall", [NROWS, D])
    c3 = din("c3", [3, D])
    cache_k = din("cache_k", [NS, PAST, 512])
    cache_v = din("cache_v", [NS, PAST, 512])
    C0 = din("C0", [NS, 4, 128, 128])
    n0 = din("n0", [NS, 4, 128])
    m0 = din("m0", [NS, 4])
    conv0 = din("conv0", [NS, 3, D])
    norm1_g = din("norm1_g", [1, D])
    norm2_g = din("norm2_g", [1, D])
    w_ada = din("w_ada", [D, 6 * D])
    b_ada = din("b_ada", [1, 6 * D])
    w_in = din("w_in", [D, INW])
    b_if = din("b_if", [8, 1])
    w_conv = din("w_conv", [4, D])
    b_conv = din("b_conv", [1, D])
    ml_norm_g = din("ml_norm_g", [1, 512])
    w_a = din("w_a", [512, D])
    w_b = din("w_b", [512, D])
    w_out = din("w_out", [D, D])
    w_ff_gate = din("w_ff_gate", [D, DFF])
    w_ff_up = din("w_ff_up", [D, DFF])
    w_ff_down = din("w_ff_down", [DFF, D])
    final_g = din("final_g", [1, D])

    y_all = dout("y_all", [NROWS, D])
    k_all = dout("k_all", [NROWS, 512])
    v_all = dout("v_all", [NROWS, 512])
    C_out = dout("C_out", [3, 4, 128, 128])
    n_out = dout("n_out", [3, 4, 128])
    m_out = dout("m_out", [3, 4])
    conv_out = dout("conv_out", [3, 3, D])

    mod_sc = nc.dram_tensor("mod_sc", [3, 6 * D], F32).ap()
    A_sc = nc.dram_tensor("A_sc", [NROWS, D], F32).ap()
    x1_sc = nc.dram_tensor("x1_sc", [NROWS, D], F32).ap()
    h_sc = nc.dram_tensor("h_sc", [NROWS, DFF], BF16).ap()

    tiles = [(0, t * 128, 128, t) for t in range(16)] + [(1, SEQ, 64, 0), (2, SEQ + 64, 64, 0)]
    if CFG["tiles"] is not None:
        tiles = [tiles[i] for i in CFG["tiles"]]

    with ExitStack() as st:
        P = Prog(nc)

        def sb(name, shape, dt=F32):
            return st.enter_context(nc.sbuf_tensor(name, list(shape), dt)), Buf(name)

        def ps(name, shape, dt=F32):
            return st.enter_context(nc.psum_tensor(name, list(shape), dt)), Buf(name, excl=True)

        SCRN = 20736
        SCR = st.enter_context(nc.sbuf_tensor("SCR", [128, SCRN], F32))
        scr = {"off": 0, "fence": []}

        def phase_begin():
            scr["off"] = 0
            scr["fence"] = P.fence()

        def fb(name):
            b = Buf(name)
            b.readers = list(scr["fence"])
            return b

        def cv(name, shape, dt=F32):
            n = 1
            for d_ in shape[1:]:
                n *= d_
            nf = (n + 1) // 2 if dt == BF16 else n
            nf = (nf + 7) // 8 * 8
            off = scr["off"]
            scr["off"] += nf
            assert scr["off"] <= SCRN, (name, scr["off"])
            v = SCR[0:shape[0], off:off + nf]
            if dt == BF16:
                v = v.bitcast(BF16)
            v = v[:, 0:n]
            if len(shape) == 3:
                v = v.rearrange("p (a b) -> p a b", a=shape[1])
            return v, fb(name)

        def E(eng, name, reads, writes, *a, **kw):
            m = getattr(P.e[eng], name)
            return P.op(eng, lambda: m(*a, **kw), reads, writes)

        def DMA(eng, key, out, in_, reads, writes, **kw):
            m = P.e[eng].dma_start
            return P.op(eng, lambda: m(out=out, in_=in_, **kw), reads, writes, dma=key)

        def MM(out, lhsT, rhs, start, stop, reads, writes):
            m = nc.tensor.matmul
            return P.op("pe", lambda: m(out, lhsT=lhsT, rhs=rhs, start=start, stop=stop), reads, writes)

        def TR(out, in_, ident, reads, writes):
            m = nc.tensor.transpose
            return P.op("pe", lambda: m(out=out, in_=in_, identity=ident), reads, writes)

        def ACT(out, in_, func, reads, writes, **kw):
            m = nc.scalar.activation
            o_ = P.op("act", lambda: m(out=out, in_=in_, func=func, **kw), reads, writes)
            if "accum_out" in kw:
                o_.attach = False
            return o_

        WB, bWB = sb("WB", [128, 46080], BF16)
        identb, bidb = sb("identb", [128, 128], BF16)
        identf, bidf = sb("identf", [128, 128], F32)
        onesb, bones = sb("onesb", [128, 512], BF16)
        zer, bzer = sb("zer", [128, 128], F32)
        sel, bsel = sb("sel", [4, 4, 128], F32)
        rstd2, brstd2 = sb("rstd2", [128, 18], F32)
        csT, bcsT = sb("csT", [128, 8, 3], BF16)
        mhalf, bmhalf = sb("mhalf", [128, 4], F32)
        cmask, bcmask = sb("cmask", [128, 128], BF16)
        gmb, bgmb = sb("gmb", [128, D], F32)
        shb, bshb = sb("shb", [128, D], F32)
        ggb, bggb = sb("ggb", [128, D], F32)
        xts = [sb("xt%d" % i, [128, D], F32) for i in range(2)]
        T1, bT1 = sb("T1", [128, D], F32)
        T2, bT2 = sb("T2", [128, D], F32)
        ub, bub = sb("ub", [128, D], BF16)
        uT, buT = sb("uT", [128, 8, 128], BF16)
        ssq, bssq = sb("ssq", [128, 4], F32)
        stg = []

        ptr, bptr = ps("ptr", [128, 1024], BF16)
        pat, bpat = ps("pat", [128, 1024], BF16)
        pa, bpa = ps("pa", [128, 512], F32)
        pb, bpb = ps("pb", [128, 512], F32)
        py, bpy = ps("py", [128, 512], F32)
        pm, bpm = ps("pm", [128, 512], F32)
        pz2, bpz2 = ps("pz2", [128, 1024], F32)
        bpz = [Buf("pz0", excl=True), Buf("pz1", excl=True)]
        pab = [(pa, bpa), (pb, bpb)]
        rot = {"ab": 0, "stg": 0, "x": 0}

        def next_ab():
            rot["ab"] = (rot["ab"] + 1) % len(pab)
            return pab[rot["ab"]]

        def next_stg():
            rot["stg"] = (rot["stg"] + 1) % 2
            return stg[rot["stg"]]

        E("pool", "memset", [], [bidf], identf[:], 1.0)
        E("pool", "affine_select", [bidf], [bidf], out=identf[:], in_=identf[:], pattern=[[-1, 128]],
          compare_op=ALU.is_equal, fill=0.0, base=0, channel_multiplier=1)
        E("pool", "tensor_copy", [bidf], [bidb], out=identb[:], in_=identf[:])
        E("pool", "memset", [], [bones], onesb[:], 1.0)
        E("pool", "memset", [], [bzer], zer[:], 0.0)
        E("pool", "memset", [], [bsel], sel[:], 1.0)
        E("pool", "affine_select", [bsel], [bsel], out=sel[:], in_=sel[:], pattern=[[-1, 4], [0, 128]],
          compare_op=ALU.is_equal, fill=0.0, base=0, channel_multiplier=1)
        E("pool", "memset", [], [brstd2], rstd2[:], 1.0)
        E("pool", "memset", [], [bmhalf], mhalf[:], -0.5)
        E("pool", "memset", [], [bcmask], cmask[:], -30000.0)
        E("pool", "affine_select", [bcmask], [bcmask], out=cmask[:], in_=cmask[:], pattern=[[1, 128]],
          compare_op=ALU.is_ge, fill=0.0, base=0, channel_multiplier=-1)

        def rstd_pow(out_ap, tmp_ap, ss_ap, npart, ncol, scale, rds, wrs):
            E("pool", "tensor_scalar", rds, wrs, out=tmp_ap, in0=ss_ap, scalar1=scale, scalar2=EPS, op0=ALU.mult, op1=ALU.add)
            E("pool", "tensor_tensor", wrs + [bmhalf], wrs, out=out_ap, in0=tmp_ap, in1=mhalf[0:npart, 0:ncol], op=ALU.pow)

        def load_w(dram, r0, nrows_chunks, c0, ncols, off, key):
            bufs = []
            for kc in range(nrows_chunks):
                for cc in range(0, ncols, 2048):
                    n = min(2048, ncols - cc)
                    bw = fb("W%s_%d_%d" % (key, kc, cc))
                    bufs.append(bw)
                    DMA("pool", "W" + key, WB[:, off + kc * ncols + cc: off + kc * ncols + cc + n],
                        dram[r0 + kc * 128: r0 + (kc + 1) * 128, c0 + cc: c0 + cc + n], [], [bw])
            return bufs

        def load_wg(dram, nrows_chunks, c0, groups, stride, off, kp, kbase=0):
            out = {}
            dv = dram.rearrange("(k p) n -> p k n", p=128)
            wv = WB[:, off: off + nrows_chunks * stride].rearrange("p (k c) -> p k c", k=nrows_chunks)
            for gi, (nm, l0, ncols) in enumerate(groups):
                bufs = []
                for cc in range(0, ncols, 2048):
                    n = min(2048, ncols - cc)
                    bw = fb("W%s_%s_%d" % (kp, nm, cc))
                    bufs.append(bw)
                    DMA("pool", "Wg%d" % (kbase + gi), wv[:, :, l0 + cc: l0 + cc + n],
                        dv[:, :, c0 + l0 + cc: c0 + l0 + cc + n], [], [bw])
                out[nm] = bufs
            return out

        def wview(off, nk, ncols):
            return WB[:, off: off + nk * ncols].rearrange("p (k c) -> p k c", k=nk)

        scr["fence"] = []
        WL_1a = load_wg(w_in, 8, 0, [("q", 0, 512), ("k", 512, 512), ("v", 1024, 512)], 1536, 0, "a")
        WL_1a["a"] = load_wg(w_a, 4, 0, [("a", 0, 1024)], 1024, 12288, "wa", kbase=3)["a"]

        phase_begin()
        stg[:] = [cv("stg%d" % i, [128, 512], F32) for i in range(2)]
        cT, bcT = cv("cT", [128, 8, 3], F32)
        for s_ in range(3):
            DMA("sp", "cst", cT[:, :, s_], c3[s_].rearrange("(k p) -> p k", p=128), [], [bcT],
                allow_slow_non_contiguous=True)
        ACT(csT[:], cT[:], AF.Silu, [bcT], [bcsT])
        WAs = [cv("WA%d" % i, [128, 8, 512], BF16) for i in range(2)]
        w_ada_v = w_ada.rearrange("(k p) n -> p k n", p=128)
        bmods = {"A": Buf("modA"), "B": Buf("modB"), "C": Buf("modC")}

        def mod_group(nch):
            return "A" if nch < 4 else ("B" if nch < 6 else "C")

        def mod_chunk_load(nch, WA, bWA, key):
            DMA("pool", key, WA[:], w_ada_v[:, :, nch * 512:(nch + 1) * 512], [], [bWA])

        def mod_chunk_compute(nch, WA, bWA, pp, bpp):
            sg, bsg = next_stg()
            DMA("sp", "bad", sg[0:3, :], b_ada[0:1, nch * 512:(nch + 1) * 512].broadcast_to([3, 512]), [], [bsg])
            for kc in range(8):
                MM(pp[0:3, :], csT[:, kc, :], WA[:, kc, :], kc == 0, kc == 7, [bcsT, bWA], [bpp])
            E("dve", "tensor_tensor", [bpp, bsg], [bsg], out=sg[0:3, :], in0=pp[0:3, :], in1=sg[0:3, :], op=ALU.add)
            DMA("sp", "modw", mod_sc[:, nch * 512:(nch + 1) * 512], sg[0:3, :], [bsg], [bmods[mod_group(nch)]])

        for nch in range(4 if "0" in PH else 0):
            WA, bWA = WAs[nch % 2]
            mod_chunk_load(nch, WA, bWA, "wa%d" % (nch % 2))
            mod_chunk_compute(nch, WA, bWA, pm, bpm)

        def load_mod(s, which, front=True, gate=True, gdst=None):
            base = 0 if which == 1 else 3 * D
            ng = norm1_g if which == 1 else norm2_g
            bf_ = bmods["A"] if which == 1 else bmods["C"]
            bg_ = bmods["B"] if which == 1 else bmods["C"]
            if front:
                DMA("sp", "modr", shb[:], mod_sc[s:s + 1, base:base + D].broadcast_to([128, D]), [bf_], [bshb])
                DMA("sp", "modr", gmb[:], mod_sc[s:s + 1, base + D:base + 2 * D].broadcast_to([128, D]), [bf_], [bgmb])
                DMA("sp", "modr", T2[:], ng[0:1, :].broadcast_to([128, D]), [], [bT2])
                E("dve", "scalar_tensor_tensor", [bgmb, bT2], [bgmb], out=gmb[:], in0=gmb[:], scalar=1.0, in1=T2[:],
                  op0=ALU.add, op1=ALU.mult)
            if gate:
                gd, bgd = gdst if gdst is not None else (ggb, bggb)
                DMA("sp", "modg", gd[:], mod_sc[s:s + 1, base + 2 * D:base + 3 * D].broadcast_to([128, D]), [bg_], [bgd])

        def rms_to_uT(xt, bxt, ntok, rstd_ap=None, dst=None):
            if rstd_ap is None:
                ACT(T2[0:ntok, :], xt[0:ntok, :], AF.Square, [bxt], [bT2, bssq], accum_out=ssq[0:ntok, 0:1])
                rstd_pow(ssq[0:ntok, 2:3], ssq[0:ntok, 1:2], ssq[0:ntok, 0:1], ntok, 1, 1.0 / D, [bssq], [bssq])
                rstd_ap = ssq[0:ntok, 2:3]
                rb = bssq
            else:
                rb = brstd2
            E("dve", "scalar_tensor_tensor", [bxt, rb, bgmb], [bT1], out=T1[0:ntok, :], in0=xt[0:ntok, :],
              scalar=rstd_ap, in1=gmb[0:ntok, :], op0=ALU.mult, op1=ALU.mult)
            E("pool", "tensor_tensor", [bT1, bshb], [bub], out=ub[0:ntok, :], in0=T1[0:ntok, :], in1=shb[0:ntok, :],
              op=ALU.add)
            for kc in range(8):
                TR(ptr[:, kc * 128: kc * 128 + ntok], ub[0:ntok, kc * 128:(kc + 1) * 128], identb[0:ntok, 0:ntok],
                   [bub, bidb], [bptr])
            uTd, buTd = dst if dst is not None else (uT, buT)
            ACT(uTd[:, :, 0:ntok], ptr[:].rearrange("p (k t) -> p k t", k=8)[:, :, 0:ntok], AF.Copy, [bptr], [buTd])

        epst, bepst = sb("epst", [128, 1], F32)
        E("pool", "memset", [], [bepst], epst[:], EPS)
        EPS_AP = epst

        def load_x(src, row0, ntok):
            rot["x"] ^= 1
            xt, bxt = xts[rot["x"]]
            DMA("sp", "xl%d" % rot["x"], xt[0:ntok, :], src[row0:row0 + ntok, :], [], [bxt])
            return xt, bxt

        def proj_tok(w3, c0, n, ntok, t0=0, wl=(), us=None):
            pp, bpp = next_ab()
            for kc in range(8):
                uTs_, buTs_ = us if us is not None else (uT, buT)
                MM(pp[0:ntok, 0:n], uTs_[:, kc, t0:t0 + ntok], w3[:, kc, c0:c0 + n], kc == 0, kc == 7, [buTs_] + list(wl), [bpp])
            return pp, bpp

        phase_begin()
        WL = WL_1a
        WLb_pre = load_wg(w_b, 4, 0, [("b", 0, 1024)], 1024, 32832, "wb", kbase=6)["b"]
        WLout_pre = load_wg(w_out, 8, 0, [("o", 0, 1024)], 1024, 36928, "wo", kbase=7)["o"]
        win_a = wview(0, 8, 1536)
        wa3 = wview(12288, 4, 1024)
        KTm = WB[:, 16384:24576].rearrange("p (c k) -> p c k", c=4)
        Vm = WB[:, 24576:32768].rearrange("p (t c) -> p t c", t=16)
        stor_main = dict(KT=KTm, Vst=Vm, bKT=[fb("KT%d" % i) for i in range(16)], bV=[fb("V%d" % i) for i in range(16)])
        KTa, _ = cv("KTalt", [128, 4, 1152], BF16)
        Va, _ = cv("Valt", [128, 9, 512], BF16)
        stor_alt = dict(KT=KTa, Vst=Va, bKT=[fb("KTa%d" % i) for i in range(9)], bV=[fb("Va%d" % i) for i in range(9)])
        stg[:] = [cv("stg%d" % i, [128, 512], F32) for i in range(2)]
        qTzs = [cv("qTz%d" % i, [128, 8, 128], BF16) for i in range(2)]
        for qz, bqz in qTzs:
            E("pool", "memset", [], [bqz], qz[:], 0.0)
        Ktok, bKtok = cv("Ktok", [128, 8, 512], BF16)
        NSL = 5
        att = [dict(g=cv("ag%d" % i, [128, 520], F32), Pb=cv("aP%d" % i, [128, 520], F32),
                    a=cv("aa%d" % i, [128, 512], BF16),
                    aT=cv("aaT%d" % i, [128, 4, 128], BF16)) for i in range(NSL)]
        ONE_REG = nc.gpsimd.to_reg(1.0)
        zerob, bzerob = cv("zerob", [128, 520], BF16)
        E("pool", "memset", [], [bzerob], zerob[:], 0.0)
        for A_ in att:
            E("pool", "memset", [], [A_["g"][1]], A_["g"][0][:], 1.0)
        ya, bya = cv("ya", [128, 512], BF16)
        yaT, byaT = cv("yaT", [128, 4, 128], BF16)
        Ast, bAst = cv("Ast", [128, D], F32)
        WAbg, bWAbg = cv("WAbg", [128, 8, 512], BF16)
        pzv = [pz2[:, 0:512], pz2[:, 512:1024]]
        patv2 = [(pat, bpat), (pm[:].bitcast(BF16), bpm)]
        job_ctr = [0]
        Abuf = {}
        x1buf = {}
        seq_loaded = [-1]

        def prologue_pieces(tile, tno, stor, first_of_seq):
            (s, row0, ntok, ti) = tile
            kpos0 = ti * 128 if s == 0 else PAST
            ktile = ti if s == 0 else 8
            qTz, bqTz = qTzs[tno % 2]
            KT, Vst, bKT, bV = stor["KT"], stor["Vst"], stor["bKT"], stor["bV"]
            cx = dict(s=s, row0=row0, ntok=ntok, kpos0=kpos0, ktile=ktile, qTz=qTz, bqTz=bqTz, stor=stor)
            hold = {}

            xslot = tno % 2
            xt, bxt = xts[xslot]
            hold = {}

            def PL():
                if first_of_seq:
                    load_mod(s, 1, gate=False)
                    if s > 0:
                        si = s - 1
                        DMA("pool", "kvc", Vst[:, 0:8, :], cache_v[si].rearrange("(k p) c -> p k c", p=128), [], bV[0:8])
                        DMA("pool", "kvc", Ktok[:], cache_k[si].rearrange("(k p) c -> p k c", p=128), [], [bKtok])
                DMA("sp", "xl%d" % xslot, xt[0:ntok, :], xall[row0:row0 + ntok, :], [], [bxt])

            def PK():
                for kt in range(8):
                    for c in range(4):
                        TR(ptr[:, c * 128:(c + 1) * 128], Ktok[:, kt, c * 128:(c + 1) * 128], identb[:],
                           [bKtok, bidb], [bptr])
                    ACT(KT[:, :, kt * 128:(kt + 1) * 128], ptr[:, 0:512].rearrange("p (c t) -> p c t", c=4), AF.Copy,
                        [bptr], [bKT[kt]])

            def PA():
                ACT(T2[0:ntok, :], xt[0:ntok, :], AF.Square, [bxt], [bT2, bssq], accum_out=ssq[0:ntok, 0:1])
                rstd_pow(ssq[0:ntok, 2:3], ssq[0:ntok, 1:2], ssq[0:ntok, 0:1], ntok, 1, 1.0 / D, [bssq], [bssq])

            def PB():
                E("dve", "scalar_tensor_tensor", [bxt, bssq, bgmb], [bT1], out=T1[0:ntok, :], in0=xt[0:ntok, :],
                  scalar=ssq[0:ntok, 2:3], in1=gmb[0:ntok, :], op0=ALU.mult, op1=ALU.mult)

            def PC():
                E("pool", "tensor_tensor", [bT1, bshb], [bub], out=ub[0:ntok, :], in0=T1[0:ntok, :], in1=shb[0:ntok, :],
                  op=ALU.add)

            def PD():
                for kc in range(8):
                    TR(ptr[:, kc * 128: kc * 128 + ntok], ub[0:ntok, kc * 128:(kc + 1) * 128], identb[0:ntok, 0:ntok],
                       [bub, bidb], [bptr])
                ACT(uT[:, :, 0:ntok], ptr[:].rearrange("p (k t) -> p k t", k=8)[:, :, 0:ntok], AF.Copy, [bptr], [buT])

            def Q1():
                pp, bpp = next_ab()
                hold["q"] = (pp, bpp)
                for c in range(4):
                    for kc in range(8):
                        MM(pp[:, c * 128: c * 128 + ntok], win_a[:, kc, c * 128:(c + 1) * 128], uT[:, kc, 0:ntok],
                           kc == 0, kc == 7, [buT] + WL["q"], [bpp])
                ppv = pp[:].rearrange("p (c t) -> p c t", c=4)
                qv = qTz[:].rearrange("p (c two) t -> p c two t", two=2)
                ACT(qv[0:64, :, 0, 0:ntok], ppv[0:64, :, 0:ntok], AF.Copy, [bpp], [bqTz])
                E("dve", "tensor_copy", [bpp], [bqTz], out=qv[64:128, :, 1, 0:ntok], in_=ppv[64:128, :, 0:ntok])

            def K1():
                pp, bpp = next_ab()
                for c in range(4):
                    for kc in range(8):
                        MM(pp[:, c * 128: c * 128 + ntok], win_a[:, kc, 512 + c * 128: 512 + (c + 1) * 128], uT[:, kc, 0:ntok],
                           kc == 0, kc == 7, [buT] + WL["k"], [bpp])
                ACT(KT[:, :, kpos0:kpos0 + ntok], pp[:].rearrange("p (c t) -> p c t", c=4)[:, :, 0:ntok], AF.Copy,
                    [bpp], [bKT[ktile]])

            def K2():
                pp, bpp = proj_tok(win_a, 512, 512, ntok, wl=WL["k"])
                sg, bsg = next_stg()
                E("dve", "tensor_copy", [bpp], [bsg], out=sg[0:ntok, :], in_=pp[0:ntok, :])
                DMA("sp", "ko", k_all[row0:row0 + ntok, :], sg[0:ntok, :], [bsg], [])

            def V1():
                pp, bpp = proj_tok(win_a, 1024, 512, ntok, wl=WL["v"])
                sg, bsg = next_stg()
                E("dve", "tensor_copy", [bpp], [bsg], out=sg[0:ntok, :], in_=pp[0:ntok, :])
                ACT(Vst[0:ntok, ktile, :], pp[0:ntok, :], AF.Copy, [bpp], [bV[ktile]])
                DMA("sp", "vo", v_all[row0:row0 + ntok, :], sg[0:ntok, :], [bsg], [])

            pk = PK if (first_of_seq and s > 0) else None
            return cx, [PL, None, pk, PA, PB, PC, None, PD, None, Q1, None, K1, None, K2, None, V1]

        def attention_jobs(cx):
            ntok, kpos0, qTz, bqTz = cx["ntok"], cx["kpos0"], cx["qTz"], cx["bqTz"]
            KT, Vst, bKT, bV = cx["stor"]["KT"], cx["stor"]["Vst"], cx["stor"]["bKT"], cx["stor"]["bV"]
            nk = kpos0 + ntok
            nblk = (nk + 511) // 512
            jobs = []
            for h in range(8):
                prev = None
                for b in range(nblk - 1, -1, -1):
                    job_ctr[0] += 1
                    J = dict(h=h, b=b, slot=job_ctr[0] % NSL, zi=job_ctr[0] % 2, prev=prev,
                             first=(b == nblk - 1), last=(b == 0))
                    jobs.append(J)
                    prev = J

            def geo(J):
                kb0 = J["b"] * 512
                nkb = min(512, nk - kb0)
                kts = list(range(kb0 // 128, (kb0 + nkb + 127) // 128))
                return kb0, nkb, kts

            def S0(J):
                kb0, nkb, kts = geo(J)
                MM(pzv[J["zi"]][0:ntok, 0:nkb], qTz[:, J["h"], 0:ntok], KT[:, J["h"] // 2, kb0:kb0 + nkb], True, not J["first"],
                   [bqTz] + [bKT[k] for k in kts], [bpz[J["zi"]]])
                if J["first"]:
                    MM(pzv[J["zi"]][0:ntok, nkb - ntok:nkb], identb[0:ntok, 0:ntok], cmask[0:ntok, 0:ntok], False, True,
                       [bidb, bcmask], [bpz[J["zi"]]])

            def S1(J):
                kb0, nkb, kts = geo(J)
                g_, bg = att[J["slot"]]["g"]
                pz = pzv[J["zi"]]
                ACT(g_[0:ntok, 512 - nkb:512], pz[0:ntok, 0:nkb], AF.Sigmoid, [bpz[J["zi"]]], [bg], scale=-0.125)

            def S2(J):
                kb0, nkb, kts = geo(J)
                A_ = att[J["slot"]]
                (g_, bg), (Pb, bP) = A_["g"], A_["Pb"]
                if J["prev"] is None:
                    init = 1.0
                    rd = [bg, bzerob]
                else:
                    pPb, bpP = att[J["prev"]["slot"]]["Pb"]
                    pk = geo(J["prev"])[1]
                    init = pPb[0:ntok, 512 - pk:512 - pk + 1]
                    rd = [bg, bzerob, bpP]
                E("dve", "tensor_tensor_scan", rd, [bP], out=Pb[0:ntok, 512 - nkb:513][:, ::-1],
                  data0=g_[0:ntok, 512 - nkb:513][:, ::-1], data1=zerob[0:ntok, 0:nkb + 1], initial=init,
                  op0=ALU.mult, op1=ALU.add)

            def S3(J):
                pass

            def S4(J):
                kb0, nkb, kts = geo(J)
                A_ = att[J["slot"]]
                (Pb, bP), (a_, ba) = A_["Pb"], A_["a"]
                E("pool", "tensor_tensor", [bP], [ba], out=a_[0:ntok, 0:nkb], in0=Pb[0:ntok, 512 - nkb + 1:513],
                  in1=Pb[0:ntok, 512 - nkb:512], op=ALU.subtract)

            def S5(J):
                kb0, nkb, kts = geo(J)
                a_, ba = att[J["slot"]]["a"]
                pT, bpT = patv2[J["zi"]]
                for j, kt in enumerate(kts):
                    ksz = min(128, nk - kt * 128)
                    TR(pT[0:ksz, j * 128: j * 128 + ntok], a_[0:ntok, j * 128: j * 128 + ksz],
                       identb[0:ntok, 0:ntok], [ba, bidb], [bpT])

            def S6(J):
                kb0, nkb, kts = geo(J)
                aT_, baT = att[J["slot"]]["aT"]
                pT, bpT = patv2[J["zi"]]
                nsub = len(kts)
                pv = pT[:, 0:512].rearrange("p (j t) -> p j t", j=4)
                lastk = min(128, nk - kts[-1] * 128)
                if lastk == 128:
                    ACT(aT_[:, 0:nsub, 0:ntok], pv[:, 0:nsub, 0:ntok], AF.Copy, [bpT], [baT])
                else:
                    if nsub > 1:
                        ACT(aT_[:, 0:nsub - 1, 0:ntok], pv[:, 0:nsub - 1, 0:ntok], AF.Copy, [bpT], [baT])
                    ACT(aT_[0:lastk, nsub - 1, 0:ntok], pv[0:lastk, nsub - 1, 0:ntok], AF.Copy, [bpT], [baT])

            def S7(J):
                kb0, nkb, kts = geo(J)
                aT_, baT = att[J["slot"]]["aT"]
                h = J["h"]
                nsub = len(kts)
                for j, kt in enumerate(kts):
                    ksz = min(128, nk - kt * 128)
                    MM(py[0:ntok, h * 64:(h + 1) * 64], aT_[0:ksz, j, 0:ntok], Vst[0:ksz, kt, h * 64:(h + 1) * 64],
                       J["first"] and j == 0, J["last"] and j == nsub - 1, [baT, bV[kt]], [bpy])

            return jobs, [S0, S1, S2, S3, S4, S5, S6, S7]

        def epilogue_pieces(cx):
            ntok, row0 = cx["ntok"], cx["row0"]

            def E0():
                ACT(ya[0:ntok, :], py[0:ntok, :], AF.Copy, [bpy], [bya])

            def E1():
                for c in range(4):
                    TR(ptr[:, c * 128: c * 128 + ntok], ya[0:ntok, c * 128:(c + 1) * 128], identb[0:ntok, 0:ntok],
                       [bya, bidb], [bptr])
                ACT(yaT[:, :, 0:ntok], ptr[:, 0:512].rearrange("p (c t) -> p c t", c=4)[:, :, 0:ntok], AF.Copy, [bptr], [byaT])

            def E2():
                for n in range(2):
                    pp, bpp = next_ab()
                    for c in range(4):
                        MM(pp[0:ntok, :], yaT[:, c, 0:ntok], wa3[:, c, n * 512:(n + 1) * 512], c == 0, c == 3,
                           [byaT] + WL["a"], [bpp])
                    E("dve", "tensor_copy", [bpp], [bAst], out=Ast[0:ntok, n * 512:(n + 1) * 512], in_=pp[0:ntok, :])
                bAsc = Buf("Asc%d" % row0)
                DMA("sp", "Aw", A_sc[row0:row0 + ntok, :], Ast[0:ntok, :], [bAst], [bAsc])
                Abuf[row0] = bAsc

            return [E0, E1, E2]

        if "1a" in PH:
            items = []
            jbase = 0
            starts = []
            njs = []
            seq_idx = -1
            prev_s = None
            for li_, tile in enumerate(tiles):
                first_of_seq = (tile[0] != prev_s)
                if first_of_seq:
                    seq_idx += 1
                    prev_s = tile[0]
                stor = stor_main if seq_idx % 2 == 0 else stor_alt
                cx, pieces = prologue_pieces(tile, li_, stor, first_of_seq)
                if li_ == 0:
                    for i_, pf in enumerate(pieces):
                        if pf is not None:
                            items.append((-100 + i_, 9.0, pf))
                else:
                    pst = starts[li_ - 1] + 1
                    if first_of_seq:
                        pst = max(pst, stor.get("last_step", -1) + 1)
                    pend = starts[li_ - 1] + njs[li_ - 1] - 1
                    avail = max(1, pend - pst)
                    L_ = len(pieces)
                    for i_, pf in enumerate(pieces):
                        if pf is not None:
                            items.append((pst + (i_ * avail) // L_, 9.0 + i_ * 0.01, pf))
                jobs, stages = attention_jobs(cx)
                starts.append(jbase)
                njs.append(len(jobs))
                NSTG = len(stages)
                for ji, J in enumerate(jobs):
                    for k, Sf in enumerate(stages):
                        items.append((jbase + ji + k, float(NSTG - 1 - k), (lambda Sf=Sf, J=J: Sf(J))))
                last_step = jbase + len(jobs) - 1 + (NSTG - 1)
                stor["last_step"] = last_step
                ep = epilogue_pieces(cx)
                items.append((last_step, 0.5, ep[0]))
                items.append((last_step + 2, 8.5, ep[1]))
                items.append((last_step + 4, 8.6, ep[2]))
                jbase += len(jobs)
            if "0" in PH:
                nsteps = jbase + 8
                gap = max(14, (nsteps - 30) // 8)
                for bi, nch in enumerate(range(4, 12)):
                    t_ = 10 + bi * gap
                    items.append((t_, 9.5, (lambda nch=nch: mod_chunk_load(nch, WAbg, bWAbg, "wabg"))))

                    def comp(nch=nch):
                        pp, bpp = next_ab()
                        mod_chunk_compute(nch, WAbg, bWAbg, pp, bpp)
                    items.append((t_ + 10, 9.6, comp))
            order = sorted(range(len(items)), key=lambda i: (items[i][0], items[i][1], i))
            for i in order:
                items[i][2]()

        phase_begin()
        NB = 4104
        WL = load_wg(w_in, 8, 1536, [("mqk", 0, 1024), ("gt", 2048, 8), ("mv", 1024, 512), ("mo", 1536, 512),
                                     ("gb", 3080, 1024), ("ga", 2056, 1024)], NB, 0, "b")
        WL["b"] = WLb_pre
        WL["out"] = WLout_pre
        winb = wview(0, 8, NB)
        wb3 = wview(32832, 4, 1024)
        wo3 = wview(36928, 8, 1024)
        wcv, bwcv = cv("wcv", [128, 8, 4], F32)
        bcv, bbcv = cv("bcv", [128, 8], F32)
        mlg, bmlg = cv("mlg", [64, 512], F32)
        bifi, bbifi = cv("bifi", [4, 1], F32)
        biff, bbiff = cv("biff", [4, 1], F32)
        for j_ in range(4):
            DMA("sp", "cst", wcv[:, :, j_], w_conv[j_].rearrange("(c p) -> p c", p=128), [], [bwcv], allow_slow_non_contiguous=True)
        DMA("sp", "cst", bcv[:], b_conv[0].rearrange("(c p) -> p c", p=128), [], [bbcv], allow_slow_non_contiguous=True)
        DMA("sp", "cst", mlg[:], ml_norm_g[0:1, :].broadcast_to([64, 512]), [], [bmlg])
        DMA("sp", "cst", bifi[:], b_if[0:4, :], [], [bbifi])
        DMA("sp", "cst", biff[:], b_if[4:8, :], [], [bbiff])
        SC_ = 5
        CS_ = 2
        xs3 = [xts[0], xts[1], cv("xt2", [128, D], F32)]
        uT3 = [(uT, buT), cv("uT1", [128, 8, 128], BF16), cv("uT2", [128, 8, 128], BF16)]
        mqk2 = [cv("mqkT%d" % i, [128, 8, 128], BF16) for i in range(2)]
        xp2 = [cv("xp%d" % i, [128, 8, 131], F32) for i in range(2)]
        gl2 = [dict(li=cv("gli%d" % i, [4, 128], F32), sf=cv("gsf%d" % i, [4, 128], F32),
                    lf=cv("glf%d" % i, [4, 128], F32)) for i in range(2)]
        ybT2 = [cv("ybT%d" % i, [128, 4, 128], BF16) for i in range(2)]
        At, bAt = cv("At", [128, D], F32)
        sgt, bsgt = cv("sgt", [128, 512], F32)
        mg, bmg = cv("mg", [128, D], BF16)
        mT, bmT = cv("mT", [128, 8, 128], BF16)
        CTs = [cv("CT%d" % i, [128, 4, 129], F32) for i in range(3)]
        ggbs = [(ggb, bggb), cv("ggb1", [128, D], F32)]
        C0t, bC0t = At[:, 0:512].rearrange("p (h k) -> p h k", h=4), bAt
        mseq, bmseq = cv("mseq", [4, 40], F32)
        CK = []
        for i in range(CS_):
            d_ = {}
            for nm in ("bb", "rr", "mmt", "wg", "wi", "emt"):
                d_[nm] = cv("g_%s%d" % (nm, i), [4, 64], F32)
            d_["dec"] = cv("g_dec%d" % i, [4, 128], F32)
            d_["gsm"] = cv("gsm%d" % i, [4, 8], F32)
            d_["gtok"] = cv("gtok%d" % i, [128, 16], F32)
            d_["Wt"] = cv("Wt%d" % i, [64, 4, 64], F32)
            d_["vaug"] = cv("vaug%d" % i, [64, 4, 129], BF16)
            d_["vw"] = cv("vw%d" % i, [64, 4, 129], BF16)
            d_["smo"] = cv("smo%d" % i, [64, 512], F32)
            d_["ktok"] = cv("ktok%d" % i, [64, 4, 128], BF16)
            d_["ST"] = cv("ST%d" % i, [64, 4, 64], BF16)
            d_["qsT"] = cv("qsT%d" % i, [128, 4, 64], BF16)
            d_["CTb"] = cv("CTb%d" % i, [128, 4, 129], BF16)
            d_["hn"] = cv("hn%d" % i, [64, 4, 129], F32)
            d_["hs"] = cv("hs%d" % i, [64, 16], F32)
            d_["T1h"] = cv("T1h%d" % i, [64, 4, 128], F32)
            d_["ybt"] = cv("ybt%d" % i, [64, 512], BF16)
            E("pool", "memset", [], [d_["vaug"][1]], d_["vaug"][0][:], 1.0)
            CK.append(d_)
        patf = pat[:, 512:1024].bitcast(F32)
        chunk_ctr = [0]
        gchunk = [0]

        def out_state(s, CT, bCT, mcol_final):
            def f():
                for h in range(4):
                    TR(pa[:, h * 128:(h + 1) * 128], CT[:, h, 0:128], identf[:], [bCT, bidf], [bpa])
                E("dve", "tensor_copy", [bpa], [bC0t], out=C0t, in_=pa[:].rearrange("p (h k) -> p h k", h=4))
                DMA("sp", "sto", C_out[s].rearrange("h v k -> v h k"), C0t, [bC0t], [])
                DMA("sp", "sto", n_out[s].rearrange("h k -> k h"), CT[:, :, 128], [bCT], [], allow_slow_non_contiguous=True)
                DMA("sp", "sto", m_out[s:s + 1, :].rearrange("o h -> h o"), mseq[:, mcol_final:mcol_final + 1],
                    [bmseq], [], allow_slow_non_contiguous=True)
            return f

        def out_conv(s, xp_last):
            def f():
                xpl, bxpl, nt_l = xp_last
                for j_ in range(3):
                    DMA("sp", "sto", conv_out[s, j_].rearrange("(c p) -> p c", p=128), xpl[:, :, nt_l + j_], [bxpl], [],
                        allow_slow_non_contiguous=True)
            return f

        def sched_tile(items, lt, tix, tile, prev_xp, sq):
            (s, row0, ntok, ti) = tile
            xt, bxt = xs3[lt % 3]
            uTs, buTs = uT3[lt % 3]
            mqkT, bmqk = mqk2[lt % 2]
            xp, bxp = xp2[lt % 2]
            G_ = gl2[lt % 2]
            (li, bli), (sf, bsf), (lf, blf) = G_["li"], G_["sf"], G_["lf"]
            ybT, bybT = ybT2[lt % 2]
            nch = ntok // 64
            base = 2 * lt * SC_
            CT, bCT = sq["CT"]
            gg_, bgg_ = sq["gg"]
            T1v = T1[:].rearrange("p (c t) -> p c t", c=8)
            T2v = T2[:].rearrange("p (c t) -> p c t", c=8)

            def F0():
                DMA("sp", "xl%d" % (lt % 3), xt[0:ntok, :], xall[row0:row0 + ntok, :], [], [bxt])
                ACT(T2[0:ntok, :], xt[0:ntok, :], AF.Square, [bxt], [bT2, bssq], accum_out=ssq[0:ntok, 0:1])
                rstd_pow(ssq[0:ntok, 2:3], ssq[0:ntok, 1:2], ssq[0:ntok, 0:1], ntok, 1, 1.0 / D, [bssq], [bssq])
                E("dve", "scalar_tensor_tensor", [bxt, bssq, bgmb], [bT1], out=T1[0:ntok, :], in0=xt[0:ntok, :],
                  scalar=ssq[0:ntok, 2:3], in1=gmb[0:ntok, :], op0=ALU.mult, op1=ALU.mult)
                E("pool", "tensor_tensor", [bT1, bshb], [bub], out=ub[0:ntok, :], in0=T1[0:ntok, :], in1=shb[0:ntok, :],
                  op=ALU.add)

            def F1():
                for kc in range(8):
                    TR(ptr[:, kc * 128: kc * 128 + ntok], ub[0:ntok, kc * 128:(kc + 1) * 128], identb[0:ntok, 0:ntok],
                       [bub, bidb], [bptr])
                ACT(uTs[:, :, 0:ntok], ptr[:].rearrange("p (k t) -> p k t", k=8)[:, :, 0:ntok], AF.Copy, [bptr], [buTs])

            def F2(half):
                def f():
                    if half == 0 and prev_xp is not None:
                        pxp, bpxp, pnt = prev_xp
                        E("pool", "tensor_copy", [bpxp], [bxp], out=xp[:, :, 0:3], in_=pxp[:, :, pnt:pnt + 3])
                    pp, bpp = next_ab()
                    for c4 in range(4):
                        ch = half * 4 + c4
                        for kc in range(8):
                            MM(pp[:, c4 * 128: c4 * 128 + ntok], winb[:, kc, ch * 128:(ch + 1) * 128], uTs[:, kc, 0:ntok],
                               kc == 0, kc == 7, [buTs] + WL["mqk"], [bpp])
                    ACT(xp[:, half * 4:(half + 1) * 4, 3:3 + ntok], pp[:].rearrange("p (c t) -> p c t", c=4)[:, :, 0:ntok],
                        AF.Copy, [bpp], [bxp])
                return f

            def F3(p):
                def f():
                    for ch in (2 * p, 2 * p + 1):
                        E("dve", "tensor_scalar", [bxp, bwcv, bbcv], [bT1], out=T1v[:, ch, 0:ntok], in0=xp[:, ch, 0:ntok],
                          scalar1=wcv[:, ch, 0:1], scalar2=bcv[:, ch:ch + 1], op0=ALU.mult, op1=ALU.add)
                        for j in range(1, 4):
                            E("dve", "scalar_tensor_tensor", [bxp, bwcv, bT1], [bT1], out=T1v[:, ch, 0:ntok],
                              in0=xp[:, ch, j:j + ntok], scalar=wcv[:, ch, j:j + 1], in1=T1v[:, ch, 0:ntok],
                              op0=ALU.mult, op1=ALU.add)

                def fpool():
                    tmpf = ub[:, 0:256].bitcast(F32)
                    for ch in (2 * p, 2 * p + 1):
                        E("pool", "tensor_scalar", [bxp, bwcv, bbcv], [bT1], out=T1v[:, ch, 0:ntok], in0=xp[:, ch, 0:ntok],
                          scalar1=wcv[:, ch, 0:1], scalar2=bcv[:, ch:ch + 1], op0=ALU.mult, op1=ALU.add)
                        for j in range(1, 4):
                            E("pool", "tensor_scalar", [bxp, bwcv], [bub], out=tmpf[:, 0:ntok], in0=xp[:, ch, j:j + ntok],
                              scalar1=wcv[:, ch, j:j + 1], scalar2=0.0, op0=ALU.mult, op1=ALU.add)
                            E("pool", "tensor_tensor", [bub, bT1], [bT1], out=T1v[:, ch, 0:ntok], in0=tmpf[:, 0:ntok],
                              in1=T1v[:, ch, 0:ntok], op=ALU.add)
                return fpool if p >= 2 else f

            def F4():
                ACT(T2v[:, :, 0:ntok], T1v[:, :, 0:ntok], AF.Sigmoid, [bT1], [bT2])
                E("dve", "tensor_tensor", [bT1, bT2], [bmqk], out=mqkT[:, 0:4, 0:ntok], in0=T1v[:, 0:4, 0:ntok],
                  in1=T2v[:, 0:4, 0:ntok], op=ALU.mult)
                E("dve", "scalar_tensor_tensor", [bT1, bT2], [bmqk], out=mqkT[:, 4:8, 0:ntok], in0=T1v[:, 4:8, 0:ntok],
                  scalar=float(1.0 / np.sqrt(128.0)), in1=T2v[:, 4:8, 0:ntok], op0=ALU.mult, op1=ALU.mult)

            def F5():
                for kc in range(8):
                    MM(pm[0:4, 0:ntok], winb[:, kc, 2048:2052], uTs[:, kc, 0:ntok], kc == 0, kc == 7, [buTs] + WL["gt"], [bpm])
                ACT(li[:, 0:ntok], pm[0:4, 0:ntok], AF.Identity, [bpm, bbifi], [bli], bias=bifi[:])
                for kc in range(8):
                    MM(pm[0:4, 128:128 + ntok], winb[:, kc, 2052:2056], uTs[:, kc, 0:ntok], kc == 0, kc == 7,
                       [buTs] + WL["gt"], [bpm])
                ACT(sf[:, 0:ntok], pm[0:4, 128:128 + ntok], AF.Sigmoid, [bpm, bbiff], [bsf], bias=biff[:])
                ACT(lf[:, 0:ntok], sf[:, 0:ntok], AF.Ln, [bsf], [blf])

            fl = [(0, F0), (1, F1), (2, F2(0)), (3, F2(1)), (3.5, F5), (4, F3(0)), (5, F3(1)), (6, F3(2)), (7, F3(3)),
                  (8, F4)]
            for j, f in fl:
                items.append((base + j, f))

            def chunk_stages(c):
                cs = slice(c * 64, (c + 1) * 64)
                g = gchunk[0]
                gchunk[0] += 1
                K = CK[g % CS_]
                Kn = CK[(g + 1) % CS_]
                Kp = CK[(g - 1) % CS_]
                mc = chunk_ctr[0]
                chunk_ctr[0] += 1
                mcur = mseq[:, mc:mc + 1]
                (bb_, bbb), (rr, brr), (mmt, bmmt), (wg, bwg), (wi, bwi), (emt, bemt), (dec, bdec) = (
                    K["bb"], K["rr"], K["mmt"], K["wg"], K["wi"], K["emt"], K["dec"])
                (gsm, bgsm), (gtok, bgtok), (Wt, bWt), (vaug, bvaug), (vw, bvw), (smo, bsmo) = (
                    K["gsm"], K["gtok"], K["Wt"], K["vaug"], K["vw"], K["smo"])
                (ktok, bktok), (STt, bST), (qsT, bqsT), (CTb, bCTb), (hn, bhn), (hs, bhs) = (
                    K["ktok"], K["ST"], K["qsT"], K["CTb"], K["hn"], K["hs"])
                hh, bhh = hn[:, :, 0:128], bhn
                (T1h, bT1h), (ybt, bybt) = K["T1h"], K["ybt"]
                mx2 = mmt[:, 63:64]

                def G0():
                    E("dve", "tensor_tensor_scan", [blf, bones], [bbb], out=bb_[:, :], data0=onesb[0:4, 0:64], data1=lf[:, cs],
                      initial=0.0, op0=ALU.mult, op1=ALU.add)
                    E("dve", "tensor_tensor", [bli, bbb], [brr], out=rr[:, :], in0=li[:, cs], in1=bb_[:, :], op=ALU.subtract)
                    E("dve", "tensor_tensor_scan", [brr, bones, bmseq], [bmmt], out=mmt[:, :], data0=onesb[0:4, 0:64],
                      data1=rr[:, :], initial=mcur, op0=ALU.mult, op1=ALU.max)
                    E("dve", "tensor_scalar", [bmmt], [bgsm], out=gsm[:, 0:1], in0=mx2, scalar1=-1.0, scalar2=None, op0=ALU.mult)
                    E("dve", "tensor_tensor", [bmseq, bmmt], [bgsm], out=gsm[:, 1:2], in0=mcur, in1=mx2, op=ALU.subtract)
                    E("dve", "tensor_tensor", [bbb, bmmt], [bmseq], out=mseq[:, mc + 1:mc + 2], in0=bb_[:, 63:64],
                      in1=mx2, op=ALU.add)
                    E("dve", "tensor_tensor", [bbb, bmmt], [bemt], out=emt[:, :], in0=bb_[:, :], in1=mmt[:, :], op=ALU.add)

                def G1():
                    ACT(wg[:, :], rr[:, :], AF.Exp, [brr, bgsm], [bwg], bias=gsm[:, 0:1])
                    ACT(dec[:, :], zer[0:4, :], AF.Exp, [bzer, bgsm], [bdec], bias=gsm[:, 1:2])
                    ACT(wi[:, :], mmt[:, :], AF.Exp, [bmmt, bmseq], [bwi], scale=-1.0, bias=mcur)
                    ACT(emt[:, :], emt[:, :], AF.Exp, [bemt], [bemt], scale=-1.0)

                def G2():
                    TR(pm[0:64, 256:260], rr[:, :], identf[0:4, 0:4], [brr, bidf], [bpm])
                    TR(pm[0:64, 260:264], emt[:, :], identf[0:4, 0:4], [bemt, bidf], [bpm])
                    TR(pm[0:64, 264:268], wg[:, :], identf[0:4, 0:4], [bwg, bidf], [bpm])
                    TR(pm[0:128, 268:272], dec[:, :], identf[0:4, 0:4], [bdec, bidf], [bpm])
                    E("dve", "tensor_copy", [bpm], [bgtok], out=gtok[0:64, 0:12], in_=pm[0:64, 256:268])
                    E("dve", "tensor_copy", [bpm], [bgtok], out=gtok[:, 12:16], in_=pm[:, 268:272])

                def G3():
                    for h in range(4):
                        MM(py[0:64, h * 64:(h + 1) * 64], sel[:, h, 0:64], mmt[:, :], True, True, [bsel, bmmt], [bpy])
                    for h in range(4):
                        MM(py[:, 256 + h * 64: 256 + (h + 1) * 64], sel[:, h, :], wi[:, :], True, True, [bsel, bwi], [bpy])
                    for h in range(4):
                        ACT(Wt[:, h, :], py[0:64, h * 64:(h + 1) * 64], AF.Exp, [bpy, bgtok], [bWt], scale=-1.0,
                            bias=gtok[0:64, h:h + 1])
                    E("dve", "tensor_tensor", [bpy, bmqk], [bqsT], out=qsT[:], in0=py[:, 256:512].rearrange("p (h t) -> p h t", h=4),
                      in1=mqkT[:, 0:4, cs], op=ALU.mult)
                    E("pool", "affine_select", [bWt], [bWt], out=Wt[:], in_=Wt[:], pattern=[[0, 4], [1, 64]],
                      compare_op=ALU.is_ge, fill=0.0, base=0, channel_multiplier=-1)

                def V0a():
                    if nch == 2 and c == 1:
                        return
                    pp, bpp = next_ab()
                    m_ = ntok if nch == 2 else 64
                    for kc in range(8):
                        MM(pp[0:m_, :], uTs[:, kc, 0:m_], winb[:, kc, 1024:1536], kc == 0, kc == 7, [buTs] + WL["mv"], [bpp])
                    E("dve", "tensor_copy", [bpp], [bvaug], out=vaug[:, :, 0:128], in_=pp[0:64, :].rearrange("p (h d) -> p h d", h=4))
                    if nch == 2:
                        vn, bvn = Kn["vaug"]
                        ACT(vn[:, :, 0:128], pp[64:128, :].rearrange("p (h d) -> p h d", h=4), AF.Copy, [bpp], [bvn])

                def V0b():
                    if nch == 2 and c == 0:
                        return
                    pp, bpp = next_ab()
                    m_ = ntok if nch == 2 else 64
                    for kc in range(8):
                        MM(pp[0:m_, :], uTs[:, kc, 0:m_], winb[:, kc, 1536:2048], kc == 0, kc == 7, [buTs] + WL["mo"], [bpp])
                    if nch == 2:
                        sp_, bsp_ = Kp["smo"]
                        ACT(sp_[:], pp[0:64, :], AF.Sigmoid, [bpp], [bsp_])
                        E("pool", "tensor_tensor", [bsp_, bmlg], [bsp_], out=sp_[:], in0=sp_[:], in1=mlg[:], op=ALU.mult)
                        ACT(smo[:], pp[64:128, :], AF.Sigmoid, [bpp], [bsmo])
                    else:
                        ACT(smo[:], pp[0:64, :], AF.Sigmoid, [bpp], [bsmo])
                    E("pool", "tensor_tensor", [bsmo, bmlg], [bsmo], out=smo[:], in0=smo[:], in1=mlg[:], op=ALU.mult)

                def V0c():
                    for h in range(4):
                        TR(pat[0:64, h * 128:(h + 1) * 128], mqkT[:, 4 + h, cs], identb[:], [bmqk, bidb], [bpat])
                    E("dve", "tensor_copy", [bpat], [bktok], out=ktok[:], in_=pat[0:64, 0:512].rearrange("p (h d) -> p h d", h=4))

                def U():
                    E("pool", "tensor_copy", [bCT], [bCTb], out=CTb[:], in_=CT[:])
                    E("dve", "tensor_tensor", [bvaug, bgtok], [bvw], out=vw[:], in0=vaug[:],
                      in1=gtok[0:64, 8:12].unsqueeze(2).broadcast_to([64, 4, 129]), op=ALU.mult)
                    for h in range(4):
                        o0 = 512 * (h // 2) + 129 * (h % 2)
                        MM(pz2[:, o0:o0 + 129], ktok[:, h, :], vw[:, h, :], True, True, [bktok, bvw], [bpz[0], bpz[1]])
                    for h in range(4):
                        o0 = 512 * (h // 2) + 129 * (h % 2)
                        E("dve", "scalar_tensor_tensor", [bCT, bgtok, bpz[0], bpz[1]], [bCT], out=CT[:, h, :], in0=CT[:, h, :],
                          scalar=gtok[:, 12 + h:13 + h], in1=pz2[:, o0:o0 + 129], op0=ALU.mult, op1=ALU.add)

                def V1():
                    for h in range(4):
                        MM(patf[0:64, h * 64:(h + 1) * 64], mqkT[:, 4 + h, cs], mqkT[:, h, cs], True, True, [bmqk], [bpat])
                    E("dve", "tensor_tensor", [bpat, bWt], [bST], out=STt[:], in0=patf[0:64, 0:256].rearrange("p (h t) -> p h t", h=4),
                      in1=Wt[:], op=ALU.mult)

                def N0():
                    for h in range(4):
                        o0 = 512 * (h // 2) + 129 * (h % 2)
                        MM(pz2[0:64, o0:o0 + 129], STt[:, h, :], vaug[:, h, :], True, False, [bST, bvaug], [bpz[0], bpz[1]])
                        MM(pz2[0:64, o0:o0 + 129], qsT[:, h, :], CTb[:, h, :], False, True, [bqsT, bCTb], [bpz[0], bpz[1]])
                    E("dve", "tensor_copy", [bpz[0], bpz[1]], [bhn], out=hn[:, 0:2, :], in_=pz2[0:64, 0:258].rearrange("p (h d) -> p h d", h=2))
                    E("dve", "tensor_copy", [bpz[0], bpz[1]], [bhn], out=hn[:, 2:4, :], in_=pz2[0:64, 512:770].rearrange("p (h d) -> p h d", h=2))

                def N1():
                    den = hn[:, :, 128]
                    E("dve", "scalar_tensor_tensor", [bhn], [bhs], out=hs[:, 0:4], in0=den, scalar=-1.0, in1=den,
                      op0=ALU.mult, op1=ALU.max)
                    E("dve", "tensor_tensor", [bhs, bgtok], [bhs], out=hs[:, 4:8], in0=hs[:, 0:4], in1=gtok[0:64, 4:8], op=ALU.max)
                    E("dve", "reciprocal", [bhs], [bhs], out=hs[:, 8:12], in_=hs[:, 4:8])
                    E("dve", "tensor_tensor", [bhn, bhs], [bhh], out=hh, in0=hn[:, :, 0:128],
                      in1=hs[:, 8:12].unsqueeze(2).broadcast_to([64, 4, 128]), op=ALU.mult)
                    E("dve", "tensor_tensor", [bhh], [bT1h], out=T1h[:], in0=hh, in1=hh, op=ALU.mult)
                    E("dve", "tensor_reduce", [bT1h], [bhs], out=hs[:, 12:16], in_=T1h[:], axis=AX.X, op=ALU.add)

                def N2():
                    rstd_pow(hs[:, 12:16], hs[:, 12:16], hs[:, 12:16], 64, 4, 1.0 / 128, [bhs], [bhs])

                def N3():
                    E("dve", "tensor_tensor", [bhh, bhs], [bT1h], out=T1h[:], in0=hh,
                      in1=hs[:, 12:16].unsqueeze(2).broadcast_to([64, 4, 128]), op=ALU.mult)
                    E("dve", "tensor_tensor", [bT1h, bsmo], [bybt], out=ybt[:], in0=T1h[:].rearrange("p h d -> p (h d)"),
                      in1=smo[:], op=ALU.mult)

                def N4():
                    for h in range(4):
                        TR(ptr[:, h * 64:(h + 1) * 64], ybt[:, h * 128:(h + 1) * 128], identb[0:64, 0:64],
                           [bybt, bidb], [bptr])
                    ACT(ybT[:, :, cs], ptr[:, 0:256].rearrange("p (h t) -> p h t", h=4), AF.Copy, [bptr], [bybT])

                return [G0, G1, G2, G3, V0a, V0b, V0c, U, V1, N0, N1, N2, N3, N4]

            for c in range(nch):
                st_ = chunk_stages(c)
                b0 = base + 7 + c * SC_
                for j, f in enumerate(st_):
                    items.append((b0 + j, f))
            dbase = base + 7 + (nch - 1) * SC_ + 14

            def D0():
                DMA("sp", "Ar", At[0:ntok, :], A_sc[row0:row0 + ntok, :], [Abuf.get(row0)], [bAt])
                for n in range(2):
                    pB, bpB = next_ab()
                    for c in range(4):
                        MM(pB[0:ntok, :], ybT[:, c, 0:ntok], wb3[:, c, n * 512:(n + 1) * 512], c == 0, c == 3, [bybT] + WL["b"], [bpB])
                    pG, bpG = next_ab()
                    for kc in range(8):
                        MM(pG[0:ntok, :], uTs[:, kc, 0:ntok], winb[:, kc, 3080 + n * 512: 3080 + (n + 1) * 512], kc == 0, kc == 7,
                           [buTs] + WL["gb"], [bpG])
                    ACT(sgt[0:ntok, :], pG[0:ntok, :], AF.Sigmoid, [bpG], [bsgt])
                    E("dve", "tensor_tensor", [bsgt, bpB], [bT2], out=T2[0:ntok, n * 512:(n + 1) * 512], in0=sgt[0:ntok, :],
                      in1=pB[0:ntok, :], op=ALU.mult)

            def D1():
                for n in range(2):
                    pG, bpG = next_ab()
                    for kc in range(8):
                        MM(pG[0:ntok, :], uTs[:, kc, 0:ntok], winb[:, kc, 2056 + n * 512: 2056 + (n + 1) * 512], kc == 0, kc == 7,
                           [buTs] + WL["ga"], [bpG])
                    ACT(sgt[0:ntok, :], pG[0:ntok, :], AF.Sigmoid, [bpG], [bsgt])
                    E("pool", "tensor_tensor", [bsgt, bAt], [bsgt], out=sgt[0:ntok, :], in0=sgt[0:ntok, :],
                      in1=At[0:ntok, n * 512:(n + 1) * 512], op=ALU.mult)
                    E("pool", "tensor_tensor", [bsgt, bT2], [bmg], out=mg[0:ntok, n * 512:(n + 1) * 512], in0=sgt[0:ntok, :],
                      in1=T2[0:ntok, n * 512:(n + 1) * 512], op=ALU.add)

            def D2():
                for kc in range(8):
                    TR(ptr[:, kc * 128: kc * 128 + ntok], mg[0:ntok, kc * 128:(kc + 1) * 128], identb[0:ntok, 0:ntok],
                       [bmg, bidb], [bptr])
                ACT(mT[:, :, 0:ntok], ptr[:].rearrange("p (k t) -> p k t", k=8)[:, :, 0:ntok], AF.Copy, [bptr], [bmT])

            def D3():
                for n in range(2):
                    pO, bpO = next_ab()
                    for kc in range(8):
                        MM(pO[0:ntok, :], mT[:, kc, 0:ntok], wo3[:, kc, n * 512:(n + 1) * 512], kc == 0, kc == 7, [bmT] + WL["out"], [bpO])
                    E("dve", "tensor_tensor", [bpO, bgg_], [bT2], out=T2[0:ntok, n * 512:(n + 1) * 512], in0=pO[0:ntok, :],
                      in1=gg_[0:ntok, n * 512:(n + 1) * 512], op=ALU.mult)
                E("pool", "tensor_tensor", [bT2, bxt], [bxt], out=xt[0:ntok, :], in0=T2[0:ntok, :], in1=xt[0:ntok, :], op=ALU.add)
                bx1 = Buf("x1sc%d" % row0)
                x1buf[row0] = bx1
                DMA("sp", "x1w%d" % (lt % 3), x1_sc[row0:row0 + ntok, :], xt[0:ntok, :], [bxt], [bx1])
                ACT(T2[0:ntok, :], xt[0:ntok, :], AF.Square, [bxt], [bT2, bssq], accum_out=ssq[0:ntok, 0:1])
                E("pool", "tensor_scalar", [bssq], [bssq], out=ssq[0:ntok, 1:2], in0=ssq[0:ntok, 0:1], scalar1=1.0 / D,
                  scalar2=EPS, op0=ALU.mult, op1=ALU.add)
                E("pool", "tensor_tensor", [bssq, bmhalf], [brstd2], out=rstd2[0:ntok, tix:tix + 1], in0=ssq[0:ntok, 1:2],
                  in1=mhalf[0:ntok, 0:1], op=ALU.pow)

            for j, f in enumerate([D0, D1, D2, D3]):
                items.append((dbase + j, f))
            sq["lastU"] = base + 7 + (nch - 1) * SC_ + 7
            sq["lastD3"] = dbase + 3
            sq["lastF4"] = base + 8
            return (xp, bxp, ntok)

        if "1b" in PH:
            seqs = []
            for tix, tile in enumerate(tiles):
                if not seqs or seqs[-1][0] != tile[0]:
                    seqs.append((tile[0], []))
                seqs[-1][1].append((tix, tile))
            items = []
            lt = 0
            d3_hist = []
            for k_, (s, tl) in enumerate(seqs):
                CTk, bCTk = CTs[k_ % 3]
                ggk = ggbs[k_ % 2]
                sq = dict(CT=(CTk, bCTk), gg=ggk)
                chunk_ctr[0] += 1
                mcol = chunk_ctr[0]
                base0 = 2 * lt * SC_
                xp0, bxp0 = xp2[lt % 2]

                def init_seq(s=s, CTk=CTk, bCTk=bCTk, mcol=mcol, xp0=xp0, bxp0=bxp0, first=(k_ == 0)):
                    load_mod(s, 1, gate=False)
                    if s == 0:
                        E("pool", "memset", [], [bCTk], CTk[:], 0.0)
                        if first:
                            E("pool", "memset", [], [bmseq], mseq[:], 0.0)
                        E("pool", "memset", [], [bxp0], xp0[:, :, 0:3], 0.0)
                    else:
                        si = s - 1
                        DMA("sp", "st", C0t, C0[si].rearrange("h v k -> v h k"), [], [bC0t])
                        for h in range(4):
                            TR(pa[:, h * 128:(h + 1) * 128], C0t[:, h, :], identf[:], [bC0t, bidf], [bpa])
                        E("dve", "tensor_copy", [bpa], [bCTk], out=CTk[:, :, 0:128], in_=pa[:].rearrange("p (h v) -> p h v", h=4))
                        DMA("sp", "st", CTk[:, :, 128], n0[si].rearrange("h k -> k h"), [], [bCTk], allow_slow_non_contiguous=True)
                        DMA("sp", "st", mseq[:, mcol:mcol + 1], m0[si:si + 1, :].rearrange("o h -> h o"), [], [bmseq],
                            allow_slow_non_contiguous=True)
                        for j_ in range(3):
                            DMA("sp", "st", xp0[:, :, j_], conv0[si, j_].rearrange("(c p) -> p c", p=128), [], [bxp0],
                                allow_slow_non_contiguous=True)

                items.append((base0 - 0.5, init_seq))
                gstep = base0 - 0.4
                if k_ >= 2:
                    gstep = max(gstep, d3_hist[k_ - 2] + 0.5)
                items.append((gstep, (lambda s=s, ggk=ggk: load_mod(s, 1, front=False, gdst=ggk))))
                prev_xp = None
                for (tix, tile) in tl:
                    prev_xp = sched_tile(items, lt, tix, tile, prev_xp, sq)
                    lt += 1
                items.append((sq["lastF4"] + 0.5, out_conv(s, prev_xp)))
                items.append((sq["lastU"] + 0.5, out_state(s, CTk, bCTk, chunk_ctr[0])))
                d3_hist.append(sq["lastD3"])
            order = sorted(range(len(items)), key=lambda i: (items[i][0], i))
            for i in order:
                items[i][1]()

        phase_begin()
        pab[:] = [(pa, bpa), (pb, bpb), (py, bpy), (pm, bpm), (pz2[:, 0:512], bpz[0]), (pz2[:, 512:1024], bpz[1])]
        WL = {}
        for ci_, c0_ in enumerate(range(0, DFF, 512)):
            n_ = min(512, DFF - c0_)
            WL["g%d" % ci_] = load_wg(w_ff_gate, 8, 0, [("g", c0_, n_)], DFF, 0, "fg%d" % ci_, kbase=2 * ci_)["g"]
            WL["u%d" % ci_] = load_wg(w_ff_up, 8, 0, [("u", c0_, n_)], DFF, 22528, "fu%d" % ci_, kbase=2 * ci_ + 1)["u"]
        wg3 = wview(0, 8, DFF)
        wu3 = wview(22528, 8, DFF)
        wdn, _ = cv("wdn", [128, 22 * D], BF16)
        xs3p = [xts[0], xts[1], cv("xt2p", [128, D], F32)]
        wd3 = wdn.rearrange("p (k c) -> p k c", k=22)
        WLd = {}
        dvd = w_ff_down.rearrange("(k p) n -> p k n", p=128)
        for n_ in range(2):
            bw = fb("Wd%d" % n_)
            DMA("pool", "Wg%d" % (12 + n_), wd3[:, :, n_ * 512:(n_ + 1) * 512], dvd[:, :, n_ * 512:(n_ + 1) * 512], [], [bw])
            WLd[n_] = [bw]
        sgts = [cv("sgt%d" % i, [128, 512], F32) for i in range(1)] * 2
        hb2 = [cv("hbuf%d" % i, [128, DFF], BF16) for i in range(2)]
        hT, bhT = cv("hT", [128, 22, 128], BF16)
        Tq, bTq = cv("O2_0", [128, D], F32)
        sq2 = [cv("sq2_%d" % i, [128, 4], F32) for i in range(2)]
        uT2a = [(uT, buT), cv("uT2a", [128, 8, 128], BF16)]
        ggbs2 = [(ggb, bggb), cv("ggb2", [128, D], F32)]
        fgb, bfgb = cv("fgb", [128, D], F32)
        DMA("sp", "cst", fgb[:], final_g[0:1, :].broadcast_to([128, D]), [], [bfgb])
        sgc = [0]
        seq2a = [-1]
        gsel = {}
        trb = [(ptr, bptr), (pat, bpat)]

        def ld2(tix):
            (s, row0, ntok, ti) = tiles[tix]
            xt, bxt = xs3p[tix % 3]
            DMA("sp", "xl%d" % (tix % 3), xt[0:ntok, :], x1_sc[row0:row0 + ntok, :], [x1buf.get(row0)], [bxt])

        def front2(tix):
            (s, row0, ntok, ti) = tiles[tix]
            if s != seq2a[0]:
                seq2a[0] = s
                load_mod(s, 2, gate=False)
            xt, bxt = xs3p[tix % 3]
            rms_to_uT(xt, bxt, ntok, rstd_ap=rstd2[0:ntok, tix:tix + 1], dst=uT2a[tix % 2])

        def up2(tix):
            (s, row0, ntok, ti) = tiles[tix]
            hbuf, bhbuf = hb2[tix % 2]
            for n0_ in range(0, DFF, 512):
                sgc[0] ^= 1
                sgt, bsgt = sgts[sgc[0]]
                n = min(512, DFF - n0_)
                pG, bpG = proj_tok(wg3, n0_, n, ntok, wl=WL["g%d" % (n0_ // 512)], us=uT2a[tix % 2])
                pU, bpU = proj_tok(wu3, n0_, n, ntok, wl=WL["u%d" % (n0_ // 512)], us=uT2a[tix % 2])
                ACT(sgt[0:ntok, 0:n], pG[0:ntok, 0:n], AF.Silu, [bpG], [bsgt])
                E("dve", "tensor_tensor", [bsgt, bpU], [bhbuf], out=hbuf[0:ntok, n0_:n0_ + n], in0=sgt[0:ntok, 0:n],
                  in1=pU[0:ntok, 0:n], op=ALU.mult)

        seqord = []
        for t_ in tiles:
            if t_[0] not in seqord:
                seqord.append(t_[0])

        def down2(tix):
            (s, row0, ntok, ti) = tiles[tix]
            k_ = seqord.index(s)
            gg2, bgg2 = ggbs2[k_ % 2]
            if s not in gsel:
                gsel[s] = True
                load_mod(s, 2, front=False, gdst=(gg2, bgg2))
            hbuf, bhbuf = hb2[tix % 2]
            xt, bxt = xs3p[tix % 3]
            sq_, bsq_ = sq2[tix % 2]
            for gi, g0 in enumerate(range(0, 22, 8)):
                ng = min(8, 22 - g0)
                pt_, bpt_ = trb[gi % 2]
                for j in range(ng):
                    k = g0 + j
                    TR(pt_[:, j * 128: j * 128 + ntok], hbuf[0:ntok, k * 128:(k + 1) * 128], identb[0:ntok, 0:ntok],
                       [bhbuf, bidb], [bpt_])
                ACT(hT[:, g0:g0 + ng, 0:ntok], pt_[:].rearrange("p (k t) -> p k t", k=8)[:, 0:ng, 0:ntok], AF.Copy,
                    [bpt_], [bhT])
            for n in range(2):
                pD, bpD = next_ab()
                for k in range(22):
                    MM(pD[0:ntok, :], hT[:, k, 0:ntok], wd3[:, k, n * 512:(n + 1) * 512], k == 0, k == 21, [bhT] + WLd[n], [bpD])
                E("dve", "tensor_tensor", [bpD, bgg2], [bTq], out=Tq[0:ntok, n * 512:(n + 1) * 512], in0=pD[0:ntok, :],
                  in1=gg2[0:ntok, n * 512:(n + 1) * 512], op=ALU.mult)
            E("pool", "tensor_tensor", [bTq, bxt], [bxt], out=xt[0:ntok, :], in0=Tq[0:ntok, :], in1=xt[0:ntok, :], op=ALU.add)
            ACT(Tq[0:ntok, :], xt[0:ntok, :], AF.Square, [bxt], [bTq, bsq_], accum_out=sq_[0:ntok, 0:1])
            rstd_pow(sq_[0:ntok, 2:3], sq_[0:ntok, 1:2], sq_[0:ntok, 0:1], ntok, 1, 1.0 / D, [bsq_], [bsq_])
            E("dve", "scalar_tensor_tensor", [bxt, bsq_, bfgb], [bTq], out=Tq[0:ntok, :], in0=xt[0:ntok, :],
              scalar=sq_[0:ntok, 2:3], in1=fgb[0:ntok, :], op0=ALU.mult, op1=ALU.mult)
            DMA("sp", "yo", y_all[row0:row0 + ntok, :], Tq[0:ntok, :], [bTq], [])

        P2ON = ("2a" in PH) or ("2b" in PH)
        if P2ON:
            NT_ = len(tiles)
            ld2(0)
            if NT_ > 1:
                ld2(1)
            front2(0)
            if NT_ > 1:
                front2(1)
            up2(0)
            for tix in range(NT_):
                if tix + 2 < NT_:
                    ld2(tix + 2)
                if tix + 1 < NT_:
                    up2(tix + 1)
                if tix + 2 < NT_:
                    front2(tix + 2)
                down2(tix)

        P.lower(lambda name: st.enter_context(nc.semaphore(name)))
        build_program.stats = P.stats
    return nc


_CACHE = {}


def kernel(**inp):
    f = lambda a: np.ascontiguousarray(np.asarray(a, dtype=np.float32))
    if "nc" not in _CACHE:
        _CACHE["nc"] = build_program()
    nc = _CACHE["nc"]
    x_prompt = f(inp["x_prompt"]); x_sample = f(inp["x_sample"])
    c_prompt = f(inp["c_prompt"]); c_sample = f(inp["c_sample"])
    ck = f(inp["cache_sb_k"])[0].reshape(16, PAST, 512)
    cv = f(inp["cache_sb_v"])[0].reshape(16, PAST, 512)
    sC = f(inp["state_mlstm_C"])[0]; sn = f(inp["state_mlstm_n"])[0]; sm = f(inp["state_mlstm_m"])[0]
    sconv = f(inp["state_conv"])[0]
    shared = {
        "norm1_g": f(inp["norm1_g"]).reshape(1, D), "norm2_g": f(inp["norm2_g"]).reshape(1, D),
        "w_ada": f(inp["w_ada"])[0], "b_ada": f(inp["b_ada"]).reshape(1, 6 * D),
        "w_in": f(inp["w_in"])[0], "b_if": f(inp["b_if"]).reshape(8, 1),
        "w_conv": f(inp["w_conv"])[0], "b_conv": f(inp["b_conv"]).reshape(1, D),
        "ml_norm_g": f(inp["ml_norm_g"]).reshape(1, 512),
        "w_a": f(inp["w_a"])[0], "w_b": f(inp["w_b"])[0], "w_out": f(inp["w_out"])[0],
        "w_ff_gate": f(inp["w_ff_gate"])[0], "w_ff_up": f(inp["w_ff_up"])[0], "w_ff_down": f(inp["w_ff_down"])[0],
        "final_g": f(inp["final_g"]).reshape(1, D),
    }
    in_maps = []
    for i in range(8):
        m = dict(shared)
        m["xall"] = np.concatenate([x_prompt[i], x_sample[2 * i], x_sample[2 * i + 1]], axis=0)
        m["c3"] = np.stack([c_prompt[i], c_sample[2 * i], c_sample[2 * i + 1]], axis=0)
        m["cache_k"] = ck[2 * i:2 * i + 2]
        m["cache_v"] = cv[2 * i:2 * i + 2]
        m["C0"] = sC[2 * i:2 * i + 2]
        m["n0"] = sn[2 * i:2 * i + 2]
        m["m0"] = sm[2 * i:2 * i + 2]
        m["conv0"] = sconv[2 * i:2 * i + 2]
        in_maps.append({k: np.ascontiguousarray(v) for k, v in m.items()})
    res = run_bass_kernel_spmd(nc, in_maps, core_ids=list(range(8)))
    R = res.results
    y_p = np.stack([R[i]["y_all"][:SEQ] for i in range(8)])
    y_s = np.stack([R[i]["y_all"][SEQ + j * LS: SEQ + (j + 1) * LS] for i in range(8) for j in range(2)])
    k_p = np.stack([R[i]["k_all"][:SEQ] for i in range(8)]).reshape(1, 8, SEQ, 8, 64)
    v_p = np.stack([R[i]["v_all"][:SEQ] for i in range(8)]).reshape(1, 8, SEQ, 8, 64)
    k_s = np.stack([R[i]["k_all"][SEQ + j * LS: SEQ + (j + 1) * LS] for i in range(8) for j in range(2)]).reshape(1, 16, LS, 8, 64)
    v_s = np.stack([R[i]["v_all"][SEQ + j * LS: SEQ + (j + 1) * LS] for i in range(8) for j in range(2)]).reshape(1, 16, LS, 8, 64)
    C_p = np.stack([R[i]["C_out"][0] for i in range(8)])[None]
    n_p = np.stack([R[i]["n_out"][0] for i in range(8)])[None]
    m_p = np.stack([R[i]["m_out"][0] for i in range(8)])[None]
    cv_p = np.stack([R[i]["conv_out"][0] for i in range(8)])[None]
    C_s = np.stack([R[i]["C_out"][1 + j] for i in range(8) for j in range(2)])[None]
    n_s = np.stack([R[i]["n_out"][1 + j] for i in range(8) for j in range(2)])[None]
    m_s = np.stack([R[i]["m_out"][1 + j] for i in range(8) for j in range(2)])[None]
    cv_s = np.stack([R[i]["conv_out"][1 + j] for i in range(8) for j in range(2)])[None]
    outs = (y_p, y_s, k_p, v_p, C_p, n_p, m_p, cv_p, k_s, v_s, C_s, n_s, m_s, cv_s)
    return tuple(np.ascontiguousarray(o, dtype=np.float32) for o in outs)
```

```python
import numpy as np
from contextlib import ExitStack
import concourse.bass as bass
import concourse.mybir as mybir
from concourse.bass_utils import run_bass_kernel_spmd

F32 = mybir.dt.float32
BF16 = mybir.dt.bfloat16
AF = mybir.ActivationFunctionType
ALU = mybir.AluOpType
AX = mybir.AxisListType

D = 1024
SEQ = 2048
NS = 2
LS = 64
PAST = 1024
DFF = 2816
INW = 5640
EPS = 1e-6
NROWS = SEQ + NS * LS


STRICT_SAME_ENGINE = True


class Buf:
    __slots__ = ("name", "last_w", "readers", "excl")

    def __init__(self, name, excl=False):
        self.name = name
        self.last_w = None
        self.readers = []
        self.excl = excl


class Op:
    __slots__ = ("eng", "fn", "reads", "writes", "dma", "seq", "signal", "waits",
                 "clock", "count", "idx", "attach")


class Prog:
    ENG = ("pe", "act", "dve", "pool", "sp")

    def __init__(self, nc):
        self.nc = nc
        self.ops = []
        self.e = {"pe": nc.tensor, "act": nc.scalar, "dve": nc.vector,
                  "pool": nc.gpsimd, "sp": nc.sync}

    def op(self, eng, fn, reads=(), writes=(), dma=None):
        o = Op()
        o.eng = eng
        o.fn = fn
        o.reads = [b for b in reads if b is not None and not b.excl]
        o.writes = [b for b in writes if b is not None] + [b for b in reads if b is not None and b.excl]
        o.dma = dma
        o.signal = False
        o.waits = []
        o.attach = (eng != "pe")
        o.idx = len(self.ops)
        self.ops.append(o)
        return o

    def fence(self):
        last = {}
        for o in self.ops:
            last[o.eng if o.dma is None else "d:" + o.dma] = o
        return list(last.values())

    def lower(self, sem_ctx):
        ops = self.ops
        seqc = {k: 0 for k in self.ENG}
        dmac = {}
        eclock = {k: {} for k in self.ENG}
        for o in ops:
            if o.dma is None:
                seqc[o.eng] += 1
                o.seq = seqc[o.eng]
            else:
                dmac[o.dma] = dmac.get(o.dma, 0) + 1
                o.seq = dmac[o.dma]
            deps = {}
            for b in o.reads:
                if b.last_w is not None:
                    deps[b.last_w.idx] = b.last_w
            for b in o.writes:
                if b.last_w is not None:
                    deps[b.last_w.idx] = b.last_w
                for r in b.readers:
                    deps[r.idx] = r
            clk = eclock[o.eng]
            for p in deps.values():
                if p is o:
                    continue
                if p.dma is None:
                    key = p.eng
                    need = p.seq
                    if p.eng == o.eng and o.dma is None:
                        if o.eng == "pe":
                            continue
                        if (not STRICT_SAME_ENGINE) and o.eng != "pool" and not any(b.last_w is p for b in o.reads):
                            continue
                else:
                    key = "d:" + p.dma
                    need = dmac[p.dma] if not (o.dma == p.dma) else dmac[p.dma] - 1
                if clk.get(key, 0) >= need:
                    continue
                if p.dma is None:
                    p.signal = True
                o.waits.append((p, need))
                for k2, v2 in p.clock.items():
                    if clk.get(k2, 0) < v2:
                        clk[k2] = v2
                if clk.get(key, 0) < need:
                    clk[key] = need
            myclk = dict(clk)
            mykey = o.eng if o.dma is None else "d:" + o.dma
            myclk[mykey] = max(myclk.get(mykey, 0), o.seq)
            o.clock = myclk
            for b in o.reads:
                b.readers.append(o)
            for b in o.writes:
                b.last_w = o
                b.readers = []
        cnt = {k: 0 for k in self.ENG}
        for o in ops:
            if o.dma is None and o.signal:
                cnt[o.eng] += 1
                o.count = cnt[o.eng]
        esem = {}
        dsem = {}
        dtot = {}
        n_waits = 0

        def get_e(k):
            if k not in esem:
                esem[k] = sem_ctx("e_" + k)
            return esem[k]

        def get_d(k):
            if k not in dsem:
                dsem[k] = sem_ctx("d_" + k)
            return dsem[k]

        for o in ops:
            eng = self.e[o.eng]
            need = {}
            for p, nd in o.waits:
                if p.dma is None:
                    s = get_e(p.eng)
                    v = p.count
                else:
                    s = get_d(p.dma)
                    v = 16 * nd
                k = id(s)
                if k not in need or need[k][1] < v:
                    need[k] = (s, v)
            nl = list(need.values())
            ride = None
            if o.attach and nl:
                ride = nl.pop()
            for s, v in nl:
                eng.wait_ge(s, v)
                n_waits += 1
            ins = o.fn()
            if ride is not None:
                ins._wait_ge(ride[0], eng.lower_val(ride[1]))
            if o.dma is not None:
                ins.then_inc(get_d(o.dma), 16)
                dtot[o.dma] = dtot.get(o.dma, 0) + 16
            elif o.signal:
                ins.then_inc(get_e(o.eng), 1)
        for k, s in dsem.items():
            self.e["sp"].wait_ge(s, dtot[k])
        self.stats = dict(n_ops=len(ops), n_waits=n_waits, n_dsem=len(dsem),
                          sig={k: cnt[k] for k in cnt})


CFG = {"phases": ("0", "1a", "1b", "2a", "2b"), "tiles": None}


def build_program():
    nc = bass.Bass("TRN2", target_bir_lowering=False)
    PH = CFG["phases"]

    def din(name, shape):
        return nc.dram_tensor(name, list(shape), F32, kind="ExternalInput").ap()

    def dout(name, shape):
        return nc.dram_tensor(name, list(shape), F32, kind="ExternalOutput").ap()

    xall = din("xall", [NROWS, D])
    c3 = din("c3", [3, D])
    cache_k = din("cache_k", [NS, PAST, 512])
    cache_v = din("cache_v", [NS, PAST, 512])
    C0 = din("C0", [NS, 4, 128, 128])
    n0 = din("n0", [NS, 4, 128])
    m0 = din("m0", [NS, 4])
    conv0 = din("conv0", [NS, 3, D])
    norm1_g = din("norm1_g", [1, D])
    norm2_g = din("norm2_g", [1, D])
    w_ada = din("w_ada", [D, 6 * D])
    b_ada = din("b_ada", [1, 6 * D])
    w_in = din("w_in", [D, INW])
    b_if = din("b_if", [8, 1])
    w_conv = din("w_conv", [4, D])
    b_conv = din("b_conv", [1, D])
    ml_norm_g = din("ml_norm_g", [1, 512])
    w_a = din("w_a", [512, D])
    w_b = din("w_b", [512, D])
    w_out = din("w_out", [D, D])
    w_ff_gate = din("w_ff_gate", [D, DFF])
    w_ff_up = din("w_ff_up", [D, DFF])
    w_ff_down = din("w_ff_down", [DFF, D])
    final_g = din("final_g", [1, D])

    y_all = dout("y_all", [NROWS, D])
    k_all = dout("k_all", [NROWS, 512])
    v_all = dout("v_all", [NROWS, 512])
    C_out = dout("C_out", [3, 4, 128, 128])
    n_out = dout("n_out", [3, 4, 128])
    m_out = dout("m_out", [3, 4])
    conv_out = dout("conv_out", [3, 3, D])

    mod_sc = nc.dram_tensor("mod_sc", [3, 6 * D], F32).ap()
    A_sc = nc.dram_tensor("A_sc", [NROWS, D], F32).ap()
    x1_sc = nc.dram_tensor("x1_sc", [NROWS, D], F32).ap()
    h_sc = nc.dram_tensor("h_sc", [NROWS, DFF], BF16).ap()

    tiles = [(0, t * 128, 128, t) for t in range(16)] + [(1, SEQ, 64, 0), (2, SEQ + 64, 64, 0)]
    if CFG["tiles"] is not None:
        tiles = [tiles[i] for i in CFG["tiles"]]

    with ExitStack() as st:
        P = Prog(nc)

        def sb(name, shape, dt=F32):
            return st.enter_context(nc.sbuf_tensor(name, list(shape), dt)), Buf(name)

        def ps(name, shape, dt=F32):
            return st.enter_context(nc.psum_tensor(name, list(shape), dt)), Buf(name, excl=True)

        SCRN = 20736
        SCR = st.enter_context(nc.sbuf_tensor("SCR", [128, SCRN], F32))
        scr = {"off": 0, "fence": []}

        def phase_begin():
            scr["off"] = 0
            scr["fence"] = P.fence()

        def fb(name):
            b = Buf(name)
            b.readers = list(scr["fence"])
            return b

        def cv(name, shape, dt=F32):
            n = 1
            for d_ in shape[1:]:
                n *= d_
            nf = (n + 1) // 2 if dt == BF16 else n
            nf = (nf + 7) // 8 * 8
            off = scr["off"]
            scr["off"] += nf
            assert scr["off"] <= SCRN, (name, scr["off"])
            v = SCR[0:shape[0], off:off + nf]
            if dt == BF16:
                v = v.bitcast(BF16)
            v = v[:, 0:n]
            if len(shape) == 3:
                v = v.rearrange("p (a b) -> p a b", a=shape[1])
            return v, fb(name)

        def E(eng, name, reads, writes, *a, **kw):
            m = getattr(P.e[eng], name)
            return P.op(eng, lambda: m(*a, **kw), reads, writes)

        def DMA(eng, key, out, in_, reads, writes, **kw):
            m = P.e[eng].dma_start
            return P.op(eng, lambda: m(out=out, in_=in_, **kw), reads, writes, dma=key)

        def MM(out, lhsT, rhs, start, stop, reads, writes):
            m = nc.tensor.matmul
            return P.op("pe", lambda: m(out, lhsT=lhsT, rhs=rhs, start=start, stop=stop), reads, writes)

        def TR(out, in_, ident, reads, writes):
            m = nc.tensor.transpose
            return P.op("pe", lambda: m(out=out, in_=in_, identity=ident), reads, writes)

        def ACT(out, in_, func, reads, writes, **kw):
            m = nc.scalar.activation
            o_ = P.op("act", lambda: m(out=out, in_=in_, func=func, **kw), reads, writes)
            if "accum_out" in kw:
                o_.attach = False
            return o_

        WB, bWB = sb("WB", [128, 46080], BF16)
        identb, bidb = sb("identb", [128, 128], BF16)
        identf, bidf = sb("identf", [128, 128], F32)
        onesb, bones = sb("onesb", [128, 512], BF16)
        zer, bzer = sb("zer", [128, 128], F32)
        sel, bsel = sb("sel", [4, 4, 128], F32)
        rstd2, brstd2 = sb("rstd2", [128, 18], F32)
        csT, bcsT = sb("csT", [128, 8, 3], BF16)
        mhalf, bmhalf = sb("mhalf", [128, 4], F32)
        cmask, bcmask = sb("cmask", [128, 128], BF16)
        gmb, bgmb = sb("gmb", [128, D], F32)
        shb, bshb = sb("shb", [128, D], F32)
        ggb, bggb = sb("ggb", [128, D], F32)
        xts = [sb("xt%d" % i, [128, D], F32) for i in range(2)]
        T1, bT1 = sb("T1", [128, D], F32)
        T2, bT2 = sb("T2", [128, D], F32)
        ub, bub = sb("ub", [128, D], BF16)
        uT, buT = sb("uT", [128, 8, 128], BF16)
        ssq, bssq = sb("ssq", [128, 4], F32)
        stg = []

        ptr, bptr = ps("ptr", [128, 1024], BF16)
        pat, bpat = ps("pat", [128, 1024], BF16)
        pa, bpa = ps("pa", [128, 512], F32)
        pb, bpb = ps("pb", [128, 512], F32)
        py, bpy = ps("py", [128, 512], F32)
        pm, bpm = ps("pm", [128, 512], F32)
        pz2, bpz2 = ps("pz2", [128, 1024], F32)
        bpz = [Buf("pz0", excl=True), Buf("pz1", excl=True)]
        pab = [(pa, bpa), (pb, bpb)]
        rot = {"ab": 0, "stg": 0, "x": 0}

        def next_ab():
            rot["ab"] = (rot["ab"] + 1) % len(pab)
            return pab[rot["ab"]]

        def next_stg():
            rot["stg"] = (rot["stg"] + 1) % 2
            return stg[rot["stg"]]

        E("pool", "memset", [], [bidf], identf[:], 1.0)
        E("pool", "affine_select", [bidf], [bidf], out=identf[:], in_=identf[:], pattern=[[-1, 128]],
          compare_op=ALU.is_equal, fill=0.0, base=0, channel_multiplier=1)
        E("pool", "tensor_copy", [bidf], [bidb], out=identb[:], in_=identf[:])
        E("pool", "memset", [], [bones], onesb[:], 1.0)
        E("pool", "memset", [], [bzer], zer[:], 0.0)
        E("pool", "memset", [], [bsel], sel[:], 1.0)
        E("pool", "affine_select", [bsel], [bsel], out=sel[:], in_=sel[:], pattern=[[-1, 4], [0, 128]],
          compare_op=ALU.is_equal, fill=0.0, base=0, channel_multiplier=1)
        E("pool", "memset", [], [brstd2], rstd2[:], 1.0)
        E("pool", "memset", [], [bmhalf], mhalf[:], -0.5)
        E("pool", "memset", [], [bcmask], cmask[:], -30000.0)
        E("pool", "affine_select", [bcmask], [bcmask], out=cmask[:], in_=cmask[:], pattern=[[1, 128]],
          compare_op=ALU.is_ge, fill=0.0, base=0, channel_multiplier=-1)

        def rstd_pow(out_ap, tmp_ap, ss_ap, npart, ncol, scale, rds, wrs):
            E("pool", "tensor_scalar", rds, wrs, out=tmp_ap, in0=ss_ap, scalar1=scale, scalar2=EPS, op0=ALU.mult, op1=ALU.add)
            E("pool", "tensor_tensor", wrs + [bmhalf], wrs, out=out_ap, in0=tmp_ap, in1=mhalf[0:npart, 0:ncol], op=ALU.pow)

        def load_w(dram, r0, nrows_chunks, c0, ncols, off, key):
            bufs = []
            for kc in range(nrows_chunks):
                for cc in range(0, ncols, 2048):
                    n = min(2048, ncols - cc)
                    bw = fb("W%s_%d_%d" % (key, kc, cc))
                    bufs.append(bw)
                    DMA("pool", "W" + key, WB[:, off + kc * ncols + cc: off + kc * ncols + cc + n],
                        dram[r0 + kc * 128: r0 + (kc + 1) * 128, c0 + cc: c0 + cc + n], [], [bw])
            return bufs

        def load_wg(dram, nrows_chunks, c0, groups, stride, off, kp, kbase=0):
            out = {}
            dv = dram.rearrange("(k p) n -> p k n", p=128)
            wv = WB[:, off: off + nrows_chunks * stride].rearrange("p (k c) -> p k c", k=nrows_chunks)
            for gi, (nm, l0, ncols) in enumerate(groups):
                bufs = []
                for cc in range(0, ncols, 2048):
                    n = min(2048, ncols - cc)
                    bw = fb("W%s_%s_%d" % (kp, nm, cc))
                    bufs.append(bw)
                    DMA("pool", "Wg%d" % (kbase + gi), wv[:, :, l0 + cc: l0 + cc + n],
                        dv[:, :, c0 + l0 + cc: c0 + l0 + cc + n], [], [bw])
                out[nm] = bufs
            return out

        def wview(off, nk, ncols):
            return WB[:, off: off + nk * ncols].rearrange("p (k c) -> p k c", k=nk)

        phase_begin()
        stg[:] = [cv("stg%d" % i, [128, 512], F32) for i in range(2)]
        cT, bcT = cv("cT", [128, 8, 3], F32)
        for s_ in range(3):
            DMA("sp", "cst", cT[:, :, s_], c3[s_].rearrange("(k p) -> p k", p=128), [], [bcT],
                allow_slow_non_contiguous=True)
        ACT(csT[:], cT[:], AF.Silu, [bcT], [bcsT])
        WAs = [cv("WA%d" % i, [128, 8, 512], BF16) for i in range(4)]
        w_ada_v = w_ada.rearrange("(k p) n -> p k n", p=128)
        bmods = {"A": Buf("modA"), "B": Buf("modB"), "C": Buf("modC")}

        def mod_group(nch):
            return "A" if nch < 4 else ("B" if nch < 6 else "C")

        def mod_chunk_load(nch, WA, bWA, key):
            DMA("pool", key, WA[:], w_ada_v[:, :, nch * 512:(nch + 1) * 512], [], [bWA])

        def mod_chunk_compute(nch, WA, bWA, pp, bpp):
            sg, bsg = next_stg()
            DMA("sp", "bad", sg[0:3, :], b_ada[0:1, nch * 512:(nch + 1) * 512].broadcast_to([3, 512]), [], [bsg])
            for kc in range(8):
                MM(pp[0:3, :], csT[:, kc, :], WA[:, kc, :], kc == 0, kc == 7, [bcsT, bWA], [bpp])
            E("dve", "tensor_tensor", [bpp, bsg], [bsg], out=sg[0:3, :], in0=pp[0:3, :], in1=sg[0:3, :], op=ALU.add)
            DMA("sp", "modw", mod_sc[:, nch * 512:(nch + 1) * 512], sg[0:3, :], [bsg], [bmods[mod_group(nch)]])

        for nch in range(4 if "0" in PH else 0):
            WA, bWA = WAs[nch]
            mod_chunk_load(nch, WA, bWA, "wa%d" % nch)
        sv_f = scr["fence"]
        scr["fence"] = []
        WL_1a = load_wg(w_in, 8, 0, [("q", 0, 512), ("k", 512, 512), ("v", 1024, 512)], 1536, 0, "a")
        WL_1a["a"] = load_wg(w_a, 4, 0, [("a", 0, 1024)], 1024, 12288, "wa", kbase=3)["a"]
        scr["fence"] = sv_f
        for nch in range(4 if "0" in PH else 0):
            WA, bWA = WAs[nch]
            mod_chunk_compute(nch, WA, bWA, pm, bpm)

        def load_mod(s, which, front=True, gate=True, gdst=None):
            base = 0 if which == 1 else 3 * D
            ng = norm1_g if which == 1 else norm2_g
            bf_ = bmods["A"] if which == 1 else bmods["C"]
            bg_ = bmods["B"] if which == 1 else bmods["C"]
            if front:
                DMA("sp", "modr", shb[:], mod_sc[s:s + 1, base:base + D].broadcast_to([128, D]), [bf_], [bshb])
                DMA("sp", "modr", gmb[:], mod_sc[s:s + 1, base + D:base + 2 * D].broadcast_to([128, D]), [bf_], [bgmb])
                DMA("sp", "modr", T2[:], ng[0:1, :].broadcast_to([128, D]), [], [bT2])
                E("dve", "scalar_tensor_tensor", [bgmb, bT2], [bgmb], out=gmb[:], in0=gmb[:], scalar=1.0, in1=T2[:],
                  op0=ALU.add, op1=ALU.mult)
            if gate:
                gd, bgd = gdst if gdst is not None else (ggb, bggb)
                DMA("sp", "modg", gd[:], mod_sc[s:s + 1, base + 2 * D:base + 3 * D].broadcast_to([128, D]), [bg_], [bgd])

        def rms_to_uT(xt, bxt, ntok, rstd_ap=None, dst=None):
            if rstd_ap is None:
                ACT(T2[0:ntok, :], xt[0:ntok, :], AF.Square, [bxt], [bT2, bssq], accum_out=ssq[0:ntok, 0:1])
                rstd_pow(ssq[0:ntok, 2:3], ssq[0:ntok, 1:2], ssq[0:ntok, 0:1], ntok, 1, 1.0 / D, [bssq], [bssq])
                rstd_ap = ssq[0:ntok, 2:3]
                rb = bssq
            else:
                rb = brstd2
            E("dve", "scalar_tensor_tensor", [bxt, rb, bgmb], [bT1], out=T1[0:ntok, :], in0=xt[0:ntok, :],
              scalar=rstd_ap, in1=gmb[0:ntok, :], op0=ALU.mult, op1=ALU.mult)
            E("pool", "tensor_tensor", [bT1, bshb], [bub], out=ub[0:ntok, :], in0=T1[0:ntok, :], in1=shb[0:ntok, :],
              op=ALU.add)
            for kc in range(8):
                TR(ptr[:, kc * 128: kc * 128 + ntok], ub[0:ntok, kc * 128:(kc + 1) * 128], identb[0:ntok, 0:ntok],
                   [bub, bidb], [bptr])
            uTd, buTd = dst if dst is not None else (uT, buT)
            ACT(uTd[:, :, 0:ntok], ptr[:].rearrange("p (k t) -> p k t", k=8)[:, :, 0:ntok], AF.Copy, [bptr], [buTd])

        epst, bepst = sb("epst", [128, 1], F32)
        E("pool", "memset", [], [bepst], epst[:], EPS)
        EPS_AP = epst

        def load_x(src, row0, ntok):
            rot["x"] ^= 1
            xt, bxt = xts[rot["x"]]
            DMA("sp", "xl%d" % rot["x"], xt[0:ntok, :], src[row0:row0 + ntok, :], [], [bxt])
            return xt, bxt

        def proj_tok(w3, c0, n, ntok, t0=0, wl=(), us=None):
            pp, bpp = next_ab()
            for kc in range(8):
                uTs_, buTs_ = us if us is not None else (uT, buT)
                MM(pp[0:ntok, 0:n], uTs_[:, kc, t0:t0 + ntok], w3[:, kc, c0:c0 + n], kc == 0, kc == 7, [buTs_] + list(wl), [bpp])
            return pp, bpp

        phase_begin()
        WL = WL_1a
        WLb_pre = load_wg(w_b, 4, 0, [("b", 0, 1024)], 1024, 32832, "wb", kbase=6)["b"]
        WLout_pre = load_wg(w_out, 8, 0, [("o", 0, 1024)], 1024, 36928, "wo", kbase=7)["o"]
        win_a = wview(0, 8, 1536)
        wa3 = wview(12288, 4, 1024)
        KTm = WB[:, 16384:24576].rearrange("p (c k) -> p c k", c=4)
        Vm = WB[:, 24576:32768].rearrange("p (t c) -> p t c", t=16)
        stor_main = dict(KT=KTm, Vst=Vm, bKT=[fb("KT%d" % i) for i in range(16)], bV=[fb("V%d" % i) for i in range(16)])
        KTa, _ = cv("KTalt", [128, 4, 1152], BF16)
        Va, _ = cv("Valt", [128, 9, 512], BF16)
        stor_alt = dict(KT=KTa, Vst=Va, bKT=[fb("KTa%d" % i) for i in range(9)], bV=[fb("Va%d" % i) for i in range(9)])
        stg[:] = [cv("stg%d" % i, [128, 512], F32) for i in range(2)]
        qTzs = [cv("qTz%d" % i, [128, 8, 128], BF16) for i in range(2)]
        for qz, bqz in qTzs:
            E("pool", "memset", [], [bqz], qz[:], 0.0)
        Ktok, bKtok = cv("Ktok", [128, 8, 512], BF16)
        NSL = 5
        att = [dict(g=cv("ag%d" % i, [128, 520], F32), Pb=cv("aP%d" % i, [128, 520], F32),
                    a=cv("aa%d" % i, [128, 512], BF16),
                    aT=cv("aaT%d" % i, [128, 4, 128], BF16)) for i in range(NSL)]
        ONE_REG = nc.gpsimd.to_reg(1.0)
        zerob, bzerob = cv("zerob", [128, 520], BF16)
        E("pool", "memset", [], [bzerob], zerob[:], 0.0)
        for A_ in att:
            E("pool", "memset", [], [A_["g"][1]], A_["g"][0][:], 1.0)
        ya, bya = cv("ya", [128, 512], BF16)
        yaT, byaT = cv("yaT", [128, 4, 128], BF16)
        Ast, bAst = cv("Ast", [128, D], F32)
        WAbg, bWAbg = cv("WAbg", [128, 8, 512], BF16)
        pzv = [pz2[:, 0:512], pz2[:, 512:1024]]
        patv2 = [(pat, bpat), (pm[:].bitcast(BF16), bpm)]
        job_ctr = [0]
        Abuf = {}
        x1buf = {}
        seq_loaded = [-1]

        def prologue_pieces(tile, tno, stor, first_of_seq):
            (s, row0, ntok, ti) = tile
            kpos0 = ti * 128 if s == 0 else PAST
            ktile = ti if s == 0 else 8
            qTz, bqTz = qTzs[tno % 2]
            KT, Vst, bKT, bV = stor["KT"], stor["Vst"], stor["bKT"], stor["bV"]
            cx = dict(s=s, row0=row0, ntok=ntok, kpos0=kpos0, ktile=ktile, qTz=qTz, bqTz=bqTz, stor=stor)
            hold = {}

            xslot = tno % 2
            xt, bxt = xts[xslot]
            hold = {}

            def PL():
                if first_of_seq:
                    load_mod(s, 1, gate=False)
                    if s > 0:
                        si = s - 1
                        DMA("pool", "kvc", Vst[:, 0:8, :], cache_v[si].rearrange("(k p) c -> p k c", p=128), [], bV[0:8])
                        DMA("pool", "kvc", Ktok[:], cache_k[si].rearrange("(k p) c -> p k c", p=128), [], [bKtok])
                DMA("sp", "xl%d" % xslot, xt[0:ntok, :], xall[row0:row0 + ntok, :], [], [bxt])

            def PK():
                for kt in range(8):
                    for c in range(4):
                        TR(ptr[:, c * 128:(c + 1) * 128], Ktok[:, kt, c * 128:(c + 1) * 128], identb[:],
                           [bKtok, bidb], [bptr])
                    ACT(KT[:, :, kt * 128:(kt + 1) * 128], ptr[:, 0:512].rearrange("p (c t) -> p c t", c=4), AF.Copy,
                        [bptr], [bKT[kt]])

            def PA():
                ACT(T2[0:ntok, :], xt[0:ntok, :], AF.Square, [bxt], [bT2, bssq], accum_out=ssq[0:ntok, 0:1])
                rstd_pow(ssq[0:ntok, 2:3], ssq[0:ntok, 1:2], ssq[0:ntok, 0:1], ntok, 1, 1.0 / D, [bssq], [bssq])

            def PB():
                E("dve", "scalar_tensor_tensor", [bxt, bssq, bgmb], [bT1], out=T1[0:ntok, :], in0=xt[0:ntok, :],
                  scalar=ssq[0:ntok, 2:3], in1=gmb[0:ntok, :], op0=ALU.mult, op1=ALU.mult)

            def PC():
                E("pool", "tensor_tensor", [bT1, bshb], [bub], out=ub[0:ntok, :], in0=T1[0:ntok, :], in1=shb[0:ntok, :],
                  op=ALU.add)

            def PD():
                for kc in range(8):
                    TR(ptr[:, kc * 128: kc * 128 + ntok], ub[0:ntok, kc * 128:(kc + 1) * 128], identb[0:ntok, 0:ntok],
                       [bub, bidb], [bptr])
                ACT(uT[:, :, 0:ntok], ptr[:].rearrange("p (k t) -> p k t", k=8)[:, :, 0:ntok], AF.Copy, [bptr], [buT])

            def Q1():
                pp, bpp = next_ab()
                hold["q"] = (pp, bpp)
                for c in range(4):
                    for kc in range(8):
                        MM(pp[:, c * 128: c * 128 + ntok], win_a[:, kc, c * 128:(c + 1) * 128], uT[:, kc, 0:ntok],
                           kc == 0, kc == 7, [buT] + WL["q"], [bpp])
                ppv = pp[:].rearrange("p (c t) -> p c t", c=4)
                qv = qTz[:].rearrange("p (c two) t -> p c two t", two=2)
                ACT(qv[0:64, :, 0, 0:ntok], ppv[0:64, :, 0:ntok], AF.Copy, [bpp], [bqTz])
                E("dve", "tensor_copy", [bpp], [bqTz], out=qv[64:128, :, 1, 0:ntok], in_=ppv[64:128, :, 0:ntok])

            def K1():
                pp, bpp = next_ab()
                for c in range(4):
                    for kc in range(8):
                        MM(pp[:, c * 128: c * 128 + ntok], win_a[:, kc, 512 + c * 128: 512 + (c + 1) * 128], uT[:, kc, 0:ntok],
                           kc == 0, kc == 7, [buT] + WL["k"], [bpp])
                ACT(KT[:, :, kpos0:kpos0 + ntok], pp[:].rearrange("p (c t) -> p c t", c=4)[:, :, 0:ntok], AF.Copy,
                    [bpp], [bKT[ktile]])

            def K2():
                pp, bpp = proj_tok(win_a, 512, 512, ntok, wl=WL["k"])
                sg, bsg = next_stg()
                E("dve", "tensor_copy", [bpp], [bsg], out=sg[0:ntok, :], in_=pp[0:ntok, :])
                DMA("sp", "ko", k_all[row0:row0 + ntok, :], sg[0:ntok, :], [bsg], [])

            def V1():
                pp, bpp = proj_tok(win_a, 1024, 512, ntok, wl=WL["v"])
                sg, bsg = next_stg()
                E("dve", "tensor_copy", [bpp], [bsg], out=sg[0:ntok, :], in_=pp[0:ntok, :])
                ACT(Vst[0:ntok, ktile, :], pp[0:ntok, :], AF.Copy, [bpp], [bV[ktile]])
                DMA("sp", "vo", v_all[row0:row0 + ntok, :], sg[0:ntok, :], [bsg], [])

            pk = PK if (first_of_seq and s > 0) else None
            return cx, [PL, None, pk, PA, PB, PC, None, PD, None, Q1, None, K1, None, K2, None, V1]

        def attention_jobs(cx):
            ntok, kpos0, qTz, bqTz = cx["ntok"], cx["kpos0"], cx["qTz"], cx["bqTz"]
            KT, Vst, bKT, bV = cx["stor"]["KT"], cx["stor"]["Vst"], cx["stor"]["bKT"], cx["stor"]["bV"]
            nk = kpos0 + ntok
            nblk = (nk + 511) // 512
            jobs = []
            for h in range(8):
                prev = None
                for b in range(nblk - 1, -1, -1):
                    job_ctr[0] += 1
                    J = dict(h=h, b=b, slot=job_ctr[0] % NSL, zi=job_ctr[0] % 2, prev=prev,
                             first=(b == nblk - 1), last=(b == 0))
                    jobs.append(J)
                    prev = J

            def geo(J):
                kb0 = J["b"] * 512
                nkb = min(512, nk - kb0)
                kts = list(range(kb0 // 128, (kb0 + nkb + 127) // 128))
                return kb0, nkb, kts

            def S0(J):
                kb0, nkb, kts = geo(J)
                MM(pzv[J["zi"]][0:ntok, 0:nkb], qTz[:, J["h"], 0:ntok], KT[:, J["h"] // 2, kb0:kb0 + nkb], True, not J["first"],
                   [bqTz] + [bKT[k] for k in kts], [bpz[J["zi"]]])
                if J["first"]:
                    MM(pzv[J["zi"]][0:ntok, nkb - ntok:nkb], identb[0:ntok, 0:ntok], cmask[0:ntok, 0:ntok], False, True,
                       [bidb, bcmask], [bpz[J["zi"]]])

            def S1(J):
                kb0, nkb, kts = geo(J)
                g_, bg = att[J["slot"]]["g"]
                pz = pzv[J["zi"]]
                ACT(g_[0:ntok, 512 - nkb:512], pz[0:ntok, 0:nkb], AF.Sigmoid, [bpz[J["zi"]]], [bg], scale=-0.125)

            def S2(J):
                kb0, nkb, kts = geo(J)
                A_ = att[J["slot"]]
                (g_, bg), (Pb, bP) = A_["g"], A_["Pb"]
                if J["prev"] is None:
                    init = 1.0
                    rd = [bg, bzerob]
                else:
                    pPb, bpP = att[J["prev"]["slot"]]["Pb"]
                    pk = geo(J["prev"])[1]
                    init = pPb[0:ntok, 512 - pk:512 - pk + 1]
                    rd = [bg, bzerob, bpP]
                E("dve", "tensor_tensor_scan", rd, [bP], out=Pb[0:ntok, 512 - nkb:513][:, ::-1],
                  data0=g_[0:ntok, 512 - nkb:513][:, ::-1], data1=zerob[0:ntok, 0:nkb + 1], initial=init,
                  op0=ALU.mult, op1=ALU.add)

            def S3(J):
                pass

            def S4(J):
                kb0, nkb, kts = geo(J)
                A_ = att[J["slot"]]
                (Pb, bP), (a_, ba) = A_["Pb"], A_["a"]
                E("pool", "tensor_tensor", [bP], [ba], out=a_[0:ntok, 0:nkb], in0=Pb[0:ntok, 512 - nkb + 1:513],
                  in1=Pb[0:ntok, 512 - nkb:512], op=ALU.subtract)

            def S5(J):
                kb0, nkb, kts = geo(J)
                a_, ba = att[J["slot"]]["a"]
                pT, bpT = patv2[J["zi"]]
                for j, kt in enumerate(kts):
                    ksz = min(128, nk - kt * 128)
                    TR(pT[0:ksz, j * 128: j * 128 + ntok], a_[0:ntok, j * 128: j * 128 + ksz],
                       identb[0:ntok, 0:ntok], [ba, bidb], [bpT])

            def S6(J):
                kb0, nkb, kts = geo(J)
                aT_, baT = att[J["slot"]]["aT"]
                pT, bpT = patv2[J["zi"]]
                nsub = len(kts)
                pv = pT[:, 0:512].rearrange("p (j t) -> p j t", j=4)
                lastk = min(128, nk - kts[-1] * 128)
                if lastk == 128:
                    ACT(aT_[:, 0:nsub, 0:ntok], pv[:, 0:nsub, 0:ntok], AF.Copy, [bpT], [baT])
                else:
                    if nsub > 1:
                        ACT(aT_[:, 0:nsub - 1, 0:ntok], pv[:, 0:nsub - 1, 0:ntok], AF.Copy, [bpT], [baT])
                    ACT(aT_[0:lastk, nsub - 1, 0:ntok], pv[0:lastk, nsub - 1, 0:ntok], AF.Copy, [bpT], [baT])

            def S7(J):
                kb0, nkb, kts = geo(J)
                aT_, baT = att[J["slot"]]["aT"]
                h = J["h"]
                nsub = len(kts)
                for j, kt in enumerate(kts):
                    ksz = min(128, nk - kt * 128)
                    MM(py[0:ntok, h * 64:(h + 1) * 64], aT_[0:ksz, j, 0:ntok], Vst[0:ksz, kt, h * 64:(h + 1) * 64],
                       J["first"] and j == 0, J["last"] and j == nsub - 1, [baT, bV[kt]], [bpy])

            return jobs, [S0, S1, S2, S3, S4, S5, S6, S7]

        def epilogue_pieces(cx):
            ntok, row0 = cx["ntok"], cx["row0"]

            def E0():
                ACT(ya[0:ntok, :], py[0:ntok, :], AF.Copy, [bpy], [bya])

            def E1():
                for c in range(4):
                    TR(ptr[:, c * 128: c * 128 + ntok], ya[0:ntok, c * 128:(c + 1) * 128], identb[0:ntok, 0:ntok],
                       [bya, bidb], [bptr])
                ACT(yaT[:, :, 0:ntok], ptr[:, 0:512].rearrange("p (c t) -> p c t", c=4)[:, :, 0:ntok], AF.Copy, [bptr], [byaT])

            def E2():
                for n in range(2):
                    pp, bpp = next_ab()
                    for c in range(4):
                        MM(pp[0:ntok, :], yaT[:, c, 0:ntok], wa3[:, c, n * 512:(n + 1) * 512], c == 0, c == 3,
                           [byaT] + WL["a"], [bpp])
                    E("dve", "tensor_copy", [bpp], [bAst], out=Ast[0:ntok, n * 512:(n + 1) * 512], in_=pp[0:ntok, :])
                bAsc = Buf("Asc%d" % row0)
                DMA("sp", "Aw", A_sc[row0:row0 + ntok, :], Ast[0:ntok, :], [bAst], [bAsc])
                Abuf[row0] = bAsc

            return [E0, E1, E2]

        if "1a" in PH:
            items = []
            jbase = 0
            starts = []
            njs = []
            seq_idx = -1
            prev_s = None
            for li_, tile in enumerate(tiles):
                first_of_seq = (tile[0] != prev_s)
                if first_of_seq:
                    seq_idx += 1
                    prev_s = tile[0]
                stor = stor_main if seq_idx % 2 == 0 else stor_alt
                cx, pieces = prologue_pieces(tile, li_, stor, first_of_seq)
                if li_ == 0:
                    for i_, pf in enumerate(pieces):
                        if pf is not None:
                            items.append((-100 + i_, 9.0, pf))
                else:
                    pst = starts[li_ - 1] + 1
                    if first_of_seq:
                        pst = max(pst, stor.get("last_step", -1) + 1)
                    pend = starts[li_ - 1] + njs[li_ - 1] - 1
                    avail = max(1, pend - pst)
                    L_ = len(pieces)
                    for i_, pf in enumerate(pieces):
                        if pf is not None:
                            items.append((pst + (i_ * avail) // L_, 9.0 + i_ * 0.01, pf))
                jobs, stages = attention_jobs(cx)
                starts.append(jbase)
                njs.append(len(jobs))
                NSTG = len(stages)
                for ji, J in enumerate(jobs):
                    for k, Sf in enumerate(stages):
                        items.append((jbase + ji + k, float(NSTG - 1 - k), (lambda Sf=Sf, J=J: Sf(J))))
                last_step = jbase + len(jobs) - 1 + (NSTG - 1)
                stor["last_step"] = last_step
                ep = epilogue_pieces(cx)
                items.append((last_step, 0.5, ep[0]))
                items.append((last_step + 2, 8.5, ep[1]))
                items.append((last_step + 4, 8.6, ep[2]))
                jbase += len(jobs)
            if "0" in PH:
                nsteps = jbase + 8
                gap = max(14, (nsteps - 30) // 8)
                for bi, nch in enumerate(range(4, 12)):
                    t_ = 10 + bi * gap
                    items.append((t_, 9.5, (lambda nch=nch: mod_chunk_load(nch, WAbg, bWAbg, "wabg"))))

                    def comp(nch=nch):
                        pp, bpp = next_ab()
                        mod_chunk_compute(nch, WAbg, bWAbg, pp, bpp)
                    items.append((t_ + 10, 9.6, comp))
            order = sorted(range(len(items)), key=lambda i: (items[i][0], items[i][1], i))
            for i in order:
                items[i][2]()

        phase_begin()
        NB = 4104
        WL = load_wg(w_in, 8, 1536, [("mqk", 0, 1024), ("gt", 2048, 8), ("mv", 1024, 512), ("mo", 1536, 512),
                                     ("gb", 3080, 1024), ("ga", 2056, 1024)], NB, 0, "b")
        WL["b"] = WLb_pre
        WL["out"] = WLout_pre
        winb = wview(0, 8, NB)
        wb3 = wview(32832, 4, 1024)
        wo3 = wview(36928, 8, 1024)
        wcv, bwcv = cv("wcv", [128, 8, 4], F32)
        bcv, bbcv = cv("bcv", [128, 8], F32)
        mlg, bmlg = cv("mlg", [64, 512], F32)
        bifi, bbifi = cv("bifi", [4, 1], F32)
        biff, bbiff = cv("biff", [4, 1], F32)
        for j_ in range(4):
            DMA("sp", "cst", wcv[:, :, j_], w_conv[j_].rearrange("(c p) -> p c", p=128), [], [bwcv], allow_slow_non_contiguous=True)
        DMA("sp", "cst", bcv[:], b_conv[0].rearrange("(c p) -> p c", p=128), [], [bbcv], allow_slow_non_contiguous=True)
        DMA("sp", "cst", mlg[:], ml_norm_g[0:1, :].broadcast_to([64, 512]), [], [bmlg])
        DMA("sp", "cst", bifi[:], b_if[0:4, :], [], [bbifi])
        DMA("sp", "cst", biff[:], b_if[4:8, :], [], [bbiff])
        SC_ = 5
        CS_ = 2
        xs3 = [xts[0], xts[1], cv("xt2", [128, D], F32)]
        uT3 = [(uT, buT), cv("uT1", [128, 8, 128], BF16), cv("uT2", [128, 8, 128], BF16)]
        mqk2 = [cv("mqkT%d" % i, [128, 8, 128], BF16) for i in range(2)]
        xp2 = [cv("xp%d" % i, [128, 8, 131], F32) for i in range(2)]
        gl2 = [dict(li=cv("gli%d" % i, [4, 128], F32), sf=cv("gsf%d" % i, [4, 128], F32),
                    lf=cv("glf%d" % i, [4, 128], F32)) for i in range(2)]
        ybT2 = [cv("ybT%d" % i, [128, 4, 128], BF16) for i in range(2)]
        At, bAt = cv("At", [128, D], F32)
        sgt, bsgt = cv("sgt", [128, 512], F32)
        mg, bmg = cv("mg", [128, D], BF16)
        mT, bmT = cv("mT", [128, 8, 128], BF16)
        CTs = [cv("CT%d" % i, [128, 4, 129], F32) for i in range(3)]
        ggbs = [(ggb, bggb), cv("ggb1", [128, D], F32)]
        C0t, bC0t = At[:, 0:512].rearrange("p (h k) -> p h k", h=4), bAt
        mseq, bmseq = cv("mseq", [4, 40], F32)
        CK = []
        for i in range(CS_):
            d_ = {}
            for nm in ("bb", "rr", "mmt", "wg", "wi", "emt"):
                d_[nm] = cv("g_%s%d" % (nm, i), [4, 64], F32)
            d_["dec"] = cv("g_dec%d" % i, [4, 128], F32)
            d_["gsm"] = cv("gsm%d" % i, [4, 8], F32)
            d_["gtok"] = cv("gtok%d" % i, [128, 16], F32)
            d_["Wt"] = cv("Wt%d" % i, [64, 4, 64], F32)
            d_["vaug"] = cv("vaug%d" % i, [64, 4, 129], BF16)
            d_["vw"] = cv("vw%d" % i, [64, 4, 129], BF16)
            d_["smo"] = cv("smo%d" % i, [64, 512], F32)
            d_["ktok"] = cv("ktok%d" % i, [64, 4, 128], BF16)
            d_["ST"] = cv("ST%d" % i, [64, 4, 64], BF16)
            d_["qsT"] = cv("qsT%d" % i, [128, 4, 64], BF16)
            d_["CTb"] = cv("CTb%d" % i, [128, 4, 129], BF16)
            d_["hn"] = cv("hn%d" % i, [64, 4, 129], F32)
            d_["hs"] = cv("hs%d" % i, [64, 16], F32)
            d_["T1h"] = cv("T1h%d" % i, [64, 4, 128], F32)
            d_["ybt"] = cv("ybt%d" % i, [64, 512], BF16)
            E("pool", "memset", [], [d_["vaug"][1]], d_["vaug"][0][:], 1.0)
            CK.append(d_)
        patf = pat[:, 512:1024].bitcast(F32)
        chunk_ctr = [0]
        gchunk = [0]

        def out_state(s, CT, bCT, mcol_final):
            def f():
                for h in range(4):
                    TR(pa[:, h * 128:(h + 1) * 128], CT[:, h, 0:128], identf[:], [bCT, bidf], [bpa])
                E("dve", "tensor_copy", [bpa], [bC0t], out=C0t, in_=pa[:].rearrange("p (h k) -> p h k", h=4))
                DMA("sp", "sto", C_out[s].rearrange("h v k -> v h k"), C0t, [bC0t], [])
                DMA("sp", "sto", n_out[s].rearrange("h k -> k h"), CT[:, :, 128], [bCT], [], allow_slow_non_contiguous=True)
                DMA("sp", "sto", m_out[s:s + 1, :].rearrange("o h -> h o"), mseq[:, mcol_final:mcol_final + 1],
                    [bmseq], [], allow_slow_non_contiguous=True)
            return f

        def out_conv(s, xp_last):
            def f():
                xpl, bxpl, nt_l = xp_last
                for j_ in range(3):
                    DMA("sp", "sto", conv_out[s, j_].rearrange("(c p) -> p c", p=128), xpl[:, :, nt_l + j_], [bxpl], [],
                        allow_slow_non_contiguous=True)
            return f

        def sched_tile(items, lt, tix, tile, prev_xp, sq):
            (s, row0, ntok, ti) = tile
            xt, bxt = xs3[lt % 3]
            uTs, buTs = uT3[lt % 3]
            mqkT, bmqk = mqk2[lt % 2]
            xp, bxp = xp2[lt % 2]
            G_ = gl2[lt % 2]
            (li, bli), (sf, bsf), (lf, blf) = G_["li"], G_["sf"], G_["lf"]
            ybT, bybT = ybT2[lt % 2]
            nch = ntok // 64
            base = 2 * lt * SC_
            CT, bCT = sq["CT"]
            gg_, bgg_ = sq["gg"]
            T1v = T1[:].rearrange("p (c t) -> p c t", c=8)
            T2v = T2[:].rearrange("p (c t) -> p c t", c=8)

            def F0():
                DMA("sp", "xl%d" % (lt % 3), xt[0:ntok, :], xall[row0:row0 + ntok, :], [], [bxt])
                ACT(T2[0:ntok, :], xt[0:ntok, :], AF.Square, [bxt], [bT2, bssq], accum_out=ssq[0:ntok, 0:1])
                rstd_pow(ssq[0:ntok, 2:3], ssq[0:ntok, 1:2], ssq[0:ntok, 0:1], ntok, 1, 1.0 / D, [bssq], [bssq])
                E("dve", "scalar_tensor_tensor", [bxt, bssq, bgmb], [bT1], out=T1[0:ntok, :], in0=xt[0:ntok, :],
                  scalar=ssq[0:ntok, 2:3], in1=gmb[0:ntok, :], op0=ALU.mult, op1=ALU.mult)
                E("pool", "tensor_tensor", [bT1, bshb], [bub], out=ub[0:ntok, :], in0=T1[0:ntok, :], in1=shb[0:ntok, :],
                  op=ALU.add)

            def F1():
                for kc in range(8):
                    TR(ptr[:, kc * 128: kc * 128 + ntok], ub[0:ntok, kc * 128:(kc + 1) * 128], identb[0:ntok, 0:ntok],
                       [bub, bidb], [bptr])
                ACT(uTs[:, :, 0:ntok], ptr[:].rearrange("p (k t) -> p k t", k=8)[:, :, 0:ntok], AF.Copy, [bptr], [buTs])

            def F2(half):
                def f():
                    if half == 0 and prev_xp is not None:
                        pxp, bpxp, pnt = prev_xp
                        E("pool", "tensor_copy", [bpxp], [bxp], out=xp[:, :, 0:3], in_=pxp[:, :, pnt:pnt + 3])
                    pp, bpp = next_ab()
                    for c4 in range(4):
                        ch = half * 4 + c4
                        for kc in range(8):
                            MM(pp[:, c4 * 128: c4 * 128 + ntok], winb[:, kc, ch * 128:(ch + 1) * 128], uTs[:, kc, 0:ntok],
                               kc == 0, kc == 7, [buTs] + WL["mqk"], [bpp])
                    ACT(xp[:, half * 4:(half + 1) * 4, 3:3 + ntok], pp[:].rearrange("p (c t) -> p c t", c=4)[:, :, 0:ntok],
                        AF.Copy, [bpp], [bxp])
                return f

            def F3(p):
                def f():
                    for ch in (2 * p, 2 * p + 1):
                        E("dve", "tensor_scalar", [bxp, bwcv, bbcv], [bT1], out=T1v[:, ch, 0:ntok], in0=xp[:, ch, 0:ntok],
                          scalar1=wcv[:, ch, 0:1], scalar2=bcv[:, ch:ch + 1], op0=ALU.mult, op1=ALU.add)
                        for j in range(1, 4):
                            E("dve", "scalar_tensor_tensor", [bxp, bwcv, bT1], [bT1], out=T1v[:, ch, 0:ntok],
                              in0=xp[:, ch, j:j + ntok], scalar=wcv[:, ch, j:j + 1], in1=T1v[:, ch, 0:ntok],
                              op0=ALU.mult, op1=ALU.add)

                def fpool():
                    tmpf = ub[:, 0:256].bitcast(F32)
                    for ch in (2 * p, 2 * p + 1):
                        E("pool", "tensor_scalar", [bxp, bwcv, bbcv], [bT1], out=T1v[:, ch, 0:ntok], in0=xp[:, ch, 0:ntok],
                          scalar1=wcv[:, ch, 0:1], scalar2=bcv[:, ch:ch + 1], op0=ALU.mult, op1=ALU.add)
                        for j in range(1, 4):
                            E("pool", "tensor_scalar", [bxp, bwcv], [bub], out=tmpf[:, 0:ntok], in0=xp[:, ch, j:j + ntok],
                              scalar1=wcv[:, ch, j:j + 1], scalar2=0.0, op0=ALU.mult, op1=ALU.add)
                            E("pool", "tensor_tensor", [bub, bT1], [bT1], out=T1v[:, ch, 0:ntok], in0=tmpf[:, 0:ntok],
                              in1=T1v[:, ch, 0:ntok], op=ALU.add)
                return fpool if p >= 2 else f

            def F4():
                ACT(T2v[:, :, 0:ntok], T1v[:, :, 0:ntok], AF.Sigmoid, [bT1], [bT2])
                E("dve", "tensor_tensor", [bT1, bT2], [bmqk], out=mqkT[:, 0:4, 0:ntok], in0=T1v[:, 0:4, 0:ntok],
                  in1=T2v[:, 0:4, 0:ntok], op=ALU.mult)
                E("dve", "scalar_tensor_tensor", [bT1, bT2], [bmqk], out=mqkT[:, 4:8, 0:ntok], in0=T1v[:, 4:8, 0:ntok],
                  scalar=float(1.0 / np.sqrt(128.0)), in1=T2v[:, 4:8, 0:ntok], op0=ALU.mult, op1=ALU.mult)

            def F5():
                for kc in range(8):
                    MM(pm[0:4, 0:ntok], winb[:, kc, 2048:2052], uTs[:, kc, 0:ntok], kc == 0, kc == 7, [buTs] + WL["gt"], [bpm])
                ACT(li[:, 0:ntok], pm[0:4, 0:ntok], AF.Identity, [bpm, bbifi], [bli], bias=bifi[:])
                for kc in range(8):
                    MM(pm[0:4, 128:128 + ntok], winb[:, kc, 2052:2056], uTs[:, kc, 0:ntok], kc == 0, kc == 7,
                       [buTs] + WL["gt"], [bpm])
                ACT(sf[:, 0:ntok], pm[0:4, 128:128 + ntok], AF.Sigmoid, [bpm, bbiff], [bsf], bias=biff[:])
                ACT(lf[:, 0:ntok], sf[:, 0:ntok], AF.Ln, [bsf], [blf])

            fl = [(0, F0), (1, F1), (2, F2(0)), (3, F2(1)), (3.5, F5), (4, F3(0)), (5, F3(1)), (6, F3(2)), (7, F3(3)),
                  (8, F4)]
            for j, f in fl:
                items.append((base + j, f))

            def chunk_stages(c):
                cs = slice(c * 64, (c + 1) * 64)
                g = gchunk[0]
                gchunk[0] += 1
                K = CK[g % CS_]
                Kn = CK[(g + 1) % CS_]
                Kp = CK[(g - 1) % CS_]
                mc = chunk_ctr[0]
                chunk_ctr[0] += 1
                mcur = mseq[:, mc:mc + 1]
                (bb_, bbb), (rr, brr), (mmt, bmmt), (wg, bwg), (wi, bwi), (emt, bemt), (dec, bdec) = (
                    K["bb"], K["rr"], K["mmt"], K["wg"], K["wi"], K["emt"], K["dec"])
                (gsm, bgsm), (gtok, bgtok), (Wt, bWt), (vaug, bvaug), (vw, bvw), (smo, bsmo) = (
                    K["gsm"], K["gtok"], K["Wt"], K["vaug"], K["vw"], K["smo"])
                (ktok, bktok), (STt, bST), (qsT, bqsT), (CTb, bCTb), (hn, bhn), (hs, bhs) = (
                    K["ktok"], K["ST"], K["qsT"], K["CTb"], K["hn"], K["hs"])
                hh, bhh = hn[:, :, 0:128], bhn
                (T1h, bT1h), (ybt, bybt) = K["T1h"], K["ybt"]
                mx2 = mmt[:, 63:64]

                def G0():
                    E("dve", "tensor_tensor_scan", [blf, bones], [bbb], out=bb_[:, :], data0=onesb[0:4, 0:64], data1=lf[:, cs],
                      initial=0.0, op0=ALU.mult, op1=ALU.add)
                    E("dve", "tensor_tensor", [bli, bbb], [brr], out=rr[:, :], in0=li[:, cs], in1=bb_[:, :], op=ALU.subtract)
                    E("dve", "tensor_tensor_scan", [brr, bones, bmseq], [bmmt], out=mmt[:, :], data0=onesb[0:4, 0:64],
                      data1=rr[:, :], initial=mcur, op0=ALU.mult, op1=ALU.max)
                    E("dve", "tensor_scalar", [bmmt], [bgsm], out=gsm[:, 0:1], in0=mx2, scalar1=-1.0, scalar2=None, op0=ALU.mult)
                    E("dve", "tensor_tensor", [bmseq, bmmt], [bgsm], out=gsm[:, 1:2], in0=mcur, in1=mx2, op=ALU.subtract)
                    E("dve", "tensor_tensor", [bbb, bmmt], [bmseq], out=mseq[:, mc + 1:mc + 2], in0=bb_[:, 63:64],
                      in1=mx2, op=ALU.add)
                    E("dve", "tensor_tensor", [bbb, bmmt], [bemt], out=emt[:, :], in0=bb_[:, :], in1=mmt[:, :], op=ALU.add)

                def G1():
                    ACT(wg[:, :], rr[:, :], AF.Exp, [brr, bgsm], [bwg], bias=gsm[:, 0:1])
                    ACT(dec[:, :], zer[0:4, :], AF.Exp, [bzer, bgsm], [bdec], bias=gsm[:, 1:2])
                    ACT(wi[:, :], mmt[:, :], AF.Exp, [bmmt, bmseq], [bwi], scale=-1.0, bias=mcur)
                    ACT(emt[:, :], emt[:, :], AF.Exp, [bemt], [bemt], scale=-1.0)

                def G2():
                    TR(pm[0:64, 256:260], rr[:, :], identf[0:4, 0:4], [brr, bidf], [bpm])
                    TR(pm[0:64, 260:264], emt[:, :], identf[0:4, 0:4], [bemt, bidf], [bpm])
                    TR(pm[0:64, 264:268], wg[:, :], identf[0:4, 0:4], [bwg, bidf], [bpm])
                    TR(pm[0:128, 268:272], dec[:, :], identf[0:4, 0:4], [bdec, bidf], [bpm])
                    E("dve", "tensor_copy", [bpm], [bgtok], out=gtok[0:64, 0:12], in_=pm[0:64, 256:268])
                    E("dve", "tensor_copy", [bpm], [bgtok], out=gtok[:, 12:16], in_=pm[:, 268:272])

                def G3():
                    for h in range(4):
                        MM(py[0:64, h * 64:(h + 1) * 64], sel[:, h, 0:64], mmt[:, :], True, True, [bsel, bmmt], [bpy])
                    for h in range(4):
                        MM(py[:, 256 + h * 64: 256 + (h + 1) * 64], sel[:, h, :], wi[:, :], True, True, [bsel, bwi], [bpy])
                    for h in range(4):
                        ACT(Wt[:, h, :], py[0:64, h * 64:(h + 1) * 64], AF.Exp, [bpy, bgtok], [bWt], scale=-1.0,
                            bias=gtok[0:64, h:h + 1])
                    E("dve", "tensor_tensor", [bpy, bmqk], [bqsT], out=qsT[:], in0=py[:, 256:512].rearrange("p (h t) -> p h t", h=4),
                      in1=mqkT[:, 0:4, cs], op=ALU.mult)
                    E("pool", "affine_select", [bWt], [bWt], out=Wt[:], in_=Wt[:], pattern=[[0, 4], [1, 64]],
                      compare_op=ALU.is_ge, fill=0.0, base=0, channel_multiplier=-1)

                def V0a():
                    if nch == 2 and c == 1:
                        return
                    pp, bpp = next_ab()
                    m_ = ntok if nch == 2 else 64
                    for kc in range(8):
                        MM(pp[0:m_, :], uTs[:, kc, 0:m_], winb[:, kc, 1024:1536], kc == 0, kc == 7, [buTs] + WL["mv"], [bpp])
                    E("dve", "tensor_copy", [bpp], [bvaug], out=vaug[:, :, 0:128], in_=pp[0:64, :].rearrange("p (h d) -> p h d", h=4))
                    if nch == 2:
                        vn, bvn = Kn["vaug"]
                        ACT(vn[:, :, 0:128], pp[64:128, :].rearrange("p (h d) -> p h d", h=4), AF.Copy, [bpp], [bvn])

                def V0b():
                    if nch == 2 and c == 0:
                        return
                    pp, bpp = next_ab()
                    m_ = ntok if nch == 2 else 64
                    for kc in range(8):
                        MM(pp[0:m_, :], uTs[:, kc, 0:m_], winb[:, kc, 1536:2048], kc == 0, kc == 7, [buTs] + WL["mo"], [bpp])
                    if nch == 2:
                        sp_, bsp_ = Kp["smo"]
                        ACT(sp_[:], pp[0:64, :], AF.Sigmoid, [bpp], [bsp_])
                        E("pool", "tensor_tensor", [bsp_, bmlg], [bsp_], out=sp_[:], in0=sp_[:], in1=mlg[:], op=ALU.mult)
                        ACT(smo[:], pp[64:128, :], AF.Sigmoid, [bpp], [bsmo])
                    else:
                        ACT(smo[:], pp[0:64, :], AF.Sigmoid, [bpp], [bsmo])
                    E("pool", "tensor_tensor", [bsmo, bmlg], [bsmo], out=smo[:], in0=smo[:], in1=mlg[:], op=ALU.mult)

                def V0c():
                    for h in range(4):
                        TR(pat[0:64, h * 128:(h + 1) * 128], mqkT[:, 4 + h, cs], identb[:], [bmqk, bidb], [bpat])
                    E("dve", "tensor_copy", [bpat], [bktok], out=ktok[:], in_=pat[0:64, 0:512].rearrange("p (h d) -> p h d", h=4))

                def U():
                    E("pool", "tensor_copy", [bCT], [bCTb], out=CTb[:], in_=CT[:])
                    E("dve", "tensor_tensor", [bvaug, bgtok], [bvw], out=vw[:], in0=vaug[:],
                      in1=gtok[0:64, 8:12].unsqueeze(2).broadcast_to([64, 4, 129]), op=ALU.mult)
                    for h in range(4):
                        o0 = 512 * (h // 2) + 129 * (h % 2)
                        MM(pz2[:, o0:o0 + 129], ktok[:, h, :], vw[:, h, :], True, True, [bktok, bvw], [bpz[0], bpz[1]])
                    for h in range(4):
                        o0 = 512 * (h // 2) + 129 * (h % 2)
                        E("dve", "scalar_tensor_tensor", [bCT, bgtok, bpz[0], bpz[1]], [bCT], out=CT[:, h, :], in0=CT[:, h, :],
                          scalar=gtok[:, 12 + h:13 + h], in1=pz2[:, o0:o0 + 129], op0=ALU.mult, op1=ALU.add)

                def V1():
                    for h in range(4):
                        MM(patf[0:64, h * 64:(h + 1) * 64], mqkT[:, 4 + h, cs], mqkT[:, h, cs], True, True, [bmqk], [bpat])
                    E("dve", "tensor_tensor", [bpat, bWt], [bST], out=STt[:], in0=patf[0:64, 0:256].rearrange("p (h t) -> p h t", h=4),
                      in1=Wt[:], op=ALU.mult)

                def N0():
                    for h in range(4):
                        o0 = 512 * (h // 2) + 129 * (h % 2)
                        MM(pz2[0:64, o0:o0 + 129], STt[:, h, :], vaug[:, h, :], True, False, [bST, bvaug], [bpz[0], bpz[1]])
                        MM(pz2[0:64, o0:o0 + 129], qsT[:, h, :], CTb[:, h, :], False, True, [bqsT, bCTb], [bpz[0], bpz[1]])
                    E("dve", "tensor_copy", [bpz[0], bpz[1]], [bhn], out=hn[:, 0:2, :], in_=pz2[0:64, 0:258].rearrange("p (h d) -> p h d", h=2))
                    E("dve", "tensor_copy", [bpz[0], bpz[1]], [bhn], out=hn[:, 2:4, :], in_=pz2[0:64, 512:770].rearrange("p (h d) -> p h d", h=2))

                def N1():
                    den = hn[:, :, 128]
                    E("dve", "scalar_tensor_tensor", [bhn], [bhs], out=hs[:, 0:4], in0=den, scalar=-1.0, in1=den,
                      op0=ALU.mult, op1=ALU.max)
                    E("dve", "tensor_tensor", [bhs, bgtok], [bhs], out=hs[:, 4:8], in0=hs[:, 0:4], in1=gtok[0:64, 4:8], op=ALU.max)
                    E("dve", "reciprocal", [bhs], [bhs], out=hs[:, 8:12], in_=hs[:, 4:8])
                    E("dve", "tensor_tensor", [bhn, bhs], [bhh], out=hh, in0=hn[:, :, 0:128],
                      in1=hs[:, 8:12].unsqueeze(2).broadcast_to([64, 4, 128]), op=ALU.mult)
                    E("dve", "tensor_tensor", [bhh], [bT1h], out=T1h[:], in0=hh, in1=hh, op=ALU.mult)
                    E("dve", "tensor_reduce", [bT1h], [bhs], out=hs[:, 12:16], in_=T1h[:], axis=AX.X, op=ALU.add)

                def N2():
                    rstd_pow(hs[:, 12:16], hs[:, 12:16], hs[:, 12:16], 64, 4, 1.0 / 128, [bhs], [bhs])

                def N3():
                    E("dve", "tensor_tensor", [bhh, bhs], [bT1h], out=T1h[:], in0=hh,
                      in1=hs[:, 12:16].unsqueeze(2).broadcast_to([64, 4, 128]), op=ALU.mult)
                    E("dve", "tensor_tensor", [bT1h, bsmo], [bybt], out=ybt[:], in0=T1h[:].rearrange("p h d -> p (h d)"),
                      in1=smo[:], op=ALU.mult)

                def N4():
                    for h in range(4):
                        TR(ptr[:, h * 64:(h + 1) * 64], ybt[:, h * 128:(h + 1) * 128], identb[0:64, 0:64],
                           [bybt, bidb], [bptr])
                    ACT(ybT[:, :, cs], ptr[:, 0:256].rearrange("p (h t) -> p h t", h=4), AF.Copy, [bptr], [bybT])

                return [G0, G1, G2, G3, V0a, V0b, V0c, U, V1, N0, N1, N2, N3, N4]

            for c in range(nch):
                st_ = chunk_stages(c)
                b0 = base + 7 + c * SC_
                for j, f in enumerate(st_):
                    items.append((b0 + j, f))
            dbase = base + 7 + (nch - 1) * SC_ + 14

            def D0():
                DMA("sp", "Ar", At[0:ntok, :], A_sc[row0:row0 + ntok, :], [Abuf.get(row0)], [bAt])
                for n in range(2):
                    pB, bpB = next_ab()
                    for c in range(4):
                        MM(pB[0:ntok, :], ybT[:, c, 0:ntok], wb3[:, c, n * 512:(n + 1) * 512], c == 0, c == 3, [bybT] + WL["b"], [bpB])
                    pG, bpG = next_ab()
                    for kc in range(8):
                        MM(pG[0:ntok, :], uTs[:, kc, 0:ntok], winb[:, kc, 3080 + n * 512: 3080 + (n + 1) * 512], kc == 0, kc == 7,
                           [buTs] + WL["gb"], [bpG])
                    ACT(sgt[0:ntok, :], pG[0:ntok, :], AF.Sigmoid, [bpG], [bsgt])
                    E("dve", "tensor_tensor", [bsgt, bpB], [bT2], out=T2[0:ntok, n * 512:(n + 1) * 512], in0=sgt[0:ntok, :],
                      in1=pB[0:ntok, :], op=ALU.mult)

            def D1():
                for n in range(2):
                    pG, bpG = next_ab()
                    for kc in range(8):
                        MM(pG[0:ntok, :], uTs[:, kc, 0:ntok], winb[:, kc, 2056 + n * 512: 2056 + (n + 1) * 512], kc == 0, kc == 7,
                           [buTs] + WL["ga"], [bpG])
                    ACT(sgt[0:ntok, :], pG[0:ntok, :], AF.Sigmoid, [bpG], [bsgt])
                    E("pool", "tensor_tensor", [bsgt, bAt], [bsgt], out=sgt[0:ntok, :], in0=sgt[0:ntok, :],
                      in1=At[0:ntok, n * 512:(n + 1) * 512], op=ALU.mult)
                    E("pool", "tensor_tensor", [bsgt, bT2], [bmg], out=mg[0:ntok, n * 512:(n + 1) * 512], in0=sgt[0:ntok, :],
                      in1=T2[0:ntok, n * 512:(n + 1) * 512], op=ALU.add)

            def D2():
                for kc in range(8):
                    TR(ptr[:, kc * 128: kc * 128 + ntok], mg[0:ntok, kc * 128:(kc + 1) * 128], identb[0:ntok, 0:ntok],
                       [bmg, bidb], [bptr])
                ACT(mT[:, :, 0:ntok], ptr[:].rearrange("p (k t) -> p k t", k=8)[:, :, 0:ntok], AF.Copy, [bptr], [bmT])

            def D3():
                for n in range(2):
                    pO, bpO = next_ab()
                    for kc in range(8):
                        MM(pO[0:ntok, :], mT[:, kc, 0:ntok], wo3[:, kc, n * 512:(n + 1) * 512], kc == 0, kc == 7, [bmT] + WL["out"], [bpO])
                    E("dve", "tensor_tensor", [bpO, bgg_], [bT2], out=T2[0:ntok, n * 512:(n + 1) * 512], in0=pO[0:ntok, :],
                      in1=gg_[0:ntok, n * 512:(n + 1) * 512], op=ALU.mult)
                E("pool", "tensor_tensor", [bT2, bxt], [bxt], out=xt[0:ntok, :], in0=T2[0:ntok, :], in1=xt[0:ntok, :], op=ALU.add)
                bx1 = Buf("x1sc%d" % row0)
                x1buf[row0] = bx1
                DMA("sp", "x1w%d" % (lt % 3), x1_sc[row0:row0 + ntok, :], xt[0:ntok, :], [bxt], [bx1])
                ACT(T2[0:ntok, :], xt[0:ntok, :], AF.Square, [bxt], [bT2, bssq], accum_out=ssq[0:ntok, 0:1])
                E("pool", "tensor_scalar", [bssq], [bssq], out=ssq[0:ntok, 1:2], in0=ssq[0:ntok, 0:1], scalar1=1.0 / D,
                  scalar2=EPS, op0=ALU.mult, op1=ALU.add)
                E("pool", "tensor_tensor", [bssq, bmhalf], [brstd2], out=rstd2[0:ntok, tix:tix + 1], in0=ssq[0:ntok, 1:2],
                  in1=mhalf[0:ntok, 0:1], op=ALU.pow)

            for j, f in enumerate([D0, D1, D2, D3]):
                items.append((dbase + j, f))
            sq["lastU"] = base + 7 + (nch - 1) * SC_ + 7
            sq["lastD3"] = dbase + 3
            sq["lastF4"] = base + 8
            return (xp, bxp, ntok)

        if "1b" in PH:
            seqs = []
            for tix, tile in enumerate(tiles):
                if not seqs or seqs[-1][0] != tile[0]:
                    seqs.append((tile[0], []))
                seqs[-1][1].append((tix, tile))
            items = []
            lt = 0
            d3_hist = []
            for k_, (s, tl) in enumerate(seqs):
                CTk, bCTk = CTs[k_ % 3]
                ggk = ggbs[k_ % 2]
                sq = dict(CT=(CTk, bCTk), gg=ggk)
                chunk_ctr[0] += 1
                mcol = chunk_ctr[0]
                base0 = 2 * lt * SC_
                xp0, bxp0 = xp2[lt % 2]

                def init_seq(s=s, CTk=CTk, bCTk=bCTk, mcol=mcol, xp0=xp0, bxp0=bxp0, first=(k_ == 0)):
                    load_mod(s, 1, gate=False)
                    if s == 0:
                        E("pool", "memset", [], [bCTk], CTk[:], 0.0)
                        if first:
                            E("pool", "memset", [], [bmseq], mseq[:], 0.0)
                        E("pool", "memset", [], [bxp0], xp0[:, :, 0:3], 0.0)
                    else:
                        si = s - 1
                        DMA("sp", "st", C0t, C0[si].rearrange("h v k -> v h k"), [], [bC0t])
                        for h in range(4):
                            TR(pa[:, h * 128:(h + 1) * 128], C0t[:, h, :], identf[:], [bC0t, bidf], [bpa])
                        E("dve", "tensor_copy", [bpa], [bCTk], out=CTk[:, :, 0:128], in_=pa[:].rearrange("p (h v) -> p h v", h=4))
                        DMA("sp", "st", CTk[:, :, 128], n0[si].rearrange("h k -> k h"), [], [bCTk], allow_slow_non_contiguous=True)
                        DMA("sp", "st", mseq[:, mcol:mcol + 1], m0[si:si + 1, :].rearrange("o h -> h o"), [], [bmseq],
                            allow_slow_non_contiguous=True)
                        for j_ in range(3):
                            DMA("sp", "st", xp0[:, :, j_], conv0[si, j_].rearrange("(c p) -> p c", p=128), [], [bxp0],
                                allow_slow_non_contiguous=True)

                items.append((base0 - 0.5, init_seq))
                gstep = base0 - 0.4
                if k_ >= 2:
                    gstep = max(gstep, d3_hist[k_ - 2] + 0.5)
                items.append((gstep, (lambda s=s, ggk=ggk: load_mod(s, 1, front=False, gdst=ggk))))
                prev_xp = None
                for (tix, tile) in tl:
                    prev_xp = sched_tile(items, lt, tix, tile, prev_xp, sq)
                    lt += 1
                items.append((sq["lastF4"] + 0.5, out_conv(s, prev_xp)))
                items.append((sq["lastU"] + 0.5, out_state(s, CTk, bCTk, chunk_ctr[0])))
                d3_hist.append(sq["lastD3"])
            order = sorted(range(len(items)), key=lambda i: (items[i][0], i))
            for i in order:
                items[i][1]()

        phase_begin()
        pab[:] = [(pa, bpa), (pb, bpb), (py, bpy), (pm, bpm), (pz2[:, 0:512], bpz[0]), (pz2[:, 512:1024], bpz[1])]
        WL = {}
        for ci_, c0_ in enumerate(range(0, DFF, 512)):
            n_ = min(512, DFF - c0_)
            WL["g%d" % ci_] = load_wg(w_ff_gate, 8, 0, [("g", c0_, n_)], DFF, 0, "fg%d" % ci_, kbase=2 * ci_)["g"]
            WL["u%d" % ci_] = load_wg(w_ff_up, 8, 0, [("u", c0_, n_)], DFF, 22528, "fu%d" % ci_, kbase=2 * ci_ + 1)["u"]
        wg3 = wview(0, 8, DFF)
        wu3 = wview(22528, 8, DFF)
        wdn, _ = cv("wdn", [128, 22 * D], BF16)
        xs3p = [xts[0], xts[1], cv("xt2p", [128, D], F32)]
        wd3 = wdn.rearrange("p (k c) -> p k c", k=22)
        WLd = {}
        dvd = w_ff_down.rearrange("(k p) n -> p k n", p=128)
        for n_ in range(2):
            bw = fb("Wd%d" % n_)
            DMA("pool", "Wg%d" % (12 + n_), wd3[:, :, n_ * 512:(n_ + 1) * 512], dvd[:, :, n_ * 512:(n_ + 1) * 512], [], [bw])
            WLd[n_] = [bw]
        sgts = [cv("sgt%d" % i, [128, 512], F32) for i in range(1)] * 2
        hb2 = [cv("hbuf%d" % i, [128, DFF], BF16) for i in range(2)]
        hT, bhT = cv("hT", [128, 22, 128], BF16)
        Tq, bTq = cv("O2_0", [128, D], F32)
        sq2 = [cv("sq2_%d" % i, [128, 4], F32) for i in range(2)]
        uT2a = [(uT, buT), cv("uT2a", [128, 8, 128], BF16)]
        ggbs2 = [(ggb, bggb), cv("ggb2", [128, D], F32)]
        fgb, bfgb = cv("fgb", [128, D], F32)
        DMA("sp", "cst", fgb[:], final_g[0:1, :].broadcast_to([128, D]), [], [bfgb])
        sgc = [0]
        seq2a = [-1]
        gsel = {}
        trb = [(ptr, bptr), (pat, bpat)]

        def ld2(tix):
            (s, row0, ntok, ti) = tiles[tix]
            xt, bxt = xs3p[tix % 3]
            DMA("sp", "xl%d" % (tix % 3), xt[0:ntok, :], x1_sc[row0:row0 + ntok, :], [x1buf.get(row0)], [bxt])

        def front2(tix):
            (s, row0, ntok, ti) = tiles[tix]
            if s != seq2a[0]:
                seq2a[0] = s
                load_mod(s, 2, gate=False)
            xt, bxt = xs3p[tix % 3]
            rms_to_uT(xt, bxt, ntok, rstd_ap=rstd2[0:ntok, tix:tix + 1], dst=uT2a[tix % 2])

        def up2(tix):
            (s, row0, ntok, ti) = tiles[tix]
            hbuf, bhbuf = hb2[tix % 2]
            for n0_ in range(0, DFF, 512):
                sgc[0] ^= 1
                sgt, bsgt = sgts[sgc[0]]
                n = min(512, DFF - n0_)
                pG, bpG = proj_tok(wg3, n0_, n, ntok, wl=WL["g%d" % (n0_ // 512)], us=uT2a[tix % 2])
                pU, bpU = proj_tok(wu3, n0_, n, ntok, wl=WL["u%d" % (n0_ // 512)], us=uT2a[tix % 2])
                ACT(sgt[0:ntok, 0:n], pG[0:ntok, 0:n], AF.Silu, [bpG], [bsgt])
                E("dve", "tensor_tensor", [bsgt, bpU], [bhbuf], out=hbuf[0:ntok, n0_:n0_ + n], in0=sgt[0:ntok, 0:n],
                  in1=pU[0:ntok, 0:n], op=ALU.mult)

        seqord = []
        for t_ in tiles:
            if t_[0] not in seqord:
                seqord.append(t_[0])

        def down2(tix):
            (s, row0, ntok, ti) = tiles[tix]
            k_ = seqord.index(s)
            gg2, bgg2 = ggbs2[k_ % 2]
            if s not in gsel:
                gsel[s] = True
                load_mod(s, 2, front=False, gdst=(gg2, bgg2))
            hbuf, bhbuf = hb2[tix % 2]
            xt, bxt = xs3p[tix % 3]
            sq_, bsq_ = sq2[tix % 2]
            for gi, g0 in enumerate(range(0, 22, 8)):
                ng = min(8, 22 - g0)
                pt_, bpt_ = trb[gi % 2]
                for j in range(ng):
                    k = g0 + j
                    TR(pt_[:, j * 128: j * 128 + ntok], hbuf[0:ntok, k * 128:(k + 1) * 128], identb[0:ntok, 0:ntok],
                       [bhbuf, bidb], [bpt_])
                ACT(hT[:, g0:g0 + ng, 0:ntok], pt_[:].rearrange("p (k t) -> p k t", k=8)[:, 0:ng, 0:ntok], AF.Copy,
                    [bpt_], [bhT])
            for n in range(2):
                pD, bpD = next_ab()
                for k in range(22):
                    MM(pD[0:ntok, :], hT[:, k, 0:ntok], wd3[:, k, n * 512:(n + 1) * 512], k == 0, k == 21, [bhT] + WLd[n], [bpD])
                E("dve", "tensor_tensor", [bpD, bgg2], [bTq], out=Tq[0:ntok, n * 512:(n + 1) * 512], in0=pD[0:ntok, :],
                  in1=gg2[0:ntok, n * 512:(n + 1) * 512], op=ALU.mult)
            E("pool", "tensor_tensor", [bTq, bxt], [bxt], out=xt[0:ntok, :], in0=Tq[0:ntok, :], in1=xt[0:ntok, :], op=ALU.add)
            ACT(Tq[0:ntok, :], xt[0:ntok, :], AF.Square, [bxt], [bTq, bsq_], accum_out=sq_[0:ntok, 0:1])
            rstd_pow(sq_[0:ntok, 2:3], sq_[0:ntok, 1:2], sq_[0:ntok, 0:1], ntok, 1, 1.0 / D, [bsq_], [bsq_])
            E("dve", "scalar_tensor_tensor", [bxt, bsq_, bfgb], [bTq], out=Tq[0:ntok, :], in0=xt[0:ntok, :],
              scalar=sq_[0:ntok, 2:3], in1=fgb[0:ntok, :], op0=ALU.mult, op1=ALU.mult)
            DMA("sp", "yo", y_all[row0:row0 + ntok, :], Tq[0:ntok, :], [bTq], [])

        P2ON = ("2a" in PH) or ("2b" in PH)
        if P2ON:
            NT_ = len(tiles)
            ld2(0)
            if NT_ > 1:
                ld2(1)
            front2(0)
            if NT_ > 1:
                front2(1)
            up2(0)
            for tix in range(NT_):
                if tix + 2 < NT_:
                    ld2(tix + 2)
                if tix + 1 < NT_:
                    up2(tix + 1)
                if tix + 2 < NT_:
                    front2(tix + 2)
                down2(tix)

        P.lower(lambda name: st.enter_context(nc.semaphore(name)))
        build_program.stats = P.stats
    return nc


_CACHE = {}


def kernel(**inp):
    f = lambda a: np.ascontiguousarray(np.asarray(a, dtype=np.float32))
    if "nc" not in _CACHE:
        _CACHE["nc"] = build_program()
    nc = _CACHE["nc"]
    x_prompt = f(inp["x_prompt"]); x_sample = f(inp["x_sample"])
    c_prompt = f(inp["c_prompt"]); c_sample = f(inp["c_sample"])
    ck = f(inp["cache_sb_k"])[0].reshape(16, PAST, 512)
    cv = f(inp["cache_sb_v"])[0].reshape(16, PAST, 512)
    sC = f(inp["state_mlstm_C"])[0]; sn = f(inp["state_mlstm_n"])[0]; sm = f(inp["state_mlstm_m"])[0]
    sconv = f(inp["state_conv"])[0]
    shared = {
        "norm1_g": f(inp["norm1_g"]).reshape(1, D), "norm2_g": f(inp["norm2_g"]).reshape(1, D),
        "w_ada": f(inp["w_ada"])[0], "b_ada": f(inp["b_ada"]).reshape(1, 6 * D),
        "w_in": f(inp["w_in"])[0], "b_if": f(inp["b_if"]).reshape(8, 1),
        "w_conv": f(inp["w_conv"])[0], "b_conv": f(inp["b_conv"]).reshape(1, D),
        "ml_norm_g": f(inp["ml_norm_g"]).reshape(1, 512),
        "w_a": f(inp["w_a"])[0], "w_b": f(inp["w_b"])[0], "w_out": f(inp["w_out"])[0],
        "w_ff_gate": f(inp["w_ff_gate"])[0], "w_ff_up": f(inp["w_ff_up"])[0], "w_ff_down": f(inp["w_ff_down"])[0],
        "final_g": f(inp["final_g"]).reshape(1, D),
    }
    in_maps = []
    for i in range(8):
        m = dict(shared)
        m["xall"] = np.concatenate([x_prompt[i], x_sample[2 * i], x_sample[2 * i + 1]], axis=0)
        m["c3"] = np.stack([c_prompt[i], c_sample[2 * i], c_sample[2 * i + 1]], axis=0)
        m["cache_k"] = ck[2 * i:2 * i + 2]
        m["cache_v"] = cv[2 * i:2 * i + 2]
        m["C0"] = sC[2 * i:2 * i + 2]
        m["n0"] = sn[2 * i:2 * i + 2]
        m["m0"] = sm[2 * i:2 * i + 2]
        m["conv0"] = sconv[2 * i:2 * i + 2]
        in_maps.append({k: np.ascontiguousarray(v) for k, v in m.items()})
    res = run_bass_kernel_spmd(nc, in_maps, core_ids=list(range(8)))
    R = res.results
    y_p = np.stack([R[i]["y_all"][:SEQ] for i in range(8)])
    y_s = np.stack([R[i]["y_all"][SEQ + j * LS: SEQ + (j + 1) * LS] for i in range(8) for j in range(2)])
    k_p = np.stack([R[i]["k_all"][:SEQ] for i in range(8)]).reshape(1, 8, SEQ, 8, 64)
    v_p = np.stack([R[i]["v_all"][:SEQ] for i in range(8)]).reshape(1, 8, SEQ, 8, 64)
    k_s = np.stack([R[i]["k_all"][SEQ + j * LS: SEQ + (j + 1) * LS] for i in range(8) for j in range(2)]).reshape(1, 16, LS, 8, 64)
    v_s = np.stack([R[i]["v_all"][SEQ + j * LS: SEQ + (j + 1) * LS] for i in range(8) for j in range(2)]).reshape(1, 16, LS, 8, 64)
    C_p = np.stack([R[i]["C_out"][0] for i in range(8)])[None]
    n_p = np.stack([R[i]["n_out"][0] for i in range(8)])[None]
    m_p = np.stack([R[i]["m_out"][0] for i in range(8)])[None]
    cv_p = np.stack([R[i]["conv_out"][0] for i in range(8)])[None]
    C_s = np.stack([R[i]["C_out"][1 + j] for i in range(8) for j in range(2)])[None]
    n_s = np.stack([R[i]["n_out"][1 + j] for i in range(8) for j in range(2)])[None]
    m_s = np.stack([R[i]["m_out"][1 + j] for i in range(8) for j in range(2)])[None]
    cv_s = np.stack([R[i]["conv_out"][1 + j] for i in range(8) for j in range(2)])[None]
    outs = (y_p, y_s, k_p, v_p, C_p, n_p, m_p, cv_p, k_s, v_s, C_s, n_s, m_s, cv_s)
    return tuple(np.ascontiguousarray(o, dtype=np.float32) for o in outs)
```

```python
import numpy as np
from contextlib import ExitStack
import concourse.bass as bass
import concourse.mybir as mybir
from concourse.bass_utils import run_bass_kernel_spmd

F32 = mybir.dt.float32
BF16 = mybir.dt.bfloat16
AF = mybir.ActivationFunctionType
ALU = mybir.AluOpType
AX = mybir.AxisListType

D = 1024
SEQ = 2048
NS = 2
LS = 64
PAST = 1024
DFF = 2816
INW = 5640
EPS = 1e-6
NROWS = SEQ + NS * LS


STRICT_SAME_ENGINE = True
PE_RIDE = True


class Buf:
    __slots__ = ("name", "last_w", "readers", "excl")

    def __init__(self, name, excl=False):
        self.name = name
        self.last_w = None
        self.readers = []
        self.excl = excl


class Op:
    __slots__ = ("eng", "fn", "reads", "writes", "dma", "seq", "signal", "waits",
                 "clock", "count", "idx", "attach")


class Prog:
    ENG = ("pe", "act", "dve", "pool", "sp")

    def __init__(self, nc):
        self.nc = nc
        self.ops = []
        self.e = {"pe": nc.tensor, "act": nc.scalar, "dve": nc.vector,
                  "pool": nc.gpsimd, "sp": nc.sync}

    def op(self, eng, fn, reads=(), writes=(), dma=None):
        o = Op()
        o.eng = eng
        o.fn = fn
        o.reads = [b for b in reads if b is not None and not b.excl]
        o.writes = [b for b in writes if b is not None] + [b for b in reads if b is not None and b.excl]
        o.dma = dma
        o.signal = False
        o.waits = []
        o.attach = (eng != "pe")
        o.idx = len(self.ops)
        self.ops.append(o)
        return o

    def fence(self):
        last = {}
        for o in self.ops:
            last[o.eng if o.dma is None else "d:" + o.dma] = o
        return list(last.values())

    def lower(self, sem_ctx):
        ops = self.ops
        seqc = {k: 0 for k in self.ENG}
        dmac = {}
        eclock = {k: {} for k in self.ENG}
        for o in ops:
            if o.dma is None:
                seqc[o.eng] += 1
                o.seq = seqc[o.eng]
            else:
                dmac[o.dma] = dmac.get(o.dma, 0) + 1
                o.seq = dmac[o.dma]
            deps = {}
            dsafe = {}
            for b in o.reads:
                if b.last_w is not None:
                    deps[b.last_w.idx] = b.last_w
                    dsafe[b.last_w.idx] = False
            for b in o.writes:
                if b.last_w is not None:
                    deps[b.last_w.idx] = b.last_w
                    dsafe[b.last_w.idx] = dsafe.get(b.last_w.idx, True) and b.excl
                for r in b.readers:
                    deps[r.idx] = r
                    dsafe[r.idx] = dsafe.get(r.idx, True) and b.excl
            clk = eclock[o.eng]
            for p in deps.values():
                if p is o:
                    continue
                if p.dma is None:
                    key = p.eng
                    need = p.seq
                    if p.eng == o.eng and o.dma is None:
                        if o.eng == "pe":
                            continue
                        if (not STRICT_SAME_ENGINE) and o.eng != "pool" and not any(b.last_w is p for b in o.reads):
                            continue
                else:
                    key = "d:" + p.dma
                    need = dmac[p.dma] if not (o.dma == p.dma) else dmac[p.dma] - 1
                if clk.get(key, 0) >= need:
                    continue
                if p.dma is None:
                    p.signal = True
                o.waits.append((p, need, dsafe.get(p.idx, False)))
                for k2, v2 in p.clock.items():
                    if clk.get(k2, 0) < v2:
                        clk[k2] = v2
                if clk.get(key, 0) < need:
                    clk[key] = need
            myclk = dict(clk)
            mykey = o.eng if o.dma is None else "d:" + o.dma
            myclk[mykey] = max(myclk.get(mykey, 0), o.seq)
            o.clock = myclk
            for b in o.reads:
                b.readers.append(o)
            for b in o.writes:
                b.last_w = o
                b.readers = []
        cnt = {k: 0 for k in self.ENG}
        for o in ops:
            if o.dma is None and o.signal:
                cnt[o.eng] += 1
                o.count = cnt[o.eng]
        esem = {}
        dsem = {}
        dtot = {}
        n_waits = 0

        def get_e(k):
            if k not in esem:
                esem[k] = sem_ctx("e_" + k)
            return esem[k]

        def get_d(k):
            if k not in dsem:
                dsem[k] = sem_ctx("d_" + k)
            return dsem[k]

        for o in ops:
            eng = self.e[o.eng]
            need = {}
            for p, nd, sf in o.waits:
                if p.dma is None:
                    s = get_e(p.eng)
                    v = p.count
                else:
                    s = get_d(p.dma)
                    v = 16 * nd
                k = id(s)
                if k not in need:
                    need[k] = (s, v, sf)
                else:
                    need[k] = (s, max(v, need[k][1]), sf and need[k][2])
            nl = [(a_, b_) for (a_, b_, c_) in need.values()]
            ride = None
            if o.attach and nl:
                ride = nl.pop()
            elif o.eng == "pe" and PE_RIDE:
                safe = [(a_, b_) for (a_, b_, c_) in need.values() if c_]
                if safe:
                    ride = safe[-1]
                    nl = [x for x in nl if x[0] is not ride[0]]
            for s, v in nl:
                eng.wait_ge(s, v)
                n_waits += 1
            ins = o.fn()
            if ride is not None:
                ins._wait_ge(ride[0], eng.lower_val(ride[1]))
            if o.dma is not None:
                ins.then_inc(get_d(o.dma), 16)
                dtot[o.dma] = dtot.get(o.dma, 0) + 16
            elif o.signal:
                ins.then_inc(get_e(o.eng), 1)
        for k, s in dsem.items():
            self.e["sp"].wait_ge(s, dtot[k])
        self.stats = dict(n_ops=len(ops), n_waits=n_waits, n_dsem=len(dsem),
                          sig={k: cnt[k] for k in cnt})


CFG = {"phases": ("0", "1a", "1b", "2a", "2b"), "tiles": None}


def build_program():
    nc = bass.Bass("TRN2", target_bir_lowering=False)
    PH = CFG["phases"]

    def din(name, shape):
        return nc.dram_tensor(name, list(shape), F32, kind="ExternalInput").ap()

    def dout(name, shape):
        return nc.dram_tensor(name, list(shape), F32, kind="ExternalOutput").ap()

    xall = din("xall", [NROWS, D])
    c3 = din("c3", [3, D])
    cache_k = din("cache_k", [NS, PAST, 512])
    cache_v = din("cache_v", [NS, PAST, 512])
    C0 = din("C0", [NS, 4, 128, 128])
    n0 = din("n0", [NS, 4, 128])
    m0 = din("m0", [NS, 4])
    conv0 = din("conv0", [NS, 3, D])
    norm1_g = din("norm1_g", [1, D])
    norm2_g = din("norm2_g", [1, D])
    w_ada = din("w_ada", [D, 6 * D])
    b_ada = din("b_ada", [1, 6 * D])
    w_in = din("w_in", [D, INW])
    b_if = din("b_if", [8, 1])
    w_conv = din("w_conv", [4, D])
    b_conv = din("b_conv", [1, D])
    ml_norm_g = din("ml_norm_g", [1, 512])
    w_a = din("w_a", [512, D])
    w_b = din("w_b", [512, D])
    w_out = din("w_out", [D, D])
    w_ff_gate = din("w_ff_gate", [D, DFF])
    w_ff_up = din("w_ff_up", [D, DFF])
    w_ff_down = din("w_ff_down", [DFF, D])
    final_g = din("final_g", [1, D])

    y_all = dout("y_all", [NROWS, D])
    k_all = dout("k_all", [NROWS, 512])
    v_all = dout("v_all", [NROWS, 512])
    C_out = dout("C_out", [3, 4, 128, 128])
    n_out = dout("n_out", [3, 4, 128])
    m_out = dout("m_out", [3, 4])
    conv_out = dout("conv_out", [3, 3, D])

    mod_sc = nc.dram_tensor("mod_sc", [3, 6 * D], F32).ap()
    A_sc = nc.dram_tensor("A_sc", [NROWS, D], F32).ap()
    x1_sc = nc.dram_tensor("x1_sc", [NROWS, D], F32).ap()
    h_sc = nc.dram_tensor("h_sc", [NROWS, DFF], BF16).ap()

    tiles = [(0, t * 128, 128, t) for t in range(16)] + [(1, SEQ, 64, 0), (2, SEQ + 64, 64, 0)]
    if CFG["tiles"] is not None:
        tiles = [tiles[i] for i in CFG["tiles"]]

    with ExitStack() as st:
        P = Prog(nc)

        def sb(name, shape, dt=F32):
            return st.enter_context(nc.sbuf_tensor(name, list(shape), dt)), Buf(name)

        def ps(name, shape, dt=F32):
            return st.enter_context(nc.psum_tensor(name, list(shape), dt)), Buf(name, excl=True)

        SCRN = 20736
        SCR = st.enter_context(nc.sbuf_tensor("SCR", [128, SCRN], F32))
        scr = {"off": 0, "fence": []}

        def phase_begin():
            scr["off"] = 0
            scr["fence"] = P.fence()

        def fb(name):
            b = Buf(name)
            b.readers = list(scr["fence"])
            return b

        def cv(name, shape, dt=F32):
            n = 1
            for d_ in shape[1:]:
                n *= d_
            nf = (n + 1) // 2 if dt == BF16 else n
            nf = (nf + 7) // 8 * 8
            off = scr["off"]
            scr["off"] += nf
            assert scr["off"] <= SCRN, (name, scr["off"])
            v = SCR[0:shape[0], off:off + nf]
            if dt == BF16:
                v = v.bitcast(BF16)
            v = v[:, 0:n]
            if len(shape) == 3:
                v = v.rearrange("p (a b) -> p a b", a=shape[1])
            return v, fb(name)

        def E(eng, name, reads, writes, *a, **kw):
            m = getattr(P.e[eng], name)
            return P.op(eng, lambda: m(*a, **kw), reads, writes)

        def DMA(eng, key, out, in_, reads, writes, **kw):
            m = P.e[eng].dma_start
            return P.op(eng, lambda: m(out=out, in_=in_, **kw), reads, writes, dma=key)

        def MM(out, lhsT, rhs, start, stop, reads, writes):
            m = nc.tensor.matmul
            return P.op("pe", lambda: m(out, lhsT=lhsT, rhs=rhs, start=start, stop=stop), reads, writes)

        def TR(out, in_, ident, reads, writes):
            m = nc.tensor.transpose
            return P.op("pe", lambda: m(out=out, in_=in_, identity=ident), reads, writes)

        def ACT(out, in_, func, reads, writes, **kw):
            m = nc.scalar.activation
            o_ = P.op("act", lambda: m(out=out, in_=in_, func=func, **kw), reads, writes)
            if "accum_out" in kw:
                o_.attach = False
            return o_

        WB, bWB = sb("WB", [128, 46080], BF16)
        identb, bidb = sb("identb", [128, 128], BF16)
        identf, bidf = sb("identf", [128, 128], F32)
        onesb, bones = sb("onesb", [128, 512], BF16)
        zer, bzer = sb("zer", [128, 128], F32)
        sel, bsel = sb("sel", [4, 4, 128], F32)
        rstd2, brstd2 = sb("rstd2", [128, 18], F32)
        csT, bcsT = sb("csT", [128, 8, 3], BF16)
        mhalf, bmhalf = sb("mhalf", [128, 4], F32)
        cmask, bcmask = sb("cmask", [128, 128], BF16)
        gmb, bgmb = sb("gmb", [128, D], F32)
        shb, bshb = sb("shb", [128, D], F32)
        ggb, bggb = sb("ggb", [128, D], F32)
        xts = [sb("xt%d" % i, [128, D], F32) for i in range(2)]
        T1, bT1 = sb("T1", [128, D], F32)
        T2, bT2 = sb("T2", [128, D], F32)
        ub, bub = sb("ub", [128, D], BF16)
        uT, buT = sb("uT", [128, 8, 128], BF16)
        ssq, bssq = sb("ssq", [128, 4], F32)
        stg = []

        ptr, bptr = ps("ptr", [128, 1024], BF16)
        pat, bpat = ps("pat", [128, 1024], BF16)
        pa, bpa = ps("pa", [128, 512], F32)
        pb, bpb = ps("pb", [128, 512], F32)
        py, bpy = ps("py", [128, 512], F32)
        pm, bpm = ps("pm", [128, 512], F32)
        pz2, bpz2 = ps("pz2", [128, 1024], F32)
        bpz = [Buf("pz0", excl=True), Buf("pz1", excl=True)]
        pab = [(pa, bpa), (pb, bpb)]
        rot = {"ab": 0, "stg": 0, "x": 0}

        def next_ab():
            rot["ab"] = (rot["ab"] + 1) % len(pab)
            return pab[rot["ab"]]

        def next_stg():
            rot["stg"] = (rot["stg"] + 1) % 2
            return stg[rot["stg"]]

        E("pool", "memset", [], [bidf], identf[:], 1.0)
        E("pool", "affine_select", [bidf], [bidf], out=identf[:], in_=identf[:], pattern=[[-1, 128]],
          compare_op=ALU.is_equal, fill=0.0, base=0, channel_multiplier=1)
        E("pool", "tensor_copy", [bidf], [bidb], out=identb[:], in_=identf[:])
        E("pool", "memset", [], [bones], onesb[:], 1.0)
        E("pool", "memset", [], [bzer], zer[:], 0.0)
        E("pool", "memset", [], [bsel], sel[:], 1.0)
        E("pool", "affine_select", [bsel], [bsel], out=sel[:], in_=sel[:], pattern=[[-1, 4], [0, 128]],
          compare_op=ALU.is_equal, fill=0.0, base=0, channel_multiplier=1)
        E("pool", "memset", [], [brstd2], rstd2[:], 1.0)
        E("pool", "memset", [], [bmhalf], mhalf[:], -0.5)
        E("pool", "memset", [], [bcmask], cmask[:], -30000.0)
        E("pool", "affine_select", [bcmask], [bcmask], out=cmask[:], in_=cmask[:], pattern=[[1, 128]],
          compare_op=ALU.is_ge, fill=0.0, base=0, channel_multiplier=-1)

        def rstd_pow(out_ap, tmp_ap, ss_ap, npart, ncol, scale, rds, wrs):
            E("pool", "tensor_scalar", rds, wrs, out=tmp_ap, in0=ss_ap, scalar1=scale, scalar2=EPS, op0=ALU.mult, op1=ALU.add)
            E("pool", "tensor_tensor", wrs + [bmhalf], wrs, out=out_ap, in0=tmp_ap, in1=mhalf[0:npart, 0:ncol], op=ALU.pow)

        def load_w(dram, r0, nrows_chunks, c0, ncols, off, key):
            bufs = []
            for kc in range(nrows_chunks):
                for cc in range(0, ncols, 2048):
                    n = min(2048, ncols - cc)
                    bw = fb("W%s_%d_%d" % (key, kc, cc))
                    bufs.append(bw)
                    DMA("pool", "W" + key, WB[:, off + kc * ncols + cc: off + kc * ncols + cc + n],
                        dram[r0 + kc * 128: r0 + (kc + 1) * 128, c0 + cc: c0 + cc + n], [], [bw])
            return bufs

        def load_wg(dram, nrows_chunks, c0, groups, stride, off, kp, kbase=0):
            out = {}
            dv = dram.rearrange("(k p) n -> p k n", p=128)
            wv = WB[:, off: off + nrows_chunks * stride].rearrange("p (k c) -> p k c", k=nrows_chunks)
            for gi, (nm, l0, ncols) in enumerate(groups):
                bufs = []
                for cc in range(0, ncols, 2048):
                    n = min(2048, ncols - cc)
                    bw = fb("W%s_%s_%d" % (kp, nm, cc))
                    bufs.append(bw)
                    DMA("pool", "Wg%d" % (kbase + gi), wv[:, :, l0 + cc: l0 + cc + n],
                        dv[:, :, c0 + l0 + cc: c0 + l0 + cc + n], [], [bw])
                out[nm] = bufs
            return out

        def wview(off, nk, ncols):
            return WB[:, off: off + nk * ncols].rearrange("p (k c) -> p k c", k=nk)

        phase_begin()
        stg[:] = [cv("stg%d" % i, [128, 512], F32) for i in range(2)]
        cT, bcT = cv("cT", [128, 8, 3], F32)
        for s_ in range(3):
            DMA("sp", "cst", cT[:, :, s_], c3[s_].rearrange("(k p) -> p k", p=128), [], [bcT],
                allow_slow_non_contiguous=True)
        ACT(csT[:], cT[:], AF.Silu, [bcT], [bcsT])
        WAs = [cv("WA%d" % i, [128, 8, 512], BF16) for i in range(4)]
        w_ada_v = w_ada.rearrange("(k p) n -> p k n", p=128)
        bmods = {"A": Buf("modA"), "B": Buf("modB"), "C": Buf("modC")}

        def mod_group(nch):
            return "A" if nch < 4 else ("B" if nch < 6 else "C")

        def mod_chunk_load(nch, WA, bWA, key):
            DMA("pool", key, WA[:], w_ada_v[:, :, nch * 512:(nch + 1) * 512], [], [bWA])

        def mod_chunk_compute(nch, WA, bWA, pp, bpp):
            sg, bsg = next_stg()
            DMA("sp", "bad", sg[0:3, :], b_ada[0:1, nch * 512:(nch + 1) * 512].broadcast_to([3, 512]), [], [bsg])
            for kc in range(8):
                MM(pp[0:3, :], csT[:, kc, :], WA[:, kc, :], kc == 0, kc == 7, [bcsT, bWA], [bpp])
            E("dve", "tensor_tensor", [bpp, bsg], [bsg], out=sg[0:3, :], in0=pp[0:3, :], in1=sg[0:3, :], op=ALU.add)
            DMA("sp", "modw", mod_sc[:, nch * 512:(nch + 1) * 512], sg[0:3, :], [bsg], [bmods[mod_group(nch)]])

        for nch in range(4 if "0" in PH else 0):
            WA, bWA = WAs[nch]
            mod_chunk_load(nch, WA, bWA, "wa%d" % nch)
        sv_f = scr["fence"]
        scr["fence"] = []
        WL_1a = load_wg(w_in, 8, 0, [("q", 0, 512), ("k", 512, 512), ("v", 1024, 512)], 1536, 0, "a")
        WL_1a["a"] = load_wg(w_a, 4, 0, [("a", 0, 1024)], 1024, 12288, "wa", kbase=3)["a"]
        scr["fence"] = sv_f
        for nch in range(4 if "0" in PH else 0):
            WA, bWA = WAs[nch]
            mod_chunk_compute(nch, WA, bWA, pm, bpm)

        def load_mod(s, which, front=True, gate=True, gdst=None):
            base = 0 if which == 1 else 3 * D
            ng = norm1_g if which == 1 else norm2_g
            bf_ = bmods["A"] if which == 1 else bmods["C"]
            bg_ = bmods["B"] if which == 1 else bmods["C"]
            if front:
                DMA("sp", "modr", shb[:], mod_sc[s:s + 1, base:base + D].broadcast_to([128, D]), [bf_], [bshb])
                DMA("sp", "modr", gmb[:], mod_sc[s:s + 1, base + D:base + 2 * D].broadcast_to([128, D]), [bf_], [bgmb])
                DMA("sp", "modr", T2[:], ng[0:1, :].broadcast_to([128, D]), [], [bT2])
                E("dve", "scalar_tensor_tensor", [bgmb, bT2], [bgmb], out=gmb[:], in0=gmb[:], scalar=1.0, in1=T2[:],
                  op0=ALU.add, op1=ALU.mult)
            if gate:
                gd, bgd = gdst if gdst is not None else (ggb, bggb)
                DMA("sp", "modg", gd[:], mod_sc[s:s + 1, base + 2 * D:base + 3 * D].broadcast_to([128, D]), [bg_], [bgd])

        def rms_to_uT(xt, bxt, ntok, rstd_ap=None, dst=None):
            if rstd_ap is None:
                ACT(T2[0:ntok, :], xt[0:ntok, :], AF.Square, [bxt], [bT2, bssq], accum_out=ssq[0:ntok, 0:1])
                rstd_pow(ssq[0:ntok, 2:3], ssq[0:ntok, 1:2], ssq[0:ntok, 0:1], ntok, 1, 1.0 / D, [bssq], [bssq])
                rstd_ap = ssq[0:ntok, 2:3]
                rb = bssq
            else:
                rb = brstd2
            E("dve", "scalar_tensor_tensor", [bxt, rb, bgmb], [bT1], out=T1[0:ntok, :], in0=xt[0:ntok, :],
              scalar=rstd_ap, in1=gmb[0:ntok, :], op0=ALU.mult, op1=ALU.mult)
            E("pool", "tensor_tensor", [bT1, bshb], [bub], out=ub[0:ntok, :], in0=T1[0:ntok, :], in1=shb[0:ntok, :],
              op=ALU.add)
            for kc in range(8):
                TR(ptr[:, kc * 128: kc * 128 + ntok], ub[0:ntok, kc * 128:(kc + 1) * 128], identb[0:ntok, 0:ntok],
                   [bub, bidb], [bptr])
            uTd, buTd = dst if dst is not None else (uT, buT)
            ACT(uTd[:, :, 0:ntok], ptr[:].rearrange("p (k t) -> p k t", k=8)[:, :, 0:ntok], AF.Copy, [bptr], [buTd])

        epst, bepst = sb("epst", [128, 1], F32)
        E("pool", "memset", [], [bepst], epst[:], EPS)
        EPS_AP = epst

        def load_x(src, row0, ntok):
            rot["x"] ^= 1
            xt, bxt = xts[rot["x"]]
            DMA("sp", "xl%d" % rot["x"], xt[0:ntok, :], src[row0:row0 + ntok, :], [], [bxt])
            return xt, bxt

        def proj_tok(w3, c0, n, ntok, t0=0, wl=(), us=None):
            pp, bpp = next_ab()
            for kc in range(8):
                uTs_, buTs_ = us if us is not None else (uT, buT)
                MM(pp[0:ntok, 0:n], uTs_[:, kc, t0:t0 + ntok], w3[:, kc, c0:c0 + n], kc == 0, kc == 7, [buTs_] + list(wl), [bpp])
            return pp, bpp

        phase_begin()
        WL = WL_1a
        WLb_pre = load_wg(w_b, 4, 0, [("b", 0, 1024)], 1024, 32832, "wb", kbase=6)["b"]
        WLout_pre = load_wg(w_out, 8, 0, [("o", 0, 1024)], 1024, 36928, "wo", kbase=7)["o"]
        win_a = wview(0, 8, 1536)
        wa3 = wview(12288, 4, 1024)
        KTm = WB[:, 16384:24576].rearrange("p (c k) -> p c k", c=4)
        Vm = WB[:, 24576:32768].rearrange("p (t c) -> p t c", t=16)
        stor_main = dict(KT=KTm, Vst=Vm, bKT=[fb("KT%d" % i) for i in range(16)], bV=[fb("V%d" % i) for i in range(16)])
        KTa, _ = cv("KTalt", [128, 4, 1152], BF16)
        Va, _ = cv("Valt", [128, 9, 512], BF16)
        stor_alt = dict(KT=KTa, Vst=Va, bKT=[fb("KTa%d" % i) for i in range(9)], bV=[fb("Va%d" % i) for i in range(9)])
        stg[:] = [cv("stg%d" % i, [128, 512], F32) for i in range(2)]
        qTzs = [cv("qTz%d" % i, [128, 8, 128], BF16) for i in range(2)]
        for qz, bqz in qTzs:
            E("pool", "memset", [], [bqz], qz[:], 0.0)
        Ktok, bKtok = cv("Ktok", [128, 8, 512], BF16)
        NSL = 5
        att = [dict(g=cv("ag%d" % i, [128, 520], F32), Pb=cv("aP%d" % i, [128, 520], F32),
                    a=cv("aa%d" % i, [128, 512], BF16),
                    aT=cv("aaT%d" % i, [128, 4, 128], BF16)) for i in range(NSL)]
        ONE_REG = nc.gpsimd.to_reg(1.0)
        zerob, bzerob = cv("zerob", [128, 520], BF16)
        E("pool", "memset", [], [bzerob], zerob[:], 0.0)
        for A_ in att:
            E("pool", "memset", [], [A_["g"][1]], A_["g"][0][:], 1.0)
        ya, bya = cv("ya", [128, 512], BF16)
        yaT, byaT = cv("yaT", [128, 4, 128], BF16)
        Ast, bAst = cv("Ast", [128, D], F32)
        WAbg, bWAbg = cv("WAbg", [128, 8, 512], BF16)
        pzv = [pz2[:, 0:512], pz2[:, 512:1024]]
        patv2 = [(pat, bpat), (pm[:].bitcast(BF16), bpm)]
        job_ctr = [0]
        Abuf = {}
        x1buf = {}
        seq_loaded = [-1]

        def prologue_pieces(tile, tno, stor, first_of_seq):
            (s, row0, ntok, ti) = tile
            kpos0 = ti * 128 if s == 0 else PAST
            ktile = ti if s == 0 else 8
            qTz, bqTz = qTzs[tno % 2]
            KT, Vst, bKT, bV = stor["KT"], stor["Vst"], stor["bKT"], stor["bV"]
            cx = dict(s=s, row0=row0, ntok=ntok, kpos0=kpos0, ktile=ktile, qTz=qTz, bqTz=bqTz, stor=stor)
            hold = {}

            xslot = tno % 2
            xt, bxt = xts[xslot]
            hold = {}

            def PL():
                if first_of_seq:
                    load_mod(s, 1, gate=False)
                    if s > 0:
                        si = s - 1
                        DMA("pool", "kvc", Vst[:, 0:8, :], cache_v[si].rearrange("(k p) c -> p k c", p=128), [], bV[0:8])
                        DMA("pool", "kvc", Ktok[:], cache_k[si].rearrange("(k p) c -> p k c", p=128), [], [bKtok])
                DMA("sp", "xl%d" % xslot, xt[0:ntok, :], xall[row0:row0 + ntok, :], [], [bxt])

            def PK():
                for kt in range(8):
                    for c in range(4):
                        TR(ptr[:, c * 128:(c + 1) * 128], Ktok[:, kt, c * 128:(c + 1) * 128], identb[:],
                           [bKtok, bidb], [bptr])
                    ACT(KT[:, :, kt * 128:(kt + 1) * 128], ptr[:, 0:512].rearrange("p (c t) -> p c t", c=4), AF.Copy,
                        [bptr], [bKT[kt]])

            def PA():
                ACT(T2[0:ntok, :], xt[0:ntok, :], AF.Square, [bxt], [bT2, bssq], accum_out=ssq[0:ntok, 0:1])
                rstd_pow(ssq[0:ntok, 2:3], ssq[0:ntok, 1:2], ssq[0:ntok, 0:1], ntok, 1, 1.0 / D, [bssq], [bssq])

            def PB():
                E("dve", "scalar_tensor_tensor", [bxt, bssq, bgmb], [bT1], out=T1[0:ntok, :], in0=xt[0:ntok, :],
                  scalar=ssq[0:ntok, 2:3], in1=gmb[0:ntok, :], op0=ALU.mult, op1=ALU.mult)

            def PC():
                E("pool", "tensor_tensor", [bT1, bshb], [bub], out=ub[0:ntok, :], in0=T1[0:ntok, :], in1=shb[0:ntok, :],
                  op=ALU.add)

            def PD():
                for kc in range(8):
                    TR(ptr[:, kc * 128: kc * 128 + ntok], ub[0:ntok, kc * 128:(kc + 1) * 128], identb[0:ntok, 0:ntok],
                       [bub, bidb], [bptr])
                ACT(uT[:, :, 0:ntok], ptr[:].rearrange("p (k t) -> p k t", k=8)[:, :, 0:ntok], AF.Copy, [bptr], [buT])

            def Q1():
                pp, bpp = next_ab()
                hold["q"] = (pp, bpp)
                for c in range(4):
                    for kc in range(8):
                        MM(pp[:, c * 128: c * 128 + ntok], win_a[:, kc, c * 128:(c + 1) * 128], uT[:, kc, 0:ntok],
                           kc == 0, kc == 7, [buT] + WL["q"], [bpp])
                ppv = pp[:].rearrange("p (c t) -> p c t", c=4)
                qv = qTz[:].rearrange("p (c two) t -> p c two t", two=2)
                ACT(qv[0:64, :, 0, 0:ntok], ppv[0:64, :, 0:ntok], AF.Copy, [bpp], [bqTz])
                E("dve", "tensor_copy", [bpp], [bqTz], out=qv[64:128, :, 1, 0:ntok], in_=ppv[64:128, :, 0:ntok])

            def K1():
                pp, bpp = next_ab()
                for c in range(4):
                    for kc in range(8):
                        MM(pp[:, c * 128: c * 128 + ntok], win_a[:, kc, 512 + c * 128: 512 + (c + 1) * 128], uT[:, kc, 0:ntok],
                           kc == 0, kc == 7, [buT] + WL["k"], [bpp])
                ACT(KT[:, :, kpos0:kpos0 + ntok], pp[:].rearrange("p (c t) -> p c t", c=4)[:, :, 0:ntok], AF.Copy,
                    [bpp], [bKT[ktile]])

            def K2():
                pp, bpp = proj_tok(win_a, 512, 512, ntok, wl=WL["k"])
                sg, bsg = next_stg()
                E("dve", "tensor_copy", [bpp], [bsg], out=sg[0:ntok, :], in_=pp[0:ntok, :])
                DMA("sp", "ko", k_all[row0:row0 + ntok, :], sg[0:ntok, :], [bsg], [])

            def V1():
                pp, bpp = proj_tok(win_a, 1024, 512, ntok, wl=WL["v"])
                sg, bsg = next_stg()
                E("dve", "tensor_copy", [bpp], [bsg], out=sg[0:ntok, :], in_=pp[0:ntok, :])
                ACT(Vst[0:ntok, ktile, :], pp[0:ntok, :], AF.Copy, [bpp], [bV[ktile]])
                DMA("sp", "vo", v_all[row0:row0 + ntok, :], sg[0:ntok, :], [bsg], [])

            pk = PK if (first_of_seq and s > 0) else None
            return cx, [PL, None, pk, PA, PB, PC, None, PD, None, Q1, None, K1, None, K2, None, V1]

        def attention_jobs(cx):
            ntok, kpos0, qTz, bqTz = cx["ntok"], cx["kpos0"], cx["qTz"], cx["bqTz"]
            KT, Vst, bKT, bV = cx["stor"]["KT"], cx["stor"]["Vst"], cx["stor"]["bKT"], cx["stor"]["bV"]
            nk = kpos0 + ntok
            nblk = (nk + 511) // 512
            jobs = []
            for h in range(8):
                prev = None
                for b in range(nblk - 1, -1, -1):
                    job_ctr[0] += 1
                    J = dict(h=h, b=b, slot=job_ctr[0] % NSL, zi=job_ctr[0] % 2, prev=prev,
                             first=(b == nblk - 1), last=(b == 0))
                    jobs.append(J)
                    prev = J

            def geo(J):
                kb0 = J["b"] * 512
                nkb = min(512, nk - kb0)
                kts = list(range(kb0 // 128, (kb0 + nkb + 127) // 128))
                return kb0, nkb, kts

            def S0(J):
                kb0, nkb, kts = geo(J)
                MM(pzv[J["zi"]][0:ntok, 0:nkb], qTz[:, J["h"], 0:ntok], KT[:, J["h"] // 2, kb0:kb0 + nkb], True, not J["first"],
                   [bqTz] + [bKT[k] for k in kts], [bpz[J["zi"]]])
                if J["first"]:
                    MM(pzv[J["zi"]][0:ntok, nkb - ntok:nkb], identb[0:ntok, 0:ntok], cmask[0:ntok, 0:ntok], False, True,
                       [bidb, bcmask], [bpz[J["zi"]]])

            def S1(J):
                kb0, nkb, kts = geo(J)
                g_, bg = att[J["slot"]]["g"]
                pz = pzv[J["zi"]]
                ACT(g_[0:ntok, 512 - nkb:512], pz[0:ntok, 0:nkb], AF.Sigmoid, [bpz[J["zi"]]], [bg], scale=-0.125)

            def S2(J):
                kb0, nkb, kts = geo(J)
                A_ = att[J["slot"]]
                (g_, bg), (Pb, bP) = A_["g"], A_["Pb"]
                if J["prev"] is None:
                    init = 1.0
                    rd = [bg, bzerob]
                else:
                    pPb, bpP = att[J["prev"]["slot"]]["Pb"]
                    pk = geo(J["prev"])[1]
                    init = pPb[0:ntok, 512 - pk:512 - pk + 1]
                    rd = [bg, bzerob, bpP]
                E("dve", "tensor_tensor_scan", rd, [bP], out=Pb[0:ntok, 512 - nkb:513][:, ::-1],
                  data0=g_[0:ntok, 512 - nkb:513][:, ::-1], data1=zerob[0:ntok, 0:nkb + 1], initial=init,
                  op0=ALU.mult, op1=ALU.add)

            def S3(J):
                pass

            def S4(J):
                kb0, nkb, kts = geo(J)
                A_ = att[J["slot"]]
                (Pb, bP), (a_, ba) = A_["Pb"], A_["a"]
                E("pool", "tensor_tensor", [bP], [ba], out=a_[0:ntok, 0:nkb], in0=Pb[0:ntok, 512 - nkb + 1:513],
                  in1=Pb[0:ntok, 512 - nkb:512], op=ALU.subtract)

            def S5(J):
                kb0, nkb, kts = geo(J)
                a_, ba = att[J["slot"]]["a"]
                pT, bpT = patv2[J["zi"]]
                for j, kt in enumerate(kts):
                    ksz = min(128, nk - kt * 128)
                    TR(pT[0:ksz, j * 128: j * 128 + ntok], a_[0:ntok, j * 128: j * 128 + ksz],
                       identb[0:ntok, 0:ntok], [ba, bidb], [bpT])

            def S6(J):
                kb0, nkb, kts = geo(J)
                aT_, baT = att[J["slot"]]["aT"]
                pT, bpT = patv2[J["zi"]]
                nsub = len(kts)
                pv = pT[:, 0:512].rearrange("p (j t) -> p j t", j=4)
                lastk = min(128, nk - kts[-1] * 128)
                if lastk == 128:
                    ACT(aT_[:, 0:nsub, 0:ntok], pv[:, 0:nsub, 0:ntok], AF.Copy, [bpT], [baT])
                else:
                    if nsub > 1:
                        ACT(aT_[:, 0:nsub - 1, 0:ntok], pv[:, 0:nsub - 1, 0:ntok], AF.Copy, [bpT], [baT])
                    ACT(aT_[0:lastk, nsub - 1, 0:ntok], pv[0:lastk, nsub - 1, 0:ntok], AF.Copy, [bpT], [baT])

            def S7(J):
                kb0, nkb, kts = geo(J)
                aT_, baT = att[J["slot"]]["aT"]
                h = J["h"]
                nsub = len(kts)
                for j, kt in enumerate(kts):
                    ksz = min(128, nk - kt * 128)
                    MM(py[0:ntok, h * 64:(h + 1) * 64], aT_[0:ksz, j, 0:ntok], Vst[0:ksz, kt, h * 64:(h + 1) * 64],
                       J["first"] and j == 0, J["last"] and j == nsub - 1, [baT, bV[kt]], [bpy])

            return jobs, [S0, S1, S2, S3, S4, S5, S6, S7]

        def epilogue_pieces(cx):
            ntok, row0 = cx["ntok"], cx["row0"]

            def E0():
                ACT(ya[0:ntok, :], py[0:ntok, :], AF.Copy, [bpy], [bya])

            def E1():
                for c in range(4):
                    TR(ptr[:, c * 128: c * 128 + ntok], ya[0:ntok, c * 128:(c + 1) * 128], identb[0:ntok, 0:ntok],
                       [bya, bidb], [bptr])
                ACT(yaT[:, :, 0:ntok], ptr[:, 0:512].rearrange("p (c t) -> p c t", c=4)[:, :, 0:ntok], AF.Copy, [bptr], [byaT])

            def E2():
                for n in range(2):
                    pp, bpp = next_ab()
                    for c in range(4):
                        MM(pp[0:ntok, :], yaT[:, c, 0:ntok], wa3[:, c, n * 512:(n + 1) * 512], c == 0, c == 3,
                           [byaT] + WL["a"], [bpp])
                    E("dve", "tensor_copy", [bpp], [bAst], out=Ast[0:ntok, n * 512:(n + 1) * 512], in_=pp[0:ntok, :])
                bAsc = Buf("Asc%d" % row0)
                DMA("sp", "Aw", A_sc[row0:row0 + ntok, :], Ast[0:ntok, :], [bAst], [bAsc])
                Abuf[row0] = bAsc

            return [E0, E1, E2]

        if "1a" in PH:
            items = []
            jbase = 0
            starts = []
            njs = []
            seq_idx = -1
            prev_s = None
            for li_, tile in enumerate(tiles):
                first_of_seq = (tile[0] != prev_s)
                if first_of_seq:
                    seq_idx += 1
                    prev_s = tile[0]
                stor = stor_main if seq_idx % 2 == 0 else stor_alt
                cx, pieces = prologue_pieces(tile, li_, stor, first_of_seq)
                if li_ == 0:
                    for i_, pf in enumerate(pieces):
                        if pf is not None:
                            items.append((-100 + i_, 9.0, pf))
                else:
                    pst = starts[li_ - 1] + 1
                    if first_of_seq:
                        pst = max(pst, stor.get("last_step", -1) + 1)
                    pend = starts[li_ - 1] + njs[li_ - 1] - 1
                    avail = max(1, pend - pst)
                    L_ = len(pieces)
                    for i_, pf in enumerate(pieces):
                        if pf is not None:
                            items.append((pst + (i_ * avail) // L_, 9.0 + i_ * 0.01, pf))
                jobs, stages = attention_jobs(cx)
                starts.append(jbase)
                njs.append(len(jobs))
                NSTG = len(stages)
                for ji, J in enumerate(jobs):
                    for k, Sf in enumerate(stages):
                        items.append((jbase + ji + k, float(NSTG - 1 - k), (lambda Sf=Sf, J=J: Sf(J))))
                last_step = jbase + len(jobs) - 1 + (NSTG - 1)
                stor["last_step"] = last_step
                ep = epilogue_pieces(cx)
                items.append((last_step, 0.5, ep[0]))
                items.append((last_step + 2, 8.5, ep[1]))
                items.append((last_step + 4, 8.6, ep[2]))
                jbase += len(jobs)
            if "0" in PH:
                nsteps = jbase + 8
                gap = max(14, (nsteps - 30) // 8)
                for bi, nch in enumerate(range(4, 12)):
                    t_ = 10 + bi * gap
                    items.append((t_, 9.5, (lambda nch=nch: mod_chunk_load(nch, WAbg, bWAbg, "wabg"))))

                    def comp(nch=nch):
                        pp, bpp = next_ab()
                        mod_chunk_compute(nch, WAbg, bWAbg, pp, bpp)
                    items.append((t_ + 10, 9.6, comp))
            order = sorted(range(len(items)), key=lambda i: (items[i][0], items[i][1], i))
            for i in order:
                items[i][2]()

        phase_begin()
        NB = 4104
        WL = load_wg(w_in, 8, 1536, [("mqk", 0, 1024), ("gt", 2048, 8), ("mv", 1024, 512), ("mo", 1536, 512),
                                     ("gb", 3080, 1024), ("ga", 2056, 1024)], NB, 0, "b")
        WL["b"] = WLb_pre
        WL["out"] = WLout_pre
        winb = wview(0, 8, NB)
        wb3 = wview(32832, 4, 1024)
        wo3 = wview(36928, 8, 1024)
        wcv, bwcv = cv("wcv", [128, 8, 4], F32)
        bcv, bbcv = cv("bcv", [128, 8], F32)
        mlg, bmlg = cv("mlg", [64, 512], F32)
        bifi, bbifi = cv("bifi", [4, 1], F32)
        biff, bbiff = cv("biff", [4, 1], F32)
        for j_ in range(4):
            DMA("sp", "cst", wcv[:, :, j_], w_conv[j_].rearrange("(c p) -> p c", p=128), [], [bwcv], allow_slow_non_contiguous=True)
        DMA("sp", "cst", bcv[:], b_conv[0].rearrange("(c p) -> p c", p=128), [], [bbcv], allow_slow_non_contiguous=True)
        DMA("sp", "cst", mlg[:], ml_norm_g[0:1, :].broadcast_to([64, 512]), [], [bmlg])
        DMA("sp", "cst", bifi[:], b_if[0:4, :], [], [bbifi])
        DMA("sp", "cst", biff[:], b_if[4:8, :], [], [bbiff])
        SC_ = 5
        CS_ = 2
        xs3 = [xts[0], xts[1], cv("xt2", [128, D], F32)]
        uT3 = [(uT, buT), cv("uT1", [128, 8, 128], BF16), cv("uT2", [128, 8, 128], BF16)]
        mqk2 = [cv("mqkT%d" % i, [128, 8, 128], BF16) for i in range(2)]
        xp2 = [cv("xp%d" % i, [128, 8, 131], F32) for i in range(2)]
        gl2 = [dict(li=cv("gli%d" % i, [4, 128], F32), sf=cv("gsf%d" % i, [4, 128], F32),
                    lf=cv("glf%d" % i, [4, 128], F32)) for i in range(2)]
        ybT2 = [cv("ybT%d" % i, [128, 4, 128], BF16) for i in range(2)]
        At, bAt = cv("At", [128, D], F32)
        sgt, bsgt = cv("sgt", [128, 512], F32)
        mg, bmg = cv("mg", [128, D], BF16)
        mT, bmT = cv("mT", [128, 8, 128], BF16)
        CTs = [cv("CT%d" % i, [128, 4, 129], F32) for i in range(3)]
        ggbs = [(ggb, bggb), cv("ggb1", [128, D], F32)]
        C0t, bC0t = At[:, 0:512].rearrange("p (h k) -> p h k", h=4), bAt
        mseq, bmseq = cv("mseq", [4, 40], F32)
        CK = []
        for i in range(CS_):
            d_ = {}
            for nm in ("bb", "rr", "mmt", "wg", "wi", "emt"):
                d_[nm] = cv("g_%s%d" % (nm, i), [4, 64], F32)
            d_["dec"] = cv("g_dec%d" % i, [4, 128], F32)
            d_["gsm"] = cv("gsm%d" % i, [4, 8], F32)
            d_["gtok"] = cv("gtok%d" % i, [128, 16], F32)
            d_["Wt"] = cv("Wt%d" % i, [64, 4, 64], F32)
            d_["vaug"] = cv("vaug%d" % i, [64, 4, 129], BF16)
            d_["vw"] = cv("vw%d" % i, [64, 4, 129], BF16)
            d_["smo"] = cv("smo%d" % i, [64, 512], F32)
            d_["ktok"] = cv("ktok%d" % i, [64, 4, 128], BF16)
            d_["ST"] = cv("ST%d" % i, [64, 4, 64], BF16)
            d_["qsT"] = cv("qsT%d" % i, [128, 4, 64], BF16)
            d_["CTb"] = cv("CTb%d" % i, [128, 4, 129], BF16)
            d_["hn"] = cv("hn%d" % i, [64, 4, 129], F32)
            d_["hs"] = cv("hs%d" % i, [64, 16], F32)
            d_["T1h"] = cv("T1h%d" % i, [64, 4, 128], F32)
            d_["ybt"] = cv("ybt%d" % i, [64, 512], BF16)
            E("pool", "memset", [], [d_["vaug"][1]], d_["vaug"][0][:], 1.0)
            CK.append(d_)
        patf = pat[:, 512:1024].bitcast(F32)
        chunk_ctr = [0]
        gchunk = [0]

        def out_state(s, CT, bCT, mcol_final):
            def f():
                for h in range(4):
                    TR(pa[:, h * 128:(h + 1) * 128], CT[:, h, 0:128], identf[:], [bCT, bidf], [bpa])
                E("dve", "tensor_copy", [bpa], [bC0t], out=C0t, in_=pa[:].rearrange("p (h k) -> p h k", h=4))
                DMA("sp", "sto", C_out[s].rearrange("h v k -> v h k"), C0t, [bC0t], [])
                DMA("sp", "sto", n_out[s].rearrange("h k -> k h"), CT[:, :, 128], [bCT], [], allow_slow_non_contiguous=True)
                DMA("sp", "sto", m_out[s:s + 1, :].rearrange("o h -> h o"), mseq[:, mcol_final:mcol_final + 1],
                    [bmseq], [], allow_slow_non_contiguous=True)
            return f

        def out_conv(s, xp_last):
            def f():
                xpl, bxpl, nt_l = xp_last
                for j_ in range(3):
                    DMA("sp", "sto", conv_out[s, j_].rearrange("(c p) -> p c", p=128), xpl[:, :, nt_l + j_], [bxpl], [],
                        allow_slow_non_contiguous=True)
            return f

        def sched_tile(items, lt, tix, tile, prev_xp, sq):
            (s, row0, ntok, ti) = tile
            xt, bxt = xs3[lt % 3]
            uTs, buTs = uT3[lt % 3]
            mqkT, bmqk = mqk2[lt % 2]
            xp, bxp = xp2[lt % 2]
            G_ = gl2[lt % 2]
            (li, bli), (sf, bsf), (lf, blf) = G_["li"], G_["sf"], G_["lf"]
            ybT, bybT = ybT2[lt % 2]
            nch = ntok // 64
            base = 2 * lt * SC_
            CT, bCT = sq["CT"]
            gg_, bgg_ = sq["gg"]
            T1v = T1[:].rearrange("p (c t) -> p c t", c=8)
            T2v = T2[:].rearrange("p (c t) -> p c t", c=8)

            def F0():
                DMA("sp", "xl%d" % (lt % 3), xt[0:ntok, :], xall[row0:row0 + ntok, :], [], [bxt])
                ACT(T2[0:ntok, :], xt[0:ntok, :], AF.Square, [bxt], [bT2, bssq], accum_out=ssq[0:ntok, 0:1])
                rstd_pow(ssq[0:ntok, 2:3], ssq[0:ntok, 1:2], ssq[0:ntok, 0:1], ntok, 1, 1.0 / D, [bssq], [bssq])
                E("dve", "scalar_tensor_tensor", [bxt, bssq, bgmb], [bT1], out=T1[0:ntok, :], in0=xt[0:ntok, :],
                  scalar=ssq[0:ntok, 2:3], in1=gmb[0:ntok, :], op0=ALU.mult, op1=ALU.mult)
                E("pool", "tensor_tensor", [bT1, bshb], [bub], out=ub[0:ntok, :], in0=T1[0:ntok, :], in1=shb[0:ntok, :],
                  op=ALU.add)

            def F1():
                for kc in range(8):
                    TR(ptr[:, kc * 128: kc * 128 + ntok], ub[0:ntok, kc * 128:(kc + 1) * 128], identb[0:ntok, 0:ntok],
                       [bub, bidb], [bptr])
                ACT(uTs[:, :, 0:ntok], ptr[:].rearrange("p (k t) -> p k t", k=8)[:, :, 0:ntok], AF.Copy, [bptr], [buTs])

            def F2(half):
                def f():
                    if half == 0 and prev_xp is not None:
                        pxp, bpxp, pnt = prev_xp
                        E("pool", "tensor_copy", [bpxp], [bxp], out=xp[:, :, 0:3], in_=pxp[:, :, pnt:pnt + 3])
                    pp, bpp = next_ab()
                    for c4 in range(4):
                        ch = half * 4 + c4
                        for kc in range(8):
                            MM(pp[:, c4 * 128: c4 * 128 + ntok], winb[:, kc, ch * 128:(ch + 1) * 128], uTs[:, kc, 0:ntok],
                               kc == 0, kc == 7, [buTs] + WL["mqk"], [bpp])
                    ACT(xp[:, half * 4:(half + 1) * 4, 3:3 + ntok], pp[:].rearrange("p (c t) -> p c t", c=4)[:, :, 0:ntok],
                        AF.Copy, [bpp], [bxp])
                return f

            def F3(p):
                def f():
                    for ch in (2 * p, 2 * p + 1):
                        E("dve", "tensor_scalar", [bxp, bwcv, bbcv], [bT1], out=T1v[:, ch, 0:ntok], in0=xp[:, ch, 0:ntok],
                          scalar1=wcv[:, ch, 0:1], scalar2=bcv[:, ch:ch + 1], op0=ALU.mult, op1=ALU.add)
                        for j in range(1, 4):
                            E("dve", "scalar_tensor_tensor", [bxp, bwcv, bT1], [bT1], out=T1v[:, ch, 0:ntok],
                              in0=xp[:, ch, j:j + ntok], scalar=wcv[:, ch, j:j + 1], in1=T1v[:, ch, 0:ntok],
                              op0=ALU.mult, op1=ALU.add)

                def fpool():
                    tmpf = ub[:, 0:256].bitcast(F32)
                    for ch in (2 * p, 2 * p + 1):
                        E("pool", "tensor_scalar", [bxp, bwcv, bbcv], [bT1], out=T1v[:, ch, 0:ntok], in0=xp[:, ch, 0:ntok],
                          scalar1=wcv[:, ch, 0:1], scalar2=bcv[:, ch:ch + 1], op0=ALU.mult, op1=ALU.add)
                        for j in range(1, 4):
                            E("pool", "tensor_scalar", [bxp, bwcv], [bub], out=tmpf[:, 0:ntok], in0=xp[:, ch, j:j + ntok],
                              scalar1=wcv[:, ch, j:j + 1], scalar2=0.0, op0=ALU.mult, op1=ALU.add)
                            E("pool", "tensor_tensor", [bub, bT1], [bT1], out=T1v[:, ch, 0:ntok], in0=tmpf[:, 0:ntok],
                              in1=T1v[:, ch, 0:ntok], op=ALU.add)
                return fpool if p >= 2 else f

            def F4():
                ACT(T2v[:, :, 0:ntok], T1v[:, :, 0:ntok], AF.Sigmoid, [bT1], [bT2])
                E("dve", "tensor_tensor", [bT1, bT2], [bmqk], out=mqkT[:, 0:4, 0:ntok], in0=T1v[:, 0:4, 0:ntok],
                  in1=T2v[:, 0:4, 0:ntok], op=ALU.mult)
                E("dve", "scalar_tensor_tensor", [bT1, bT2], [bmqk], out=mqkT[:, 4:8, 0:ntok], in0=T1v[:, 4:8, 0:ntok],
                  scalar=float(1.0 / np.sqrt(128.0)), in1=T2v[:, 4:8, 0:ntok], op0=ALU.mult, op1=ALU.mult)

            def F5():
                for kc in range(8):
                    MM(pm[0:4, 0:ntok], winb[:, kc, 2048:2052], uTs[:, kc, 0:ntok], kc == 0, kc == 7, [buTs] + WL["gt"], [bpm])
                ACT(li[:, 0:ntok], pm[0:4, 0:ntok], AF.Identity, [bpm, bbifi], [bli], bias=bifi[:])
                for kc in range(8):
                    MM(pm[0:4, 128:128 + ntok], winb[:, kc, 2052:2056], uTs[:, kc, 0:ntok], kc == 0, kc == 7,
                       [buTs] + WL["gt"], [bpm])
                ACT(sf[:, 0:ntok], pm[0:4, 128:128 + ntok], AF.Sigmoid, [bpm, bbiff], [bsf], bias=biff[:])
                ACT(lf[:, 0:ntok], sf[:, 0:ntok], AF.Ln, [bsf], [blf])

            fl = [(0, F0), (1, F1), (2, F2(0)), (3, F2(1)), (3.5, F5), (4, F3(0)), (5, F3(1)), (6, F3(2)), (7, F3(3)),
                  (8, F4)]
            for j, f in fl:
                items.append((base + j, f))

            def chunk_stages(c):
                cs = slice(c * 64, (c + 1) * 64)
                g = gchunk[0]
                gchunk[0] += 1
                K = CK[g % CS_]
                Kn = CK[(g + 1) % CS_]
                Kp = CK[(g - 1) % CS_]
                mc = chunk_ctr[0]
                chunk_ctr[0] += 1
                mcur = mseq[:, mc:mc + 1]
                (bb_, bbb), (rr, brr), (mmt, bmmt), (wg, bwg), (wi, bwi), (emt, bemt), (dec, bdec) = (
                    K["bb"], K["rr"], K["mmt"], K["wg"], K["wi"], K["emt"], K["dec"])
                (gsm, bgsm), (gtok, bgtok), (Wt, bWt), (vaug, bvaug), (vw, bvw), (smo, bsmo) = (
                    K["gsm"], K["gtok"], K["Wt"], K["vaug"], K["vw"], K["smo"])
                (ktok, bktok), (STt, bST), (qsT, bqsT), (CTb, bCTb), (hn, bhn), (hs, bhs) = (
                    K["ktok"], K["ST"], K["qsT"], K["CTb"], K["hn"], K["hs"])
                hh, bhh = hn[:, :, 0:128], bhn
                (T1h, bT1h), (ybt, bybt) = K["T1h"], K["ybt"]
                mx2 = mmt[:, 63:64]

                def G0():
                    E("dve", "tensor_tensor_scan", [blf, bones], [bbb], out=bb_[:, :], data0=onesb[0:4, 0:64], data1=lf[:, cs],
                      initial=0.0, op0=ALU.mult, op1=ALU.add)
                    E("dve", "tensor_tensor", [bli, bbb], [brr], out=rr[:, :], in0=li[:, cs], in1=bb_[:, :], op=ALU.subtract)
                    E("dve", "tensor_tensor_scan", [brr, bones, bmseq], [bmmt], out=mmt[:, :], data0=onesb[0:4, 0:64],
                      data1=rr[:, :], initial=mcur, op0=ALU.mult, op1=ALU.max)
                    E("dve", "tensor_scalar", [bmmt], [bgsm], out=gsm[:, 0:1], in0=mx2, scalar1=-1.0, scalar2=None, op0=ALU.mult)
                    E("dve", "tensor_tensor", [bmseq, bmmt], [bgsm], out=gsm[:, 1:2], in0=mcur, in1=mx2, op=ALU.subtract)
                    E("dve", "tensor_tensor", [bbb, bmmt], [bmseq], out=mseq[:, mc + 1:mc + 2], in0=bb_[:, 63:64],
                      in1=mx2, op=ALU.add)
                    E("dve", "tensor_tensor", [bbb, bmmt], [bemt], out=emt[:, :], in0=bb_[:, :], in1=mmt[:, :], op=ALU.add)

                def G1():
                    ACT(wg[:, :], rr[:, :], AF.Exp, [brr, bgsm], [bwg], bias=gsm[:, 0:1])
                    ACT(dec[:, :], zer[0:4, :], AF.Exp, [bzer, bgsm], [bdec], bias=gsm[:, 1:2])
                    ACT(wi[:, :], mmt[:, :], AF.Exp, [bmmt, bmseq], [bwi], scale=-1.0, bias=mcur)
                    ACT(emt[:, :], emt[:, :], AF.Exp, [bemt], [bemt], scale=-1.0)

                def G2():
                    TR(pm[0:64, 256:260], rr[:, :], identf[0:4, 0:4], [brr, bidf], [bpm])
                    TR(pm[0:64, 260:264], emt[:, :], identf[0:4, 0:4], [bemt, bidf], [bpm])
                    TR(pm[0:64, 264:268], wg[:, :], identf[0:4, 0:4], [bwg, bidf], [bpm])
                    TR(pm[0:128, 268:272], dec[:, :], identf[0:4, 0:4], [bdec, bidf], [bpm])
                    E("dve", "tensor_copy", [bpm], [bgtok], out=gtok[0:64, 0:12], in_=pm[0:64, 256:268])
                    E("dve", "tensor_copy", [bpm], [bgtok], out=gtok[:, 12:16], in_=pm[:, 268:272])

                def G3():
                    for h in range(4):
                        MM(py[0:64, h * 64:(h + 1) * 64], sel[:, h, 0:64], mmt[:, :], True, True, [bsel, bmmt], [bpy])
                    for h in range(4):
                        MM(py[:, 256 + h * 64: 256 + (h + 1) * 64], sel[:, h, :], wi[:, :], True, True, [bsel, bwi], [bpy])
                    for h in range(4):
                        ACT(Wt[:, h, :], py[0:64, h * 64:(h + 1) * 64], AF.Exp, [bpy, bgtok], [bWt], scale=-1.0,
                            bias=gtok[0:64, h:h + 1])
                    E("dve", "tensor_tensor", [bpy, bmqk], [bqsT], out=qsT[:], in0=py[:, 256:512].rearrange("p (h t) -> p h t", h=4),
                      in1=mqkT[:, 0:4, cs], op=ALU.mult)
                    E("pool", "affine_select", [bWt], [bWt], out=Wt[:], in_=Wt[:], pattern=[[0, 4], [1, 64]],
                      compare_op=ALU.is_ge, fill=0.0, base=0, channel_multiplier=-1)

                def V0a():
                    if nch == 2 and c == 1:
                        return
                    pp, bpp = next_ab()
                    m_ = ntok if nch == 2 else 64
                    for kc in range(8):
                        MM(pp[0:m_, :], uTs[:, kc, 0:m_], winb[:, kc, 1024:1536], kc == 0, kc == 7, [buTs] + WL["mv"], [bpp])
                    E("dve", "tensor_copy", [bpp], [bvaug], out=vaug[:, :, 0:128], in_=pp[0:64, :].rearrange("p (h d) -> p h d", h=4))
                    if nch == 2:
                        vn, bvn = Kn["vaug"]
                        ACT(vn[:, :, 0:128], pp[64:128, :].rearrange("p (h d) -> p h d", h=4), AF.Copy, [bpp], [bvn])

                def V0b():
                    if nch == 2 and c == 0:
                        return
                    pp, bpp = next_ab()
                    m_ = ntok if nch == 2 else 64
                    for kc in range(8):
                        MM(pp[0:m_, :], uTs[:, kc, 0:m_], winb[:, kc, 1536:2048], kc == 0, kc == 7, [buTs] + WL["mo"], [bpp])
                    if nch == 2:
                        sp_, bsp_ = Kp["smo"]
                        ACT(sp_[:], pp[0:64, :], AF.Sigmoid, [bpp], [bsp_])
                        E("pool", "tensor_tensor", [bsp_, bmlg], [bsp_], out=sp_[:], in0=sp_[:], in1=mlg[:], op=ALU.mult)
                        ACT(smo[:], pp[64:128, :], AF.Sigmoid, [bpp], [bsmo])
                    else:
                        ACT(smo[:], pp[0:64, :], AF.Sigmoid, [bpp], [bsmo])
                    E("pool", "tensor_tensor", [bsmo, bmlg], [bsmo], out=smo[:], in0=smo[:], in1=mlg[:], op=ALU.mult)

                def V0c():
                    for h in range(4):
                        TR(pat[0:64, h * 128:(h + 1) * 128], mqkT[:, 4 + h, cs], identb[:], [bmqk, bidb], [bpat])
                    E("dve", "tensor_copy", [bpat], [bktok], out=ktok[:], in_=pat[0:64, 0:512].rearrange("p (h d) -> p h d", h=4))

                def U():
                    E("pool", "tensor_copy", [bCT], [bCTb], out=CTb[:], in_=CT[:])
                    E("dve", "tensor_tensor", [bvaug, bgtok], [bvw], out=vw[:], in0=vaug[:],
                      in1=gtok[0:64, 8:12].unsqueeze(2).broadcast_to([64, 4, 129]), op=ALU.mult)
                    for h in range(4):
                        o0 = 512 * (h // 2) + 129 * (h % 2)
                        MM(pz2[:, o0:o0 + 129], ktok[:, h, :], vw[:, h, :], True, True, [bktok, bvw], [bpz[0], bpz[1]])
                    for h in range(4):
                        o0 = 512 * (h // 2) + 129 * (h % 2)
                        E("dve", "scalar_tensor_tensor", [bCT, bgtok, bpz[0], bpz[1]], [bCT], out=CT[:, h, :], in0=CT[:, h, :],
                          scalar=gtok[:, 12 + h:13 + h], in1=pz2[:, o0:o0 + 129], op0=ALU.mult, op1=ALU.add)

                def V1():
                    for h in range(4):
                        MM(patf[0:64, h * 64:(h + 1) * 64], mqkT[:, 4 + h, cs], mqkT[:, h, cs], True, True, [bmqk], [bpat])
                    E("dve", "tensor_tensor", [bpat, bWt], [bST], out=STt[:], in0=patf[0:64, 0:256].rearrange("p (h t) -> p h t", h=4),
                      in1=Wt[:], op=ALU.mult)

                def N0():
                    for h in range(4):
                        o0 = 512 * (h // 2) + 129 * (h % 2)
                        MM(pz2[0:64, o0:o0 + 129], STt[:, h, :], vaug[:, h, :], True, False, [bST, bvaug], [bpz[0], bpz[1]])
                        MM(pz2[0:64, o0:o0 + 129], qsT[:, h, :], CTb[:, h, :], False, True, [bqsT, bCTb], [bpz[0], bpz[1]])
                    E("dve", "tensor_copy", [bpz[0], bpz[1]], [bhn], out=hn[:, 0:2, :], in_=pz2[0:64, 0:258].rearrange("p (h d) -> p h d", h=2))
                    E("dve", "tensor_copy", [bpz[0], bpz[1]], [bhn], out=hn[:, 2:4, :], in_=pz2[0:64, 512:770].rearrange("p (h d) -> p h d", h=2))

                def N1():
                    den = hn[:, :, 128]
                    E("dve", "scalar_tensor_tensor", [bhn], [bhs], out=hs[:, 0:4], in0=den, scalar=-1.0, in1=den,
                      op0=ALU.mult, op1=ALU.max)
                    E("dve", "tensor_tensor", [bhs, bgtok], [bhs], out=hs[:, 4:8], in0=hs[:, 0:4], in1=gtok[0:64, 4:8], op=ALU.max)
                    E("dve", "reciprocal", [bhs], [bhs], out=hs[:, 8:12], in_=hs[:, 4:8])
                    E("dve", "tensor_tensor", [bhn, bhs], [bhh], out=hh, in0=hn[:, :, 0:128],
                      in1=hs[:, 8:12].unsqueeze(2).broadcast_to([64, 4, 128]), op=ALU.mult)
                    E("dve", "tensor_tensor", [bhh], [bT1h], out=T1h[:], in0=hh, in1=hh, op=ALU.mult)
                    E("dve", "tensor_reduce", [bT1h], [bhs], out=hs[:, 12:16], in_=T1h[:], axis=AX.X, op=ALU.add)

                def N2():
                    rstd_pow(hs[:, 12:16], hs[:, 12:16], hs[:, 12:16], 64, 4, 1.0 / 128, [bhs], [bhs])

                def N3():
                    E("dve", "tensor_tensor", [bhh, bhs], [bT1h], out=T1h[:], in0=hh,
                      in1=hs[:, 12:16].unsqueeze(2).broadcast_to([64, 4, 128]), op=ALU.mult)
                    E("dve", "tensor_tensor", [bT1h, bsmo], [bybt], out=ybt[:], in0=T1h[:].rearrange("p h d -> p (h d)"),
                      in1=smo[:], op=ALU.mult)

                def N4():
                    for h in range(4):
                        TR(ptr[:, h * 64:(h + 1) * 64], ybt[:, h * 128:(h + 1) * 128], identb[0:64, 0:64],
                           [bybt, bidb], [bptr])
                    ACT(ybT[:, :, cs], ptr[:, 0:256].rearrange("p (h t) -> p h t", h=4), AF.Copy, [bptr], [bybT])

                return [G0, G1, G2, G3, V0a, V0b, V0c, U, V1, N0, N1, N2, N3, N4]

            for c in range(nch):
                st_ = chunk_stages(c)
                b0 = base + 7 + c * SC_
                for j, f in enumerate(st_):
                    items.append((b0 + j, f))
            dbase = base + 7 + (nch - 1) * SC_ + 14

            def D0():
                DMA("sp", "Ar", At[0:ntok, :], A_sc[row0:row0 + ntok, :], [Abuf.get(row0)], [bAt])
                for n in range(2):
                    pB, bpB = next_ab()
                    for c in range(4):
                        MM(pB[0:ntok, :], ybT[:, c, 0:ntok], wb3[:, c, n * 512:(n + 1) * 512], c == 0, c == 3, [bybT] + WL["b"], [bpB])
                    pG, bpG = next_ab()
                    for kc in range(8):
                        MM(pG[0:ntok, :], uTs[:, kc, 0:ntok], winb[:, kc, 3080 + n * 512: 3080 + (n + 1) * 512], kc == 0, kc == 7,
                           [buTs] + WL["gb"], [bpG])
                    ACT(sgt[0:ntok, :], pG[0:ntok, :], AF.Sigmoid, [bpG], [bsgt])
                    E("dve", "tensor_tensor", [bsgt, bpB], [bT2], out=T2[0:ntok, n * 512:(n + 1) * 512], in0=sgt[0:ntok, :],
                      in1=pB[0:ntok, :], op=ALU.mult)

            def D1():
                for n in range(2):
                    pG, bpG = next_ab()
                    for kc in range(8):
                        MM(pG[0:ntok, :], uTs[:, kc, 0:ntok], winb[:, kc, 2056 + n * 512: 2056 + (n + 1) * 512], kc == 0, kc == 7,
                           [buTs] + WL["ga"], [bpG])
                    ACT(sgt[0:ntok, :], pG[0:ntok, :], AF.Sigmoid, [bpG], [bsgt])
                    E("pool", "tensor_tensor", [bsgt, bAt], [bsgt], out=sgt[0:ntok, :], in0=sgt[0:ntok, :],
                      in1=At[0:ntok, n * 512:(n + 1) * 512], op=ALU.mult)
                    E("pool", "tensor_tensor", [bsgt, bT2], [bmg], out=mg[0:ntok, n * 512:(n + 1) * 512], in0=sgt[0:ntok, :],
                      in1=T2[0:ntok, n * 512:(n + 1) * 512], op=ALU.add)

            def D2():
                for kc in range(8):
                    TR(ptr[:, kc * 128: kc * 128 + ntok], mg[0:ntok, kc * 128:(kc + 1) * 128], identb[0:ntok, 0:ntok],
                       [bmg, bidb], [bptr])
                ACT(mT[:, :, 0:ntok], ptr[:].rearrange("p (k t) -> p k t", k=8)[:, :, 0:ntok], AF.Copy, [bptr], [bmT])

            def D3():
                for n in range(2):
                    pO, bpO = next_ab()
                    for kc in range(8):
                        MM(pO[0:ntok, :], mT[:, kc, 0:ntok], wo3[:, kc, n * 512:(n + 1) * 512], kc == 0, kc == 7, [bmT] + WL["out"], [bpO])
                    E("dve", "tensor_tensor", [bpO, bgg_], [bT2], out=T2[0:ntok, n * 512:(n + 1) * 512], in0=pO[0:ntok, :],
                      in1=gg_[0:ntok, n * 512:(n + 1) * 512], op=ALU.mult)
                E("pool", "tensor_tensor", [bT2, bxt], [bxt], out=xt[0:ntok, :], in0=T2[0:ntok, :], in1=xt[0:ntok, :], op=ALU.add)
                bx1 = Buf("x1sc%d" % row0)
                x1buf[row0] = bx1
                DMA("sp", "x1w%d" % (lt % 3), x1_sc[row0:row0 + ntok, :], xt[0:ntok, :], [bxt], [bx1])
                ACT(T2[0:ntok, :], xt[0:ntok, :], AF.Square, [bxt], [bT2, bssq], accum_out=ssq[0:ntok, 0:1])
                E("pool", "tensor_scalar", [bssq], [bssq], out=ssq[0:ntok, 1:2], in0=ssq[0:ntok, 0:1], scalar1=1.0 / D,
                  scalar2=EPS, op0=ALU.mult, op1=ALU.add)
                E("pool", "tensor_tensor", [bssq, bmhalf], [brstd2], out=rstd2[0:ntok, tix:tix + 1], in0=ssq[0:ntok, 1:2],
                  in1=mhalf[0:ntok, 0:1], op=ALU.pow)

            for j, f in enumerate([D0, D1, D2, D3]):
                items.append((dbase + j, f))
            sq["lastU"] = base + 7 + (nch - 1) * SC_ + 7
            sq["lastD3"] = dbase + 3
            sq["lastF4"] = base + 8
            return (xp, bxp, ntok)

        if "1b" in PH:
            seqs = []
            for tix, tile in enumerate(tiles):
                if not seqs or seqs[-1][0] != tile[0]:
                    seqs.append((tile[0], []))
                seqs[-1][1].append((tix, tile))
            items = []
            lt = 0
            d3_hist = []
            for k_, (s, tl) in enumerate(seqs):
                CTk, bCTk = CTs[k_ % 3]
                ggk = ggbs[k_ % 2]
                sq = dict(CT=(CTk, bCTk), gg=ggk)
                chunk_ctr[0] += 1
                mcol = chunk_ctr[0]
                base0 = 2 * lt * SC_
                xp0, bxp0 = xp2[lt % 2]

                def init_seq(s=s, CTk=CTk, bCTk=bCTk, mcol=mcol, xp0=xp0, bxp0=bxp0, first=(k_ == 0)):
                    load_mod(s, 1, gate=False)
                    if s == 0:
                        E("pool", "memset", [], [bCTk], CTk[:], 0.0)
                        if first:
                            E("pool", "memset", [], [bmseq], mseq[:], 0.0)
                        E("pool", "memset", [], [bxp0], xp0[:, :, 0:3], 0.0)
                    else:
                        si = s - 1
                        DMA("sp", "st", C0t, C0[si].rearrange("h v k -> v h k"), [], [bC0t])
                        for h in range(4):
                            TR(pa[:, h * 128:(h + 1) * 128], C0t[:, h, :], identf[:], [bC0t, bidf], [bpa])
                        E("dve", "tensor_copy", [bpa], [bCTk], out=CTk[:, :, 0:128], in_=pa[:].rearrange("p (h v) -> p h v", h=4))
                        DMA("sp", "st", CTk[:, :, 128], n0[si].rearrange("h k -> k h"), [], [bCTk], allow_slow_non_contiguous=True)
                        DMA("sp", "st", mseq[:, mcol:mcol + 1], m0[si:si + 1, :].rearrange("o h -> h o"), [], [bmseq],
                            allow_slow_non_contiguous=True)
                        for j_ in range(3):
                            DMA("sp", "st", xp0[:, :, j_], conv0[si, j_].rearrange("(c p) -> p c", p=128), [], [bxp0],
                                allow_slow_non_contiguous=True)

                items.append((base0 - 0.5, init_seq))
                gstep = base0 - 0.4
                if k_ >= 2:
                    gstep = max(gstep, d3_hist[k_ - 2] + 0.5)
                items.append((gstep, (lambda s=s, ggk=ggk: load_mod(s, 1, front=False, gdst=ggk))))
                prev_xp = None
                for (tix, tile) in tl:
                    prev_xp = sched_tile(items, lt, tix, tile, prev_xp, sq)
                    lt += 1
                items.append((sq["lastF4"] + 0.5, out_conv(s, prev_xp)))
                items.append((sq["lastU"] + 0.5, out_state(s, CTk, bCTk, chunk_ctr[0])))
                d3_hist.append(sq["lastD3"])
            order = sorted(range(len(items)), key=lambda i: (items[i][0], i))
            for i in order:
                items[i][1]()

        phase_begin()
        pab[:] = [(pa, bpa), (pb, bpb), (py, bpy), (pm, bpm), (pz2[:, 0:512], bpz[0]), (pz2[:, 512:1024], bpz[1])]
        WL = {}
        for ci_, c0_ in enumerate(range(0, DFF, 512)):
            n_ = min(512, DFF - c0_)
            WL["g%d" % ci_] = load_wg(w_ff_gate, 8, 0, [("g", c0_, n_)], DFF, 0, "fg%d" % ci_, kbase=2 * ci_)["g"]
            WL["u%d" % ci_] = load_wg(w_ff_up, 8, 0, [("u", c0_, n_)], DFF, 22528, "fu%d" % ci_, kbase=2 * ci_ + 1)["u"]
        wg3 = wview(0, 8, DFF)
        wu3 = wview(22528, 8, DFF)
        wdn, _ = cv("wdn", [128, 22 * D], BF16)
        xs3p = [xts[0], xts[1], cv("xt2p", [128, D], F32)]
        wd3 = wdn.rearrange("p (k c) -> p k c", k=22)
        WLd = {}
        dvd = w_ff_down.rearrange("(k p) n -> p k n", p=128)
        for n_ in range(2):
            bw = fb("Wd%d" % n_)
            DMA("pool", "Wg%d" % (12 + n_), wd3[:, :, n_ * 512:(n_ + 1) * 512], dvd[:, :, n_ * 512:(n_ + 1) * 512], [], [bw])
            WLd[n_] = [bw]
        sgts = [cv("sgt%d" % i, [128, 512], F32) for i in range(1)] * 2
        hb2 = [cv("hbuf%d" % i, [128, DFF], BF16) for i in range(2)]
        hT, bhT = cv("hT", [128, 22, 128], BF16)
        Tq, bTq = cv("O2_0", [128, D], F32)
        sq2 = [cv("sq2_%d" % i, [128, 4], F32) for i in range(2)]
        uT2a = [(uT, buT), cv("uT2a", [128, 8, 128], BF16)]
        ggbs2 = [(ggb, bggb), cv("ggb2", [128, D], F32)]
        fgb, bfgb = cv("fgb", [128, D], F32)
        DMA("sp", "cst", fgb[:], final_g[0:1, :].broadcast_to([128, D]), [], [bfgb])
        sgc = [0]
        seq2a = [-1]
        gsel = {}
        trb = [(ptr, bptr), (pat, bpat)]

        def ld2(tix):
            (s, row0, ntok, ti) = tiles[tix]
            xt, bxt = xs3p[tix % 3]
            DMA("sp", "xl%d" % (tix % 3), xt[0:ntok, :], x1_sc[row0:row0 + ntok, :], [x1buf.get(row0)], [bxt])

        def front2(tix):
            (s, row0, ntok, ti) = tiles[tix]
            if s != seq2a[0]:
                seq2a[0] = s
                load_mod(s, 2, gate=False)
            xt, bxt = xs3p[tix % 3]
            rms_to_uT(xt, bxt, ntok, rstd_ap=rstd2[0:ntok, tix:tix + 1], dst=uT2a[tix % 2])

        def up2(tix):
            (s, row0, ntok, ti) = tiles[tix]
            hbuf, bhbuf = hb2[tix % 2]
            for n0_ in range(0, DFF, 512):
                sgc[0] ^= 1
                sgt, bsgt = sgts[sgc[0]]
                n = min(512, DFF - n0_)
                pG, bpG = proj_tok(wg3, n0_, n, ntok, wl=WL["g%d" % (n0_ // 512)], us=uT2a[tix % 2])
                pU, bpU = proj_tok(wu3, n0_, n, ntok, wl=WL["u%d" % (n0_ // 512)], us=uT2a[tix % 2])
                ACT(sgt[0:ntok, 0:n], pG[0:ntok, 0:n], AF.Silu, [bpG], [bsgt])
                E("dve", "tensor_tensor", [bsgt, bpU], [bhbuf], out=hbuf[0:ntok, n0_:n0_ + n], in0=sgt[0:ntok, 0:n],
                  in1=pU[0:ntok, 0:n], op=ALU.mult)

        seqord = []
        for t_ in tiles:
            if t_[0] not in seqord:
                seqord.append(t_[0])

        def down2(tix):
            (s, row0, ntok, ti) = tiles[tix]
            k_ = seqord.index(s)
            gg2, bgg2 = ggbs2[k_ % 2]
            if s not in gsel:
                gsel[s] = True
                load_mod(s, 2, front=False, gdst=(gg2, bgg2))
            hbuf, bhbuf = hb2[tix % 2]
            xt, bxt = xs3p[tix % 3]
            sq_, bsq_ = sq2[tix % 2]
            for gi, g0 in enumerate(range(0, 22, 8)):
                ng = min(8, 22 - g0)
                pt_, bpt_ = trb[gi % 2]
                for j in range(ng):
                    k = g0 + j
                    TR(pt_[:, j * 128: j * 128 + ntok], hbuf[0:ntok, k * 128:(k + 1) * 128], identb[0:ntok, 0:ntok],
                       [bhbuf, bidb], [bpt_])
                ACT(hT[:, g0:g0 + ng, 0:ntok], pt_[:].rearrange("p (k t) -> p k t", k=8)[:, 0:ng, 0:ntok], AF.Copy,
                    [bpt_], [bhT])
            for n in range(2):
                pD, bpD = next_ab()
                for k in range(22):
                    MM(pD[0:ntok, :], hT[:, k, 0:ntok], wd3[:, k, n * 512:(n + 1) * 512], k == 0, k == 21, [bhT] + WLd[n], [bpD])
                E("dve", "tensor_tensor", [bpD, bgg2], [bTq], out=Tq[0:ntok, n * 512:(n + 1) * 512], in0=pD[0:ntok, :],
                  in1=gg2[0:ntok, n * 512:(n + 1) * 512], op=ALU.mult)
            E("pool", "tensor_tensor", [bTq, bxt], [bxt], out=xt[0:ntok, :], in0=Tq[0:ntok, :], in1=xt[0:ntok, :], op=ALU.add)
            ACT(Tq[0:ntok, :], xt[0:ntok, :], AF.Square, [bxt], [bTq, bsq_], accum_out=sq_[0:ntok, 0:1])
            rstd_pow(sq_[0:ntok, 2:3], sq_[0:ntok, 1:2], sq_[0:ntok, 0:1], ntok, 1, 1.0 / D, [bsq_], [bsq_])
            E("dve", "scalar_tensor_tensor", [bxt, bsq_, bfgb], [bTq], out=Tq[0:ntok, :], in0=xt[0:ntok, :],
              scalar=sq_[0:ntok, 2:3], in1=fgb[0:ntok, :], op0=ALU.mult, op1=ALU.mult)
            DMA("sp", "yo", y_all[row0:row0 + ntok, :], Tq[0:ntok, :], [bTq], [])

        P2ON = ("2a" in PH) or ("2b" in PH)
        if P2ON:
            NT_ = len(tiles)
            ld2(0)
            if NT_ > 1:
                ld2(1)
            front2(0)
            if NT_ > 1:
                front2(1)
            up2(0)
            for tix in range(NT_):
                if tix + 2 < NT_:
                    ld2(tix + 2)
                if tix + 1 < NT_:
                    up2(tix + 1)
                if tix + 2 < NT_:
                    front2(tix + 2)
                down2(tix)

        P.lower(lambda name: st.enter_context(nc.semaphore(name)))
        build_program.stats = P.stats
    return nc


_CACHE = {}


def kernel(**inp):
    f = lambda a: np.ascontiguousarray(np.asarray(a, dtype=np.float32))
    if "nc" not in _CACHE:
        _CACHE["nc"] = build_program()
    nc = _CACHE["nc"]
    x_prompt = f(inp["x_prompt"]); x_sample = f(inp["x_sample"])
    c_prompt = f(inp["c_prompt"]); c_sample = f(inp["c_sample"])
    ck = f(inp["cache_sb_k"])[0].reshape(16, PAST, 512)
    cv = f(inp["cache_sb_v"])[0].reshape(16, PAST, 512)
    sC = f(inp["state_mlstm_C"])[0]; sn = f(inp["state_mlstm_n"])[0]; sm = f(inp["state_mlstm_m"])[0]
    sconv = f(inp["state_conv"])[0]
    shared = {
        "norm1_g": f(inp["norm1_g"]).reshape(1, D), "norm2_g": f(inp["norm2_g"]).reshape(1, D),
        "w_ada": f(inp["w_ada"])[0], "b_ada": f(inp["b_ada"]).reshape(1, 6 * D),
        "w_in": f(inp["w_in"])[0], "b_if": f(inp["b_if"]).reshape(8, 1),
        "w_conv": f(inp["w_conv"])[0], "b_conv": f(inp["b_conv"]).reshape(1, D),
        "ml_norm_g": f(inp["ml_norm_g"]).reshape(1, 512),
        "w_a": f(inp["w_a"])[0], "w_b": f(inp["w_b"])[0], "w_out": f(inp["w_out"])[0],
        "w_ff_gate": f(inp["w_ff_gate"])[0], "w_ff_up": f(inp["w_ff_up"])[0], "w_ff_down": f(inp["w_ff_down"])[0],
        "final_g": f(inp["final_g"]).reshape(1, D),
    }
    in_maps = []
    for i in range(8):
        m = dict(shared)
        m["xall"] = np.concatenate([x_prompt[i], x_sample[2 * i], x_sample[2 * i + 1]], axis=0)
        m["c3"] = np.stack([c_prompt[i], c_sample[2 * i], c_sample[2 * i + 1]], axis=0)
        m["cache_k"] = ck[2 * i:2 * i + 2]
        m["cache_v"] = cv[2 * i:2 * i + 2]
        m["C0"] = sC[2 * i:2 * i + 2]
        m["n0"] = sn[2 * i:2 * i + 2]
        m["m0"] = sm[2 * i:2 * i + 2]
        m["conv0"] = sconv[2 * i:2 * i + 2]
        in_maps.append({k: np.ascontiguousarray(v) for k, v in m.items()})
    res = run_bass_kernel_spmd(nc, in_maps, core_ids=list(range(8)))
    R = res.results
    y_p = np.stack([R[i]["y_all"][:SEQ] for i in range(8)])
    y_s = np.stack([R[i]["y_all"][SEQ + j * LS: SEQ + (j + 1) * LS] for i in range(8) for j in range(2)])
    k_p = np.stack([R[i]["k_all"][:SEQ] for i in range(8)]).reshape(1, 8, SEQ, 8, 64)
    v_p = np.stack([R[i]["v_all"][:SEQ] for i in range(8)]).reshape(1, 8, SEQ, 8, 64)
    k_s = np.stack([R[i]["k_all"][SEQ + j * LS: SEQ + (j + 1) * LS] for i in range(8) for j in range(2)]).reshape(1, 16, LS, 8, 64)
    v_s = np.stack([R[i]["v_all"][SEQ + j * LS: SEQ + (j + 1) * LS] for i in range(8) for j in range(2)]).reshape(1, 16, LS, 8, 64)
    C_p = np.stack([R[i]["C_out"][0] for i in range(8)])[None]
    n_p = np.stack([R[i]["n_out"][0] for i in range(8)])[None]
    m_p = np.stack([R[i]["m_out"][0] for i in range(8)])[None]
    cv_p = np.stack([R[i]["conv_out"][0] for i in range(8)])[None]
    C_s = np.stack([R[i]["C_out"][1 + j] for i in range(8) for j in range(2)])[None]
    n_s = np.stack([R[i]["n_out"][1 + j] for i in range(8) for j in range(2)])[None]
    m_s = np.stack([R[i]["m_out"][1 + j] for i in range(8) for j in range(2)])[None]
    cv_s = np.stack([R[i]["conv_out"][1 + j] for i in range(8) for j in range(2)])[None]
    outs = (y_p, y_s, k_p, v_p, C_p, n_p, m_p, cv_p, k_s, v_s, C_s, n_s, m_s, cv_s)
    return tuple(np.ascontiguousarray(o, dtype=np.float32) for o in outs)
```

```python
import numpy as np
from contextlib import ExitStack
import concourse.bass as bass
import concourse.mybir as mybir
from concourse.bass_utils import run_bass_kernel_spmd

F32 = mybir.dt.float32
BF16 = mybir.dt.bfloat16
AF = mybir.ActivationFunctionType
ALU = mybir.AluOpType
AX = mybir.AxisListType

D = 1024
SEQ = 2048
NS = 2
LS = 64
PAST = 1024
DFF = 2816
INW = 5640
EPS = 1e-6
NROWS = SEQ + NS * LS


STRICT_SAME_ENGINE = False
PE_RIDE = True


class Buf:
    __slots__ = ("name", "last_w", "readers", "excl")

    def __init__(self, name, excl=False):
        self.name = name
        self.last_w = None
        self.readers = []
        self.excl = excl


class Op:
    __slots__ = ("eng", "fn", "reads", "writes", "dma", "seq", "signal", "waits",
                 "clock", "count", "idx", "attach")


class Prog:
    ENG = ("pe", "act", "dve", "pool", "sp")

    def __init__(self, nc):
        self.nc = nc
        self.ops = []
        self.e = {"pe": nc.tensor, "act": nc.scalar, "dve": nc.vector,
                  "pool": nc.gpsimd, "sp": nc.sync}

    def op(self, eng, fn, reads=(), writes=(), dma=None):
        o = Op()
        o.eng = eng
        o.fn = fn
        o.reads = [b for b in reads if b is not None and not b.excl]
        o.writes = [b for b in writes if b is not None] + [b for b in reads if b is not None and b.excl]
        o.dma = dma
        o.signal = False
        o.waits = []
        o.attach = (eng != "pe")
        o.idx = len(self.ops)
        self.ops.append(o)
        return o

    def fence(self):
        last = {}
        for o in self.ops:
            last[o.eng if o.dma is None else "d:" + o.dma] = o
        return list(last.values())

    def lower(self, sem_ctx):
        ops = self.ops
        seqc = {k: 0 for k in self.ENG}
        dmac = {}
        eclock = {k: {} for k in self.ENG}
        for o in ops:
            if o.dma is None:
                seqc[o.eng] += 1
                o.seq = seqc[o.eng]
            else:
                dmac[o.dma] = dmac.get(o.dma, 0) + 1
                o.seq = dmac[o.dma]
            deps = {}
            dsafe = {}
            for b in o.reads:
                if b.last_w is not None:
                    deps[b.last_w.idx] = b.last_w
                    dsafe[b.last_w.idx] = False
            for b in o.writes:
                if b.last_w is not None:
                    deps[b.last_w.idx] = b.last_w
                    dsafe[b.last_w.idx] = dsafe.get(b.last_w.idx, True) and b.excl
                for r in b.readers:
                    deps[r.idx] = r
                    dsafe[r.idx] = dsafe.get(r.idx, True) and b.excl
            clk = eclock[o.eng]
            for p in deps.values():
                if p is o:
                    continue
                if p.dma is None:
                    key = p.eng
                    need = p.seq
                    if p.eng == o.eng and o.dma is None:
                        if o.eng == "pe":
                            continue
                        if (not STRICT_SAME_ENGINE) and o.eng != "pool" and not any(b.last_w is p for b in o.reads):
                            continue
                else:
                    key = "d:" + p.dma
                    need = dmac[p.dma] if not (o.dma == p.dma) else dmac[p.dma] - 1
                if clk.get(key, 0) >= need:
                    continue
                if p.dma is None:
                    p.signal = True
                o.waits.append((p, need, dsafe.get(p.idx, False)))
                for k2, v2 in p.clock.items():
                    if clk.get(k2, 0) < v2:
                        clk[k2] = v2
                if clk.get(key, 0) < need:
                    clk[key] = need
            myclk = dict(clk)
            mykey = o.eng if o.dma is None else "d:" + o.dma
            myclk[mykey] = max(myclk.get(mykey, 0), o.seq)
            o.clock = myclk
            for b in o.reads:
                b.readers.append(o)
            for b in o.writes:
                b.last_w = o
                b.readers = []
        cnt = {k: 0 for k in self.ENG}
        for o in ops:
            if o.dma is None and o.signal:
                cnt[o.eng] += 1
                o.count = cnt[o.eng]
        esem = {}
        dsem = {}
        dtot = {}
        n_waits = 0

        def get_e(k):
            if k not in esem:
                esem[k] = sem_ctx("e_" + k)
            return esem[k]

        def get_d(k):
            if k not in dsem:
                dsem[k] = sem_ctx("d_" + k)
            return dsem[k]

        for o in ops:
            eng = self.e[o.eng]
            need = {}
            for p, nd, sf in o.waits:
                if p.dma is None:
                    s = get_e(p.eng)
                    v = p.count
                else:
                    s = get_d(p.dma)
                    v = 16 * nd
                k = id(s)
                if k not in need:
                    need[k] = (s, v, sf)
                else:
                    need[k] = (s, max(v, need[k][1]), sf and need[k][2])
            nl = [(a_, b_) for (a_, b_, c_) in need.values()]
            ride = None
            if o.attach and nl:
                ride = nl.pop()
            elif o.eng == "pe" and PE_RIDE:
                safe = [(a_, b_) for (a_, b_, c_) in need.values() if c_]
                if safe:
                    ride = safe[-1]
                    nl = [x for x in nl if x[0] is not ride[0]]
            for s, v in nl:
                eng.wait_ge(s, v)
                n_waits += 1
            ins = o.fn()
            if ride is not None:
                ins._wait_ge(ride[0], eng.lower_val(ride[1]))
            if o.dma is not None:
                ins.then_inc(get_d(o.dma), 16)
                dtot[o.dma] = dtot.get(o.dma, 0) + 16
            elif o.signal:
                ins.then_inc(get_e(o.eng), 1)
        for k, s in dsem.items():
            self.e["sp"].wait_ge(s, dtot[k])
        self.stats = dict(n_ops=len(ops), n_waits=n_waits, n_dsem=len(dsem),
                          sig={k: cnt[k] for k in cnt})


CFG = {"phases": ("0", "1a", "1b", "2a", "2b"), "tiles": None}


def build_program():
    nc = bass.Bass("TRN2", target_bir_lowering=False)
    PH = CFG["phases"]

    def din(name, shape):
        return nc.dram_tensor(name, list(shape), F32, kind="ExternalInput").ap()

    def dout(name, shape):
        return nc.dram_tensor(name, list(shape), F32, kind="ExternalOutput").ap()

    xall = din("xall", [NROWS, D])
    c3 = din("c3", [3, D])
    cache_k = din("cache_k", [NS, PAST, 512])
    cache_v = din("cache_v", [NS, PAST, 512])
    C0 = din("C0", [NS, 4, 128, 128])
    n0 = din("n0", [NS, 4, 128])
    m0 = din("m0", [NS, 4])
    conv0 = din("conv0", [NS, 3, D])
    norm1_g = din("norm1_g", [1, D])
    norm2_g = din("norm2_g", [1, D])
    w_ada = din("w_ada", [D, 6 * D])
    b_ada = din("b_ada", [1, 6 * D])
    w_in = din("w_in", [D, INW])
    b_if = din("b_if", [8, 1])
    w_conv = din("w_conv", [4, D])
    b_conv = din("b_conv", [1, D])
    ml_norm_g = din("ml_norm_g", [1, 512])
    w_a = din("w_a", [512, D])
    w_b = din("w_b", [512, D])
    w_out = din("w_out", [D, D])
    w_ff_gate = din("w_ff_gate", [D, DFF])
    w_ff_up = din("w_ff_up", [D, DFF])
    w_ff_down = din("w_ff_down", [DFF, D])
    final_g = din("final_g", [1, D])

    y_all = dout("y_all", [NROWS, D])
    k_all = dout("k_all", [NROWS, 512])
    v_all = dout("v_all", [NROWS, 512])
    C_out = dout("C_out", [3, 4, 128, 128])
    n_out = dout("n_out", [3, 4, 128])
    m_out = dout("m_out", [3, 4])
    conv_out = dout("conv_out", [3, 3, D])

    mod_sc = nc.dram_tensor("mod_sc", [3, 6 * D], F32).ap()
    A_sc = nc.dram_tensor("A_sc", [NROWS, D], F32).ap()
    x1_sc = nc.dram_tensor("x1_sc", [NROWS, D], F32).ap()
    h_sc = nc.dram_tensor("h_sc", [NROWS, DFF], BF16).ap()

    tiles = [(0, t * 128, 128, t) for t in range(16)] + [(1, SEQ, 64, 0), (2, SEQ + 64, 64, 0)]
    if CFG["tiles"] is not None:
        tiles = [tiles[i] for i in CFG["tiles"]]

    with ExitStack() as st:
        P = Prog(nc)

        def sb(name, shape, dt=F32):
            return st.enter_context(nc.sbuf_tensor(name, list(shape), dt)), Buf(name)

        def ps(name, shape, dt=F32):
            return st.enter_context(nc.psum_tensor(name, list(shape), dt)), Buf(name, excl=True)

        SCRN = 20736
        SCR = st.enter_context(nc.sbuf_tensor("SCR", [128, SCRN], F32))
        scr = {"off": 0, "fence": []}

        def phase_begin():
            scr["off"] = 0
            scr["fence"] = P.fence()

        def fb(name):
            b = Buf(name)
            b.readers = list(scr["fence"])
            return b

        def cv(name, shape, dt=F32):
            n = 1
            for d_ in shape[1:]:
                n *= d_
            nf = (n + 1) // 2 if dt == BF16 else n
            nf = (nf + 7) // 8 * 8
            off = scr["off"]
            scr["off"] += nf
            assert scr["off"] <= SCRN, (name, scr["off"])
            v = SCR[0:shape[0], off:off + nf]
            if dt == BF16:
                v = v.bitcast(BF16)
            v = v[:, 0:n]
            if len(shape) == 3:
                v = v.rearrange("p (a b) -> p a b", a=shape[1])
            return v, fb(name)

        def E(eng, name, reads, writes, *a, **kw):
            m = getattr(P.e[eng], name)
            return P.op(eng, lambda: m(*a, **kw), reads, writes)

        def DMA(eng, key, out, in_, reads, writes, **kw):
            m = P.e[eng].dma_start
            return P.op(eng, lambda: m(out=out, in_=in_, **kw), reads, writes, dma=key)

        def MM(out, lhsT, rhs, start, stop, reads, writes):
            m = nc.tensor.matmul
            return P.op("pe", lambda: m(out, lhsT=lhsT, rhs=rhs, start=start, stop=stop), reads, writes)

        def TR(out, in_, ident, reads, writes):
            m = nc.tensor.transpose
            return P.op("pe", lambda: m(out=out, in_=in_, identity=ident), reads, writes)

        def ACT(out, in_, func, reads, writes, **kw):
            m = nc.scalar.activation
            o_ = P.op("act", lambda: m(out=out, in_=in_, func=func, **kw), reads, writes)
            if "accum_out" in kw:
                o_.attach = False
            return o_

        WB, bWB = sb("WB", [128, 46080], BF16)
        identb, bidb = sb("identb", [128, 128], BF16)
        identf, bidf = sb("identf", [128, 128], F32)
        onesb, bones = sb("onesb", [128, 512], BF16)
        zer, bzer = sb("zer", [128, 128], F32)
        sel, bsel = sb("sel", [4, 4, 128], F32)
        rstd2, brstd2 = sb("rstd2", [128, 18], F32)
        csT, bcsT = sb("csT", [128, 8, 3], BF16)
        mhalf, bmhalf = sb("mhalf", [128, 4], F32)
        cmask, bcmask = sb("cmask", [128, 128], BF16)
        gmb, bgmb = sb("gmb", [128, D], F32)
        shb, bshb = sb("shb", [128, D], F32)
        ggb, bggb = sb("ggb", [128, D], F32)
        xts = [sb("xt%d" % i, [128, D], F32) for i in range(2)]
        T1, bT1 = sb("T1", [128, D], F32)
        T2, bT2 = sb("T2", [128, D], F32)
        ub, bub = sb("ub", [128, D], BF16)
        uT, buT = sb("uT", [128, 8, 128], BF16)
        ssq, bssq = sb("ssq", [128, 4], F32)
        stg = []

        ptr, bptr = ps("ptr", [128, 1024], BF16)
        pat, bpat = ps("pat", [128, 1024], BF16)
        pa, bpa = ps("pa", [128, 512], F32)
        pb, bpb = ps("pb", [128, 512], F32)
        py, bpy = ps("py", [128, 512], F32)
        pm, bpm = ps("pm", [128, 512], F32)
        pz2, bpz2 = ps("pz2", [128, 1024], F32)
        bpz = [Buf("pz0", excl=True), Buf("pz1", excl=True)]
        pab = [(pa, bpa), (pb, bpb)]
        rot = {"ab": 0, "stg": 0, "x": 0}

        def next_ab():
            rot["ab"] = (rot["ab"] + 1) % len(pab)
            return pab[rot["ab"]]

        def next_stg():
            rot["stg"] = (rot["stg"] + 1) % 2
            return stg[rot["stg"]]

        E("pool", "memset", [], [bidf], identf[:], 1.0)
        E("pool", "affine_select", [bidf], [bidf], out=identf[:], in_=identf[:], pattern=[[-1, 128]],
          compare_op=ALU.is_equal, fill=0.0, base=0, channel_multiplier=1)
        E("pool", "tensor_copy", [bidf], [bidb], out=identb[:], in_=identf[:])
        E("pool", "memset", [], [bones], onesb[:], 1.0)
        E("pool", "memset", [], [bzer], zer[:], 0.0)
        E("pool", "memset", [], [bsel], sel[:], 1.0)
        E("pool", "affine_select", [bsel], [bsel], out=sel[:], in_=sel[:], pattern=[[-1, 4], [0, 128]],
          compare_op=ALU.is_equal, fill=0.0, base=0, channel_multiplier=1)
        E("pool", "memset", [], [brstd2], rstd2[:], 1.0)
        E("pool", "memset", [], [bmhalf], mhalf[:], -0.5)
        E("pool", "memset", [], [bcmask], cmask[:], -30000.0)
        E("pool", "affine_select", [bcmask], [bcmask], out=cmask[:], in_=cmask[:], pattern=[[1, 128]],
          compare_op=ALU.is_ge, fill=0.0, base=0, channel_multiplier=-1)

        def rstd_pow(out_ap, tmp_ap, ss_ap, npart, ncol, scale, rds, wrs):
            E("pool", "tensor_scalar", rds, wrs, out=tmp_ap, in0=ss_ap, scalar1=scale, scalar2=EPS, op0=ALU.mult, op1=ALU.add)
            E("pool", "tensor_tensor", wrs + [bmhalf], wrs, out=out_ap, in0=tmp_ap, in1=mhalf[0:npart, 0:ncol], op=ALU.pow)

        def load_w(dram, r0, nrows_chunks, c0, ncols, off, key):
            bufs = []
            for kc in range(nrows_chunks):
                for cc in range(0, ncols, 2048):
                    n = min(2048, ncols - cc)
                    bw = fb("W%s_%d_%d" % (key, kc, cc))
                    bufs.append(bw)
                    DMA("pool", "W" + key, WB[:, off + kc * ncols + cc: off + kc * ncols + cc + n],
                        dram[r0 + kc * 128: r0 + (kc + 1) * 128, c0 + cc: c0 + cc + n], [], [bw])
            return bufs

        def load_wg(dram, nrows_chunks, c0, groups, stride, off, kp, kbase=0):
            out = {}
            dv = dram.rearrange("(k p) n -> p k n", p=128)
            wv = WB[:, off: off + nrows_chunks * stride].rearrange("p (k c) -> p k c", k=nrows_chunks)
            for gi, (nm, l0, ncols) in enumerate(groups):
                bufs = []
                for cc in range(0, ncols, 2048):
                    n = min(2048, ncols - cc)
                    bw = fb("W%s_%s_%d" % (kp, nm, cc))
                    bufs.append(bw)
                    DMA("pool", "Wg%d" % (kbase + gi), wv[:, :, l0 + cc: l0 + cc + n],
                        dv[:, :, c0 + l0 + cc: c0 + l0 + cc + n], [], [bw])
                out[nm] = bufs
            return out

        def wview(off, nk, ncols):
            return WB[:, off: off + nk * ncols].rearrange("p (k c) -> p k c", k=nk)

        phase_begin()
        stg[:] = [cv("stg%d" % i, [128, 512], F32) for i in range(2)]
        cT, bcT = cv("cT", [128, 8, 3], F32)
        for s_ in range(3):
            DMA("sp", "cst", cT[:, :, s_], c3[s_].rearrange("(k p) -> p k", p=128), [], [bcT],
                allow_slow_non_contiguous=True)
        ACT(csT[:], cT[:], AF.Silu, [bcT], [bcsT])
        WAs = [cv("WA%d" % i, [128, 8, 512], BF16) for i in range(4)]
        w_ada_v = w_ada.rearrange("(k p) n -> p k n", p=128)
        bmods = {"A": Buf("modA"), "B": Buf("modB"), "C": Buf("modC")}

        def mod_group(nch):
            return "A" if nch < 4 else ("B" if nch < 6 else "C")

        def mod_chunk_load(nch, WA, bWA, key):
            DMA("pool", key, WA[:], w_ada_v[:, :, nch * 512:(nch + 1) * 512], [], [bWA])

        def mod_chunk_compute(nch, WA, bWA, pp, bpp):
            sg, bsg = next_stg()
            DMA("sp", "bad", sg[0:3, :], b_ada[0:1, nch * 512:(nch + 1) * 512].broadcast_to([3, 512]), [], [bsg])
            for kc in range(8):
                MM(pp[0:3, :], csT[:, kc, :], WA[:, kc, :], kc == 0, kc == 7, [bcsT, bWA], [bpp])
            E("dve", "tensor_tensor", [bpp, bsg], [bsg], out=sg[0:3, :], in0=pp[0:3, :], in1=sg[0:3, :], op=ALU.add)
            DMA("sp", "modw", mod_sc[:, nch * 512:(nch + 1) * 512], sg[0:3, :], [bsg], [bmods[mod_group(nch)]])

        for nch in range(4 if "0" in PH else 0):
            WA, bWA = WAs[nch]
            mod_chunk_load(nch, WA, bWA, "wa%d" % nch)
        sv_f = scr["fence"]
        scr["fence"] = []
        WL_1a = load_wg(w_in, 8, 0, [("q", 0, 512), ("k", 512, 512), ("v", 1024, 512)], 1536, 0, "a")
        WL_1a["a"] = load_wg(w_a, 4, 0, [("a", 0, 1024)], 1024, 12288, "wa", kbase=3)["a"]
        scr["fence"] = sv_f
        for nch in range(4 if "0" in PH else 0):
            WA, bWA = WAs[nch]
            mod_chunk_compute(nch, WA, bWA, pm, bpm)

        def load_mod(s, which, front=True, gate=True, gdst=None):
            base = 0 if which == 1 else 3 * D
            ng = norm1_g if which == 1 else norm2_g
            bf_ = bmods["A"] if which == 1 else bmods["C"]
            bg_ = bmods["B"] if which == 1 else bmods["C"]
            if front:
                DMA("sp", "modr", shb[:], mod_sc[s:s + 1, base:base + D].broadcast_to([128, D]), [bf_], [bshb])
                DMA("sp", "modr", gmb[:], mod_sc[s:s + 1, base + D:base + 2 * D].broadcast_to([128, D]), [bf_], [bgmb])
                DMA("sp", "modr", T2[:], ng[0:1, :].broadcast_to([128, D]), [], [bT2])
                E("dve", "scalar_tensor_tensor", [bgmb, bT2], [bgmb], out=gmb[:], in0=gmb[:], scalar=1.0, in1=T2[:],
                  op0=ALU.add, op1=ALU.mult)
            if gate:
                gd, bgd = gdst if gdst is not None else (ggb, bggb)
                DMA("sp", "modg", gd[:], mod_sc[s:s + 1, base + 2 * D:base + 3 * D].broadcast_to([128, D]), [bg_], [bgd])

        def rms_to_uT(xt, bxt, ntok, rstd_ap=None, dst=None):
            if rstd_ap is None:
                ACT(T2[0:ntok, :], xt[0:ntok, :], AF.Square, [bxt], [bT2, bssq], accum_out=ssq[0:ntok, 0:1])
                rstd_pow(ssq[0:ntok, 2:3], ssq[0:ntok, 1:2], ssq[0:ntok, 0:1], ntok, 1, 1.0 / D, [bssq], [bssq])
                rstd_ap = ssq[0:ntok, 2:3]
                rb = bssq
            else:
                rb = brstd2
            E("dve", "scalar_tensor_tensor", [bxt, rb, bgmb], [bT1], out=T1[0:ntok, :], in0=xt[0:ntok, :],
              scalar=rstd_ap, in1=gmb[0:ntok, :], op0=ALU.mult, op1=ALU.mult)
            E("pool", "tensor_tensor", [bT1, bshb], [bub], out=ub[0:ntok, :], in0=T1[0:ntok, :], in1=shb[0:ntok, :],
              op=ALU.add)
            for kc in range(8):
                TR(ptr[:, kc * 128: kc * 128 + ntok], ub[0:ntok, kc * 128:(kc + 1) * 128], identb[0:ntok, 0:ntok],
                   [bub, bidb], [bptr])
            uTd, buTd = dst if dst is not None else (uT, buT)
            ACT(uTd[:, :, 0:ntok], ptr[:].rearrange("p (k t) -> p k t", k=8)[:, :, 0:ntok], AF.Copy, [bptr], [buTd])

        epst, bepst = sb("epst", [128, 1], F32)
        E("pool", "memset", [], [bepst], epst[:], EPS)
        EPS_AP = epst

        def load_x(src, row0, ntok):
            rot["x"] ^= 1
            xt, bxt = xts[rot["x"]]
            DMA("sp", "xl%d" % rot["x"], xt[0:ntok, :], src[row0:row0 + ntok, :], [], [bxt])
            return xt, bxt

        def proj_tok(w3, c0, n, ntok, t0=0, wl=(), us=None):
            pp, bpp = next_ab()
            for kc in range(8):
                uTs_, buTs_ = us if us is not None else (uT, buT)
                MM(pp[0:ntok, 0:n], uTs_[:, kc, t0:t0 + ntok], w3[:, kc, c0:c0 + n], kc == 0, kc == 7, [buTs_] + list(wl), [bpp])
            return pp, bpp

        phase_begin()
        WL = WL_1a
        WLb_pre = load_wg(w_b, 4, 0, [("b", 0, 1024)], 1024, 32832, "wb", kbase=6)["b"]
        WLout_pre = load_wg(w_out, 8, 0, [("o", 0, 1024)], 1024, 36928, "wo", kbase=7)["o"]
        win_a = wview(0, 8, 1536)
        wa3 = wview(12288, 4, 1024)
        KTm = WB[:, 16384:24576].rearrange("p (c k) -> p c k", c=4)
        Vm = WB[:, 24576:32768].rearrange("p (t c) -> p t c", t=16)
        stor_main = dict(KT=KTm, Vst=Vm, bKT=[fb("KT%d" % i) for i in range(16)], bV=[fb("V%d" % i) for i in range(16)])
        KTa, _ = cv("KTalt", [128, 4, 1152], BF16)
        Va, _ = cv("Valt", [128, 9, 512], BF16)
        stor_alt = dict(KT=KTa, Vst=Va, bKT=[fb("KTa%d" % i) for i in range(9)], bV=[fb("Va%d" % i) for i in range(9)])
        stg[:] = [cv("stg%d" % i, [128, 512], F32) for i in range(2)]
        qTzs = [cv("qTz%d" % i, [128, 8, 128], BF16) for i in range(2)]
        for qz, bqz in qTzs:
            E("pool", "memset", [], [bqz], qz[:], 0.0)
        Ktok, bKtok = cv("Ktok", [128, 8, 512], BF16)
        NSL = 5
        att = [dict(g=cv("ag%d" % i, [128, 520], F32), Pb=cv("aP%d" % i, [128, 520], F32),
                    a=cv("aa%d" % i, [128, 512], BF16),
                    aT=cv("aaT%d" % i, [128, 4, 128], BF16)) for i in range(NSL)]
        ONE_REG = nc.gpsimd.to_reg(1.0)
        zerob, bzerob = cv("zerob", [128, 520], BF16)
        E("pool", "memset", [], [bzerob], zerob[:], 0.0)
        for A_ in att:
            E("pool", "memset", [], [A_["g"][1]], A_["g"][0][:], 1.0)
        ya, bya = cv("ya", [128, 512], BF16)
        yaT, byaT = cv("yaT", [128, 4, 128], BF16)
        Ast, bAst = cv("Ast", [128, D], F32)
        WAbg, bWAbg = cv("WAbg", [128, 8, 512], BF16)
        pzv = [pz2[:, 0:512], pz2[:, 512:1024]]
        patv2 = [(pat, bpat), (pm[:].bitcast(BF16), bpm)]
        job_ctr = [0]
        Abuf = {}
        x1buf = {}
        seq_loaded = [-1]

        def prologue_pieces(tile, tno, stor, first_of_seq):
            (s, row0, ntok, ti) = tile
            kpos0 = ti * 128 if s == 0 else PAST
            ktile = ti if s == 0 else 8
            qTz, bqTz = qTzs[tno % 2]
            KT, Vst, bKT, bV = stor["KT"], stor["Vst"], stor["bKT"], stor["bV"]
            cx = dict(s=s, row0=row0, ntok=ntok, kpos0=kpos0, ktile=ktile, qTz=qTz, bqTz=bqTz, stor=stor)
            hold = {}

            xslot = tno % 2
            xt, bxt = xts[xslot]
            hold = {}

            def PL():
                if first_of_seq:
                    load_mod(s, 1, gate=False)
                    if s > 0:
                        si = s - 1
                        DMA("pool", "kvc", Vst[:, 0:8, :], cache_v[si].rearrange("(k p) c -> p k c", p=128), [], bV[0:8])
                        DMA("pool", "kvc", Ktok[:], cache_k[si].rearrange("(k p) c -> p k c", p=128), [], [bKtok])
                DMA("sp", "xl%d" % xslot, xt[0:ntok, :], xall[row0:row0 + ntok, :], [], [bxt])

            def PK():
                for kt in range(8):
                    for c in range(4):
                        TR(ptr[:, c * 128:(c + 1) * 128], Ktok[:, kt, c * 128:(c + 1) * 128], identb[:],
                           [bKtok, bidb], [bptr])
                    ACT(KT[:, :, kt * 128:(kt + 1) * 128], ptr[:, 0:512].rearrange("p (c t) -> p c t", c=4), AF.Copy,
                        [bptr], [bKT[kt]])

            def PA():
                ACT(T2[0:ntok, :], xt[0:ntok, :], AF.Square, [bxt], [bT2, bssq], accum_out=ssq[0:ntok, 0:1])
                rstd_pow(ssq[0:ntok, 2:3], ssq[0:ntok, 1:2], ssq[0:ntok, 0:1], ntok, 1, 1.0 / D, [bssq], [bssq])

            def PB():
                E("dve", "scalar_tensor_tensor", [bxt, bssq, bgmb], [bT1], out=T1[0:ntok, :], in0=xt[0:ntok, :],
                  scalar=ssq[0:ntok, 2:3], in1=gmb[0:ntok, :], op0=ALU.mult, op1=ALU.mult)

            def PC():
                E("pool", "tensor_tensor", [bT1, bshb], [bub], out=ub[0:ntok, :], in0=T1[0:ntok, :], in1=shb[0:ntok, :],
                  op=ALU.add)

            def PD():
                for kc in range(8):
                    TR(ptr[:, kc * 128: kc * 128 + ntok], ub[0:ntok, kc * 128:(kc + 1) * 128], identb[0:ntok, 0:ntok],
                       [bub, bidb], [bptr])
                ACT(uT[:, :, 0:ntok], ptr[:].rearrange("p (k t) -> p k t", k=8)[:, :, 0:ntok], AF.Copy, [bptr], [buT])

            def Q1():
                pp, bpp = next_ab()
                hold["q"] = (pp, bpp)
                for c in range(4):
                    for kc in range(8):
                        MM(pp[:, c * 128: c * 128 + ntok], win_a[:, kc, c * 128:(c + 1) * 128], uT[:, kc, 0:ntok],
                           kc == 0, kc == 7, [buT] + WL["q"], [bpp])
                ppv = pp[:].rearrange("p (c t) -> p c t", c=4)
                qv = qTz[:].rearrange("p (c two) t -> p c two t", two=2)
                ACT(qv[0:64, :, 0, 0:ntok], ppv[0:64, :, 0:ntok], AF.Copy, [bpp], [bqTz])
                E("dve", "tensor_copy", [bpp], [bqTz], out=qv[64:128, :, 1, 0:ntok], in_=ppv[64:128, :, 0:ntok])

            def K1():
                pp, bpp = next_ab()
                for c in range(4):
                    for kc in range(8):
                        MM(pp[:, c * 128: c * 128 + ntok], win_a[:, kc, 512 + c * 128: 512 + (c + 1) * 128], uT[:, kc, 0:ntok],
                           kc == 0, kc == 7, [buT] + WL["k"], [bpp])
                ACT(KT[:, :, kpos0:kpos0 + ntok], pp[:].rearrange("p (c t) -> p c t", c=4)[:, :, 0:ntok], AF.Copy,
                    [bpp], [bKT[ktile]])

            def K2():
                pp, bpp = proj_tok(win_a, 512, 512, ntok, wl=WL["k"])
                sg, bsg = next_stg()
                E("dve", "tensor_copy", [bpp], [bsg], out=sg[0:ntok, :], in_=pp[0:ntok, :])
                DMA("sp", "ko", k_all[row0:row0 + ntok, :], sg[0:ntok, :], [bsg], [])

            def V1():
                pp, bpp = proj_tok(win_a, 1024, 512, ntok, wl=WL["v"])
                sg, bsg = next_stg()
                E("dve", "tensor_copy", [bpp], [bsg], out=sg[0:ntok, :], in_=pp[0:ntok, :])
                ACT(Vst[0:ntok, ktile, :], pp[0:ntok, :], AF.Copy, [bpp], [bV[ktile]])
                DMA("sp", "vo", v_all[row0:row0 + ntok, :], sg[0:ntok, :], [bsg], [])

            pk = PK if (first_of_seq and s > 0) else None
            return cx, [PL, None, pk, PA, PB, PC, None, PD, None, Q1, None, K1, None, K2, None, V1]

        def attention_jobs(cx):
            ntok, kpos0, qTz, bqTz = cx["ntok"], cx["kpos0"], cx["qTz"], cx["bqTz"]
            KT, Vst, bKT, bV = cx["stor"]["KT"], cx["stor"]["Vst"], cx["stor"]["bKT"], cx["stor"]["bV"]
            nk = kpos0 + ntok
            nblk = (nk + 511) // 512
            jobs = []
            for h in range(8):
                prev = None
                for b in range(nblk - 1, -1, -1):
                    job_ctr[0] += 1
                    J = dict(h=h, b=b, slot=job_ctr[0] % NSL, zi=job_ctr[0] % 2, prev=prev,
                             first=(b == nblk - 1), last=(b == 0))
                    jobs.append(J)
                    prev = J

            def geo(J):
                kb0 = J["b"] * 512
                nkb = min(512, nk - kb0)
                kts = list(range(kb0 // 128, (kb0 + nkb + 127) // 128))
                return kb0, nkb, kts

            def S0(J):
                kb0, nkb, kts = geo(J)
                MM(pzv[J["zi"]][0:ntok, 0:nkb], qTz[:, J["h"], 0:ntok], KT[:, J["h"] // 2, kb0:kb0 + nkb], True, not J["first"],
                   [bqTz] + [bKT[k] for k in kts], [bpz[J["zi"]]])
                if J["first"]:
                    MM(pzv[J["zi"]][0:ntok, nkb - ntok:nkb], identb[0:ntok, 0:ntok], cmask[0:ntok, 0:ntok], False, True,
                       [bidb, bcmask], [bpz[J["zi"]]])

            def S1(J):
                kb0, nkb, kts = geo(J)
                g_, bg = att[J["slot"]]["g"]
                pz = pzv[J["zi"]]
                ACT(g_[0:ntok, 512 - nkb:512], pz[0:ntok, 0:nkb], AF.Sigmoid, [bpz[J["zi"]]], [bg], scale=-0.125)

            def S2(J):
                kb0, nkb, kts = geo(J)
                A_ = att[J["slot"]]
                (g_, bg), (Pb, bP) = A_["g"], A_["Pb"]
                if J["prev"] is None:
                    init = 1.0
                    rd = [bg, bzerob]
                else:
                    pPb, bpP = att[J["prev"]["slot"]]["Pb"]
                    pk = geo(J["prev"])[1]
                    init = pPb[0:ntok, 512 - pk:512 - pk + 1]
                    rd = [bg, bzerob, bpP]
                E("dve", "tensor_tensor_scan", rd, [bP], out=Pb[0:ntok, 512 - nkb:513][:, ::-1],
                  data0=g_[0:ntok, 512 - nkb:513][:, ::-1], data1=zerob[0:ntok, 0:nkb + 1], initial=init,
                  op0=ALU.mult, op1=ALU.add)

            def S3(J):
                pass

            def S4(J):
                kb0, nkb, kts = geo(J)
                A_ = att[J["slot"]]
                (Pb, bP), (a_, ba) = A_["Pb"], A_["a"]
                E("pool", "tensor_tensor", [bP], [ba], out=a_[0:ntok, 0:nkb], in0=Pb[0:ntok, 512 - nkb + 1:513],
                  in1=Pb[0:ntok, 512 - nkb:512], op=ALU.subtract)

            def S5(J):
                kb0, nkb, kts = geo(J)
                a_, ba = att[J["slot"]]["a"]
                pT, bpT = patv2[J["zi"]]
                for j, kt in enumerate(kts):
                    ksz = min(128, nk - kt * 128)
                    TR(pT[0:ksz, j * 128: j * 128 + ntok], a_[0:ntok, j * 128: j * 128 + ksz],
                       identb[0:ntok, 0:ntok], [ba, bidb], [bpT])

            def S6(J):
                kb0, nkb, kts = geo(J)
                aT_, baT = att[J["slot"]]["aT"]
                pT, bpT = patv2[J["zi"]]
                nsub = len(kts)
                pv = pT[:, 0:512].rearrange("p (j t) -> p j t", j=4)
                lastk = min(128, nk - kts[-1] * 128)
                if lastk == 128:
                    ACT(aT_[:, 0:nsub, 0:ntok], pv[:, 0:nsub, 0:ntok], AF.Copy, [bpT], [baT])
                else:
                    if nsub > 1:
                        ACT(aT_[:, 0:nsub - 1, 0:ntok], pv[:, 0:nsub - 1, 0:ntok], AF.Copy, [bpT], [baT])
                    ACT(aT_[0:lastk, nsub - 1, 0:ntok], pv[0:lastk, nsub - 1, 0:ntok], AF.Copy, [bpT], [baT])

            def S7(J):
                kb0, nkb, kts = geo(J)
                aT_, baT = att[J["slot"]]["aT"]
                h = J["h"]
                nsub = len(kts)
                for j, kt in enumerate(kts):
                    ksz = min(128, nk - kt * 128)
                    MM(py[0:ntok, h * 64:(h + 1) * 64], aT_[0:ksz, j, 0:ntok], Vst[0:ksz, kt, h * 64:(h + 1) * 64],
                       J["first"] and j == 0, J["last"] and j == nsub - 1, [baT, bV[kt]], [bpy])

            return jobs, [S0, S1, S2, S3, S4, S5, S6, S7]

        def epilogue_pieces(cx):
            ntok, row0 = cx["ntok"], cx["row0"]

            def E0():
                ACT(ya[0:ntok, :], py[0:ntok, :], AF.Copy, [bpy], [bya])

            def E1():
                for c in range(4):
                    TR(ptr[:, c * 128: c * 128 + ntok], ya[0:ntok, c * 128:(c + 1) * 128], identb[0:ntok, 0:ntok],
                       [bya, bidb], [bptr])
                ACT(yaT[:, :, 0:ntok], ptr[:, 0:512].rearrange("p (c t) -> p c t", c=4)[:, :, 0:ntok], AF.Copy, [bptr], [byaT])

            def E2():
                for n in range(2):
                    pp, bpp = next_ab()
                    for c in range(4):
                        MM(pp[0:ntok, :], yaT[:, c, 0:ntok], wa3[:, c, n * 512:(n + 1) * 512], c == 0, c == 3,
                           [byaT] + WL["a"], [bpp])
                    E("dve", "tensor_copy", [bpp], [bAst], out=Ast[0:ntok, n * 512:(n + 1) * 512], in_=pp[0:ntok, :])
                bAsc = Buf("Asc%d" % row0)
                DMA("sp", "Aw", A_sc[row0:row0 + ntok, :], Ast[0:ntok, :], [bAst], [bAsc])
                Abuf[row0] = bAsc

            return [E0, E1, E2]

        if "1a" in PH:
            items = []
            jbase = 0
            starts = []
            njs = []
            seq_idx = -1
            prev_s = None
            for li_, tile in enumerate(tiles):
                first_of_seq = (tile[0] != prev_s)
                if first_of_seq:
                    seq_idx += 1
                    prev_s = tile[0]
                stor = stor_main if seq_idx % 2 == 0 else stor_alt
                cx, pieces = prologue_pieces(tile, li_, stor, first_of_seq)
                if li_ == 0:
                    for i_, pf in enumerate(pieces):
                        if pf is not None:
                            items.append((-100 + i_, 9.0, pf))
                else:
                    pst = starts[li_ - 1] + 1
                    if first_of_seq:
                        pst = max(pst, stor.get("last_step", -1) + 1)
                    pend = starts[li_ - 1] + njs[li_ - 1] - 1
                    avail = max(1, pend - pst)
                    L_ = len(pieces)
                    for i_, pf in enumerate(pieces):
                        if pf is not None:
                            items.append((pst + (i_ * avail) // L_, 9.0 + i_ * 0.01, pf))
                jobs, stages = attention_jobs(cx)
                starts.append(jbase)
                njs.append(len(jobs))
                NSTG = len(stages)
                for ji, J in enumerate(jobs):
                    for k, Sf in enumerate(stages):
                        items.append((jbase + ji + k, float(NSTG - 1 - k), (lambda Sf=Sf, J=J: Sf(J))))
                last_step = jbase + len(jobs) - 1 + (NSTG - 1)
                stor["last_step"] = last_step
                ep = epilogue_pieces(cx)
                items.append((last_step, 0.5, ep[0]))
                items.append((last_step + 2, 8.5, ep[1]))
                items.append((last_step + 4, 8.6, ep[2]))
                jbase += len(jobs)
            if "0" in PH:
                nsteps = jbase + 8
                gap = max(14, (nsteps - 30) // 8)
                for bi, nch in enumerate(range(4, 12)):
                    t_ = 10 + bi * gap
                    items.append((t_, 9.5, (lambda nch=nch: mod_chunk_load(nch, WAbg, bWAbg, "wabg"))))

                    def comp(nch=nch):
                        pp, bpp = next_ab()
                        mod_chunk_compute(nch, WAbg, bWAbg, pp, bpp)
                    items.append((t_ + 10, 9.6, comp))
            order = sorted(range(len(items)), key=lambda i: (items[i][0], items[i][1], i))
            for i in order:
                items[i][2]()

        phase_begin()
        NB = 4104
        WL = load_wg(w_in, 8, 1536, [("mqk", 0, 1024), ("gt", 2048, 8), ("mv", 1024, 512), ("mo", 1536, 512),
                                     ("gb", 3080, 1024), ("ga", 2056, 1024)], NB, 0, "b")
        WL["b"] = WLb_pre
        WL["out"] = WLout_pre
        winb = wview(0, 8, NB)
        wb3 = wview(32832, 4, 1024)
        wo3 = wview(36928, 8, 1024)
        wcv, bwcv = cv("wcv", [128, 8, 4], F32)
        bcv, bbcv = cv("bcv", [128, 8], F32)
        mlg, bmlg = cv("mlg", [64, 512], F32)
        bifi, bbifi = cv("bifi", [4, 1], F32)
        biff, bbiff = cv("biff", [4, 1], F32)
        for j_ in range(4):
            DMA("sp", "cst", wcv[:, :, j_], w_conv[j_].rearrange("(c p) -> p c", p=128), [], [bwcv], allow_slow_non_contiguous=True)
        DMA("sp", "cst", bcv[:], b_conv[0].rearrange("(c p) -> p c", p=128), [], [bbcv], allow_slow_non_contiguous=True)
        DMA("sp", "cst", mlg[:], ml_norm_g[0:1, :].broadcast_to([64, 512]), [], [bmlg])
        DMA("sp", "cst", bifi[:], b_if[0:4, :], [], [bbifi])
        DMA("sp", "cst", biff[:], b_if[4:8, :], [], [bbiff])
        SC_ = 5
        CS_ = 2
        xs3 = [xts[0], xts[1], cv("xt2", [128, D], F32)]
        uT3 = [(uT, buT), cv("uT1", [128, 8, 128], BF16), cv("uT2", [128, 8, 128], BF16)]
        mqk2 = [cv("mqkT%d" % i, [128, 8, 128], BF16) for i in range(2)]
        xp2 = [cv("xp%d" % i, [128, 8, 131], F32) for i in range(2)]
        gl2 = [dict(li=cv("gli%d" % i, [4, 128], F32), sf=cv("gsf%d" % i, [4, 128], F32),
                    lf=cv("glf%d" % i, [4, 128], F32)) for i in range(2)]
        ybT2 = [cv("ybT%d" % i, [128, 4, 128], BF16) for i in range(2)]
        At, bAt = cv("At", [128, D], F32)
        sgt, bsgt = cv("sgt", [128, 512], F32)
        mg, bmg = cv("mg", [128, D], BF16)
        mT, bmT = cv("mT", [128, 8, 128], BF16)
        CTs = [cv("CT%d" % i, [128, 4, 129], F32) for i in range(3)]
        ggbs = [(ggb, bggb), cv("ggb1", [128, D], F32)]
        C0t, bC0t = At[:, 0:512].rearrange("p (h k) -> p h k", h=4), bAt
        mseq, bmseq = cv("mseq", [4, 40], F32)
        CK = []
        for i in range(CS_):
            d_ = {}
            for nm in ("bb", "rr", "mmt", "wg", "wi", "emt"):
                d_[nm] = cv("g_%s%d" % (nm, i), [4, 64], F32)
            d_["dec"] = cv("g_dec%d" % i, [4, 128], F32)
            d_["gsm"] = cv("gsm%d" % i, [4, 8], F32)
            d_["gtok"] = cv("gtok%d" % i, [128, 16], F32)
            d_["Wt"] = cv("Wt%d" % i, [64, 4, 64], F32)
            d_["vaug"] = cv("vaug%d" % i, [64, 4, 129], BF16)
            d_["vw"] = cv("vw%d" % i, [64, 4, 129], BF16)
            d_["smo"] = cv("smo%d" % i, [64, 512], F32)
            d_["ktok"] = cv("ktok%d" % i, [64, 4, 128], BF16)
            d_["ST"] = cv("ST%d" % i, [64, 4, 64], BF16)
            d_["qsT"] = cv("qsT%d" % i, [128, 4, 64], BF16)
            d_["CTb"] = cv("CTb%d" % i, [128, 4, 129], BF16)
            d_["hn"] = cv("hn%d" % i, [64, 4, 129], F32)
            d_["hs"] = cv("hs%d" % i, [64, 16], F32)
            d_["T1h"] = cv("T1h%d" % i, [64, 4, 128], F32)
            d_["ybt"] = cv("ybt%d" % i, [64, 512], BF16)
            E("pool", "memset", [], [d_["vaug"][1]], d_["vaug"][0][:], 1.0)
            CK.append(d_)
        patf = pat[:, 512:1024].bitcast(F32)
        chunk_ctr = [0]
        gchunk = [0]

        def out_state(s, CT, bCT, mcol_final):
            def f():
                for h in range(4):
                    TR(pa[:, h * 128:(h + 1) * 128], CT[:, h, 0:128], identf[:], [bCT, bidf], [bpa])
                E("dve", "tensor_copy", [bpa], [bC0t], out=C0t, in_=pa[:].rearrange("p (h k) -> p h k", h=4))
                DMA("sp", "sto", C_out[s].rearrange("h v k -> v h k"), C0t, [bC0t], [])
                DMA("sp", "sto", n_out[s].rearrange("h k -> k h"), CT[:, :, 128], [bCT], [], allow_slow_non_contiguous=True)
                DMA("sp", "sto", m_out[s:s + 1, :].rearrange("o h -> h o"), mseq[:, mcol_final:mcol_final + 1],
                    [bmseq], [], allow_slow_non_contiguous=True)
            return f

        def out_conv(s, xp_last):
            def f():
                xpl, bxpl, nt_l = xp_last
                for j_ in range(3):
                    DMA("sp", "sto", conv_out[s, j_].rearrange("(c p) -> p c", p=128), xpl[:, :, nt_l + j_], [bxpl], [],
                        allow_slow_non_contiguous=True)
            return f

        def sched_tile(items, lt, tix, tile, prev_xp, sq):
            (s, row0, ntok, ti) = tile
            xt, bxt = xs3[lt % 3]
            uTs, buTs = uT3[lt % 3]
            mqkT, bmqk = mqk2[lt % 2]
            xp, bxp = xp2[lt % 2]
            G_ = gl2[lt % 2]
            (li, bli), (sf, bsf), (lf, blf) = G_["li"], G_["sf"], G_["lf"]
            ybT, bybT = ybT2[lt % 2]
            nch = ntok // 64
            base = 2 * lt * SC_
            CT, bCT = sq["CT"]
            gg_, bgg_ = sq["gg"]
            T1v = T1[:].rearrange("p (c t) -> p c t", c=8)
            T2v = T2[:].rearrange("p (c t) -> p c t", c=8)

            def F0():
                DMA("sp", "xl%d" % (lt % 3), xt[0:ntok, :], xall[row0:row0 + ntok, :], [], [bxt])
                ACT(T2[0:ntok, :], xt[0:ntok, :], AF.Square, [bxt], [bT2, bssq], accum_out=ssq[0:ntok, 0:1])
                rstd_pow(ssq[0:ntok, 2:3], ssq[0:ntok, 1:2], ssq[0:ntok, 0:1], ntok, 1, 1.0 / D, [bssq], [bssq])
                E("dve", "scalar_tensor_tensor", [bxt, bssq, bgmb], [bT1], out=T1[0:ntok, :], in0=xt[0:ntok, :],
                  scalar=ssq[0:ntok, 2:3], in1=gmb[0:ntok, :], op0=ALU.mult, op1=ALU.mult)
                E("pool", "tensor_tensor", [bT1, bshb], [bub], out=ub[0:ntok, :], in0=T1[0:ntok, :], in1=shb[0:ntok, :],
                  op=ALU.add)

            def F1():
                for kc in range(8):
                    TR(ptr[:, kc * 128: kc * 128 + ntok], ub[0:ntok, kc * 128:(kc + 1) * 128], identb[0:ntok, 0:ntok],
                       [bub, bidb], [bptr])
                ACT(uTs[:, :, 0:ntok], ptr[:].rearrange("p (k t) -> p k t", k=8)[:, :, 0:ntok], AF.Copy, [bptr], [buTs])

            def F2(half):
                def f():
                    if half == 0 and prev_xp is not None:
                        pxp, bpxp, pnt = prev_xp
                        E("pool", "tensor_copy", [bpxp], [bxp], out=xp[:, :, 0:3], in_=pxp[:, :, pnt:pnt + 3])
                    pp, bpp = next_ab()
                    for c4 in range(4):
                        ch = half * 4 + c4
                        for kc in range(8):
                            MM(pp[:, c4 * 128: c4 * 128 + ntok], winb[:, kc, ch * 128:(ch + 1) * 128], uTs[:, kc, 0:ntok],
                               kc == 0, kc == 7, [buTs] + WL["mqk"], [bpp])
                    ACT(xp[:, half * 4:(half + 1) * 4, 3:3 + ntok], pp[:].rearrange("p (c t) -> p c t", c=4)[:, :, 0:ntok],
                        AF.Copy, [bpp], [bxp])
                return f

            def F3(p):
                def f():
                    for ch in (2 * p, 2 * p + 1):
                        E("dve", "tensor_scalar", [bxp, bwcv, bbcv], [bT1], out=T1v[:, ch, 0:ntok], in0=xp[:, ch, 0:ntok],
                          scalar1=wcv[:, ch, 0:1], scalar2=bcv[:, ch:ch + 1], op0=ALU.mult, op1=ALU.add)
                        for j in range(1, 4):
                            E("dve", "scalar_tensor_tensor", [bxp, bwcv, bT1], [bT1], out=T1v[:, ch, 0:ntok],
                              in0=xp[:, ch, j:j + ntok], scalar=wcv[:, ch, j:j + 1], in1=T1v[:, ch, 0:ntok],
                              op0=ALU.mult, op1=ALU.add)

                def fpool():
                    tmpf = ub[:, 0:256].bitcast(F32)
                    for ch in (2 * p, 2 * p + 1):
                        E("pool", "tensor_scalar", [bxp, bwcv, bbcv], [bT1], out=T1v[:, ch, 0:ntok], in0=xp[:, ch, 0:ntok],
                          scalar1=wcv[:, ch, 0:1], scalar2=bcv[:, ch:ch + 1], op0=ALU.mult, op1=ALU.add)
                        for j in range(1, 4):
                            E("pool", "tensor_scalar", [bxp, bwcv], [bub], out=tmpf[:, 0:ntok], in0=xp[:, ch, j:j + ntok],
                              scalar1=wcv[:, ch, j:j + 1], scalar2=0.0, op0=ALU.mult, op1=ALU.add)
                            E("pool", "tensor_tensor", [bub, bT1], [bT1], out=T1v[:, ch, 0:ntok], in0=tmpf[:, 0:ntok],
                              in1=T1v[:, ch, 0:ntok], op=ALU.add)
                return fpool if p >= 2 else f

            def F4():
                ACT(T2v[:, :, 0:ntok], T1v[:, :, 0:ntok], AF.Sigmoid, [bT1], [bT2])
                E("dve", "tensor_tensor", [bT1, bT2], [bmqk], out=mqkT[:, 0:4, 0:ntok], in0=T1v[:, 0:4, 0:ntok],
                  in1=T2v[:, 0:4, 0:ntok], op=ALU.mult)
                E("dve", "scalar_tensor_tensor", [bT1, bT2], [bmqk], out=mqkT[:, 4:8, 0:ntok], in0=T1v[:, 4:8, 0:ntok],
                  scalar=float(1.0 / np.sqrt(128.0)), in1=T2v[:, 4:8, 0:ntok], op0=ALU.mult, op1=ALU.mult)

            def F5():
                for kc in range(8):
                    MM(pm[0:4, 0:ntok], winb[:, kc, 2048:2052], uTs[:, kc, 0:ntok], kc == 0, kc == 7, [buTs] + WL["gt"], [bpm])
                ACT(li[:, 0:ntok], pm[0:4, 0:ntok], AF.Identity, [bpm, bbifi], [bli], bias=bifi[:])
                for kc in range(8):
                    MM(pm[0:4, 128:128 + ntok], winb[:, kc, 2052:2056], uTs[:, kc, 0:ntok], kc == 0, kc == 7,
                       [buTs] + WL["gt"], [bpm])
                ACT(sf[:, 0:ntok], pm[0:4, 128:128 + ntok], AF.Sigmoid, [bpm, bbiff], [bsf], bias=biff[:])
                ACT(lf[:, 0:ntok], sf[:, 0:ntok], AF.Ln, [bsf], [blf])

            fl = [(0, F0), (1, F1), (2, F2(0)), (3, F2(1)), (3.5, F5), (4, F3(0)), (5, F3(1)), (6, F3(2)), (7, F3(3)),
                  (8, F4)]
            for j, f in fl:
                items.append((base + j, f))

            def chunk_stages(c):
                cs = slice(c * 64, (c + 1) * 64)
                g = gchunk[0]
                gchunk[0] += 1
                K = CK[g % CS_]
                Kn = CK[(g + 1) % CS_]
                Kp = CK[(g - 1) % CS_]
                mc = chunk_ctr[0]
                chunk_ctr[0] += 1
                mcur = mseq[:, mc:mc + 1]
                (bb_, bbb), (rr, brr), (mmt, bmmt), (wg, bwg), (wi, bwi), (emt, bemt), (dec, bdec) = (
                    K["bb"], K["rr"], K["mmt"], K["wg"], K["wi"], K["emt"], K["dec"])
                (gsm, bgsm), (gtok, bgtok), (Wt, bWt), (vaug, bvaug), (vw, bvw), (smo, bsmo) = (
                    K["gsm"], K["gtok"], K["Wt"], K["vaug"], K["vw"], K["smo"])
                (ktok, bktok), (STt, bST), (qsT, bqsT), (CTb, bCTb), (hn, bhn), (hs, bhs) = (
                    K["ktok"], K["ST"], K["qsT"], K["CTb"], K["hn"], K["hs"])
                hh, bhh = hn[:, :, 0:128], bhn
                (T1h, bT1h), (ybt, bybt) = K["T1h"], K["ybt"]
                mx2 = mmt[:, 63:64]

                def G0():
                    E("dve", "tensor_tensor_scan", [blf, bones], [bbb], out=bb_[:, :], data0=onesb[0:4, 0:64], data1=lf[:, cs],
                      initial=0.0, op0=ALU.mult, op1=ALU.add)
                    E("dve", "tensor_tensor", [bli, bbb], [brr], out=rr[:, :], in0=li[:, cs], in1=bb_[:, :], op=ALU.subtract)
                    E("dve", "tensor_tensor_scan", [brr, bones, bmseq], [bmmt], out=mmt[:, :], data0=onesb[0:4, 0:64],
                      data1=rr[:, :], initial=mcur, op0=ALU.mult, op1=ALU.max)
                    E("dve", "tensor_scalar", [bmmt], [bgsm], out=gsm[:, 0:1], in0=mx2, scalar1=-1.0, scalar2=None, op0=ALU.mult)
                    E("dve", "tensor_tensor", [bmseq, bmmt], [bgsm], out=gsm[:, 1:2], in0=mcur, in1=mx2, op=ALU.subtract)
                    E("dve", "tensor_tensor", [bbb, bmmt], [bmseq], out=mseq[:, mc + 1:mc + 2], in0=bb_[:, 63:64],
                      in1=mx2, op=ALU.add)
                    E("dve", "tensor_tensor", [bbb, bmmt], [bemt], out=emt[:, :], in0=bb_[:, :], in1=mmt[:, :], op=ALU.add)

                def G1():
                    ACT(wg[:, :], rr[:, :], AF.Exp, [brr, bgsm], [bwg], bias=gsm[:, 0:1])
                    ACT(dec[:, :], zer[0:4, :], AF.Exp, [bzer, bgsm], [bdec], bias=gsm[:, 1:2])
                    ACT(wi[:, :], mmt[:, :], AF.Exp, [bmmt, bmseq], [bwi], scale=-1.0, bias=mcur)
                    ACT(emt[:, :], emt[:, :], AF.Exp, [bemt], [bemt], scale=-1.0)

                def G2():
                    TR(pm[0:64, 256:260], rr[:, :], identf[0:4, 0:4], [brr, bidf], [bpm])
                    TR(pm[0:64, 260:264], emt[:, :], identf[0:4, 0:4], [bemt, bidf], [bpm])
                    TR(pm[0:64, 264:268], wg[:, :], identf[0:4, 0:4], [bwg, bidf], [bpm])
                    TR(pm[0:128, 268:272], dec[:, :], identf[0:4, 0:4], [bdec, bidf], [bpm])
                    E("dve", "tensor_copy", [bpm], [bgtok], out=gtok[0:64, 0:12], in_=pm[0:64, 256:268])
                    E("dve", "tensor_copy", [bpm], [bgtok], out=gtok[:, 12:16], in_=pm[:, 268:272])

                def G3():
                    for h in range(4):
                        MM(py[0:64, h * 64:(h + 1) * 64], sel[:, h, 0:64], mmt[:, :], True, True, [bsel, bmmt], [bpy])
                    for h in range(4):
                        MM(py[:, 256 + h * 64: 256 + (h + 1) * 64], sel[:, h, :], wi[:, :], True, True, [bsel, bwi], [bpy])
                    for h in range(4):
                        ACT(Wt[:, h, :], py[0:64, h * 64:(h + 1) * 64], AF.Exp, [bpy, bgtok], [bWt], scale=-1.0,
                            bias=gtok[0:64, h:h + 1])
                    E("dve", "tensor_tensor", [bpy, bmqk], [bqsT], out=qsT[:], in0=py[:, 256:512].rearrange("p (h t) -> p h t", h=4),
                      in1=mqkT[:, 0:4, cs], op=ALU.mult)
                    E("pool", "affine_select", [bWt], [bWt], out=Wt[:], in_=Wt[:], pattern=[[0, 4], [1, 64]],
                      compare_op=ALU.is_ge, fill=0.0, base=0, channel_multiplier=-1)

                def V0a():
                    if nch == 2 and c == 1:
                        return
                    pp, bpp = next_ab()
                    m_ = ntok if nch == 2 else 64
                    for kc in range(8):
                        MM(pp[0:m_, :], uTs[:, kc, 0:m_], winb[:, kc, 1024:1536], kc == 0, kc == 7, [buTs] + WL["mv"], [bpp])
                    E("dve", "tensor_copy", [bpp], [bvaug], out=vaug[:, :, 0:128], in_=pp[0:64, :].rearrange("p (h d) -> p h d", h=4))
                    if nch == 2:
                        vn, bvn = Kn["vaug"]
                        ACT(vn[:, :, 0:128], pp[64:128, :].rearrange("p (h d) -> p h d", h=4), AF.Copy, [bpp], [bvn])

                def V0b():
                    if nch == 2 and c == 0:
                        return
                    pp, bpp = next_ab()
                    m_ = ntok if nch == 2 else 64
                    for kc in range(8):
                        MM(pp[0:m_, :], uTs[:, kc, 0:m_], winb[:, kc, 1536:2048], kc == 0, kc == 7, [buTs] + WL["mo"], [bpp])
                    if nch == 2:
                        sp_, bsp_ = Kp["smo"]
                        ACT(sp_[:], pp[0:64, :], AF.Sigmoid, [bpp], [bsp_])
                        E("pool", "tensor_tensor", [bsp_, bmlg], [bsp_], out=sp_[:], in0=sp_[:], in1=mlg[:], op=ALU.mult)
                        ACT(smo[:], pp[64:128, :], AF.Sigmoid, [bpp], [bsmo])
                    else:
                        ACT(smo[:], pp[0:64, :], AF.Sigmoid, [bpp], [bsmo])
                    E("pool", "tensor_tensor", [bsmo, bmlg], [bsmo], out=smo[:], in0=smo[:], in1=mlg[:], op=ALU.mult)

                def V0c():
                    for h in range(4):
                        TR(pat[0:64, h * 128:(h + 1) * 128], mqkT[:, 4 + h, cs], identb[:], [bmqk, bidb], [bpat])
                    E("dve", "tensor_copy", [bpat], [bktok], out=ktok[:], in_=pat[0:64, 0:512].rearrange("p (h d) -> p h d", h=4))

                def U():
                    E("pool", "tensor_copy", [bCT], [bCTb], out=CTb[:], in_=CT[:])
                    E("dve", "tensor_tensor", [bvaug, bgtok], [bvw], out=vw[:], in0=vaug[:],
                      in1=gtok[0:64, 8:12].unsqueeze(2).broadcast_to([64, 4, 129]), op=ALU.mult)
                    for h in range(4):
                        o0 = 512 * (h // 2) + 129 * (h % 2)
                        MM(pz2[:, o0:o0 + 129], ktok[:, h, :], vw[:, h, :], True, True, [bktok, bvw], [bpz[0], bpz[1]])
                    for h in range(4):
                        o0 = 512 * (h // 2) + 129 * (h % 2)
                        E("dve", "scalar_tensor_tensor", [bCT, bgtok, bpz[0], bpz[1]], [bCT], out=CT[:, h, :], in0=CT[:, h, :],
                          scalar=gtok[:, 12 + h:13 + h], in1=pz2[:, o0:o0 + 129], op0=ALU.mult, op1=ALU.add)

                def V1():
                    for h in range(4):
                        MM(patf[0:64, h * 64:(h + 1) * 64], mqkT[:, 4 + h, cs], mqkT[:, h, cs], True, True, [bmqk], [bpat])
                    E("dve", "tensor_tensor", [bpat, bWt], [bST], out=STt[:], in0=patf[0:64, 0:256].rearrange("p (h t) -> p h t", h=4),
                      in1=Wt[:], op=ALU.mult)

                def N0():
                    for h in range(4):
                        o0 = 512 * (h // 2) + 129 * (h % 2)
                        MM(pz2[0:64, o0:o0 + 129], STt[:, h, :], vaug[:, h, :], True, False, [bST, bvaug], [bpz[0], bpz[1]])
                        MM(pz2[0:64, o0:o0 + 129], qsT[:, h, :], CTb[:, h, :], False, True, [bqsT, bCTb], [bpz[0], bpz[1]])
                    E("dve", "tensor_copy", [bpz[0], bpz[1]], [bhn], out=hn[:, 0:2, :], in_=pz2[0:64, 0:258].rearrange("p (h d) -> p h d", h=2))
                    E("dve", "tensor_copy", [bpz[0], bpz[1]], [bhn], out=hn[:, 2:4, :], in_=pz2[0:64, 512:770].rearrange("p (h d) -> p h d", h=2))

                def N1():
                    den = hn[:, :, 128]
                    E("dve", "scalar_tensor_tensor", [bhn], [bhs], out=hs[:, 0:4], in0=den, scalar=-1.0, in1=den,
                      op0=ALU.mult, op1=ALU.max)
                    E("dve", "tensor_tensor", [bhs, bgtok], [bhs], out=hs[:, 4:8], in0=hs[:, 0:4], in1=gtok[0:64, 4:8], op=ALU.max)
                    E("dve", "reciprocal", [bhs], [bhs], out=hs[:, 8:12], in_=hs[:, 4:8])
                    E("dve", "tensor_tensor", [bhn, bhs], [bhh], out=hh, in0=hn[:, :, 0:128],
                      in1=hs[:, 8:12].unsqueeze(2).broadcast_to([64, 4, 128]), op=ALU.mult)
                    E("dve", "tensor_tensor", [bhh], [bT1h], out=T1h[:], in0=hh, in1=hh, op=ALU.mult)
                    E("dve", "tensor_reduce", [bT1h], [bhs], out=hs[:, 12:16], in_=T1h[:], axis=AX.X, op=ALU.add)

                def N2():
                    rstd_pow(hs[:, 12:16], hs[:, 12:16], hs[:, 12:16], 64, 4, 1.0 / 128, [bhs], [bhs])

                def N3():
                    E("dve", "tensor_tensor", [bhh, bhs], [bT1h], out=T1h[:], in0=hh,
                      in1=hs[:, 12:16].unsqueeze(2).broadcast_to([64, 4, 128]), op=ALU.mult)
                    E("dve", "tensor_tensor", [bT1h, bsmo], [bybt], out=ybt[:], in0=T1h[:].rearrange("p h d -> p (h d)"),
                      in1=smo[:], op=ALU.mult)

                def N4():
                    for h in range(4):
                        TR(ptr[:, h * 64:(h + 1) * 64], ybt[:, h * 128:(h + 1) * 128], identb[0:64, 0:64],
                           [bybt, bidb], [bptr])
                    ACT(ybT[:, :, cs], ptr[:, 0:256].rearrange("p (h t) -> p h t", h=4), AF.Copy, [bptr], [bybT])

                return [G0, G1, G2, G3, V0a, V0b, V0c, U, V1, N0, N1, N2, N3, N4]

            for c in range(nch):
                st_ = chunk_stages(c)
                b0 = base + 7 + c * SC_
                for j, f in enumerate(st_):
                    items.append((b0 + j, f))
            dbase = base + 7 + (nch - 1) * SC_ + 14

            def D0():
                DMA("sp", "Ar", At[0:ntok, :], A_sc[row0:row0 + ntok, :], [Abuf.get(row0)], [bAt])
                for n in range(2):
                    pB, bpB = next_ab()
                    for c in range(4):
                        MM(pB[0:ntok, :], ybT[:, c, 0:ntok], wb3[:, c, n * 512:(n + 1) * 512], c == 0, c == 3, [bybT] + WL["b"], [bpB])
                    pG, bpG = next_ab()
                    for kc in range(8):
                        MM(pG[0:ntok, :], uTs[:, kc, 0:ntok], winb[:, kc, 3080 + n * 512: 3080 + (n + 1) * 512], kc == 0, kc == 7,
                           [buTs] + WL["gb"], [bpG])
                    ACT(sgt[0:ntok, :], pG[0:ntok, :], AF.Sigmoid, [bpG], [bsgt])
                    E("dve", "tensor_tensor", [bsgt, bpB], [bT2], out=T2[0:ntok, n * 512:(n + 1) * 512], in0=sgt[0:ntok, :],
                      in1=pB[0:ntok, :], op=ALU.mult)

            def D1():
                for n in range(2):
                    pG, bpG = next_ab()
                    for kc in range(8):
                        MM(pG[0:ntok, :], uTs[:, kc, 0:ntok], winb[:, kc, 2056 + n * 512: 2056 + (n + 1) * 512], kc == 0, kc == 7,
                           [buTs] + WL["ga"], [bpG])
                    ACT(sgt[0:ntok, :], pG[0:ntok, :], AF.Sigmoid, [bpG], [bsgt])
                    E("pool", "tensor_tensor", [bsgt, bAt], [bsgt], out=sgt[0:ntok, :], in0=sgt[0:ntok, :],
                      in1=At[0:ntok, n * 512:(n + 1) * 512], op=ALU.mult)
                    E("pool", "tensor_tensor", [bsgt, bT2], [bmg], out=mg[0:ntok, n * 512:(n + 1) * 512], in0=sgt[0:ntok, :],
                      in1=T2[0:ntok, n * 512:(n + 1) * 512], op=ALU.add)

            def D2():
                for kc in range(8):
                    TR(ptr[:, kc * 128: kc * 128 + ntok], mg[0:ntok, kc * 128:(kc + 1) * 128], identb[0:ntok, 0:ntok],
                       [bmg, bidb], [bptr])
                ACT(mT[:, :, 0:ntok], ptr[:].rearrange("p (k t) -> p k t", k=8)[:, :, 0:ntok], AF.Copy, [bptr], [bmT])

            def D3():
                for n in range(2):
                    pO, bpO = next_ab()
                    for kc in range(8):
                        MM(pO[0:ntok, :], mT[:, kc, 0:ntok], wo3[:, kc, n * 512:(n + 1) * 512], kc == 0, kc == 7, [bmT] + WL["out"], [bpO])
                    E("dve", "tensor_tensor", [bpO, bgg_], [bT2], out=T2[0:ntok, n * 512:(n + 1) * 512], in0=pO[0:ntok, :],
                      in1=gg_[0:ntok, n * 512:(n + 1) * 512], op=ALU.mult)
                E("pool", "tensor_tensor", [bT2, bxt], [bxt], out=xt[0:ntok, :], in0=T2[0:ntok, :], in1=xt[0:ntok, :], op=ALU.add)
                bx1 = Buf("x1sc%d" % row0)
                x1buf[row0] = bx1
                DMA("sp", "x1w%d" % (lt % 3), x1_sc[row0:row0 + ntok, :], xt[0:ntok, :], [bxt], [bx1])
                ACT(T2[0:ntok, :], xt[0:ntok, :], AF.Square, [bxt], [bT2, bssq], accum_out=ssq[0:ntok, 0:1])
                E("pool", "tensor_scalar", [bssq], [bssq], out=ssq[0:ntok, 1:2], in0=ssq[0:ntok, 0:1], scalar1=1.0 / D,
                  scalar2=EPS, op0=ALU.mult, op1=ALU.add)
                E("pool", "tensor_tensor", [bssq, bmhalf], [brstd2], out=rstd2[0:ntok, tix:tix + 1], in0=ssq[0:ntok, 1:2],
                  in1=mhalf[0:ntok, 0:1], op=ALU.pow)

            for j, f in enumerate([D0, D1, D2, D3]):
                items.append((dbase + j, f))
            sq["lastU"] = base + 7 + (nch - 1) * SC_ + 7
            sq["lastD3"] = dbase + 3
            sq["lastF4"] = base + 8
            return (xp, bxp, ntok)

        if "1b" in PH:
            seqs = []
            for tix, tile in enumerate(tiles):
                if not seqs or seqs[-1][0] != tile[0]:
                    seqs.append((tile[0], []))
                seqs[-1][1].append((tix, tile))
            items = []
            lt = 0
            d3_hist = []
            for k_, (s, tl) in enumerate(seqs):
                CTk, bCTk = CTs[k_ % 3]
                ggk = ggbs[k_ % 2]
                sq = dict(CT=(CTk, bCTk), gg=ggk)
                chunk_ctr[0] += 1
                mcol = chunk_ctr[0]
                base0 = 2 * lt * SC_
                xp0, bxp0 = xp2[lt % 2]

                def init_seq(s=s, CTk=CTk, bCTk=bCTk, mcol=mcol, xp0=xp0, bxp0=bxp0, first=(k_ == 0)):
                    load_mod(s, 1, gate=False)
                    if s == 0:
                        E("pool", "memset", [], [bCTk], CTk[:], 0.0)
                        if first:
                            E("pool", "memset", [], [bmseq], mseq[:], 0.0)
                        E("pool", "memset", [], [bxp0], xp0[:, :, 0:3], 0.0)
                    else:
                        si = s - 1
                        DMA("sp", "st", C0t, C0[si].rearrange("h v k -> v h k"), [], [bC0t])
                        for h in range(4):
                            TR(pa[:, h * 128:(h + 1) * 128], C0t[:, h, :], identf[:], [bC0t, bidf], [bpa])
                        E("dve", "tensor_copy", [bpa], [bCTk], out=CTk[:, :, 0:128], in_=pa[:].rearrange("p (h v) -> p h v", h=4))
                        DMA("sp", "st", CTk[:, :, 128], n0[si].rearrange("h k -> k h"), [], [bCTk], allow_slow_non_contiguous=True)
                        DMA("sp", "st", mseq[:, mcol:mcol + 1], m0[si:si + 1, :].rearrange("o h -> h o"), [], [bmseq],
                            allow_slow_non_contiguous=True)
                        for j_ in range(3):
                            DMA("sp", "st", xp0[:, :, j_], conv0[si, j_].rearrange("(c p) -> p c", p=128), [], [bxp0],
                                allow_slow_non_contiguous=True)

                items.append((base0 - 0.5, init_seq))
                gstep = base0 - 0.4
                if k_ >= 2:
                    gstep = max(gstep, d3_hist[k_ - 2] + 0.5)
                items.append((gstep, (lambda s=s, ggk=ggk: load_mod(s, 1, front=False, gdst=ggk))))
                prev_xp = None
                for (tix, tile) in tl:
                    prev_xp = sched_tile(items, lt, tix, tile, prev_xp, sq)
                    lt += 1
                items.append((sq["lastF4"] + 0.5, out_conv(s, prev_xp)))
                items.append((sq["lastU"] + 0.5, out_state(s, CTk, bCTk, chunk_ctr[0])))
                d3_hist.append(sq["lastD3"])
            order = sorted(range(len(items)), key=lambda i: (items[i][0], i))
            for i in order:
                items[i][1]()

        phase_begin()
        pab[:] = [(pa, bpa), (pb, bpb), (py, bpy), (pm, bpm), (pz2[:, 0:512], bpz[0]), (pz2[:, 512:1024], bpz[1])]
        WL = {}
        for ci_, c0_ in enumerate(range(0, DFF, 512)):
            n_ = min(512, DFF - c0_)
            WL["g%d" % ci_] = load_wg(w_ff_gate, 8, 0, [("g", c0_, n_)], DFF, 0, "fg%d" % ci_, kbase=2 * ci_)["g"]
            WL["u%d" % ci_] = load_wg(w_ff_up, 8, 0, [("u", c0_, n_)], DFF, 22528, "fu%d" % ci_, kbase=2 * ci_ + 1)["u"]
        wg3 = wview(0, 8, DFF)
        wu3 = wview(22528, 8, DFF)
        wdn, _ = cv("wdn", [128, 22 * D], BF16)
        xs3p = [xts[0], xts[1], cv("xt2p", [128, D], F32)]
        wd3 = wdn.rearrange("p (k c) -> p k c", k=22)
        WLd = {}
        dvd = w_ff_down.rearrange("(k p) n -> p k n", p=128)
        for n_ in range(2):
            bw = fb("Wd%d" % n_)
            DMA("pool", "Wg%d" % (12 + n_), wd3[:, :, n_ * 512:(n_ + 1) * 512], dvd[:, :, n_ * 512:(n_ + 1) * 512], [], [bw])
            WLd[n_] = [bw]
        sgts = [cv("sgt%d" % i, [128, 512], F32) for i in range(1)] * 2
        hb2 = [cv("hbuf%d" % i, [128, DFF], BF16) for i in range(2)]
        hT, bhT = cv("hT", [128, 22, 128], BF16)
        Tq, bTq = cv("O2_0", [128, D], F32)
        sq2 = [cv("sq2_%d" % i, [128, 4], F32) for i in range(2)]
        uT2a = [(uT, buT), cv("uT2a", [128, 8, 128], BF16)]
        ggbs2 = [(ggb, bggb), cv("ggb2", [128, D], F32)]
        fgb, bfgb = cv("fgb", [128, D], F32)
        DMA("sp", "cst", fgb[:], final_g[0:1, :].broadcast_to([128, D]), [], [bfgb])
        sgc = [0]
        seq2a = [-1]
        gsel = {}
        trb = [(ptr, bptr), (pat, bpat)]

        def ld2(tix):
            (s, row0, ntok, ti) = tiles[tix]
            xt, bxt = xs3p[tix % 3]
            DMA("sp", "xl%d" % (tix % 3), xt[0:ntok, :], x1_sc[row0:row0 + ntok, :], [x1buf.get(row0)], [bxt])

        def front2(tix):
            (s, row0, ntok, ti) = tiles[tix]
            if s != seq2a[0]:
                seq2a[0] = s
                load_mod(s, 2, gate=False)
            xt, bxt = xs3p[tix % 3]
            rms_to_uT(xt, bxt, ntok, rstd_ap=rstd2[0:ntok, tix:tix + 1], dst=uT2a[tix % 2])

        def up2(tix):
            (s, row0, ntok, ti) = tiles[tix]
            hbuf, bhbuf = hb2[tix % 2]
            for n0_ in range(0, DFF, 512):
                sgc[0] ^= 1
                sgt, bsgt = sgts[sgc[0]]
                n = min(512, DFF - n0_)
                pG, bpG = proj_tok(wg3, n0_, n, ntok, wl=WL["g%d" % (n0_ // 512)], us=uT2a[tix % 2])
                pU, bpU = proj_tok(wu3, n0_, n, ntok, wl=WL["u%d" % (n0_ // 512)], us=uT2a[tix % 2])
                ACT(sgt[0:ntok, 0:n], pG[0:ntok, 0:n], AF.Silu, [bpG], [bsgt])
                E("dve", "tensor_tensor", [bsgt, bpU], [bhbuf], out=hbuf[0:ntok, n0_:n0_ + n], in0=sgt[0:ntok, 0:n],
                  in1=pU[0:ntok, 0:n], op=ALU.mult)

        seqord = []
        for t_ in tiles:
            if t_[0] not in seqord:
                seqord.append(t_[0])

        def down2(tix):
            (s, row0, ntok, ti) = tiles[tix]
            k_ = seqord.index(s)
            gg2, bgg2 = ggbs2[k_ % 2]
            if s not in gsel:
                gsel[s] = True
                load_mod(s, 2, front=False, gdst=(gg2, bgg2))
            hbuf, bhbuf = hb2[tix % 2]
            xt, bxt = xs3p[tix % 3]
            sq_, bsq_ = sq2[tix % 2]
            for gi, g0 in enumerate(range(0, 22, 8)):
                ng = min(8, 22 - g0)
                pt_, bpt_ = trb[gi % 2]
                for j in range(ng):
                    k = g0 + j
                    TR(pt_[:, j * 128: j * 128 + ntok], hbuf[0:ntok, k * 128:(k + 1) * 128], identb[0:ntok, 0:ntok],
                       [bhbuf, bidb], [bpt_])
                ACT(hT[:, g0:g0 + ng, 0:ntok], pt_[:].rearrange("p (k t) -> p k t", k=8)[:, 0:ng, 0:ntok], AF.Copy,
                    [bpt_], [bhT])
            for n in range(2):
                pD, bpD = next_ab()
                for k in range(22):
                    MM(pD[0:ntok, :], hT[:, k, 0:ntok], wd3[:, k, n * 512:(n + 1) * 512], k == 0, k == 21, [bhT] + WLd[n], [bpD])
                E("dve", "tensor_tensor", [bpD, bgg2], [bTq], out=Tq[0:ntok, n * 512:(n + 1) * 512], in0=pD[0:ntok, :],
                  in1=gg2[0:ntok, n * 512:(n + 1) * 512], op=ALU.mult)
            E("pool", "tensor_tensor", [bTq, bxt], [bxt], out=xt[0:ntok, :], in0=Tq[0:ntok, :], in1=xt[0:ntok, :], op=ALU.add)
            ACT(Tq[0:ntok, :], xt[0:ntok, :], AF.Square, [bxt], [bTq, bsq_], accum_out=sq_[0:ntok, 0:1])
            rstd_pow(sq_[0:ntok, 2:3], sq_[0:ntok, 1:2], sq_[0:ntok, 0:1], ntok, 1, 1.0 / D, [bsq_], [bsq_])
            E("dve", "scalar_tensor_tensor", [bxt, bsq_, bfgb], [bTq], out=Tq[0:ntok, :], in0=xt[0:ntok, :],
              scalar=sq_[0:ntok, 2:3], in1=fgb[0:ntok, :], op0=ALU.mult, op1=ALU.mult)
            DMA("sp", "yo", y_all[row0:row0 + ntok, :], Tq[0:ntok, :], [bTq], [])

        P2ON = ("2a" in PH) or ("2b" in PH)
        if P2ON:
            NT_ = len(tiles)
            ld2(0)
            if NT_ > 1:
                ld2(1)
            front2(0)
            if NT_ > 1:
                front2(1)
            up2(0)
            for tix in range(NT_):
                if tix + 2 < NT_:
                    ld2(tix + 2)
                if tix + 1 < NT_:
                    up2(tix + 1)
                if tix + 2 < NT_:
                    front2(tix + 2)
                down2(tix)

        P.lower(lambda name: st.enter_context(nc.semaphore(name)))
        build_program.stats = P.stats
    return nc


_CACHE = {}


def kernel(**inp):
    f = lambda a: np.ascontiguousarray(np.asarray(a, dtype=np.float32))
    if "nc" not in _CACHE:
        _CACHE["nc"] = build_program()
    nc = _CACHE["nc"]
    x_prompt = f(inp["x_prompt"]); x_sample = f(inp["x_sample"])
    c_prompt = f(inp["c_prompt"]); c_sample = f(inp["c_sample"])
    ck = f(inp["cache_sb_k"])[0].reshape(16, PAST, 512)
    cv = f(inp["cache_sb_v"])[0].reshape(16, PAST, 512)
    sC = f(inp["state_mlstm_C"])[0]; sn = f(inp["state_mlstm_n"])[0]; sm = f(inp["state_mlstm_m"])[0]
    sconv = f(inp["state_conv"])[0]
    shared = {
        "norm1_g": f(inp["norm1_g"]).reshape(1, D), "norm2_g": f(inp["norm2_g"]).reshape(1, D),
        "w_ada": f(inp["w_ada"])[0], "b_ada": f(inp["b_ada"]).reshape(1, 6 * D),
        "w_in": f(inp["w_in"])[0], "b_if": f(inp["b_if"]).reshape(8, 1),
        "w_conv": f(inp["w_conv"])[0], "b_conv": f(inp["b_conv"]).reshape(1, D),
        "ml_norm_g": f(inp["ml_norm_g"]).reshape(1, 512),
        "w_a": f(inp["w_a"])[0], "w_b": f(inp["w_b"])[0], "w_out": f(inp["w_out"])[0],
        "w_ff_gate": f(inp["w_ff_gate"])[0], "w_ff_up": f(inp["w_ff_up"])[0], "w_ff_down": f(inp["w_ff_down"])[0],
        "final_g": f(inp["final_g"]).reshape(1, D),
    }
    in_maps = []
    for i in range(8):
        m = dict(shared)
        m["xall"] = np.concatenate([x_prompt[i], x_sample[2 * i], x_sample[2 * i + 1]], axis=0)
        m["c3"] = np.stack([c_prompt[i], c_sample[2 * i], c_sample[2 * i + 1]], axis=0)
        m["cache_k"] = ck[2 * i:2 * i + 2]
        m["cache_v"] = cv[2 * i:2 * i + 2]
        m["C0"] = sC[2 * i:2 * i + 2]
        m["n0"] = sn[2 * i:2 * i + 2]
        m["m0"] = sm[2 * i:2 * i + 2]
        m["conv0"] = sconv[2 * i:2 * i + 2]
        in_maps.append({k: np.ascontiguousarray(v) for k, v in m.items()})
    res = run_bass_kernel_spmd(nc, in_maps, core_ids=list(range(8)))
    R = res.results
    y_p = np.stack([R[i]["y_all"][:SEQ] for i in range(8)])
    y_s = np.stack([R[i]["y_all"][SEQ + j * LS: SEQ + (j + 1) * LS] for i in range(8) for j in range(2)])
    k_p = np.stack([R[i]["k_all"][:SEQ] for i in range(8)]).reshape(1, 8, SEQ, 8, 64)
    v_p = np.stack([R[i]["v_all"][:SEQ] for i in range(8)]).reshape(1, 8, SEQ, 8, 64)
    k_s = np.stack([R[i]["k_all"][SEQ + j * LS: SEQ + (j + 1) * LS] for i in range(8) for j in range(2)]).reshape(1, 16, LS, 8, 64)
    v_s = np.stack([R[i]["v_all"][SEQ + j * LS: SEQ + (j + 1) * LS] for i in range(8) for j in range(2)]).reshape(1, 16, LS, 8, 64)
    C_p = np.stack([R[i]["C_out"][0] for i in range(8)])[None]
    n_p = np.stack([R[i]["n_out"][0] for i in range(8)])[None]
    m_p = np.stack([R[i]["m_out"][0] for i in range(8)])[None]
    cv_p = np.stack([R[i]["conv_out"][0] for i in range(8)])[None]
    C_s = np.stack([R[i]["C_out"][1 + j] for i in range(8) for j in range(2)])[None]
    n_s = np.stack([R[i]["n_out"][1 + j] for i in range(8) for j in range(2)])[None]
    m_s = np.stack([R[i]["m_out"][1 + j] for i in range(8) for j in range(2)])[None]
    cv_s = np.stack([R[i]["conv_out"][1 + j] for i in range(8) for j in range(2)])[None]
    outs = (y_p, y_s, k_p, v_p, C_p, n_p, m_p, cv_p, k_s, v_s, C_s, n_s, m_s, cv_s)
    return tuple(np.ascontiguousarray(o, dtype=np.float32) for o in outs)
```
